# Optimizing a Trainium2 kernel written in Bass

```python
import jax
import jax.numpy as jnp
from jax import lax
import numpy as np

D_MODEL = 1024
BATCH = 8
SEQ = 2048
DEPTH = 1
DEC_BATCH = 128
DEC_SEQ = 4
PAST_LEN = 8192
PAGE_SIZE = 128

A_GROUPS = 8
A_GROUP_DIM = 64
A_WIDTH = A_GROUPS * A_GROUP_DIM
CHUNK = 128
MLA_HEADS = 8
Q_LORA = 256
KV_LORA = 256
QK_NOPE = 64
QK_ROPE = 32
QK_HEAD = QK_NOPE + QK_ROPE
V_HEAD = 64
B_WIDTH = MLA_HEADS * V_HEAD
MIX_WIDTH = A_WIDTH + B_WIDTH
ROPE_THETA = 10000.0
Q_BLOCK = 128
OFF_CQ = 2 * A_WIDTH
OFF_CKV = OFF_CQ + Q_LORA
OFF_KPE = OFF_CKV + KV_LORA
N_IN = OFF_KPE + QK_ROPE
MEM_LEN = 256
MEM_HEADS = 4
MEM_HEAD_DIM = 128
MEM_WIDTH = MEM_HEADS * MEM_HEAD_DIM
D_FF = 2816
CONV_W = 3
EPS = 1e-6
NEG_INF = -1e30

kernel_name = 'hymba_gmlp_mla_memxattn_convffn_step'


def rms_norm(x, g):
    xf = x.astype(jnp.float32)
    y = xf * lax.rsqrt(jnp.mean(xf * xf, axis=-1, keepdims=True) + EPS)
    return (y * g.astype(jnp.float32)).astype(x.dtype)


def layer_norm(x, g, b):
    xf = x.astype(jnp.float32)
    xc = xf - jnp.mean(xf, axis=-1, keepdims=True)
    y = xc * lax.rsqrt(jnp.mean(xc * xc, axis=-1, keepdims=True) + EPS)
    return (y * g.astype(jnp.float32) + b.astype(jnp.float32)).astype(x.dtype)


def split_in(z):
    return (z[..., :A_WIDTH], z[..., A_WIDTH:OFF_CQ], z[..., OFF_CQ:OFF_CKV],
            z[..., OFF_CKV:OFF_KPE], z[..., OFF_KPE:N_IN])


def qk_rope(x, pos):
    half = QK_ROPE // 2
    inv_freq = ROPE_THETA ** (-jnp.arange(half, dtype=jnp.float32) / half)
    ang = pos.astype(jnp.float32)[:, None] * inv_freq[None, :]
    cos = jnp.cos(ang)[:, None, :]
    sin = jnp.sin(ang)[:, None, :]
    x1 = x[..., QK_NOPE:QK_NOPE + half].astype(jnp.float32)
    x2 = x[..., QK_NOPE + half:].astype(jnp.float32)
    rot = jnp.concatenate([x1 * cos - x2 * sin, x2 * cos + x1 * sin], axis=-1).astype(x.dtype)
    return jnp.concatenate([x[..., :QK_NOPE], rot], axis=-1)


def chunk_gmlp(u, v, w_s, b_s, ln_g, ln_b):
    n, t, _ = u.shape
    c = min(t, CHUNK)
    vg = layer_norm(jax.nn.gelu(v).reshape(n, t // c, c, A_GROUPS, A_GROUP_DIM), ln_g, ln_b)
    w = jnp.where(jnp.tril(jnp.ones((c, c), dtype=bool)), w_s[:, :c, :c], 0.0).astype(vg.dtype)
    mixed = jnp.einsum('gts,nmsgd->nmtgd', w, vg) + b_s[:, :c].T[:, :, None].astype(vg.dtype)
    out = jax.nn.gelu(u) * mixed.reshape(n, t, A_WIDTH)
    return out, vg.reshape(n, t, A_GROUPS, A_GROUP_DIM)


def mla_queries(cq, pos, g_q_a, w_uq, g_qk_q):
    n, t, _ = cq.shape
    q = (rms_norm(cq, g_q_a) @ w_uq).reshape(n, t, MLA_HEADS, QK_HEAD)
    return qk_rope(rms_norm(q, g_qk_q), pos)


def mla_keys_values(ckv_n, kpe, pos, w_uk, w_uv, g_qk_k):
    lead = ckv_n.shape[:-1]
    k_nope = (ckv_n @ w_uk).reshape(*lead, MLA_HEADS, QK_NOPE)
    v = (ckv_n @ w_uv).reshape(*lead, MLA_HEADS, V_HEAD)
    k_pe = jnp.broadcast_to(kpe[..., None, :], (*lead, MLA_HEADS, QK_ROPE))
    k = qk_rope(rms_norm(jnp.concatenate([k_nope, k_pe], axis=-1), g_qk_k), pos)
    return k, v


def masked_attend(q, k, v, q_pos, k_pos):
    s = jnp.einsum('...qhd,...khd->...hqk', q, k).astype(jnp.float32) * (QK_HEAD ** -0.5)
    s = jnp.where(k_pos[None, :] <= q_pos[:, None], s, NEG_INF)
    p = jax.nn.softmax(s, axis=-1).astype(v.dtype)
    return jnp.einsum('...hqk,...khd->...qhd', p, v)


def mla_prompt_attend(q, k, v):
    n, s = q.shape[:2]
    nb = s // Q_BLOCK
    pos = jnp.arange(s)
    qb = q.reshape(n, nb, Q_BLOCK, MLA_HEADS, QK_HEAD).swapaxes(0, 1)

    def block(args):
        q_blk, start = args
        return masked_attend(q_blk, k, v, start + jnp.arange(Q_BLOCK), pos)

    o = lax.map(block, (qb, jnp.arange(nb) * Q_BLOCK))
    return o.swapaxes(0, 1).reshape(n, s, B_WIDTH)


def mla_sample_attend(q, ckv_n, kpe, pool_ckv, pool_kpe, page_table, w_uk, w_uv, g_qk_k):
    n, t = q.shape[:2]
    q_pos = PAST_LEN + jnp.arange(t)
    k_pos = jnp.arange(PAST_LEN + t)

    def one(args):
        pages, q1, c1, pe1 = args
        c_all = jnp.concatenate([pool_ckv[pages].reshape(PAST_LEN, KV_LORA), c1.astype(pool_ckv.dtype)], axis=0)
        pe_all = jnp.concatenate([pool_kpe[pages].reshape(PAST_LEN, QK_ROPE), pe1.astype(pool_kpe.dtype)], axis=0)
        k, v = mla_keys_values(c_all, pe_all, k_pos, w_uk, w_uv, g_qk_k)
        return masked_attend(q1, k, v, q_pos, k_pos)

    o = lax.map(one, (page_table, q, ckv_n, kpe))
    return o.reshape(n, t, B_WIDTH)


def merge_out(a_out, b_out, g_out_a, g_out_b, w_o):
    return jnp.concatenate([rms_norm(a_out, g_out_a), rms_norm(b_out, g_out_b)], axis=-1) @ w_o


def mem_kv(mem, g_mem_in, w_mk, w_mv, g_mk):
    n, m, _ = mem.shape
    hm = rms_norm(mem, g_mem_in)
    k = rms_norm((hm @ w_mk).reshape(n, m, MEM_HEADS, MEM_HEAD_DIM), g_mk)
    v = (hm @ w_mv).reshape(n, m, MEM_HEADS, MEM_HEAD_DIM)
    return k, v


def mem_attend(h, mk, mv, w_mq, g_mq, w_mo):
    n, t, _ = h.shape
    q = rms_norm((h @ w_mq).reshape(n, t, MEM_HEADS, MEM_HEAD_DIM), g_mq)
    s = jnp.einsum('nthd,nmhd->nhtm', q, mk.astype(q.dtype)).astype(jnp.float32) * (MEM_HEAD_DIM ** -0.5)
    p = jax.nn.softmax(s, axis=-1).astype(q.dtype)
    o = jnp.einsum('nhtm,nmhd->nthd', p, mv.astype(q.dtype)).reshape(n, t, MEM_WIDTH)
    return o @ w_mo


def conv_ffn(h, buf, w_up, w_conv, b_conv, w_down):
    t = h.shape[1]
    gate, val = jnp.split(h @ w_up, 2, axis=-1)
    gpad = jnp.concatenate([buf.astype(gate.dtype), gate], axis=1)
    conv = b_conv + sum(gpad[:, j:j + t] * w_conv[j] for j in range(CONV_W))
    return (jax.nn.silu(conv) * val) @ w_down, gpad[:, t:]


def setup_inputs(seed: int = 0) -> dict:
    key = jax.random.key(seed)
    ks = iter(jax.random.split(key, 48))

    def nrm(shape, scale=1.0):
        return jax.random.normal(next(ks), shape, jnp.float32) * scale

    def gain(shape):
        return 1.0 + 0.02 * jax.random.normal(next(ks), shape, jnp.float32)

    n_pages = PAST_LEN // PAGE_SIZE
    n_phys = (DEC_BATCH * n_pages * 5) // 4
    L = DEPTH
    inputs = {
        'x_prompt': nrm((BATCH, SEQ, D_MODEL)),
        'x_sample': nrm((DEC_BATCH, DEC_SEQ, D_MODEL)),
        'cache_ckv': nrm((L, n_phys, PAGE_SIZE, KV_LORA)),
        'cache_kpe': nrm((L, n_phys, PAGE_SIZE, QK_ROPE)),
        'cache_mem_k': nrm((L, DEC_BATCH, MEM_LEN, MEM_HEADS, MEM_HEAD_DIM)),
        'cache_mem_v': nrm((L, DEC_BATCH, MEM_LEN, MEM_HEADS, MEM_HEAD_DIM)),
        'state_ffn_conv': nrm((L, DEC_BATCH, CONV_W - 1, D_FF)),
        'page_table': jax.random.permutation(next(ks), n_phys)[:DEC_BATCH * n_pages]
                      .reshape(DEC_BATCH, n_pages).astype(jnp.int32),
        'mem_prompt': nrm((BATCH, MEM_LEN, D_MODEL)),
        'g_mix': gain((L, D_MODEL)),
        'w_in': nrm((L, D_MODEL, N_IN), D_MODEL ** -0.5),
        'ln_v_g': gain((L, A_GROUPS, A_GROUP_DIM)),
        'ln_v_b': nrm((L, A_GROUPS, A_GROUP_DIM), 0.02),
        'w_s': nrm((L, A_GROUPS, CHUNK, CHUNK), CHUNK ** -0.5),
        'b_s': gain((L, A_GROUPS, CHUNK)),
        'g_q_a': gain((L, Q_LORA)),
        'w_uq': nrm((L, Q_LORA, MLA_HEADS * QK_HEAD), Q_LORA ** -0.5),
        'g_kv_a': gain((L, KV_LORA)),
        'w_uk': nrm((L, KV_LORA, MLA_HEADS * QK_NOPE), KV_LORA ** -0.5),
        'w_uv': nrm((L, KV_LORA, MLA_HEADS * V_HEAD), KV_LORA ** -0.5),
        'g_qk_q': gain((L, QK_HEAD)),
        'g_qk_k': gain((L, QK_HEAD)),
        'g_out_a': gain((L, A_WIDTH)),
        'g_out_b': gain((L, B_WIDTH)),
        'w_o': nrm((L, MIX_WIDTH, D_MODEL), MIX_WIDTH ** -0.5),
        'g_mem_x': gain((L, D_MODEL)),
        'g_mem_in': gain((L, D_MODEL)),
        'w_mq': nrm((L, D_MODEL, MEM_WIDTH), D_MODEL ** -0.5),
        'w_mk': nrm((L, D_MODEL, MEM_WIDTH), D_MODEL ** -0.5),
        'w_mv': nrm((L, D_MODEL, MEM_WIDTH), D_MODEL ** -0.5),
        'g_mq': gain((L, MEM_HEAD_DIM)),
        'g_mk': gain((L, MEM_HEAD_DIM)),
        'w_mo': nrm((L, MEM_WIDTH, D_MODEL), MEM_WIDTH ** -0.5),
        'g_ffn': gain((L, D_MODEL)),
        'w_up': nrm((L, D_MODEL, 2 * D_FF), D_MODEL ** -0.5),
        'w_conv': nrm((L, CONV_W, D_FF), CONV_W ** -0.5),
        'b_conv': nrm((L, D_FF), 0.02),
        'w_down': nrm((L, D_FF, D_MODEL), D_FF ** -0.5),
    }
    return inputs


def reference(x_prompt, x_sample, cache_ckv, cache_kpe, cache_mem_k, cache_mem_v, state_ffn_conv,
              page_table, mem_prompt,
              g_mix, w_in, ln_v_g, ln_v_b, w_s, b_s, g_q_a, w_uq, g_kv_a, w_uk, w_uv, g_qk_q, g_qk_k,
              g_out_a, g_out_b, w_o, g_mem_x, g_mem_in, w_mq, w_mk, w_mv, g_mq, g_mk, w_mo,
              g_ffn, w_up, w_conv, b_conv, w_down):
    n_p, s = x_prompt.shape[:2]
    t = x_sample.shape[1]
    pos_p = jnp.arange(s)
    pos_s = PAST_LEN + jnp.arange(t)
    xp, xs = x_prompt, x_sample
    p_ckv, p_kpe, p_mk, p_mv, p_conv = [], [], [], [], []
    s_ckv, s_kpe, s_chunk_v, s_conv = [], [], [], []
    for l in range(DEPTH):
        u, v, cq, ckv, kpe = split_in(rms_norm(xp, g_mix[l]) @ w_in[l])
        a_out, _ = chunk_gmlp(u, v, w_s[l], b_s[l], ln_v_g[l], ln_v_b[l])
        ckv_n = rms_norm(ckv, g_kv_a[l])
        q = mla_queries(cq, pos_p, g_q_a[l], w_uq[l], g_qk_q[l])
        k, vv = mla_keys_values(ckv_n, kpe, pos_p, w_uk[l], w_uv[l], g_qk_k[l])
        b_out = mla_prompt_attend(q, k, vv)
        xp = xp + merge_out(a_out, b_out, g_out_a[l], g_out_b[l], w_o[l])
        p_ckv.append(ckv_n)
        p_kpe.append(kpe)
        mk, mv = mem_kv(mem_prompt, g_mem_in[l], w_mk[l], w_mv[l], g_mk[l])
        xp = xp + mem_attend(rms_norm(xp, g_mem_x[l]), mk, mv, w_mq[l], g_mq[l], w_mo[l])
        p_mk.append(mk)
        p_mv.append(mv)
        y, buf = conv_ffn(rms_norm(xp, g_ffn[l]), jnp.zeros((n_p, CONV_W - 1, D_FF), xp.dtype),
                          w_up[l], w_conv[l], b_conv[l], w_down[l])
        xp = xp + y
        p_conv.append(buf)

        u, v, cq, ckv, kpe = split_in(rms_norm(xs, g_mix[l]) @ w_in[l])
        a_out, v_rows = chunk_gmlp(u, v, w_s[l], b_s[l], ln_v_g[l], ln_v_b[l])
        ckv_n = rms_norm(ckv, g_kv_a[l])
        q = mla_queries(cq, pos_s, g_q_a[l], w_uq[l], g_qk_q[l])
        b_out = mla_sample_attend(q, ckv_n, kpe, cache_ckv[l], cache_kpe[l], page_table,
                                  w_uk[l], w_uv[l], g_qk_k[l])
        xs = xs + merge_out(a_out, b_out, g_out_a[l], g_out_b[l], w_o[l])
        s_ckv.append(ckv_n)
        s_kpe.append(kpe)
        s_chunk_v.append(v_rows)
        xs = xs + mem_attend(rms_norm(xs, g_mem_x[l]), cache_mem_k[l], cache_mem_v[l], w_mq[l], g_mq[l], w_mo[l])
        y, buf = conv_ffn(rms_norm(xs, g_ffn[l]), state_ffn_conv[l], w_up[l], w_conv[l], b_conv[l], w_down[l])
        xs = xs + y
        s_conv.append(buf)
    return (xp, xs, jnp.stack(p_ckv), jnp.stack(p_kpe), jnp.stack(p_mk), jnp.stack(p_mv), jnp.stack(p_conv),
            jnp.stack(s_ckv), jnp.stack(s_kpe), jnp.stack(s_chunk_v), jnp.stack(s_conv))
```

```python
import numpy as np
from contextlib import ExitStack
import concourse.bass as bass
import concourse.mybir as mybir
from concourse.bass_utils import run_bass_kernel_spmd

F32 = mybir.dt.float32
BF16 = mybir.dt.bfloat16
I32 = mybir.dt.int32
AF = mybir.ActivationFunctionType
ALU = mybir.AluOpType
AX = mybir.AxisListType

NCORES = 8
D = 1024
TP = 2048
NT = 16
NSEQ = 16
TS = 64
NPG = 64
NIN = 1568
DFF = 2816
NFC = 22
EPS = 1e-6
PAST = 8192

COMPUTE = ('pe', 'dve', 'act', 'pool')
NPOOL = 24


class V:
    __slots__ = ('tl', 'ap')

    def __init__(self, tl, ap):
        self.tl = tl
        self.ap = ap

    def __getitem__(self, k):
        return V(self.tl, self.ap[k])

    def r(self, s, **kw):
        return V(self.tl, self.ap.rearrange(s, **kw))

    def un(self, ax):
        return V(self.tl, self.ap.unsqueeze(ax))

    def bc(self, shape):
        return V(self.tl, self.ap.to_broadcast(list(shape)))


class Tl:
    __slots__ = ('t', 'name', 'lw', 'rd', 'psum')

    def __init__(self, t, name, psum=False, init_rd=()):
        self.t = t
        self.name = name
        self.lw = None
        self.rd = list(init_rd)
        self.psum = psum

    def __getitem__(self, k):
        return V(self, self.t[k])

    @property
    def v(self):
        return V(self, self.t[:])


class Sched:
    def __init__(self, nc, es):
        self.nc = nc
        self.es = es
        self.scopes = [(es, [])]
        self.prog = {k: [] for k in ('pe', 'dve', 'act', 'pool', 'sp')}
        self.sems = {}
        self.cnt = {}
        self.seen = {k: {} for k in self.prog}
        for k in COMPUTE:
            self.sems[k] = es.enter_context(nc.semaphore('s_' + k))
            self.cnt[k] = 0
        self.dpool = {}
        for q in ('sp', 'pool', 'act'):
            lst = []
            for i in range(NPOOL):
                key = 'd_%s_%d' % (q, i)
                self.sems[key] = es.enter_context(nc.semaphore(key))
                self.cnt[key] = 0
                lst.append(key)
            self.dpool[q] = [lst, 0]
        self.out_waits = []
        self.ntile = 0
        self.pending = []
        self.rots = {}

    def push(self):
        es = ExitStack()
        self.scopes.append((es, []))
        self.scope_ctr = getattr(self, 'scope_ctr', 0) + 1
        self.scope_ids = getattr(self, 'scope_ids', [0]) + [self.scope_ctr]

    def pop(self):
        es, tiles = self.scopes.pop()
        best = {}
        for k, v, e in self.pending:
            if best.get(k, (0, None))[0] < v:
                best[k] = (v, e)
        for t in tiles:
            deps = list(t.rd)
            if t.lw is not None:
                deps.append(t.lw)
            for k, v, e in deps:
                if best.get(k, (0, None))[0] < v:
                    best[k] = (v, e)
        self.pending = [(k, v, 'freed') for k, (v, e) in best.items()]
        self.scope_ids = self.scope_ids[:-1]
        es.close()

    def sb(self, shape, dt, name=None):
        self.ntile += 1
        name = (name or 't') + '_%d' % self.ntile
        es, tiles = self.scopes[-1]
        t = es.enter_context(self.nc.sbuf_tensor(name, list(shape), dt))
        tl = Tl(t, name, init_rd=self.pending)
        tiles.append(tl)
        return tl

    def ps(self, shape, dt=F32, name=None):
        self.ntile += 1
        name = (name or 'p') + '_%d' % self.ntile
        es, tiles = self.scopes[-1]
        t = es.enter_context(self.nc.psum_tensor(name, list(shape), dt))
        tl = Tl(t, name, psum=True, init_rd=self.pending)
        tiles.append(tl)
        return tl

    def rot(self, key, shape, dt, n=2):
        k = (key, getattr(self, 'scope_ids', [0])[-1])
        if k not in self.rots:
            self.rots[k] = [[self.sb(shape, dt, key) for _ in range(n)], 0]
        ent = self.rots[k]
        t = ent[0][ent[1] % n]
        ent[1] += 1
        return t

    def _collect(self, eng, reads, writes):
        need = {}

        def add(dep, pe_psum=False):
            key, val, deng = dep
            if deng == eng and pe_psum:
                return
            if need.get(key, 0) < val:
                need[key] = val

        for r in reads:
            if r.lw is not None:
                add(r.lw)
        for w in writes:
            pp = (eng == 'pe' and w.psum)
            if w.lw is not None:
                add(w.lw, pp)
            for d in w.rd:
                add(d)
        out = []
        seen = self.seen[eng]
        for key, val in need.items():
            if seen.get(key, 0) >= val:
                continue
            seen[key] = val
            out.append((key, val))
        return out

    def _mark(self, reads, writes, dep):
        for w in writes:
            w.lw = dep
            w.rd = []
        for r in reads:
            if r in writes:
                continue
            r.rd.append(dep)
            if len(r.rd) > 48:
                best = {}
                for k, v, e in r.rd:
                    if best.get(k, (0, None))[0] < v:
                        best[k] = (v, e)
                r.rd = [(k, v, e) for k, (v, e) in best.items()]

    def op(self, eng, fn, reads=(), writes=()):
        reads = list({id(t): t for t in reads}.values())
        writes = list({id(t): t for t in writes}.values())
        waits = self._collect(eng, reads, writes)
        self.cnt[eng] += 1
        val = self.cnt[eng]
        self.prog[eng].append((waits, fn, (eng, 1)))
        self._mark(reads, writes, (eng, val, eng))

    def dma(self, q, fn, reads=(), writes=(), is_output=False):
        reads = list({id(t): t for t in reads}.values())
        writes = list({id(t): t for t in writes}.values())
        waits = self._collect(q, reads, writes)
        lst, idx = self.dpool[q]
        key = lst[idx % NPOOL]
        self.dpool[q][1] = idx + 1
        prev = self.cnt[key]
        if prev > 0 and self.seen[q].get(key, 0) < prev:
            self.seen[q][key] = prev
            waits.append((key, prev))
        self.cnt[key] += 16
        val = self.cnt[key]
        self.prog[q].append((waits, fn, (key, 16)))
        self._mark(reads, writes, (key, val, 'dma_' + q))
        if is_output:
            self.out_waits.append((key, val))

    def finish(self):
        need = {}
        for key, val in self.out_waits:
            if need.get(key, 0) < val:
                need[key] = val
        self.prog['sp'].append((list(need.items()), None, None))
        nc, sems, prog = self.nc, self.sems, self.prog

        def run(e, lst):
            for waits, fn, inc in lst:
                for key, val in waits:
                    e.wait_ge(sems[key], val)
                if fn is not None:
                    fn(e).then_inc(sems[inc[0]], inc[1])

        with nc.Block() as block:
            @block.tensor
            def _(e):
                run(e, prog['pe'])

            @block.vector
            def _(e):
                run(e, prog['dve'])

            @block.scalar
            def _(e):
                run(e, prog['act'])

            @block.gpsimd
            def _(e):
                run(e, prog['pool'])

            @block.sync
            def _(e):
                run(e, prog['sp'])

    def act(self, out, in_, func, bias=None, scale=None, accum=None, eng='act'):
        kw = {}
        rd = [in_.tl]
        wr = [out.tl]
        if bias is not None:
            kw['bias'] = bias.ap
            rd.append(bias.tl)
        if scale is not None:
            if isinstance(scale, V):
                kw['scale'] = scale.ap
                rd.append(scale.tl)
            else:
                kw['scale'] = float(scale)
        if accum is not None:
            kw['accum_out'] = accum.ap
            wr.append(accum.tl)
        self.op(eng, lambda e: e.activation(out=out.ap, in_=in_.ap, func=func, **kw), rd, wr)

    def tt(self, out, a, b, op, eng='dve'):
        self.op(eng, lambda e: e.tensor_tensor(out=out.ap, in0=a.ap, in1=b.ap, op=op), [a.tl, b.tl], [out.tl])

    def ts(self, out, a, s1, op0, s2=None, op1=None, eng='dve', accum=None):
        rd = [a.tl]
        wr = [out.tl]
        x1 = s1
        if isinstance(s1, V):
            rd.append(s1.tl)
            x1 = s1.ap
        x2 = s2
        if isinstance(s2, V):
            rd.append(s2.tl)
            x2 = s2.ap
        kw = {}
        if op1 is not None:
            kw['op1'] = op1
        if accum is not None:
            kw['accum_out'] = accum.ap
            wr.append(accum.tl)
        self.op(eng, lambda e: e.tensor_scalar(out=out.ap, in0=a.ap, scalar1=x1, scalar2=x2, op0=op0, **kw), rd, wr)

    def red(self, out, in_, op=ALU.add, eng='dve'):
        self.op(eng, lambda e: e.tensor_reduce(out=out.ap, in_=in_.ap, axis=AX.X, op=op), [in_.tl], [out.tl])

    def copy(self, out, in_, eng='dve'):
        if eng == 'act':
            self.act(out, in_, AF.Copy)
        else:
            self.op(eng, lambda e: e.tensor_copy(out=out.ap, in_=in_.ap), [in_.tl], [out.tl])

    def recip(self, out, in_, eng='dve'):
        self.op(eng, lambda e: e.reciprocal(out=out.ap, in_=in_.ap), [in_.tl], [out.tl])

    def memset(self, out, val, eng='pool'):
        self.op(eng, lambda e: e.memset(out.ap, val), [], [out.tl])

    def mm(self, out, lhsT, rhs, start=True, stop=True):
        self.op('pe', lambda e: e.matmul(out.ap, lhsT=lhsT.ap, rhs=rhs.ap, start=start, stop=stop),
                [lhsT.tl, rhs.tl], [out.tl])

    def tr(self, out, in_, ident):
        self.op('pe', lambda e: e.transpose(out=out.ap, in_=in_.ap, identity=ident.ap),
                [in_.tl, ident.tl], [out.tl])

    def load(self, out, src, q='sp', **kw):
        self.dma(q, lambda e: e.dma_start(out=out.ap, in_=src, **kw), [], [out.tl])

    def store(self, dst, in_, q='sp', **kw):
        self.dma(q, lambda e: e.dma_start(out=dst, in_=in_.ap, **kw), [in_.tl], [], is_output=True)


class _Stop(Exception):
    pass


def build(n_phys, stop=None):
    nc = bass.Bass("TRN2", target_bir_lowering=False)
    try:
        _build(nc, n_phys, stop)
    except _Stop:
        pass
    return nc


def _build(nc, n_phys, stop):

    def din(name, shape, dt=F32):
        return nc.dram_tensor(name, list(shape), dt, kind="ExternalInput").ap()

    def dout(name, shape):
        return nc.dram_tensor(name, list(shape), F32, kind="ExternalOutput").ap()

    xp = din('xp', [TP, D])
    xs = din('xs', [TS, D])
    cckv = din('cckv', [n_phys, 128 * 256])
    ckpe = din('ckpe', [n_phys, 128 * 32])
    cmk = din('cmk', [NSEQ, 256, 512])
    cmv = din('cmv', [NSEQ, 256, 512])
    cst = din('cst', [NSEQ * 2, DFF])
    ptab = din('ptab', [NSEQ * NPG, 1], I32)
    memp = din('memp', [256, D])
    W = {}
    for nm, shp in [('g_mix', [D]), ('w_in', [D, NIN]), ('ln_v_g', [512]), ('ln_v_b', [512]), ('w_s', [8, 128, 128]),
                    ('b_s', [8, 128]), ('g_q_a', [256]), ('w_uq', [256, 768]), ('g_kv_a', [256]), ('w_uk', [256, 512]),
                    ('w_uv', [256, 512]), ('g_qk_q', [96]), ('g_qk_k', [96]), ('g_out_a', [512]), ('g_out_b', [512]),
                    ('w_o', [D, D]), ('g_mem_x', [D]), ('g_mem_in', [D]), ('w_mq', [D, 512]), ('w_mk', [D, 512]),
                    ('w_mv', [D, 512]), ('g_mq', [128]), ('g_mk', [128]), ('w_mo', [512, D]), ('g_ffn', [D]),
                    ('w_up', [D, 2 * DFF]), ('w_conv', [3, DFF]), ('b_conv', [DFF]), ('w_down', [DFF, D])]:
        W[nm] = din(nm, shp)
    c_ident = din('c_ident', [128, 128])
    c_tri = din('c_tri', [128, 128])
    c_bd = din('c_bd', [64, 64])
    c_cosp = din('c_cosp', [128, NT * 16])
    c_sinp = din('c_sinp', [128, NT * 16])
    c_coss = din('c_coss', [64, 16])
    c_sins = din('c_sins', [64, 16])
    c_cosk = din('c_cosk', [128, 128 * 16])
    c_sink = din('c_sink', [128, 128 * 16])

    y_p = dout('y_p', [TP, D])
    y_s = dout('y_s', [TS, D])
    o_pckv = dout('o_pckv', [TP, 256])
    o_pkpe = dout('o_pkpe', [TP, 32])
    o_pmk = dout('o_pmk', [256, 512])
    o_pmv = dout('o_pmv', [256, 512])
    o_pconv = dout('o_pconv', [2, DFF])
    o_sckv = dout('o_sckv', [TS, 256])
    o_skpe = dout('o_skpe', [TS, 32])
    o_scv = dout('o_scv', [TS, 512])
    o_sconv = dout('o_sconv', [NSEQ * 2, DFF])

    SC_MLA = 96.0 ** -0.5
    SC_MEM = 128.0 ** -0.5

    with ExitStack() as es:
        S = Sched(nc, es)

        def ckpt(name):
            if stop == name:
                while len(S.scopes) > 1:
                    S.pop()
                S.finish()
                raise _Stop()
        PS = [S.ps([128, 512], F32, 'bank%d' % i) for i in range(8)]

        ident = S.sb([128, 128], F32, 'ident')
        S.load(ident.v, c_ident)
        identb = S.sb([128, 128], BF16, 'identb')
        S.copy(identb.v, ident.v, eng='pool')
        tri = S.sb([128, 128], F32, 'tri')
        S.load(tri.v, c_tri)
        trib = S.sb([128, 128], BF16, 'trib')
        S.copy(trib.v, tri.v, eng='pool')
        bd = S.sb([64, 64], F32, 'bd')
        S.load(bd.v, c_bd)
        onesb = S.sb([128, 128], BF16, 'onesb')
        S.memset(onesb.v, 1.0)
        epsT = S.sb([128, 1], F32, 'eps')
        S.memset(epsT.v, EPS)
        cosp = S.sb([128, NT, 16], F32, 'cosp')
        sinp = S.sb([128, NT, 16], F32, 'sinp')
        S.load(cosp.v, c_cosp.rearrange("p (t f) -> p t f", f=16))
        S.load(sinp.v, c_sinp.rearrange("p (t f) -> p t f", f=16))
        coss = S.sb([64, 16], F32, 'coss')
        sins = S.sb([64, 16], F32, 'sins')
        S.load(coss.v, c_coss)
        S.load(sins.v, c_sins)

        def bcast_vec(name, n):
            t = S.sb([128, n], F32, 'bc_' + name)
            S.load(t.v, W[name].partition_broadcast(128))
            return t

        def col_vec(name, n):
            t = S.sb([128, n // 128], F32, 'col_' + name)
            S.load(t.v, W[name].rearrange("(c p) -> p c", p=128), allow_slow_non_contiguous=True)
            return t

        g_kv_bc = bcast_vec('g_kv_a', 256)
        g_qq_bc = bcast_vec('g_qk_q', 96)
        g_qk_bc = bcast_vec('g_qk_k', 96)
        lng_bc = bcast_vec('ln_v_g', 512)
        lnb_bc = bcast_vec('ln_v_b', 512)
        g_mq_bc = bcast_vec('g_mq', 128)
        g_mk_bc = bcast_vec('g_mk', 128)

        def bound(ga, gb, n, sc, name):
            ma = S.sb([128, 1], F32, name + 'a')
            mb = S.sb([128, 1], F32, name + 'b')
            S.op('dve', lambda e: e.tensor_reduce(out=ma.v.ap, in_=ga.v.ap, axis=AX.X, op=ALU.max,
                                                  apply_absolute_value=True), [ga], [ma])
            S.op('dve', lambda e: e.tensor_reduce(out=mb.v.ap, in_=gb.v.ap, axis=AX.X, op=ALU.max,
                                                  apply_absolute_value=True), [gb], [mb])
            c = S.sb([128, 1], F32, name)
            S.tt(c.v, ma.v, mb.v, ALU.mult)
            S.ts(c.v, c.v, -float(n) * sc, ALU.mult)
            return c
        negC = bound(g_qq_bc, g_qk_bc, 96, SC_MLA, 'negC')
        negCm = bound(g_mq_bc, g_mk_bc, 128, SC_MEM, 'negCm')

        def load_w(name, K, N, n0=0, tname=None):
            kc = K // 128
            t = S.sb([128, kc, N], BF16, tname or ('w_' + name))
            src = W[name].rearrange("(c p) n -> p c n", p=128)
            for c in range(kc):
                for a in range(0, N, 1024):
                    b = min(N, a + 1024)
                    S.load(t[:, c, a:b], src[:, c, n0 + a:n0 + b], q='pool')
            return t

        def junk_tile(P, n):
            j = S.rot('junk', [128, 1024], BF16, 1)
            return j[0:P, 0:n]

        def rinv_of(src, n, P):
            ss = S.rot('ss', [128, 1], F32, 2)
            S.act(junk_tile(P, n), src, AF.Square, accum=ss[0:P, :])
            rt = S.rot('rt', [128, 1], F32, 2)
            S.act(rt[0:P, :], ss[0:P, :], AF.Sqrt, bias=epsT[0:P, :], scale=1.0 / n)
            ri = S.rot('ri', [128, 1], F32, 2)
            S.recip(ri[0:P, :], rt[0:P, :])
            return ri

        def group_rinv(src3, G, Dg, P):
            sq = S.rot('gsq%d' % (G * Dg), [128, G, Dg], F32, 1)
            S.tt(sq[0:P], src3, src3, ALU.mult, eng='pool')
            ss = S.rot('gss%d' % G, [128, G], F32, 2)
            S.red(ss[0:P], sq[0:P])
            rt = S.rot('grt%d' % G, [128, G], F32, 2)
            S.act(rt[0:P], ss[0:P], AF.Sqrt, bias=epsT[0:P, :], scale=1.0 / Dg)
            ri = S.rot('gri%d' % G, [128, G], F32, 2)
            S.recip(ri[0:P], rt[0:P])
            return ri

        def transposes(dst, srcs, P, bank, scale=None):
            for i0 in range(0, len(srcs), 4):
                grp = srcs[i0:i0 + 4]
                for j, s in enumerate(grp):
                    w = s.ap.shape[-1]
                    S.tr(bank[0:w, j * 128:j * 128 + P], s, ident[0:P, 0:P])
                for j, s in enumerate(grp):
                    w = s.ap.shape[-1]
                    if scale is None:
                        S.copy(dst(i0 + j), bank[0:w, j * 128:j * 128 + P], eng='act')
                    else:
                        S.act(dst(i0 + j), bank[0:w, j * 128:j * 128 + P], AF.Copy, scale=scale(i0 + j))

        def rope(dst1, dst2, x1, x2, cs, sn, P):
            H = x1.ap.shape[1]
            cb = cs.un(1).bc([P, H, 16])
            sb_ = sn.un(1).bc([P, H, 16])
            t1 = S.rot('rp1', [128, H, 16], F32, 1)
            t2 = S.rot('rp2', [128, H, 16], F32, 1)
            S.tt(t1[0:P], x1, cb, ALU.mult)
            S.tt(t2[0:P], x2, sb_, ALU.mult, eng='pool')
            S.tt(dst1, t1[0:P], t2[0:P], ALU.subtract)
            t3 = S.rot('rp3', [128, H, 16], F32, 1)
            t4 = S.rot('rp4', [128, H, 16], F32, 1)
            S.tt(t3[0:P], x2, cb, ALU.mult)
            S.tt(t4[0:P], x1, sb_, ALU.mult, eng='pool')
            S.tt(dst2, t3[0:P], t4[0:P], ALU.add)

        arena = S.sb([128, 17 * 1024], F32, 'arena')

        def alias(lo, hi, parts, dt, pattern=None, **kw):
            ap = arena.t[0:parts, lo:hi]
            if dt == BF16:
                ap = ap.bitcast(BF16)
            if pattern:
                ap = ap.rearrange(pattern, **kw)
            return Tl(ap, 'alias_%d' % lo)
        kT = alias(0, 8192, 96, BF16, "p (h t) -> p h t", h=8)
        Vaug = alias(8192, 12416, 128, BF16, "p (t h c) -> p t h c", t=NT, h=8)
        qT = alias(12416, 14464, 96, BF16, "p (h t) -> p h t", h=8)
        b_out = alias(14464, 16512, 128, F32, "p (t c) -> p t c", t=4)
        arena_alias = [kT, Vaug, qT, b_out]

        mkT = S.sb([128, 4, 256], BF16, 'mkT')
        mvb = S.sb([128, 2, 512], BF16, 'mvb')

        g_oa_col = col_vec('g_out_a', 512)
        g_ob_col = col_vec('g_out_b', 512)

        S.push()
        aT = S.sb([128, 4, TP + TS], BF16, 'aT')
        bT = S.sb([128, 4, TP + TS], BF16, 'bT')

        S.push()
        w_uk = load_w('w_uk', 256, 512)
        w_uv = load_w('w_uv', 256, 512)
        qnT_s = S.sb([128, 4, TS], BF16, 'qnT_s')
        qpeT_s = S.sb([32, 8, TS], BF16, 'qpeT_s')
        kT_s = S.sb([96, 8, TS], BF16, 'kT_s')
        qT_s = S.sb([96, 8, TS], BF16, 'qT_s')
        Cb_new = S.sb([64, 264], BF16, 'Cb_new')
        S.memset(Cb_new.v, 1.0)

        S.push()
        g_mix_col = col_vec('g_mix', D)
        g_qa_col = col_vec('g_q_a', 256)
        w_in = load_w('w_in', D, NIN)
        w_uq = load_w('w_uq', 256, 768)
        WsT = S.sb([128, 8, 128], BF16, 'WsT')
        WsT_s = S.sb([64, 8, 64], BF16, 'WsT_s')
        bsT = S.sb([128, 8], F32, 'bsT')
        S.load(bsT.v, W['b_s'].rearrange("g t -> t g"), allow_slow_non_contiguous=True)
        bsT_s = S.sb([64, 8], F32, 'bsT_s')
        for sq in range(NSEQ):
            S.dma('act', (lambda sq: lambda e: e.dma_start(out=bsT_s.t[sq * 4:(sq + 1) * 4, :],
                                                          in_=W['b_s'][:, 0:4].rearrange("g t -> t g"),
                                                          allow_slow_non_contiguous=True))(sq), [], [bsT_s])
        S.push()
        wsf = S.sb([128, 8, 128], F32, 'wsf')
        S.load(wsf.v, W['w_s'].rearrange("g t s -> t g s"))
        for g0 in range(0, 8, 4):
            for j in range(4):
                S.tr(PS[0][:, j * 128:(j + 1) * 128], wsf[:, g0 + j, :], ident.v)
            S.tt(WsT[:, g0:g0 + 4, :], PS[0].v.r("p (j t) -> p j t", j=4), tri.v.un(1).bc([128, 4, 128]), ALU.mult)
        wsf_s = S.sb([64, 8, 64], F32, 'wsf_s')
        S.memset(wsf_s.v.r("p g s -> p (g s)"), 0.0)
        for sq in range(NSEQ):
            S.dma('act', (lambda sq: lambda e: e.dma_start(
                out=wsf_s.t[sq * 4:(sq + 1) * 4, :, sq * 4:(sq + 1) * 4],
                in_=W['w_s'][:, 0:4, 0:4].rearrange("g t s -> t g s")))(sq), [], [wsf_s])
        for g0 in range(0, 8, 4):
            for j in range(4):
                S.tr(PS[1][0:64, j * 128:j * 128 + 64], wsf_s[:, g0 + j, :], ident[0:64, 0:64])
            S.tt(WsT_s[:, g0:g0 + 4, :], PS[1][0:64, :].r("p (j t) -> p j t", j=4)[:, :, 0:64],
                 bd.v.un(1).bc([64, 4, 64]), ALU.mult)
        S.pop()
        S.memset(Vaug.v.r("p t h c -> p (t h c)"), 1.0)

        ckpt('c1')

        def phase1(ti, P, xsrc, is_sample, qcol):
            tok0 = ti * 128
            x_t = S.rot('x_t', [128, D], F32, 2)
            S.load(x_t[0:P, :], xsrc)
            ri = rinv_of(x_t[0:P, :], D, P)
            S.ts(x_t[0:P, :], x_t[0:P, :], ri[0:P, 0:1], ALU.mult, eng='pool')
            xT = S.rot('xT', [128, 8, 128], BF16, 1)
            transposes(lambda i: xT[:, i, 0:P], [x_t[0:P, i * 128:(i + 1) * 128] for i in range(8)], P, PS[0],
                       scale=lambda i: g_mix_col[:, i:i + 1])
            zb = [PS[1], PS[2], PS[3], PS[4]]
            for n in range(4):
                n0 = n * 512
                n1 = min(NIN, n0 + 512)
                for k in range(8):
                    S.mm(zb[n][0:P, 0:n1 - n0], xT[:, k, 0:P], w_in[:, k, n0:n1], start=(k == 0), stop=(k == 7))
            gu = S.rot('gu', [128, 512], F32, 1)
            S.act(gu[0:P], zb[0][0:P, :], AF.Gelu_apprx_tanh)
            gv = S.rot('gv', [128, 8, 64], F32, 1)
            S.act(gv[0:P].r("p g d -> p (g d)"), zb[1][0:P, :], AF.Gelu_apprx_tanh)
            s1 = S.rot('s1', [128, 8], F32, 2)
            S.red(s1[0:P], gv[0:P])
            S.ts(s1[0:P], s1[0:P], -1.0 / 64, ALU.mult)
            cen = S.rot('cen', [128, 8, 64], F32, 1)
            S.tt(cen[0:P], gv[0:P], s1[0:P].un(2).bc([P, 8, 64]), ALU.add)
            sq = S.rot('lsq', [128, 8, 64], F32, 1)
            S.tt(sq[0:P], cen[0:P], cen[0:P], ALU.mult, eng='pool')
            var = S.rot('var', [128, 8], F32, 2)
            S.red(var[0:P], sq[0:P])
            S.act(var[0:P], var[0:P], AF.Sqrt, bias=epsT[0:P, :], scale=1.0 / 64)
            S.recip(var[0:P], var[0:P])
            S.tt(cen[0:P], cen[0:P], var[0:P].un(2).bc([P, 8, 64]), ALU.mult)
            S.tt(cen[0:P], cen[0:P], lng_bc[0:P].r("p (g d) -> p g d", g=8), ALU.mult, eng='pool')
            vg = S.rot('vg', [128, 8, 64], BF16, 1)
            if is_sample:
                S.tt(sq[0:P], cen[0:P], lnb_bc[0:P].r("p (g d) -> p g d", g=8), ALU.add)
                S.store(o_scv, sq[0:P].r("p g d -> p (g d)"))
                S.copy(vg[0:P], sq[0:P], eng='pool')
            else:
                S.tt(vg[0:P], cen[0:P], lnb_bc[0:P].r("p (g d) -> p g d", g=8), ALU.add)
            sp_ps = PS[5]
            for g in range(8):
                lw = WsT_s[:, g, :] if is_sample else WsT[:, g, :]
                S.mm(sp_ps[0:P, g * 64:(g + 1) * 64], lw, vg[0:P, g, :])
            bt = bsT_s if is_sample else bsT
            S.tt(cen[0:P], sp_ps[0:P, :].r("p (g d) -> p g d", g=8), bt[0:P].un(2).bc([P, 8, 64]), ALU.add)
            a_o = gv
            S.tt(a_o[0:P].r("p g d -> p (g d)"), cen[0:P].r("p g d -> p (g d)"), gu[0:P], ALU.mult, eng='pool')
            a_f = a_o[0:P].r("p g d -> p (g d)")
            ria = rinv_of(a_f, 512, P)
            S.ts(a_f, a_f, ria[0:P, 0:1], ALU.mult, eng='pool')
            acol = TP if is_sample else tok0
            transposes(lambda i: aT[:, i, acol:acol + P], [a_f[:, i * 128:(i + 1) * 128] for i in range(4)], P, PS[6],
                       scale=lambda i: g_oa_col[:, i:i + 1])
            c3 = zb[2]
            cq = S.rot('cq', [128, 256], F32, 1)
            riq = rinv_of(c3[0:P, 0:256], 256, P)
            S.act(cq[0:P], c3[0:P, 0:256], AF.Copy, scale=riq[0:P, 0:1])
            cqT = S.rot('cqT', [128, 2, 128], BF16, 1)
            transposes(lambda i: cqT[:, i, 0:P], [cq[0:P, i * 128:(i + 1) * 128] for i in range(2)], P, PS[6],
                       scale=lambda i: g_qa_col[:, i:i + 1])
            ckn = S.rot('ckn', [128, 256], F32, 2)
            rik = rinv_of(c3[0:P, 256:512], 256, P)
            S.act(ckn[0:P], c3[0:P, 256:512], AF.Copy, scale=rik[0:P, 0:1])
            S.tt(ckn[0:P], ckn[0:P], g_kv_bc[0:P], ALU.mult)
            kpe = S.rot('kpe', [128, 32], F32, 2)
            S.copy(kpe[0:P], zb[3][0:P, 0:32], eng='act')
            if is_sample:
                S.store(o_sckv, ckn[0:P])
                S.store(o_skpe, kpe[0:P])
                S.copy(Cb_new[:, 0:256], ckn[0:P], eng='pool')
            else:
                S.store(o_pckv[tok0:tok0 + P, :], ckn[0:P])
                S.store(o_pkpe[tok0:tok0 + P, :], kpe[0:P])
            ckT = S.rot('ckT', [128, 2, 128], BF16, 1)
            transposes(lambda i: ckT[:, i, 0:P], [ckn[0:P, i * 128:(i + 1) * 128] for i in range(2)], P, PS[6])
            q_ps0, q_ps1 = PS[7], PS[5]
            for k in range(2):
                S.mm(q_ps0[0:P, :], cqT[:, k, 0:P], w_uq[:, k, 0:512], start=(k == 0), stop=(k == 1))
            q_sb = S.rot('q_sb', [128, 8, 96], F32, 1)
            S.copy(q_sb[0:P].r("p h d -> p (h d)")[:, 0:512], q_ps0[0:P, :], eng='act')
            for k in range(2):
                S.mm(q_ps1[0:P, 0:256], cqT[:, k, 0:P], w_uq[:, k, 512:768], start=(k == 0), stop=(k == 1))
            S.copy(q_sb[0:P].r("p h d -> p (h d)")[:, 512:768], q_ps1[0:P, 0:256], eng='act')
            kn_ps, v_ps = PS[1], PS[2]
            for k in range(2):
                S.mm(kn_ps[0:P, :], ckT[:, k, 0:P], w_uk[:, k, :], start=(k == 0), stop=(k == 1))
            for k in range(2):
                S.mm(v_ps[0:P, :], ckT[:, k, 0:P], w_uv[:, k, :], start=(k == 0), stop=(k == 1))
            k_sb = S.rot('k_sb', [128, 8, 96], F32, 1)
            S.copy(k_sb[0:P, :, 0:64], kn_ps[0:P, :].r("p (h d) -> p h d", h=8), eng='act')
            S.copy(k_sb[0:P, :, 64:96], kpe[0:P].un(1).bc([P, 8, 32]), eng='pool')
            if not is_sample:
                S.copy(Vaug[0:P, ti, :, 0:64], v_ps[0:P, :].r("p (h d) -> p h d", h=8), eng='act')
            cs = coss.v if is_sample else cosp[:, ti, :]
            sn = sins.v if is_sample else sinp[:, ti, :]
            for nm, src, gbc in (('q', q_sb, g_qq_bc), ('k', k_sb, g_qk_bc)):
                rg = group_rinv(src[0:P], 8, 96, P)
                S.tt(src[0:P], src[0:P], rg[0:P].un(2).bc([P, 8, 96]), ALU.mult)
                S.tt(src[0:P], src[0:P], gbc[0:P].un(1).bc([P, 8, 96]), ALU.mult, eng='pool')
                fin = S.rot('fin', [128, 8, 96], F32, 1)
                S.copy(fin[0:P, :, 0:64], src[0:P, :, 0:64], eng='pool')
                rope(fin[0:P, :, 64:80], fin[0:P, :, 80:96], src[0:P, :, 64:80], src[0:P, :, 80:96], cs[0:P], sn[0:P], P)
                if is_sample:
                    dstT = qT_s if nm == 'q' else kT_s
                    transposes(lambda h: dstT[:, h, 0:P], [fin[0:P, h, :] for h in range(8)], P, PS[6])
                    if nm == 'q':
                        qg = S.rot('qg', [128, 8, 64], F32, 1)
                        S.tt(qg[0:P], fin[0:P, :, 0:64], g_qk_bc[0:P, 0:64].un(1).bc([P, 8, 64]), ALU.mult)
                        transposes(lambda j: qnT_s[:, j, 0:P],
                                   [qg[0:P, 2 * j:2 * j + 2, :].r("p h d -> p (h d)") for j in range(4)], P, PS[6])
                        qpe = S.rot('qpe', [128, 8, 32], F32, 1)
                        S.copy(qpe[0:P], fin[0:P, :, 64:96], eng='pool')
                        transposes(lambda h: qpeT_s[:, h, 0:P], [qpe[0:P, h, :] for h in range(8)], P, PS[6])
                else:
                    if nm == 'q':
                        transposes(lambda h: qT[:, h, qcol:qcol + P], [fin[0:P, h, :] for h in range(8)], P, PS[6])
                    else:
                        transposes(lambda h: kT[:, h, tok0:tok0 + P], [fin[0:P, h, :] for h in range(8)], P, PS[7])

        att_cnt = [0]

        def attention(qb):
            for h in range(8):
                ot = PS[4 + (att_cnt[0] % 2)]
                att_cnt[0] += 1
                nkb = 4 * qb + 4
                for kb in range(nkb):
                    c0 = max(0, kb - 4 * qb) * 128
                    st = PS[kb % 2]
                    S.mm(st[:, c0:512], kT[:, h, kb * 128:(kb + 1) * 128], qT[:, h, c0:512])
                    pt = S.rot('pt', [128, 512], BF16, 3)
                    S.act(pt[:, c0:512], st[:, c0:512], AF.Exp, bias=negC[:, 0:1], scale=SC_MLA)
                    if kb >= 4 * qb:
                        S.tt(pt[:, c0:c0 + 128], pt[:, c0:c0 + 128], trib.v, ALU.mult, eng='pool')
                    S.mm(ot[0:65, c0:512], Vaug[:, kb, h, 0:65], pt[:, c0:512], start=(kb == 0), stop=(kb == nkb - 1))
                ot_sb = S.rot('ot_sb', [65, 512], F32, 2)
                S.copy(ot_sb.v, ot[0:65, :], eng='act')
                tp = PS[6 + (att_cnt[0] % 2)]
                for j in range(4):
                    S.tr(tp[:, j * 128:j * 128 + 65], ot_sb[:, j * 128:(j + 1) * 128], ident[0:65, 0:65])
                tpv = tp.v.r("p (j c) -> p j c", j=4)
                rd = S.rot('rd', [128, 4, 1], F32, 2)
                S.recip(rd.v, tpv[:, :, 64:65])
                S.tt(b_out[:, :, h * 64:(h + 1) * 64], tpv[:, :, 0:64], rd.v.bc([128, 4, 64]), ALU.mult)
            for t in range(4):
                ti = 4 * qb + t
                rib = rinv_of(b_out[:, t, :], 512, 128)
                S.ts(b_out[:, t, :], b_out[:, t, :], rib[:, 0:1], ALU.mult, eng='pool')
                transposes(lambda i: bT[:, i, ti * 128:(ti + 1) * 128], [b_out[:, t, i * 128:(i + 1) * 128] for i in range(4)],
                           128, PS[2 + t % 2], scale=lambda i: g_ob_col[:, i:i + 1])

        for qb in range(4):
            for t in range(4):
                ti = 4 * qb + t
                phase1(ti, 128, xp[ti * 128:(ti + 1) * 128, :], False, t * 128)
                ckpt('c2')
            if qb == 3:
                phase1(0, TS, xs[:, :], True, 0)
            attention(qb)
            ckpt('c3')
        S.pop()
        ckpt('c4')

        S.push()
        wukT = S.sb([128, 4, 256], BF16, 'wukT')
        S.push()
        wukf = S.sb([128, 2, 512], F32, 'wukf')
        S.load(wukf.v, W['w_uk'].rearrange("(c p) n -> p c n", p=128))
        for j in range(4):
            for cc in range(2):
                S.tr(PS[0][:, cc * 128:(cc + 1) * 128], wukf[:, cc, j * 128:(j + 1) * 128], ident.v)
            S.copy(wukT[:, j, :], PS[0][:, 0:256], eng='act')
        S.pop()
        ckpt('c40')
        qlatT = S.sb([128, 2, 8, TS], BF16, 'qlatT')
        for h in range(8):
            j, a = h // 2, h % 2
            for cc in range(2):
                col = (j * 2 + cc) * 64
                S.mm(PS[1 + a][:, col:col + TS],
                     wukT[a * 64:(a + 1) * 64, j, cc * 128:(cc + 1) * 128], qnT_s[a * 64:(a + 1) * 64, j, :])
        for a in range(2):
            S.copy(qlatT.v.r("p c (j a) t -> p a j c t", a=2)[:, a],
                   PS[1 + a].v.r("p (j c t) -> p j c t", j=4, c=2), eng='act')
        ckpt('c41')
        ptt = S.sb([128, NSEQ // 2], I32, 'ptt')
        S.load(ptt.v, ptab.rearrange("(g p) o -> p (g o)", p=128), allow_slow_non_contiguous=True)
        ptf = S.sb([128, NSEQ // 2], F32, 'ptf')
        S.copy(ptf.v, ptt.v)
        io16 = S.sb([128, 16], F32, 'io16')
        S.op('pool', lambda e: e.iota(io16.v.ap, pattern=[[1, 16]], base=0, channel_multiplier=0,
                                      allow_small_or_imprecise_dtypes=True), [], [io16])
        idxf = S.sb([128, NSEQ // 2, 16], F32, 'idxf')
        S.ts(idxf.v, ptf.v.un(2).bc([128, NSEQ // 2, 16]), 16.0, ALU.mult)
        idxc = S.sb([128, NSEQ // 2, 16], I32, 'idxc')
        S.tt(idxc.v, idxf.v, io16.v.un(1).bc([128, NSEQ // 2, 16]), ALU.add)
        ckpt('c42')
        cckv16 = cckv.rearrange("n (j x) -> (n j) x", j=16)
        cosk = S.sb([128, 128, 16], F32, 'cosk')
        sink = S.sb([128, 128, 16], F32, 'sink')
        S.load(cosk.v, c_cosk.rearrange("p (r f) -> p r f", f=16))
        S.load(sink.v, c_sink.rearrange("p (r f) -> p r f", f=16))
        gpe_bc = g_qk_bc[:, 64:96]
        b_s_T = S.sb([128, 4, TS], F32, 'b_s_T')
        RCH = 8
        ckpt('c4a')
        for pr in range(NSEQ // 2):
            idx = ptt[:, pr:pr + 1]
            KP = S.rot('KP', [128, 128, 32], F32, 1)
            S.dma('pool', (lambda KP, idx: lambda e: e.indirect_dma_start(
                out=KP.v.ap.rearrange("p r f -> p (r f)"), out_offset=None, in_=ckpe[:, :],
                in_offset=bass.IndirectOffsetOnAxis(ap=idx.ap, axis=0)))(KP, idx), [ptt], [KP])
            sspe = S.rot('sspe', [128, 128], F32, 1)
            kr = S.rot('kr', [128, 128, 32], BF16, 1)
            for hf in range(2):
                rs = slice(hf * 64, (hf + 1) * 64)
                ksq = S.rot('ksq', [128, 64, 32], F32, 1)
                S.tt(ksq.v, KP[:, rs, :], KP[:, rs, :], ALU.mult, eng='pool')
                S.red(sspe[:, rs], ksq.v)
                S.tt(ksq.v, KP[:, rs, :], gpe_bc.un(1).bc([128, 64, 32]), ALU.mult, eng='pool')
                t1 = S.rot('kt1', [128, 64, 16], F32, 1)
                t2 = S.rot('kt2', [128, 64, 16], F32, 1)
                S.tt(t1.v, ksq[:, :, 0:16], cosk[:, rs, :], ALU.mult)
                S.tt(t2.v, ksq[:, :, 16:32], sink[:, rs, :], ALU.mult, eng='pool')
                S.tt(kr[:, rs, 0:16], t1.v, t2.v, ALU.subtract)
                S.tt(t1.v, ksq[:, :, 16:32], cosk[:, rs, :], ALU.mult)
                S.tt(t2.v, ksq[:, :, 0:16], sink[:, rs, :], ALU.mult, eng='pool')
                S.tt(kr[:, rs, 16:32], t1.v, t2.v, ALU.add)
            oacc = [PS[6], PS[7]]
            first = True
            ckpt('c4b')
            for ch in range(128 // RCH):
                Cb = S.rot('Cb', [128, RCH, 256], BF16, 2)
                ixc = idxc[:, pr, ch:ch + 1]
                S.dma('pool', (lambda Cb, ixc: lambda e: e.indirect_dma_start(
                    out=Cb.v.ap.rearrange("p r c -> p (r c)"), out_offset=None, in_=cckv16,
                    in_offset=bass.IndirectOffsetOnAxis(ap=ixc.ap, axis=0)))(Cb, ixc), [idxc], [Cb])
                ckpt('c4c')
                for g0 in range(0, RCH, 4):
                    ssn = S.rot('ssn', [128, 4, 8], F32, 2)
                    stb = PS[4 + ((g0 // 4) % 2)]
                    for g in range(4):
                        rl = g0 + g
                        r = ch * RCH + rl
                        ctp = PS[2 + (g % 2)]
                        for cc in range(2):
                            S.mm(ctp[:, cc * 128:(cc + 1) * 128], Cb[:, rl, cc * 128:(cc + 1) * 128], identb.v)
                        S.mm(ctp[0:32, 256:384], kr[:, r, :], identb.v)
                        cT = S.rot('cT', [128, 384], BF16, 3)
                        S.copy(cT[:, 0:256], ctp[:, 0:256], eng='act')
                        S.copy(cT[0:32, 256:384], ctp[0:32, 256:384], eng='act')
                        knp = PS[g % 2]
                        for cc in range(2):
                            S.mm(knp.v, cT[:, cc * 128:(cc + 1) * 128], w_uk[:, cc, :], start=(cc == 0), stop=(cc == 1))
                        sqk = S.rot('sqk', [128, 8, 64], BF16, 3)
                        S.act(sqk.v.r("p h d -> p (h d)"), knp.v, AF.Square)
                        S.red(ssn[:, g, :], sqk.v)
                        for cc in range(2):
                            S.mm(stb[:, g * 64:(g + 1) * 64], cT[:, cc * 128:(cc + 1) * 128],
                                 qlatT[:, cc, :, pr * 8:pr * 8 + 8].r("p h (a q) -> p a h q", a=2),
                                 start=(cc == 0), stop=False)
                        S.mm(stb[:, g * 64:(g + 1) * 64], cT[0:32, 256:384],
                             qpeT_s[:, :, pr * 8:pr * 8 + 8].r("p h (a q) -> p a h q", a=2), start=False, stop=True)
                    r0 = ch * RCH + g0
                    tot = S.rot('tot', [128, 4, 8], F32, 2)
                    S.tt(tot.v, ssn.v, sspe[:, r0:r0 + 4].un(2).bc([128, 4, 8]), ALU.add)
                    S.act(tot.v, tot.v, AF.Sqrt, bias=epsT[:, 0:1], scale=1.0 / 96)
                    S.recip(tot.v, tot.v)
                    snm = S.rot('snm', [128, 4, 8, 4], F32, 2)
                    ptb = S.rot('ptb', [128, 4, 64], BF16, 2)
                    for a in range(2):
                        pa = slice(a * 64, (a + 1) * 64)
                        S.tt(snm[pa], stb[pa, 0:256].r("p (g a h q) -> p g a h q", g=4, a=2, h=8)[:, :, a, :, :],
                             tot[pa].un(3).bc([64, 4, 8, 4]), ALU.mult)
                        S.act(ptb[pa].r("p g (a h q) -> p g a h q", a=2, h=8)[:, :, a, :, :], snm[pa], AF.Exp,
                              bias=negC[pa, 0:1], scale=SC_MLA)
                    for g in range(4):
                        rl = g0 + g
                        for a in range(2):
                            pa = slice(a * 64, (a + 1) * 64)
                            S.mm(oacc[a][0:32, 0:256], ptb[pa, g, a * 32:(a + 1) * 32], Cb[pa, rl, :],
                                 start=first, stop=False)
                            S.op('pe', (lambda o_, l_, r_: lambda e: e.matmul(o_.ap, lhsT=l_.ap, rhs=r_.ap, start=False,
                                                                              stop=False, skip_group_check=True))(
                                oacc[a][0:32, 256:257], ptb[pa, g, a * 32:(a + 1) * 32], onesb[pa, 0:1]),
                                [ptb, onesb], [oacc[a]])
                        first = False
                    ckpt('c4d')
            for a in range(2):
                sq_ = pr * 2 + a
                tk = slice(sq_ * 4, sq_ * 4 + 4)
                snew = PS[4]
                for h in range(8):
                    S.mm(snew[0:64, h * 4:(h + 1) * 4], kT_s[:, h, :], qT_s[:, h, tk])
                pn = S.rot('pn', [64, 8, 4], BF16, 2)
                S.act(pn.v.r("p h q -> p (h q)"), snew[0:64, 0:32], AF.Exp, bias=negC[0:64, 0:1], scale=SC_MLA)
                S.tt(pn.v, pn.v, bd[:, tk].un(1).bc([64, 8, 4]), ALU.mult)
                S.mm(oacc[a][0:32, 0:257], pn.v.r("p h q -> p (h q)"), Cb_new[:, 0:257], start=False, stop=True)
                ol = S.rot('ol', [32, 264], F32, 2)
                S.copy(ol[:, 0:257], oacc[a][0:32, 0:257], eng='act')
                rl_ = S.rot('rl_', [32, 1], F32, 2)
                S.recip(rl_.v, ol[:, 256:257])
                S.ts(ol[:, 0:256], ol[:, 0:256], rl_[:, 0:1], ALU.mult)
                olT = S.rot('olT', [128, 2, 32], BF16, 2)
                transposes(lambda i: olT[:, i, :], [ol[:, i * 128:(i + 1) * 128] for i in range(2)], 32, PS[4])
                bps = PS[5]
                for h in range(8):
                    j, par = h // 2, h % 2
                    for cc in range(2):
                        S.mm(bps[par * 64:(par + 1) * 64, j * 4:(j + 1) * 4], w_uv[:, cc, h * 64:(h + 1) * 64],
                             olT[:, cc, h * 4:(h + 1) * 4], start=(cc == 0), stop=(cc == 1))
                S.copy(b_s_T[:, :, tk], bps[:, 0:16].r("p (j q) -> p j q", j=4), eng='act')
            ckpt('c5')
        bsq = S.sb([128, 4, TS], BF16, 'bsq')
        S.tt(bsq.v, b_s_T.v, b_s_T.v, ALU.mult)
        for j in range(4):
            S.mm(PS[0][:, 0:TS], onesb.v, bsq[:, j, :], start=(j == 0), stop=(j == 3))
        rbs = S.sb([128, TS], F32, 'rbs')
        S.act(rbs.v, PS[0][:, 0:TS], AF.Sqrt, bias=epsT[:, 0:1], scale=1.0 / 512)
        S.recip(rbs.v, rbs.v)
        S.tt(b_s_T.v, b_s_T.v, rbs.v.un(1).bc([128, 4, TS]), ALU.mult)
        for j in range(4):
            S.ts(bT[:, j, TP:TP + TS], b_s_T[:, j, :], g_ob_col[:, j:j + 1], ALU.mult)
        if stop == 'c6':
            S.store(y_s.rearrange("t (two d) -> (t two) d", two=2)[:, 0:256], b_s_T.v.r("p j t -> p (j t)"))
        S.pop()
        S.pop()
        ckpt('c6')

        dep_best = {}
        for tl in arena_alias:
            for d in list(tl.rd) + ([tl.lw] if tl.lw is not None else []):
                if dep_best.get(d[0], (0, None))[0] < d[1]:
                    dep_best[d[0]] = (d[1], d[2])
        arena_deps = [(k, v, 'freed') for k, (v, e) in dep_best.items()]
        xres = []
        for t in range(NT + 1):
            tl = Tl(arena.t[:, t * 1024:(t + 1) * 1024], 'xres%d' % t)
            tl.rd = list(arena_deps)
            xres.append(tl)
        S.push()
        w_oa = load_w('w_o', D, D, tname='w_o')
        for t in range(NT + 1):
            is_s = (t == NT)
            P = TS if is_s else 128
            tc0 = t * 128
            x_t = S.rot('x_t3', [128, D], F32, 2)
            S.load(x_t[0:P], xs[:, :] if is_s else xp[tc0:tc0 + P, :])
            for n in range(2):
                ps = PS[(2 * t + n) % 4]
                for k in range(8):
                    src = aT if k < 4 else bT
                    S.mm(ps[0:P, :], src[:, k % 4, tc0:tc0 + P], w_oa[:, k, n * 512:(n + 1) * 512],
                         start=(k == 0), stop=(k == 7))
                S.tt(xres[t][0:P, n * 512:(n + 1) * 512], ps[0:P, :], x_t[0:P, n * 512:(n + 1) * 512], ALU.add)
        S.pop()
        S.pop()
        ckpt('c7')

        S.push()
        g_min_col = col_vec('g_mem_in', D)
        w_mk = load_w('w_mk', D, 512)
        w_mv = load_w('w_mv', D, 512)
        for mt in range(2):
            m_t = S.rot('m_t', [128, D], F32, 2)
            S.load(m_t.v, memp[mt * 128:(mt + 1) * 128, :])
            ri = rinv_of(m_t.v, D, 128)
            S.ts(m_t.v, m_t.v, ri[:, 0:1], ALU.mult, eng='pool')
            mT = S.rot('mT', [128, 8, 128], BF16, 2)
            transposes(lambda i: mT[:, i, :], [m_t[:, i * 128:(i + 1) * 128] for i in range(8)], 128, PS[0],
                       scale=lambda i: g_min_col[:, i:i + 1])
            for k in range(8):
                S.mm(PS[1].v, mT[:, k, :], w_mk[:, k, :], start=(k == 0), stop=(k == 7))
            for k in range(8):
                S.mm(PS[2].v, mT[:, k, :], w_mv[:, k, :], start=(k == 0), stop=(k == 7))
            mk_sb = S.rot('mk_sb', [128, 4, 128], F32, 2)
            S.copy(mk_sb.v.r("p h d -> p (h d)"), PS[1].v, eng='act')
            rg = group_rinv(mk_sb.v, 4, 128, 128)
            S.tt(mk_sb.v, mk_sb.v, rg.v.un(2).bc([128, 4, 128]), ALU.mult)
            S.tt(mk_sb.v, mk_sb.v, g_mk_bc.v.un(1).bc([128, 4, 128]), ALU.mult, eng='pool')
            S.store(o_pmk[mt * 128:(mt + 1) * 128, :], mk_sb.v.r("p h d -> p (h d)"))
            transposes(lambda h: mkT[:, h, mt * 128:(mt + 1) * 128], [mk_sb[:, h, :] for h in range(4)], 128, PS[3])
            mv_sb = S.rot('mv_sb', [128, 512], F32, 2)
            S.copy(mv_sb.v, PS[2].v, eng='act')
            S.store(o_pmv[mt * 128:(mt + 1) * 128, :], mv_sb.v)
            S.copy(mvb[:, mt, :], mv_sb.v, eng='pool')
        S.pop()

        ckpt('c8')
        blocks = [(i * 512, 512, False) for i in range(4)] + [(TP, TS, True)]

        def norm_T(dst, c0, NB, gcol):
            ntile = max(1, NB // 128)
            P = min(NB, 128)
            for t in range(ntile):
                xr = xres[c0 // 128 + t]
                ri = rinv_of(xr[0:P, :], D, P)
                xh = S.rot('xh3', [128, D], F32, 2)
                S.ts(xh[0:P], xr[0:P, :], ri[0:P, 0:1], ALU.mult, eng='pool')
                transposes(lambda i: dst[:, i, t * 128:t * 128 + P], [xh[0:P, i * 128:(i + 1) * 128] for i in range(8)],
                           P, PS[t % 2], scale=lambda i: gcol[:, i:i + 1])

        S.push()
        g_mx_col = col_vec('g_mem_x', D)
        w_mq = load_w('w_mq', D, 512)
        w_mo = load_w('w_mo', 512, D)
        for (c0, NB, is_s) in blocks:
            ntile = max(1, NB // 128)
            P = min(NB, 128)
            hT = S.rot('hT4', [128, 8, 512], BF16, 1)
            norm_T(hT, c0, NB, g_mx_col)
            qmT = S.rot('qmT', [128, 4, 512], BF16, 1)
            for t in range(ntile):
                ps = PS[2 + t % 2]
                for k in range(8):
                    S.mm(ps[0:P, :], hT[:, k, t * 128:t * 128 + P], w_mq[:, k, :], start=(k == 0), stop=(k == 7))
                qm = S.rot('qm', [128, 4, 128], F32, 2)
                S.copy(qm[0:P].r("p h d -> p (h d)"), ps[0:P, :], eng='act')
                rg = group_rinv(qm[0:P], 4, 128, P)
                S.tt(qm[0:P], qm[0:P], rg[0:P].un(2).bc([P, 4, 128]), ALU.mult)
                S.tt(qm[0:P], qm[0:P], g_mq_bc[0:P].un(1).bc([P, 4, 128]), ALU.mult, eng='pool')
                transposes(lambda h: qmT[:, h, t * 128:t * 128 + P], [qm[0:P, h, :] for h in range(4)], P, PS[4 + t % 2])
            omT = S.rot('omT', [128, 4, 512], BF16, 1)
            if not is_s:
                for h in range(4):
                    o_ps, d_ps = PS[4], PS[5]
                    for kb in range(2):
                        st = PS[kb]
                        S.mm(st.v, mkT[:, h, kb * 128:(kb + 1) * 128], qmT[:, h, :])
                        pm = S.rot('pm', [128, 512], BF16, 2)
                        S.act(pm.v, st.v, AF.Exp, bias=negCm[:, 0:1], scale=SC_MEM)
                        S.mm(o_ps.v, mvb[:, kb, h * 128:(h + 1) * 128], pm.v, start=(kb == 0), stop=(kb == 1))
                        S.mm(d_ps.v, onesb.v, pm.v, start=(kb == 0), stop=(kb == 1))
                    rden = S.rot('rden', [128, 512], F32, 1)
                    S.act(rden.v, d_ps.v, AF.Ln)
                    S.act(rden.v, rden.v, AF.Exp, scale=-1.0)
                    S.tt(omT[:, h, :], o_ps.v, rden.v, ALU.mult)
            else:
                for sq_ in range(NSEQ):
                    tk = slice(sq_ * 4, sq_ * 4 + 4)
                    mk_s = S.rot('mk_s', [128, 2, 512], F32, 2)
                    S.load(mk_s.v, cmk[sq_].rearrange("(b p) f -> p b f", p=128))
                    mv_s = S.rot('mv_s', [128, 2, 512], BF16, 2)
                    S.load(mv_s.v, cmv[sq_].rearrange("(b p) f -> p b f", p=128), q='pool')
                    mkT_s = S.rot('mkT_s', [128, 4, 256], BF16, 2)
                    for kb in range(2):
                        for h in range(4):
                            S.tr(PS[kb][:, h * 128:(h + 1) * 128], mk_s[:, kb, h * 128:(h + 1) * 128], ident.v)
                        S.copy(mkT_s[:, :, kb * 128:(kb + 1) * 128], PS[kb].v.r("p (h k) -> p h k", h=4), eng='act')
                    st = PS[2]
                    for kb in range(2):
                        for h in range(4):
                            cl = (kb * 4 + h) * 4
                            S.mm(st[:, cl:cl + 4], mkT_s[:, h, kb * 128:(kb + 1) * 128], qmT[:, h, tk])
                    pm = S.rot('pm_s', [128, 2, 4, 4], BF16, 2)
                    S.act(pm.v.r("p b h q -> p (b h q)"), st[:, 0:32], AF.Exp, bias=negCm[:, 0:1], scale=SC_MEM)
                    o_ps, d_ps = PS[4], PS[5]
                    for h in range(4):
                        for kb in range(2):
                            S.mm(o_ps[:, h * 4:(h + 1) * 4], mv_s[:, kb, h * 128:(h + 1) * 128], pm[:, kb, h, :],
                                 start=(kb == 0), stop=(kb == 1))
                    for kb in range(2):
                        S.mm(d_ps[:, 0:16], onesb.v, pm[:, kb, :, :].r("p h q -> p (h q)"), start=(kb == 0), stop=(kb == 1))
                    rden = S.rot('rden_s', [128, 16], F32, 2)
                    S.recip(rden.v, d_ps[:, 0:16])
                    S.tt(omT[:, :, tk], o_ps[:, 0:16].r("p (h q) -> p h q", h=4), rden.v.r("p (h q) -> p h q", h=4), ALU.mult)
            for t in range(ntile):
                xr = xres[c0 // 128 + t]
                for n in range(2):
                    ps = PS[6 + n]
                    for k in range(4):
                        S.mm(ps[0:P, :], omT[:, k, t * 128:t * 128 + P], w_mo[:, k, n * 512:(n + 1) * 512],
                             start=(k == 0), stop=(k == 3))
                    S.tt(xr[0:P, n * 512:(n + 1) * 512], ps[0:P, :], xr[0:P, n * 512:(n + 1) * 512], ALU.add)
        S.pop()

        ckpt('c9')
        S.push()
        g_ffn_col = col_vec('g_ffn', D)
        wc_col = S.sb([128, 3, NFC], F32, 'wc_col')
        S.load(wc_col.v, W['w_conv'].rearrange("j (c p) -> p j c", p=128), allow_slow_non_contiguous=True)
        bc_col = S.sb([128, NFC], F32, 'bc_col')
        S.load(bc_col.v, W['b_conv'].rearrange("(c p) -> p c", p=128), allow_slow_non_contiguous=True)
        carry = S.sb([128, NFC, 2], F32, 'carry')
        S.memset(carry.v, 0.0)
        w_down = load_w('w_down', DFF, D)
        srcw = W['w_up'].rearrange("(c p) n -> p c n", p=128)
        for (c0, NB, is_s) in blocks:
            ntile = max(1, NB // 128)
            P = min(NB, 128)
            hT = S.rot('hT5', [128, 8, 512], BF16, 1)
            norm_T(hT, c0, NB, g_ffn_col)
            aF = S.rot('aF', [128, NFC, 512], BF16, 1)
            for fc in range(NFC):
                wg = S.rot('wg', [128, 8, 128], BF16, 3)
                wv = S.rot('wv', [128, 8, 128], BF16, 3)
                S.load(wg.v, srcw[:, :, fc * 128:(fc + 1) * 128], q='pool')
                S.load(wv.v, srcw[:, :, DFF + fc * 128:DFF + (fc + 1) * 128], q='pool')
                gps, vps = PS[(fc % 2) * 2], PS[(fc % 2) * 2 + 1]
                for k in range(8):
                    S.mm(gps[:, 0:NB], wg[:, k, :], hT[:, k, 0:NB], start=(k == 0), stop=(k == 7))
                for k in range(8):
                    S.mm(vps[:, 0:NB], wv[:, k, :], hT[:, k, 0:NB], start=(k == 0), stop=(k == 7))
                w0, w1, w2 = (wc_col[:, j, fc:fc + 1] for j in range(3))
                cv = S.rot('cv', [128, 512], F32, 2)
                if not is_s:
                    gb = S.rot('gb', [128, 514], F32, 2)
                    S.copy(gb[:, 0:2], carry[:, fc, :], eng='pool')
                    S.copy(gb[:, 2:514], gps.v, eng='act')
                    S.copy(carry[:, fc, :], gb[:, 512:514], eng='pool')
                    S.ts(cv.v, gb[:, 0:512], w0, ALU.mult, bc_col[:, fc:fc + 1], ALU.add)
                    S.op('dve', (lambda cv, gb, w1: lambda e: e.scalar_tensor_tensor(
                        out=cv.v.ap, in0=gb[:, 1:513].ap, scalar=w1.ap, in1=cv.v.ap, op0=ALU.mult, op1=ALU.add))(cv, gb, w1),
                        [gb, wc_col, cv], [cv])
                    S.op('dve', (lambda cv, gb, w2: lambda e: e.scalar_tensor_tensor(
                        out=cv.v.ap, in0=gb[:, 2:514].ap, scalar=w2.ap, in1=cv.v.ap, op0=ALU.mult, op1=ALU.add))(cv, gb, w2),
                        [gb, wc_col, cv], [cv])
                    if c0 + NB == TP:
                        S.tr(PS[6][0:2, 0:128], gb[:, 512:514], ident.v)
                        pcs = S.rot('pcs', [2, 128], F32, 2)
                        S.copy(pcs.v, PS[6][0:2, 0:128], eng='act')
                        S.store(o_pconv[:, fc * 128:(fc + 1) * 128], pcs.v)
                else:
                    cs_t = S.rot('cs_t', [32, 128], F32, 2)
                    S.load(cs_t.v, cst[:, fc * 128:(fc + 1) * 128])
                    S.tr(PS[6][:, 0:32], cs_t.v, ident[0:32, 0:32])
                    gb = S.rot('gbs', [128, NSEQ, 6], F32, 2)
                    S.copy(gb[:, :, 0:2], PS[6][:, 0:32].r("p (s j) -> p s j", j=2), eng='act')
                    S.copy(gb[:, :, 2:6], gps[:, 0:TS].r("p (s t) -> p s t", t=4), eng='act')
                    cv3 = cv[:, 0:TS].r("p (s t) -> p s t", t=4)
                    S.ts(cv3, gb[:, :, 0:4], w0, ALU.mult, bc_col[:, fc:fc + 1], ALU.add)
                    tmpc = S.rot('tmpc', [128, NSEQ, 4], F32, 2)
                    S.ts(tmpc.v, gb[:, :, 1:5], w1, ALU.mult)
                    S.tt(cv3, cv3, tmpc.v, ALU.add)
                    S.ts(tmpc.v, gb[:, :, 2:6], w2, ALU.mult)
                    S.tt(cv3, cv3, tmpc.v, ALU.add)
                    gl = S.rot('gl', [128, NSEQ, 2], F32, 2)
                    S.copy(gl.v, gb[:, :, 4:6], eng='pool')
                    S.tr(PS[7][0:32, 0:128], gl.v.r("p s j -> p (s j)"), ident.v)
                    scs = S.rot('scs', [32, 128], F32, 2)
                    S.copy(scs.v, PS[7][0:32, 0:128], eng='act')
                    S.store(o_sconv[:, fc * 128:(fc + 1) * 128], scs.v)
                S.act(cv[:, 0:NB], cv[:, 0:NB], AF.Silu)
                S.tt(aF[:, fc, 0:NB], cv[:, 0:NB], vps[:, 0:NB], ALU.mult)
            for t in range(ntile):
                tc0 = c0 + t * 128
                xr = xres[c0 // 128 + t]
                y_t = S.rot('y_t', [128, D], F32, 2)
                for n in range(2):
                    ps = PS[4 + n]
                    for fc in range(NFC):
                        S.mm(ps[0:P, :], aF[:, fc, t * 128:t * 128 + P], w_down[:, fc, n * 512:(n + 1) * 512],
                             start=(fc == 0), stop=(fc == NFC - 1))
                    S.tt(y_t[0:P, n * 512:(n + 1) * 512], ps[0:P, :], xr[0:P, n * 512:(n + 1) * 512], ALU.add)
                if is_s:
                    S.store(y_s[:, :], y_t[0:P])
                else:
                    S.store(y_p[tc0:tc0 + P, :], y_t[0:P])
        S.pop()
        S.finish()
    return nc


def _consts():
    half = 16
    inv_freq = (10000.0 ** (-np.arange(half, dtype=np.float32) / half)).astype(np.float32)

    def cs(pos):
        ang = pos.astype(np.float32)[..., None] * inv_freq
        return np.cos(ang).astype(np.float32), np.sin(ang).astype(np.float32)
    p = np.arange(128)
    cp, sp = cs(np.arange(NT)[None, :] * 128 + p[:, None])
    cS, sS = cs(PAST + (np.arange(TS) % 4))
    ck, sk = cs((p[:, None] % NPG) * 128 + np.arange(128)[None, :])
    tri = (p[:, None] <= p[None, :]).astype(np.float32)
    t64 = np.arange(64)
    bd = ((t64[:, None] // 4 == t64[None, :] // 4) & (t64[:, None] % 4 <= t64[None, :] % 4)).astype(np.float32)
    return {
        'c_ident': np.eye(128, dtype=np.float32), 'c_tri': tri, 'c_bd': bd,
        'c_cosp': cp.reshape(128, -1), 'c_sinp': sp.reshape(128, -1),
        'c_coss': cS, 'c_sins': sS,
        'c_cosk': ck.reshape(128, -1), 'c_sink': sk.reshape(128, -1),
    }


_CACHE = {}


def kernel(**inp):
    f = lambda a: np.ascontiguousarray(np.asarray(a))
    n_phys = inp['cache_ckv'].shape[1]
    if n_phys not in _CACHE:
        import os as _os
        _CACHE[n_phys] = build(n_phys, _os.environ.get('MK_STOP'))
    nc = _CACHE[n_phys]
    consts = _consts()
    cckv = f(inp['cache_ckv']).reshape(n_phys, 128 * 256)
    ckpe = f(inp['cache_kpe']).reshape(n_phys, 128 * 32)
    wnames = ['g_mix', 'w_in', 'ln_v_g', 'ln_v_b', 'w_s', 'b_s', 'g_q_a', 'w_uq', 'g_kv_a', 'w_uk', 'w_uv', 'g_qk_q',
              'g_qk_k', 'g_out_a', 'g_out_b', 'w_o', 'g_mem_x', 'g_mem_in', 'w_mq', 'w_mk', 'w_mv', 'g_mq', 'g_mk',
              'w_mo', 'g_ffn', 'w_up', 'w_conv', 'b_conv', 'w_down']
    shared = {nm: f(inp[nm])[0].reshape(-1) if inp[nm].ndim == 2 else f(inp[nm])[0] for nm in wnames}
    shared['ln_v_g'] = shared['ln_v_g'].reshape(-1)
    shared['ln_v_b'] = shared['ln_v_b'].reshape(-1)
    shared.update(consts)
    shared['cckv'] = cckv
    shared['ckpe'] = ckpe
    in_maps = []
    for c in range(NCORES):
        sl = slice(c * NSEQ, (c + 1) * NSEQ)
        m = dict(shared)
        m['xp'] = f(inp['x_prompt'][c])
        m['xs'] = f(inp['x_sample'][sl]).reshape(TS, D)
        m['cmk'] = f(inp['cache_mem_k'][0, sl]).reshape(NSEQ, 256, 512)
        m['cmv'] = f(inp['cache_mem_v'][0, sl]).reshape(NSEQ, 256, 512)
        m['cst'] = f(inp['state_ffn_conv'][0, sl]).reshape(NSEQ * 2, DFF)
        m['ptab'] = f(inp['page_table'][sl]).reshape(NSEQ * NPG, 1).astype(np.int32)
        m['memp'] = f(inp['mem_prompt'][c])
        in_maps.append(m)
    res = run_bass_kernel_spmd(nc, in_maps, core_ids=list(range(NCORES))).results
    cat = lambda k: np.concatenate([r[k] for r in res], axis=0)
    y_p = cat('y_p').reshape(8, 2048, D)
    y_s = cat('y_s').reshape(128, 4, D)
    return (y_p, y_s,
            cat('o_pckv').reshape(1, 8, 2048, 256), cat('o_pkpe').reshape(1, 8, 2048, 32),
            cat('o_pmk').reshape(1, 8, 256, 4, 128), cat('o_pmv').reshape(1, 8, 256, 4, 128),
            cat('o_pconv').reshape(1, 8, 2, DFF),
            cat('o_sckv').reshape(1, 128, 4, 256), cat('o_skpe').reshape(1, 128, 4, 32),
            cat('o_scv').reshape(1, 128, 4, 8, 64), cat('o_sconv').reshape(1, 128, 2, DFF))
```

```python
import numpy as np
from contextlib import ExitStack
import concourse.bass as bass
import concourse.mybir as mybir
from concourse.bass_utils import run_bass_kernel_spmd

F32 = mybir.dt.float32
BF16 = mybir.dt.bfloat16
I32 = mybir.dt.int32
AF = mybir.ActivationFunctionType
ALU = mybir.AluOpType
AX = mybir.AxisListType

NCORES = 8
D = 1024
TP = 2048
NT = 16
NSEQ = 16
TS = 64
NPG = 64
NIN = 1568
DFF = 2816
NFC = 22
EPS = 1e-6
PAST = 8192

COMPUTE = ('pe', 'dve', 'act', 'pool')
NPOOL = 24


class V:
    __slots__ = ('tl', 'ap')

    def __init__(self, tl, ap):
        self.tl = tl
        self.ap = ap

    def __getitem__(self, k):
        return V(self.tl, self.ap[k])

    def r(self, s, **kw):
        return V(self.tl, self.ap.rearrange(s, **kw))

    def un(self, ax):
        return V(self.tl, self.ap.unsqueeze(ax))

    def bc(self, shape):
        return V(self.tl, self.ap.to_broadcast(list(shape)))


class Tl:
    __slots__ = ('t', 'name', 'lw', 'rd', 'psum')

    def __init__(self, t, name, psum=False, init_rd=()):
        self.t = t
        self.name = name
        self.lw = None
        self.rd = list(init_rd)
        self.psum = psum

    def __getitem__(self, k):
        return V(self, self.t[k])

    @property
    def v(self):
        return V(self, self.t[:])


class Sched:
    def __init__(self, nc, es):
        self.nc = nc
        self.es = es
        self.scopes = [(es, [])]
        self.prog = {k: [] for k in ('pe', 'dve', 'act', 'pool', 'sp')}
        self.sems = {}
        self.cnt = {}
        self.seen = {k: {} for k in self.prog}
        for k in COMPUTE:
            self.sems[k] = es.enter_context(nc.semaphore('s_' + k))
            self.cnt[k] = 0
        self.dpool = {}
        for q in ('sp', 'pool', 'act'):
            lst = []
            for i in range(NPOOL):
                key = 'd_%s_%d' % (q, i)
                self.sems[key] = es.enter_context(nc.semaphore(key))
                self.cnt[key] = 0
                lst.append(key)
            self.dpool[q] = [lst, 0]
        self.out_waits = []
        self.ntile = 0
        self.pending = []
        self.rots = {}

    def push(self):
        es = ExitStack()
        self.scopes.append((es, []))
        self.scope_ctr = getattr(self, 'scope_ctr', 0) + 1
        self.scope_ids = getattr(self, 'scope_ids', [0]) + [self.scope_ctr]

    def pop(self):
        es, tiles = self.scopes.pop()
        best = {}
        for k, v, e in self.pending:
            if best.get(k, (0, None))[0] < v:
                best[k] = (v, e)
        for t in tiles:
            deps = list(t.rd)
            if t.lw is not None:
                deps.append(t.lw)
            for k, v, e in deps:
                if best.get(k, (0, None))[0] < v:
                    best[k] = (v, e)
        self.pending = [(k, v, 'freed') for k, (v, e) in best.items()]
        self.scope_ids = self.scope_ids[:-1]
        es.close()

    def sb(self, shape, dt, name=None):
        self.ntile += 1
        name = (name or 't') + '_%d' % self.ntile
        es, tiles = self.scopes[-1]
        t = es.enter_context(self.nc.sbuf_tensor(name, list(shape), dt))
        tl = Tl(t, name, init_rd=self.pending)
        tiles.append(tl)
        return tl

    def ps(self, shape, dt=F32, name=None):
        self.ntile += 1
        name = (name or 'p') + '_%d' % self.ntile
        es, tiles = self.scopes[-1]
        t = es.enter_context(self.nc.psum_tensor(name, list(shape), dt))
        tl = Tl(t, name, psum=True, init_rd=self.pending)
        tiles.append(tl)
        return tl

    def rot(self, key, shape, dt, n=2):
        k = (key, getattr(self, 'scope_ids', [0])[-1])
        if k not in self.rots:
            self.rots[k] = [[self.sb(shape, dt, key) for _ in range(n)], 0]
        ent = self.rots[k]
        t = ent[0][ent[1] % n]
        ent[1] += 1
        return t

    def _collect(self, eng, reads, writes):
        need = {}

        def add(dep, same_ok=False):
            key, val, deng = dep
            if deng == eng and same_ok:
                return
            if need.get(key, 0) < val:
                need[key] = val

        for r in reads:
            if r.lw is not None:
                add(r.lw, same_ok=(eng == 'pe' and r.psum))
        for w in writes:
            if w.lw is not None:
                add(w.lw, same_ok=True)
            for d in w.rd:
                add(d, same_ok=True)
        out = []
        seen = self.seen[eng]
        for key, val in need.items():
            if seen.get(key, 0) >= val:
                continue
            seen[key] = val
            out.append((key, val))
        return out

    def _mark(self, reads, writes, dep):
        for w in writes:
            w.lw = dep
            w.rd = []
        for r in reads:
            if r in writes:
                continue
            r.rd.append(dep)
            if len(r.rd) > 48:
                best = {}
                for k, v, e in r.rd:
                    if best.get(k, (0, None))[0] < v:
                        best[k] = (v, e)
                r.rd = [(k, v, e) for k, (v, e) in best.items()]

    def op(self, eng, fn, reads=(), writes=()):
        reads = list({id(t): t for t in reads}.values())
        writes = list({id(t): t for t in writes}.values())
        waits = self._collect(eng, reads, writes)
        self.cnt[eng] += 1
        val = self.cnt[eng]
        self.prog[eng].append((waits, fn, (eng, 1)))
        self._mark(reads, writes, (eng, val, eng))

    def dma(self, q, fn, reads=(), writes=(), is_output=False):
        reads = list({id(t): t for t in reads}.values())
        writes = list({id(t): t for t in writes}.values())
        waits = self._collect(q, reads, writes)
        lst, idx = self.dpool[q]
        key = lst[idx % NPOOL]
        self.dpool[q][1] = idx + 1
        prev = self.cnt[key]
        if prev > 0 and self.seen[q].get(key, 0) < prev:
            self.seen[q][key] = prev
            waits.append((key, prev))
        self.cnt[key] += 16
        val = self.cnt[key]
        self.prog[q].append((waits, fn, (key, 16)))
        self._mark(reads, writes, (key, val, 'dma_' + q))
        if is_output:
            self.out_waits.append((key, val))

    def finish(self):
        need = {}
        for key, val in self.out_waits:
            if need.get(key, 0) < val:
                need[key] = val
        self.prog['sp'].append((list(need.items()), None, None))
        nc, sems, prog = self.nc, self.sems, self.prog

        def run(e, lst):
            for waits, fn, inc in lst:
                for key, val in waits:
                    e.wait_ge(sems[key], val)
                if fn is not None:
                    fn(e).then_inc(sems[inc[0]], inc[1])

        with nc.Block() as block:
            @block.tensor
            def _(e):
                run(e, prog['pe'])

            @block.vector
            def _(e):
                run(e, prog['dve'])

            @block.scalar
            def _(e):
                run(e, prog['act'])

            @block.gpsimd
            def _(e):
                run(e, prog['pool'])

            @block.sync
            def _(e):
                run(e, prog['sp'])

    def act(self, out, in_, func, bias=None, scale=None, accum=None, eng='act'):
        kw = {}
        rd = [in_.tl]
        wr = [out.tl]
        if bias is not None:
            kw['bias'] = bias.ap
            rd.append(bias.tl)
        if scale is not None:
            if isinstance(scale, V):
                kw['scale'] = scale.ap
                rd.append(scale.tl)
            else:
                kw['scale'] = float(scale)
        if accum is not None:
            kw['accum_out'] = accum.ap
            wr.append(accum.tl)
        self.op(eng, lambda e: e.activation(out=out.ap, in_=in_.ap, func=func, **kw), rd, wr)

    def tt(self, out, a, b, op, eng='dve'):
        self.op(eng, lambda e: e.tensor_tensor(out=out.ap, in0=a.ap, in1=b.ap, op=op), [a.tl, b.tl], [out.tl])

    def ts(self, out, a, s1, op0, s2=None, op1=None, eng='dve', accum=None):
        rd = [a.tl]
        wr = [out.tl]
        x1 = s1
        if isinstance(s1, V):
            rd.append(s1.tl)
            x1 = s1.ap
        x2 = s2
        if isinstance(s2, V):
            rd.append(s2.tl)
            x2 = s2.ap
        kw = {}
        if op1 is not None:
            kw['op1'] = op1
        if accum is not None:
            kw['accum_out'] = accum.ap
            wr.append(accum.tl)
        self.op(eng, lambda e: e.tensor_scalar(out=out.ap, in0=a.ap, scalar1=x1, scalar2=x2, op0=op0, **kw), rd, wr)

    def red(self, out, in_, op=ALU.add, eng='dve'):
        self.op(eng, lambda e: e.tensor_reduce(out=out.ap, in_=in_.ap, axis=AX.X, op=op), [in_.tl], [out.tl])

    def copy(self, out, in_, eng='dve'):
        if eng == 'act':
            self.act(out, in_, AF.Copy)
        else:
            self.op(eng, lambda e: e.tensor_copy(out=out.ap, in_=in_.ap), [in_.tl], [out.tl])

    def recip(self, out, in_, eng='dve'):
        self.op(eng, lambda e: e.reciprocal(out=out.ap, in_=in_.ap), [in_.tl], [out.tl])

    def memset(self, out, val, eng='pool'):
        self.op(eng, lambda e: e.memset(out.ap, val), [], [out.tl])

    def mm(self, out, lhsT, rhs, start=True, stop=True):
        self.op('pe', lambda e: e.matmul(out.ap, lhsT=lhsT.ap, rhs=rhs.ap, start=start, stop=stop),
                [lhsT.tl, rhs.tl], [out.tl])

    def tr(self, out, in_, ident):
        self.op('pe', lambda e: e.transpose(out=out.ap, in_=in_.ap, identity=ident.ap),
                [in_.tl, ident.tl], [out.tl])

    def load(self, out, src, q='sp', **kw):
        self.dma(q, lambda e: e.dma_start(out=out.ap, in_=src, **kw), [], [out.tl])

    def store(self, dst, in_, q='sp', **kw):
        self.dma(q, lambda e: e.dma_start(out=dst, in_=in_.ap, **kw), [in_.tl], [], is_output=True)


class _Stop(Exception):
    pass


def build(n_phys, stop=None):
    nc = bass.Bass("TRN2", target_bir_lowering=False)
    try:
        _build(nc, n_phys, stop)
    except _Stop:
        pass
    return nc


def _build(nc, n_phys, stop):

    def din(name, shape, dt=F32):
        return nc.dram_tensor(name, list(shape), dt, kind="ExternalInput").ap()

    def dout(name, shape):
        return nc.dram_tensor(name, list(shape), F32, kind="ExternalOutput").ap()

    xp = din('xp', [TP, D])
    xs = din('xs', [TS, D])
    cckv = din('cckv', [n_phys, 128 * 256])
    ckpe = din('ckpe', [n_phys, 128 * 32])
    cmk = din('cmk', [NSEQ, 256, 512])
    cmv = din('cmv', [NSEQ, 256, 512])
    cst = din('cst', [NSEQ * 2, DFF])
    ptab = din('ptab', [NSEQ * NPG, 1], I32)
    memp = din('memp', [256, D])
    W = {}
    for nm, shp in [('g_mix', [D]), ('w_in', [D, NIN]), ('ln_v_g', [512]), ('ln_v_b', [512]), ('w_s', [8, 128, 128]),
                    ('b_s', [8, 128]), ('g_q_a', [256]), ('w_uq', [256, 768]), ('g_kv_a', [256]), ('w_uk', [256, 512]),
                    ('w_uv', [256, 512]), ('g_qk_q', [96]), ('g_qk_k', [96]), ('g_out_a', [512]), ('g_out_b', [512]),
                    ('w_o', [D, D]), ('g_mem_x', [D]), ('g_mem_in', [D]), ('w_mq', [D, 512]), ('w_mk', [D, 512]),
                    ('w_mv', [D, 512]), ('g_mq', [128]), ('g_mk', [128]), ('w_mo', [512, D]), ('g_ffn', [D]),
                    ('w_up', [D, 2 * DFF]), ('w_conv', [3, DFF]), ('b_conv', [DFF]), ('w_down', [DFF, D])]:
        W[nm] = din(nm, shp)
    c_ident = din('c_ident', [128, 128])
    c_tri = din('c_tri', [128, 128])
    c_bd = din('c_bd', [64, 64])
    c_cosp = din('c_cosp', [128, NT * 16])
    c_sinp = din('c_sinp', [128, NT * 16])
    c_coss = din('c_coss', [64, 16])
    c_sins = din('c_sins', [64, 16])
    c_cosk = din('c_cosk', [128, 128 * 16])
    c_sink = din('c_sink', [128, 128 * 16])

    y_p = dout('y_p', [TP, D])
    y_s = dout('y_s', [TS, D])
    o_pckv = dout('o_pckv', [TP, 256])
    o_pkpe = dout('o_pkpe', [TP, 32])
    o_pmk = dout('o_pmk', [256, 512])
    o_pmv = dout('o_pmv', [256, 512])
    o_pconv = dout('o_pconv', [2, DFF])
    o_sckv = dout('o_sckv', [TS, 256])
    o_skpe = dout('o_skpe', [TS, 32])
    o_scv = dout('o_scv', [TS, 512])
    o_sconv = dout('o_sconv', [NSEQ * 2, DFF])

    SC_MLA = 96.0 ** -0.5
    SC_MEM = 128.0 ** -0.5

    with ExitStack() as es:
        S = Sched(nc, es)

        def ckpt(name):
            if stop == name:
                while len(S.scopes) > 1:
                    S.pop()
                S.finish()
                raise _Stop()
        PS = [S.ps([128, 512], F32, 'bank%d' % i) for i in range(8)]

        ident = S.sb([128, 128], F32, 'ident')
        S.load(ident.v, c_ident)
        identb = S.sb([128, 128], BF16, 'identb')
        S.copy(identb.v, ident.v, eng='pool')
        tri = S.sb([128, 128], F32, 'tri')
        S.load(tri.v, c_tri)
        trib = S.sb([128, 128], BF16, 'trib')
        S.copy(trib.v, tri.v, eng='pool')
        bd = S.sb([64, 64], F32, 'bd')
        S.load(bd.v, c_bd)
        onesb = S.sb([128, 128], BF16, 'onesb')
        S.memset(onesb.v, 1.0)
        epsT = S.sb([128, 1], F32, 'eps')
        S.memset(epsT.v, EPS)
        cosp = S.sb([128, NT, 16], F32, 'cosp')
        sinp = S.sb([128, NT, 16], F32, 'sinp')
        S.load(cosp.v, c_cosp.rearrange("p (t f) -> p t f", f=16))
        S.load(sinp.v, c_sinp.rearrange("p (t f) -> p t f", f=16))
        coss = S.sb([64, 16], F32, 'coss')
        sins = S.sb([64, 16], F32, 'sins')
        S.load(coss.v, c_coss)
        S.load(sins.v, c_sins)

        def bcast_vec(name, n):
            t = S.sb([128, n], F32, 'bc_' + name)
            S.load(t.v, W[name].partition_broadcast(128))
            return t

        def col_vec(name, n):
            t = S.sb([128, n // 128], F32, 'col_' + name)
            S.load(t.v, W[name].rearrange("(c p) -> p c", p=128), allow_slow_non_contiguous=True)
            return t

        g_kv_bc = bcast_vec('g_kv_a', 256)
        g_qq_bc = bcast_vec('g_qk_q', 96)
        g_qk_bc = bcast_vec('g_qk_k', 96)
        lng_bc = bcast_vec('ln_v_g', 512)
        lnb_bc = bcast_vec('ln_v_b', 512)
        g_mq_bc = bcast_vec('g_mq', 128)
        g_mk_bc = bcast_vec('g_mk', 128)

        def bound(ga, gb, n, sc, name):
            ma = S.sb([128, 1], F32, name + 'a')
            mb = S.sb([128, 1], F32, name + 'b')
            S.op('dve', lambda e: e.tensor_reduce(out=ma.v.ap, in_=ga.v.ap, axis=AX.X, op=ALU.max,
                                                  apply_absolute_value=True), [ga], [ma])
            S.op('dve', lambda e: e.tensor_reduce(out=mb.v.ap, in_=gb.v.ap, axis=AX.X, op=ALU.max,
                                                  apply_absolute_value=True), [gb], [mb])
            c = S.sb([128, 1], F32, name)
            S.tt(c.v, ma.v, mb.v, ALU.mult)
            S.ts(c.v, c.v, -float(n) * sc, ALU.mult)
            return c
        negC = bound(g_qq_bc, g_qk_bc, 96, SC_MLA, 'negC')
        negCm = bound(g_mq_bc, g_mk_bc, 128, SC_MEM, 'negCm')

        def load_w(name, K, N, n0=0, tname=None):
            kc = K // 128
            t = S.sb([128, kc, N], BF16, tname or ('w_' + name))
            src = W[name].rearrange("(c p) n -> p c n", p=128)
            for c in range(kc):
                for a in range(0, N, 1024):
                    b = min(N, a + 1024)
                    S.load(t[:, c, a:b], src[:, c, n0 + a:n0 + b], q='pool')
            return t

        def junk_tile(P, n):
            j = S.rot('junk', [128, 1024], BF16, 1)
            return j[0:P, 0:n]

        def rinv_of(src, n, P):
            ss = S.rot('ss', [128, 1], F32, 2)
            S.act(junk_tile(P, n), src, AF.Square, accum=ss[0:P, :])
            rt = S.rot('rt', [128, 1], F32, 2)
            S.act(rt[0:P, :], ss[0:P, :], AF.Sqrt, bias=epsT[0:P, :], scale=1.0 / n)
            ri = S.rot('ri', [128, 1], F32, 2)
            S.recip(ri[0:P, :], rt[0:P, :])
            return ri

        def group_rinv(src3, G, Dg, P):
            sq = S.rot('gsq%d' % (G * Dg), [128, G, Dg], F32, 1)
            S.tt(sq[0:P], src3, src3, ALU.mult, eng='pool')
            ss = S.rot('gss%d' % G, [128, G], F32, 2)
            S.red(ss[0:P], sq[0:P])
            rt = S.rot('grt%d' % G, [128, G], F32, 2)
            S.act(rt[0:P], ss[0:P], AF.Sqrt, bias=epsT[0:P, :], scale=1.0 / Dg)
            ri = S.rot('gri%d' % G, [128, G], F32, 2)
            S.recip(ri[0:P], rt[0:P])
            return ri

        def transposes(dst, srcs, P, bank, scale=None):
            for i0 in range(0, len(srcs), 4):
                grp = srcs[i0:i0 + 4]
                for j, s in enumerate(grp):
                    w = s.ap.shape[-1]
                    S.tr(bank[0:w, j * 128:j * 128 + P], s, ident[0:P, 0:P])
                for j, s in enumerate(grp):
                    w = s.ap.shape[-1]
                    if scale is None:
                        S.copy(dst(i0 + j), bank[0:w, j * 128:j * 128 + P], eng='act')
                    else:
                        S.act(dst(i0 + j), bank[0:w, j * 128:j * 128 + P], AF.Copy, scale=scale(i0 + j))

        def rope(dst1, dst2, x1, x2, cs, sn, P):
            H = x1.ap.shape[1]
            cb = cs.un(1).bc([P, H, 16])
            sb_ = sn.un(1).bc([P, H, 16])
            t1 = S.rot('rp1', [128, H, 16], F32, 1)
            t2 = S.rot('rp2', [128, H, 16], F32, 1)
            S.tt(t1[0:P], x1, cb, ALU.mult)
            S.tt(t2[0:P], x2, sb_, ALU.mult, eng='pool')
            S.tt(dst1, t1[0:P], t2[0:P], ALU.subtract)
            t3 = S.rot('rp3', [128, H, 16], F32, 1)
            t4 = S.rot('rp4', [128, H, 16], F32, 1)
            S.tt(t3[0:P], x2, cb, ALU.mult)
            S.tt(t4[0:P], x1, sb_, ALU.mult, eng='pool')
            S.tt(dst2, t3[0:P], t4[0:P], ALU.add)

        arena = S.sb([128, 17 * 1024], F32, 'arena')

        def alias(lo, hi, parts, dt, pattern=None, **kw):
            ap = arena.t[0:parts, lo:hi]
            if dt == BF16:
                ap = ap.bitcast(BF16)
            if pattern:
                ap = ap.rearrange(pattern, **kw)
            return Tl(ap, 'alias_%d' % lo)
        kT = alias(0, 8192, 96, BF16, "p (h t) -> p h t", h=8)
        Vaug = alias(8192, 12416, 128, BF16, "p (t h c) -> p t h c", t=NT, h=8)
        qT = alias(12416, 14464, 96, BF16, "p (h t) -> p h t", h=8)
        b_out = alias(14464, 16512, 128, F32, "p (t c) -> p t c", t=4)
        arena_alias = [kT, Vaug, qT, b_out]

        mkT = S.sb([128, 4, 256], BF16, 'mkT')
        mvb = S.sb([128, 2, 512], BF16, 'mvb')

        g_oa_col = col_vec('g_out_a', 512)
        g_ob_col = col_vec('g_out_b', 512)

        S.push()
        aT = S.sb([128, 4, TP + TS], BF16, 'aT')
        bT = S.sb([128, 4, TP + TS], BF16, 'bT')

        S.push()
        w_uk = load_w('w_uk', 256, 512)
        w_uv = load_w('w_uv', 256, 512)
        qnT_s = S.sb([128, 4, TS], BF16, 'qnT_s')
        qpeT_s = S.sb([32, 8, TS], BF16, 'qpeT_s')
        kT_s = S.sb([96, 8, TS], BF16, 'kT_s')
        qT_s = S.sb([96, 8, TS], BF16, 'qT_s')
        Cb_new = S.sb([64, 264], BF16, 'Cb_new')
        S.memset(Cb_new.v, 1.0)

        S.push()
        g_mix_col = col_vec('g_mix', D)
        g_qa_col = col_vec('g_q_a', 256)
        w_in = load_w('w_in', D, NIN)
        w_uq = load_w('w_uq', 256, 768)
        WsT = S.sb([128, 8, 128], BF16, 'WsT')
        WsT_s = S.sb([64, 8, 64], BF16, 'WsT_s')
        bsT = S.sb([128, 8], F32, 'bsT')
        S.load(bsT.v, W['b_s'].rearrange("g t -> t g"), allow_slow_non_contiguous=True)
        bsT_s = S.sb([64, 8], F32, 'bsT_s')
        for sq in range(NSEQ):
            S.dma('act', (lambda sq: lambda e: e.dma_start(out=bsT_s.t[sq * 4:(sq + 1) * 4, :],
                                                          in_=W['b_s'][:, 0:4].rearrange("g t -> t g"),
                                                          allow_slow_non_contiguous=True))(sq), [], [bsT_s])
        S.push()
        wsf = S.sb([128, 8, 128], F32, 'wsf')
        S.load(wsf.v, W['w_s'].rearrange("g t s -> t g s"))
        for g0 in range(0, 8, 4):
            for j in range(4):
                S.tr(PS[0][:, j * 128:(j + 1) * 128], wsf[:, g0 + j, :], ident.v)
            S.tt(WsT[:, g0:g0 + 4, :], PS[0].v.r("p (j t) -> p j t", j=4), tri.v.un(1).bc([128, 4, 128]), ALU.mult)
        wsf_s = S.sb([64, 8, 64], F32, 'wsf_s')
        S.memset(wsf_s.v.r("p g s -> p (g s)"), 0.0)
        for sq in range(NSEQ):
            S.dma('act', (lambda sq: lambda e: e.dma_start(
                out=wsf_s.t[sq * 4:(sq + 1) * 4, :, sq * 4:(sq + 1) * 4],
                in_=W['w_s'][:, 0:4, 0:4].rearrange("g t s -> t g s")))(sq), [], [wsf_s])
        for g0 in range(0, 8, 4):
            for j in range(4):
                S.tr(PS[1][0:64, j * 128:j * 128 + 64], wsf_s[:, g0 + j, :], ident[0:64, 0:64])
            S.tt(WsT_s[:, g0:g0 + 4, :], PS[1][0:64, :].r("p (j t) -> p j t", j=4)[:, :, 0:64],
                 bd.v.un(1).bc([64, 4, 64]), ALU.mult)
        S.pop()
        S.memset(Vaug.v.r("p t h c -> p (t h c)"), 1.0)

        ckpt('c1')

        def phase1(ti, P, xsrc, is_sample, qcol):
            tok0 = ti * 128
            x_t = S.rot('x_t', [128, D], F32, 2)
            S.load(x_t[0:P, :], xsrc)
            ri = rinv_of(x_t[0:P, :], D, P)
            S.ts(x_t[0:P, :], x_t[0:P, :], ri[0:P, 0:1], ALU.mult, eng='pool')
            xT = S.rot('xT', [128, 8, 128], BF16, 1)
            transposes(lambda i: xT[:, i, 0:P], [x_t[0:P, i * 128:(i + 1) * 128] for i in range(8)], P, PS[0],
                       scale=lambda i: g_mix_col[:, i:i + 1])
            zb = [PS[1], PS[2], PS[3], PS[4]]
            for n in range(4):
                n0 = n * 512
                n1 = min(NIN, n0 + 512)
                for k in range(8):
                    S.mm(zb[n][0:P, 0:n1 - n0], xT[:, k, 0:P], w_in[:, k, n0:n1], start=(k == 0), stop=(k == 7))
            gu = S.rot('gu', [128, 512], F32, 1)
            S.act(gu[0:P], zb[0][0:P, :], AF.Gelu_apprx_tanh)
            gv = S.rot('gv', [128, 8, 64], F32, 1)
            S.act(gv[0:P].r("p g d -> p (g d)"), zb[1][0:P, :], AF.Gelu_apprx_tanh)
            s1 = S.rot('s1', [128, 8], F32, 2)
            S.red(s1[0:P], gv[0:P])
            S.ts(s1[0:P], s1[0:P], -1.0 / 64, ALU.mult)
            cen = S.rot('cen', [128, 8, 64], F32, 1)
            S.tt(cen[0:P], gv[0:P], s1[0:P].un(2).bc([P, 8, 64]), ALU.add)
            sq = S.rot('lsq', [128, 8, 64], F32, 1)
            S.tt(sq[0:P], cen[0:P], cen[0:P], ALU.mult, eng='pool')
            var = S.rot('var', [128, 8], F32, 2)
            S.red(var[0:P], sq[0:P])
            S.act(var[0:P], var[0:P], AF.Sqrt, bias=epsT[0:P, :], scale=1.0 / 64)
            S.recip(var[0:P], var[0:P])
            S.tt(cen[0:P], cen[0:P], var[0:P].un(2).bc([P, 8, 64]), ALU.mult)
            S.tt(cen[0:P], cen[0:P], lng_bc[0:P].r("p (g d) -> p g d", g=8), ALU.mult, eng='pool')
            vg = S.rot('vg', [128, 8, 64], BF16, 1)
            if is_sample:
                S.tt(sq[0:P], cen[0:P], lnb_bc[0:P].r("p (g d) -> p g d", g=8), ALU.add)
                S.store(o_scv, sq[0:P].r("p g d -> p (g d)"))
                S.copy(vg[0:P], sq[0:P], eng='pool')
            else:
                S.tt(vg[0:P], cen[0:P], lnb_bc[0:P].r("p (g d) -> p g d", g=8), ALU.add)
            sp_ps = PS[5]
            for g in range(8):
                lw = WsT_s[:, g, :] if is_sample else WsT[:, g, :]
                S.mm(sp_ps[0:P, g * 64:(g + 1) * 64], lw, vg[0:P, g, :])
            bt = bsT_s if is_sample else bsT
            S.tt(cen[0:P], sp_ps[0:P, :].r("p (g d) -> p g d", g=8), bt[0:P].un(2).bc([P, 8, 64]), ALU.add)
            a_o = gv
            S.tt(a_o[0:P].r("p g d -> p (g d)"), cen[0:P].r("p g d -> p (g d)"), gu[0:P], ALU.mult, eng='pool')
            a_f = a_o[0:P].r("p g d -> p (g d)")
            ria = rinv_of(a_f, 512, P)
            S.ts(a_f, a_f, ria[0:P, 0:1], ALU.mult, eng='pool')
            acol = TP if is_sample else tok0
            transposes(lambda i: aT[:, i, acol:acol + P], [a_f[:, i * 128:(i + 1) * 128] for i in range(4)], P, PS[6],
                       scale=lambda i: g_oa_col[:, i:i + 1])
            c3 = zb[2]
            cq = S.rot('cq', [128, 256], F32, 1)
            riq = rinv_of(c3[0:P, 0:256], 256, P)
            S.act(cq[0:P], c3[0:P, 0:256], AF.Copy, scale=riq[0:P, 0:1])
            cqT = S.rot('cqT', [128, 2, 128], BF16, 1)
            transposes(lambda i: cqT[:, i, 0:P], [cq[0:P, i * 128:(i + 1) * 128] for i in range(2)], P, PS[6],
                       scale=lambda i: g_qa_col[:, i:i + 1])
            ckn = S.rot('ckn', [128, 256], F32, 2)
            rik = rinv_of(c3[0:P, 256:512], 256, P)
            S.act(ckn[0:P], c3[0:P, 256:512], AF.Copy, scale=rik[0:P, 0:1])
            S.tt(ckn[0:P], ckn[0:P], g_kv_bc[0:P], ALU.mult)
            kpe = S.rot('kpe', [128, 32], F32, 2)
            S.copy(kpe[0:P], zb[3][0:P, 0:32], eng='act')
            if is_sample:
                S.store(o_sckv, ckn[0:P])
                S.store(o_skpe, kpe[0:P])
                S.copy(Cb_new[:, 0:256], ckn[0:P], eng='pool')
            else:
                S.store(o_pckv[tok0:tok0 + P, :], ckn[0:P])
                S.store(o_pkpe[tok0:tok0 + P, :], kpe[0:P])
            ckT = S.rot('ckT', [128, 2, 128], BF16, 1)
            transposes(lambda i: ckT[:, i, 0:P], [ckn[0:P, i * 128:(i + 1) * 128] for i in range(2)], P, PS[6])
            q_ps0, q_ps1 = PS[7], PS[5]
            for k in range(2):
                S.mm(q_ps0[0:P, :], cqT[:, k, 0:P], w_uq[:, k, 0:512], start=(k == 0), stop=(k == 1))
            q_sb = S.rot('q_sb', [128, 8, 96], F32, 1)
            S.copy(q_sb[0:P].r("p h d -> p (h d)")[:, 0:512], q_ps0[0:P, :], eng='act')
            for k in range(2):
                S.mm(q_ps1[0:P, 0:256], cqT[:, k, 0:P], w_uq[:, k, 512:768], start=(k == 0), stop=(k == 1))
            S.copy(q_sb[0:P].r("p h d -> p (h d)")[:, 512:768], q_ps1[0:P, 0:256], eng='act')
            kn_ps, v_ps = PS[1], PS[2]
            for k in range(2):
                S.mm(kn_ps[0:P, :], ckT[:, k, 0:P], w_uk[:, k, :], start=(k == 0), stop=(k == 1))
            for k in range(2):
                S.mm(v_ps[0:P, :], ckT[:, k, 0:P], w_uv[:, k, :], start=(k == 0), stop=(k == 1))
            k_sb = S.rot('k_sb', [128, 8, 96], F32, 1)
            S.copy(k_sb[0:P, :, 0:64], kn_ps[0:P, :].r("p (h d) -> p h d", h=8), eng='act')
            S.copy(k_sb[0:P, :, 64:96], kpe[0:P].un(1).bc([P, 8, 32]), eng='pool')
            if not is_sample:
                S.copy(Vaug[0:P, ti, :, 0:64], v_ps[0:P, :].r("p (h d) -> p h d", h=8), eng='act')
            cs = coss.v if is_sample else cosp[:, ti, :]
            sn = sins.v if is_sample else sinp[:, ti, :]
            for nm, src, gbc in (('q', q_sb, g_qq_bc), ('k', k_sb, g_qk_bc)):
                rg = group_rinv(src[0:P], 8, 96, P)
                S.tt(src[0:P], src[0:P], rg[0:P].un(2).bc([P, 8, 96]), ALU.mult)
                S.tt(src[0:P], src[0:P], gbc[0:P].un(1).bc([P, 8, 96]), ALU.mult, eng='pool')
                fin = S.rot('fin', [128, 8, 96], F32, 1)
                S.copy(fin[0:P, :, 0:64], src[0:P, :, 0:64], eng='pool')
                rope(fin[0:P, :, 64:80], fin[0:P, :, 80:96], src[0:P, :, 64:80], src[0:P, :, 80:96], cs[0:P], sn[0:P], P)
                if is_sample:
                    dstT = qT_s if nm == 'q' else kT_s
                    transposes(lambda h: dstT[:, h, 0:P], [fin[0:P, h, :] for h in range(8)], P, PS[6])
                    if nm == 'q':
                        qg = S.rot('qg', [128, 8, 64], F32, 1)
                        S.tt(qg[0:P], fin[0:P, :, 0:64], g_qk_bc[0:P, 0:64].un(1).bc([P, 8, 64]), ALU.mult)
                        transposes(lambda j: qnT_s[:, j, 0:P],
                                   [qg[0:P, 2 * j:2 * j + 2, :].r("p h d -> p (h d)") for j in range(4)], P, PS[6])
                        qpe = S.rot('qpe', [128, 8, 32], F32, 1)
                        S.copy(qpe[0:P], fin[0:P, :, 64:96], eng='pool')
                        transposes(lambda h: qpeT_s[:, h, 0:P], [qpe[0:P, h, :] for h in range(8)], P, PS[6])
                else:
                    if nm == 'q':
                        transposes(lambda h: qT[:, h, qcol:qcol + P], [fin[0:P, h, :] for h in range(8)], P, PS[6])
                    else:
                        transposes(lambda h: kT[:, h, tok0:tok0 + P], [fin[0:P, h, :] for h in range(8)], P, PS[7])

        att_cnt = [0]

        def attention(qb):
            for h in range(8):
                ot = PS[4 + (att_cnt[0] % 2)]
                att_cnt[0] += 1
                nkb = 4 * qb + 4

                def s_stage(kb):
                    c0 = max(0, kb - 4 * qb) * 128
                    st = PS[kb % 2]
                    S.mm(st[:, c0:512], kT[:, h, kb * 128:(kb + 1) * 128], qT[:, h, c0:512])
                    pt = S.rot('pt', [128, 512], BF16, 3)
                    S.act(pt[:, c0:512], st[:, c0:512], AF.Exp, bias=negC[:, 0:1], scale=SC_MLA)
                    if kb >= 4 * qb:
                        S.tt(pt[:, c0:c0 + 128], pt[:, c0:c0 + 128], trib.v, ALU.mult, eng='pool')
                    return pt, c0
                nxt = s_stage(0)
                for kb in range(nkb):
                    pt, c0 = nxt
                    if kb + 1 < nkb:
                        nxt = s_stage(kb + 1)
                    S.mm(ot[0:65, c0:512], Vaug[:, kb, h, 0:65], pt[:, c0:512], start=(kb == 0), stop=(kb == nkb - 1))
                ot_sb = S.rot('ot_sb', [65, 512], F32, 2)
                S.copy(ot_sb.v, ot[0:65, :], eng='act')
                tp = PS[6 + (att_cnt[0] % 2)]
                for j in range(4):
                    S.tr(tp[:, j * 128:j * 128 + 65], ot_sb[:, j * 128:(j + 1) * 128], ident[0:65, 0:65])
                tpv = tp.v.r("p (j c) -> p j c", j=4)
                rd = S.rot('rd', [128, 4, 1], F32, 2)
                S.recip(rd.v, tpv[:, :, 64:65])
                S.tt(b_out[:, :, h * 64:(h + 1) * 64], tpv[:, :, 0:64], rd.v.bc([128, 4, 64]), ALU.mult)
            for t in range(4):
                ti = 4 * qb + t
                rib = rinv_of(b_out[:, t, :], 512, 128)
                S.ts(b_out[:, t, :], b_out[:, t, :], rib[:, 0:1], ALU.mult, eng='pool')
                transposes(lambda i: bT[:, i, ti * 128:(ti + 1) * 128], [b_out[:, t, i * 128:(i + 1) * 128] for i in range(4)],
                           128, PS[2 + t % 2], scale=lambda i: g_ob_col[:, i:i + 1])

        for qb in range(4):
            for t in range(4):
                ti = 4 * qb + t
                phase1(ti, 128, xp[ti * 128:(ti + 1) * 128, :], False, t * 128)
                ckpt('c2')
            if qb == 3:
                phase1(0, TS, xs[:, :], True, 0)
            attention(qb)
            ckpt('c3')
        S.pop()
        ckpt('c4')

        S.push()
        wukT = S.sb([128, 4, 256], BF16, 'wukT')
        S.push()
        wukf = S.sb([128, 2, 512], F32, 'wukf')
        S.load(wukf.v, W['w_uk'].rearrange("(c p) n -> p c n", p=128))
        for j in range(4):
            for cc in range(2):
                S.tr(PS[0][:, cc * 128:(cc + 1) * 128], wukf[:, cc, j * 128:(j + 1) * 128], ident.v)
            S.copy(wukT[:, j, :], PS[0][:, 0:256], eng='act')
        S.pop()
        ckpt('c40')
        qlatT = S.sb([128, 2, 8, TS], BF16, 'qlatT')
        for h in range(8):
            j, a = h // 2, h % 2
            for cc in range(2):
                col = (j * 2 + cc) * 64
                S.mm(PS[1 + a][:, col:col + TS],
                     wukT[a * 64:(a + 1) * 64, j, cc * 128:(cc + 1) * 128], qnT_s[a * 64:(a + 1) * 64, j, :])
        for a in range(2):
            S.copy(qlatT.v.r("p c (j a) t -> p a j c t", a=2)[:, a],
                   PS[1 + a].v.r("p (j c t) -> p j c t", j=4, c=2), eng='act')
        ckpt('c41')
        ptt = S.sb([128, NSEQ // 2], I32, 'ptt')
        S.load(ptt.v, ptab.rearrange("(g p) o -> p (g o)", p=128), allow_slow_non_contiguous=True)
        ptf = S.sb([128, NSEQ // 2], F32, 'ptf')
        S.copy(ptf.v, ptt.v)
        io16 = S.sb([128, 16], F32, 'io16')
        S.op('pool', lambda e: e.iota(io16.v.ap, pattern=[[1, 16]], base=0, channel_multiplier=0,
                                      allow_small_or_imprecise_dtypes=True), [], [io16])
        idxf = S.sb([128, NSEQ // 2, 16], F32, 'idxf')
        S.ts(idxf.v, ptf.v.un(2).bc([128, NSEQ // 2, 16]), 16.0, ALU.mult)
        idxc = S.sb([128, NSEQ // 2, 16], I32, 'idxc')
        S.tt(idxc.v, idxf.v, io16.v.un(1).bc([128, NSEQ // 2, 16]), ALU.add)
        ckpt('c42')
        cckv16 = cckv.rearrange("n (j x) -> (n j) x", j=16)
        ckpe2 = ckpe.rearrange("n (j x) -> (n j) x", j=2)
        idx2 = S.sb([128, NSEQ // 2, 2], I32, 'idx2')
        S.ts(idxf[:, :, 0:2], ptf.v.un(2).bc([128, NSEQ // 2, 2]), 2.0, ALU.mult)
        S.tt(idx2.v, idxf[:, :, 0:2], io16[:, 0:2].un(1).bc([128, NSEQ // 2, 2]), ALU.add)
        cosk = S.sb([128, 128, 16], F32, 'cosk')
        sink = S.sb([128, 128, 16], F32, 'sink')
        S.load(cosk.v, c_cosk.rearrange("p (r f) -> p r f", f=16))
        S.load(sink.v, c_sink.rearrange("p (r f) -> p r f", f=16))
        gpe_bc = g_qk_bc[:, 64:96]
        b_s_T = S.sb([128, 4, TS], F32, 'b_s_T')
        RCH = 8
        ckpt('c4a')
        STB = [PS[4], PS[5]]
        CT3 = [PS[2], PS[3]]
        for pr in range(NSEQ // 2):
            idx = ptt[:, pr:pr + 1]
            sspe = S.rot('sspe', [128, 128], F32, 1)
            kr = S.rot('kr', [128, 128, 32], BF16, 1)
            for hf in range(2):
                rs = slice(hf * 64, (hf + 1) * 64)
                KP = S.rot('KP', [128, 64, 32], F32, 1)
                ix2 = idx2[:, pr, hf:hf + 1]
                S.dma('pool', (lambda KP, ix2: lambda e: e.indirect_dma_start(
                    out=KP.v.ap.rearrange("p r f -> p (r f)"), out_offset=None, in_=ckpe2,
                    in_offset=bass.IndirectOffsetOnAxis(ap=ix2.ap, axis=0)))(KP, ix2), [idx2], [KP])
                ksq = S.rot('ksq', [128, 64, 32], F32, 1)
                S.tt(ksq.v, KP.v, KP.v, ALU.mult, eng='pool')
                S.red(sspe[:, rs], ksq.v)
                S.tt(ksq.v, KP.v, gpe_bc.un(1).bc([128, 64, 32]), ALU.mult, eng='pool')
                t1 = S.rot('kt1', [128, 64, 16], F32, 1)
                t2 = S.rot('kt2', [128, 64, 16], F32, 1)
                S.tt(t1.v, ksq[:, :, 0:16], cosk[:, rs, :], ALU.mult)
                S.tt(t2.v, ksq[:, :, 16:32], sink[:, rs, :], ALU.mult, eng='pool')
                S.tt(kr[:, rs, 0:16], t1.v, t2.v, ALU.subtract)
                S.tt(t1.v, ksq[:, :, 16:32], cosk[:, rs, :], ALU.mult)
                S.tt(t2.v, ksq[:, :, 0:16], sink[:, rs, :], ALU.mult, eng='pool')
                S.tt(kr[:, rs, 16:32], t1.v, t2.v, ALU.add)
            oacc = [PS[6], PS[7]]
            ckpt('c4b')
            NCH = 128 // RCH
            Cbs, cTs, ssns, ptbs = {}, {}, {}, {}
            state = {'first': True}

            def load_chunk(ch):
                Cb = S.rot('Cb', [128, RCH, 256], BF16, 3)
                ixc = idxc[:, pr, ch:ch + 1]
                S.dma('pool', (lambda Cb, ixc: lambda e: e.indirect_dma_start(
                    out=Cb.v.ap.rearrange("p r c -> p (r c)"), out_offset=None, in_=cckv16,
                    in_offset=bass.IndirectOffsetOnAxis(ap=ixc.ap, axis=0)))(Cb, ixc), [idxc], [Cb])
                Cbs[ch] = Cb

            def st_T(r):
                ch, rl = divmod(r, RCH)
                Cb = Cbs[ch]
                ctp = CT3[r % 2]
                for cc in range(2):
                    S.mm(ctp[:, cc * 128:(cc + 1) * 128], Cb[:, rl, cc * 128:(cc + 1) * 128], identb.v)
                S.mm(ctp[0:32, 256:384], kr[:, r, :], identb.v)
                cT = S.rot('cT', [128, 384], BF16, 4)
                S.copy(cT[:, 0:256], ctp[:, 0:256], eng='act')
                S.copy(cT[0:32, 256:384], ctp[0:32, 256:384], eng='dve')
                cTs[r] = cT

            def st_K(r):
                cT = cTs.pop(r)
                grp, g = divmod(r, 4)
                if g == 0:
                    ssns[grp] = S.rot('ssn', [128, 4, 8], F32, 3)
                knp = PS[r % 2]
                for cc in range(2):
                    S.mm(knp.v, cT[:, cc * 128:(cc + 1) * 128], w_uk[:, cc, :], start=(cc == 0), stop=(cc == 1))
                sqk = S.rot('sqk', [128, 8, 64], BF16, 3)
                S.act(sqk.v.r("p h d -> p (h d)"), knp.v, AF.Square)
                S.red(ssns[grp][:, g, :], sqk.v)
                stb = STB[grp % 2]
                for cc in range(2):
                    S.mm(stb[:, g * 64:(g + 1) * 64], cT[:, cc * 128:(cc + 1) * 128],
                         qlatT[:, cc, :, pr * 8:pr * 8 + 8].r("p h (a q) -> p a h q", a=2),
                         start=(cc == 0), stop=False)
                S.mm(stb[:, g * 64:(g + 1) * 64], cT[0:32, 256:384],
                     qpeT_s[:, :, pr * 8:pr * 8 + 8].r("p h (a q) -> p a h q", a=2), start=False, stop=True)

            def st_G(grp):
                r0 = grp * 4
                stb = STB[grp % 2]
                tot = S.rot('tot', [128, 4, 8], F32, 2)
                S.tt(tot.v, ssns.pop(grp).v, sspe[:, r0:r0 + 4].un(2).bc([128, 4, 8]), ALU.add)
                S.act(tot.v, tot.v, AF.Sqrt, bias=epsT[:, 0:1], scale=1.0 / 96)
                S.recip(tot.v, tot.v)
                snm = S.rot('snm', [128, 4, 8, 4], F32, 2)
                ptb = S.rot('ptb', [128, 4, 64], BF16, 3)
                for a in range(2):
                    pa = slice(a * 64, (a + 1) * 64)
                    S.tt(snm[pa], stb[pa, 0:256].r("p (g a h q) -> p g a h q", g=4, a=2, h=8)[:, :, a, :, :],
                         tot[pa].un(3).bc([64, 4, 8, 4]), ALU.mult)
                    S.act(ptb[pa].r("p g (a h q) -> p g a h q", a=2, h=8)[:, :, a, :, :], snm[pa], AF.Exp,
                          bias=negC[pa, 0:1], scale=SC_MLA)
                ptbs[grp] = ptb

            def st_PV(grp):
                ptb = ptbs.pop(grp)
                for g in range(4):
                    ch, rl = divmod(grp * 4 + g, RCH)
                    Cb = Cbs[ch]
                    for a in range(2):
                        pa = slice(a * 64, (a + 1) * 64)
                        S.mm(oacc[a][0:32, 0:256], ptb[pa, g, a * 32:(a + 1) * 32], Cb[pa, rl, :],
                             start=state['first'], stop=False)
                        S.op('pe', (lambda o_, l_, r_: lambda e: e.matmul(o_.ap, lhsT=l_.ap, rhs=r_.ap, start=False,
                                                                          stop=False, skip_group_check=True))(
                            oacc[a][0:32, 256:257], ptb[pa, g, a * 32:(a + 1) * 32], onesb[pa, 0:1]),
                            [ptb, onesb], [oacc[a]])
                    state['first'] = False

            load_chunk(0)
            load_chunk(1)
            for r in range(128 + 1):
                if r < 128:
                    st_T(r)
                if r >= 1:
                    rr = r - 1
                    st_K(rr)
                    if rr % 4 == 3:
                        grp = rr // 4
                        st_G(grp)
                        if grp >= 1:
                            st_PV(grp - 1)
                            if (grp - 1) % (RCH // 4) == (RCH // 4) - 1:
                                nxtc = (grp - 1) // (RCH // 4) + 3
                                if nxtc < NCH:
                                    load_chunk(nxtc)
                        if grp == 0 and 2 < NCH:
                            load_chunk(2)
            st_PV(31)
            for a in range(2):
                sq_ = pr * 2 + a
                tk = slice(sq_ * 4, sq_ * 4 + 4)
                snew = PS[2]
                for h in range(8):
                    S.mm(snew[0:64, h * 4:(h + 1) * 4], kT_s[:, h, :], qT_s[:, h, tk])
                pn = S.rot('pn', [64, 8, 4], BF16, 2)
                S.act(pn.v.r("p h q -> p (h q)"), snew[0:64, 0:32], AF.Exp, bias=negC[0:64, 0:1], scale=SC_MLA)
                S.tt(pn.v, pn.v, bd[:, tk].un(1).bc([64, 8, 4]), ALU.mult)
                S.mm(oacc[a][0:32, 0:257], pn.v.r("p h q -> p (h q)"), Cb_new[:, 0:257], start=False, stop=True)
                ol = S.rot('ol', [32, 264], F32, 2)
                S.copy(ol[:, 0:257], oacc[a][0:32, 0:257], eng='act')
                rl_ = S.rot('rl_', [32, 1], F32, 2)
                S.recip(rl_.v, ol[:, 256:257])
                S.ts(ol[:, 0:256], ol[:, 0:256], rl_[:, 0:1], ALU.mult)
                olT = S.rot('olT', [128, 2, 32], BF16, 2)
                transposes(lambda i: olT[:, i, :], [ol[:, i * 128:(i + 1) * 128] for i in range(2)], 32, PS[2])
                bps = PS[3]
                for h in range(8):
                    j, par = h // 2, h % 2
                    for cc in range(2):
                        S.mm(bps[par * 64:(par + 1) * 64, j * 4:(j + 1) * 4], w_uv[:, cc, h * 64:(h + 1) * 64],
                             olT[:, cc, h * 4:(h + 1) * 4], start=(cc == 0), stop=(cc == 1))
                S.copy(b_s_T[:, :, tk], bps[:, 0:16].r("p (j q) -> p j q", j=4), eng='act')
            ckpt('c5')
        bsq = S.sb([128, 4, TS], BF16, 'bsq')
        S.tt(bsq.v, b_s_T.v, b_s_T.v, ALU.mult)
        for j in range(4):
            S.mm(PS[0][:, 0:TS], onesb.v, bsq[:, j, :], start=(j == 0), stop=(j == 3))
        rbs = S.sb([128, TS], F32, 'rbs')
        S.act(rbs.v, PS[0][:, 0:TS], AF.Sqrt, bias=epsT[:, 0:1], scale=1.0 / 512)
        S.recip(rbs.v, rbs.v)
        S.tt(b_s_T.v, b_s_T.v, rbs.v.un(1).bc([128, 4, TS]), ALU.mult)
        for j in range(4):
            S.ts(bT[:, j, TP:TP + TS], b_s_T[:, j, :], g_ob_col[:, j:j + 1], ALU.mult)
        if stop == 'c6':
            S.store(y_s.rearrange("t (two d) -> (t two) d", two=2)[:, 0:256], b_s_T.v.r("p j t -> p (j t)"))
        S.pop()
        S.pop()
        ckpt('c6')

        dep_best = {}
        for tl in arena_alias:
            for d in list(tl.rd) + ([tl.lw] if tl.lw is not None else []):
                if dep_best.get(d[0], (0, None))[0] < d[1]:
                    dep_best[d[0]] = (d[1], d[2])
        arena_deps = [(k, v, 'freed') for k, (v, e) in dep_best.items()]
        xres = []
        for t in range(NT + 1):
            tl = Tl(arena.t[:, t * 1024:(t + 1) * 1024], 'xres%d' % t)
            tl.rd = list(arena_deps)
            xres.append(tl)
        S.push()
        w_oa = load_w('w_o', D, D, tname='w_o')
        for t in range(NT + 1):
            is_s = (t == NT)
            P = TS if is_s else 128
            tc0 = t * 128
            x_t = S.rot('x_t3', [128, D], F32, 2)
            S.load(x_t[0:P], xs[:, :] if is_s else xp[tc0:tc0 + P, :])
            for n in range(2):
                ps = PS[(2 * t + n) % 4]
                for k in range(8):
                    src = aT if k < 4 else bT
                    S.mm(ps[0:P, :], src[:, k % 4, tc0:tc0 + P], w_oa[:, k, n * 512:(n + 1) * 512],
                         start=(k == 0), stop=(k == 7))
                S.tt(xres[t][0:P, n * 512:(n + 1) * 512], ps[0:P, :], x_t[0:P, n * 512:(n + 1) * 512], ALU.add)
        S.pop()
        S.pop()
        ckpt('c7')

        S.push()
        g_min_col = col_vec('g_mem_in', D)
        w_mk = load_w('w_mk', D, 512)
        w_mv = load_w('w_mv', D, 512)
        for mt in range(2):
            m_t = S.rot('m_t', [128, D], F32, 2)
            S.load(m_t.v, memp[mt * 128:(mt + 1) * 128, :])
            ri = rinv_of(m_t.v, D, 128)
            S.ts(m_t.v, m_t.v, ri[:, 0:1], ALU.mult, eng='pool')
            mT = S.rot('mT', [128, 8, 128], BF16, 2)
            transposes(lambda i: mT[:, i, :], [m_t[:, i * 128:(i + 1) * 128] for i in range(8)], 128, PS[0],
                       scale=lambda i: g_min_col[:, i:i + 1])
            for k in range(8):
                S.mm(PS[1].v, mT[:, k, :], w_mk[:, k, :], start=(k == 0), stop=(k == 7))
            for k in range(8):
                S.mm(PS[2].v, mT[:, k, :], w_mv[:, k, :], start=(k == 0), stop=(k == 7))
            mk_sb = S.rot('mk_sb', [128, 4, 128], F32, 2)
            S.copy(mk_sb.v.r("p h d -> p (h d)"), PS[1].v, eng='act')
            rg = group_rinv(mk_sb.v, 4, 128, 128)
            S.tt(mk_sb.v, mk_sb.v, rg.v.un(2).bc([128, 4, 128]), ALU.mult)
            S.tt(mk_sb.v, mk_sb.v, g_mk_bc.v.un(1).bc([128, 4, 128]), ALU.mult, eng='pool')
            S.store(o_pmk[mt * 128:(mt + 1) * 128, :], mk_sb.v.r("p h d -> p (h d)"))
            transposes(lambda h: mkT[:, h, mt * 128:(mt + 1) * 128], [mk_sb[:, h, :] for h in range(4)], 128, PS[3])
            mv_sb = S.rot('mv_sb', [128, 512], F32, 2)
            S.copy(mv_sb.v, PS[2].v, eng='act')
            S.store(o_pmv[mt * 128:(mt + 1) * 128, :], mv_sb.v)
            S.copy(mvb[:, mt, :], mv_sb.v, eng='pool')
        S.pop()

        ckpt('c8')
        blocks = [(i * 512, 512, False) for i in range(4)] + [(TP, TS, True)]

        def norm_T(dst, c0, NB, gcol):
            ntile = max(1, NB // 128)
            P = min(NB, 128)
            for t in range(ntile):
                xr = xres[c0 // 128 + t]
                ri = rinv_of(xr[0:P, :], D, P)
                xh = S.rot('xh3', [128, D], F32, 2)
                S.ts(xh[0:P], xr[0:P, :], ri[0:P, 0:1], ALU.mult, eng='pool')
                transposes(lambda i: dst[:, i, t * 128:t * 128 + P], [xh[0:P, i * 128:(i + 1) * 128] for i in range(8)],
                           P, PS[t % 2], scale=lambda i: gcol[:, i:i + 1])

        S.push()
        g_mx_col = col_vec('g_mem_x', D)
        w_mq = load_w('w_mq', D, 512)
        w_mo = load_w('w_mo', 512, D)
        for (c0, NB, is_s) in blocks:
            ntile = max(1, NB // 128)
            P = min(NB, 128)
            hT = S.rot('hT4', [128, 8, 512], BF16, 1)
            norm_T(hT, c0, NB, g_mx_col)
            qmT = S.rot('qmT', [128, 4, 512], BF16, 1)
            for t in range(ntile):
                ps = PS[2 + t % 2]
                for k in range(8):
                    S.mm(ps[0:P, :], hT[:, k, t * 128:t * 128 + P], w_mq[:, k, :], start=(k == 0), stop=(k == 7))
                qm = S.rot('qm', [128, 4, 128], F32, 2)
                S.copy(qm[0:P].r("p h d -> p (h d)"), ps[0:P, :], eng='act')
                rg = group_rinv(qm[0:P], 4, 128, P)
                S.tt(qm[0:P], qm[0:P], rg[0:P].un(2).bc([P, 4, 128]), ALU.mult)
                S.tt(qm[0:P], qm[0:P], g_mq_bc[0:P].un(1).bc([P, 4, 128]), ALU.mult, eng='pool')
                transposes(lambda h: qmT[:, h, t * 128:t * 128 + P], [qm[0:P, h, :] for h in range(4)], P, PS[4 + t % 2])
            omT = S.rot('omT', [128, 4, 512], BF16, 1)
            if not is_s:
                for h in range(4):
                    o_ps, d_ps = PS[4], PS[5]
                    for kb in range(2):
                        st = PS[kb]
                        S.mm(st.v, mkT[:, h, kb * 128:(kb + 1) * 128], qmT[:, h, :])
                        pm = S.rot('pm', [128, 512], BF16, 2)
                        S.act(pm.v, st.v, AF.Exp, bias=negCm[:, 0:1], scale=SC_MEM)
                        S.mm(o_ps.v, mvb[:, kb, h * 128:(h + 1) * 128], pm.v, start=(kb == 0), stop=(kb == 1))
                        S.mm(d_ps.v, onesb.v, pm.v, start=(kb == 0), stop=(kb == 1))
                    rden = S.rot('rden', [128, 512], F32, 1)
                    S.act(rden.v, d_ps.v, AF.Ln)
                    S.act(rden.v, rden.v, AF.Exp, scale=-1.0)
                    S.tt(omT[:, h, :], o_ps.v, rden.v, ALU.mult)
            else:
                for sq_ in range(NSEQ):
                    tk = slice(sq_ * 4, sq_ * 4 + 4)
                    mk_s = S.rot('mk_s', [128, 2, 512], F32, 2)
                    S.load(mk_s.v, cmk[sq_].rearrange("(b p) f -> p b f", p=128))
                    mv_s = S.rot('mv_s', [128, 2, 512], BF16, 2)
                    S.load(mv_s.v, cmv[sq_].rearrange("(b p) f -> p b f", p=128), q='pool')
                    mkT_s = S.rot('mkT_s', [128, 4, 256], BF16, 2)
                    for kb in range(2):
                        for h in range(4):
                            S.tr(PS[kb][:, h * 128:(h + 1) * 128], mk_s[:, kb, h * 128:(h + 1) * 128], ident.v)
                        S.copy(mkT_s[:, :, kb * 128:(kb + 1) * 128], PS[kb].v.r("p (h k) -> p h k", h=4), eng='act')
                    st = PS[2]
                    for kb in range(2):
                        for h in range(4):
                            cl = (kb * 4 + h) * 4
                            S.mm(st[:, cl:cl + 4], mkT_s[:, h, kb * 128:(kb + 1) * 128], qmT[:, h, tk])
                    pm = S.rot('pm_s', [128, 2, 4, 4], BF16, 2)
                    S.act(pm.v.r("p b h q -> p (b h q)"), st[:, 0:32], AF.Exp, bias=negCm[:, 0:1], scale=SC_MEM)
                    o_ps, d_ps = PS[4], PS[5]
                    for h in range(4):
                        for kb in range(2):
                            S.mm(o_ps[:, h * 4:(h + 1) * 4], mv_s[:, kb, h * 128:(h + 1) * 128], pm[:, kb, h, :],
                                 start=(kb == 0), stop=(kb == 1))
                    for kb in range(2):
                        S.mm(d_ps[:, 0:16], onesb.v, pm[:, kb, :, :].r("p h q -> p (h q)"), start=(kb == 0), stop=(kb == 1))
                    rden = S.rot('rden_s', [128, 16], F32, 2)
                    S.recip(rden.v, d_ps[:, 0:16])
                    S.tt(omT[:, :, tk], o_ps[:, 0:16].r("p (h q) -> p h q", h=4), rden.v.r("p (h q) -> p h q", h=4), ALU.mult)
            for t in range(ntile):
                xr = xres[c0 // 128 + t]
                for n in range(2):
                    ps = PS[6 + n]
                    for k in range(4):
                        S.mm(ps[0:P, :], omT[:, k, t * 128:t * 128 + P], w_mo[:, k, n * 512:(n + 1) * 512],
                             start=(k == 0), stop=(k == 3))
                    S.tt(xr[0:P, n * 512:(n + 1) * 512], ps[0:P, :], xr[0:P, n * 512:(n + 1) * 512], ALU.add)
        S.pop()

        ckpt('c9')
        S.push()
        g_ffn_col = col_vec('g_ffn', D)
        wc_col = S.sb([128, 3, NFC], F32, 'wc_col')
        S.load(wc_col.v, W['w_conv'].rearrange("j (c p) -> p j c", p=128), allow_slow_non_contiguous=True)
        bc_col = S.sb([128, NFC], F32, 'bc_col')
        S.load(bc_col.v, W['b_conv'].rearrange("(c p) -> p c", p=128), allow_slow_non_contiguous=True)
        carry = S.sb([128, NFC, 2], F32, 'carry')
        S.memset(carry.v, 0.0)
        w_down = load_w('w_down', DFF, D)
        srcw = W['w_up'].rearrange("(c p) n -> p c n", p=128)
        for (c0, NB, is_s) in blocks:
            ntile = max(1, NB // 128)
            P = min(NB, 128)
            hT = S.rot('hT5', [128, 8, 512], BF16, 1)
            norm_T(hT, c0, NB, g_ffn_col)
            aF = S.rot('aF', [128, NFC, 512], BF16, 1)
            for fc in range(NFC):
                wg = S.rot('wg', [128, 8, 128], BF16, 4)
                wv = S.rot('wv', [128, 8, 128], BF16, 4)
                S.load(wg.v, srcw[:, :, fc * 128:(fc + 1) * 128], q='pool')
                S.load(wv.v, srcw[:, :, DFF + fc * 128:DFF + (fc + 1) * 128], q='pool')
                gps, vps = PS[(fc % 2) * 2], PS[(fc % 2) * 2 + 1]
                for k in range(8):
                    S.mm(gps[:, 0:NB], wg[:, k, :], hT[:, k, 0:NB], start=(k == 0), stop=(k == 7))
                for k in range(8):
                    S.mm(vps[:, 0:NB], wv[:, k, :], hT[:, k, 0:NB], start=(k == 0), stop=(k == 7))
                w0, w1, w2 = (wc_col[:, j, fc:fc + 1] for j in range(3))
                cv = S.rot('cv', [128, 512], F32, 2)
                if not is_s:
                    gb = S.rot('gb', [128, 514], F32, 2)
                    S.copy(gb[:, 0:2], carry[:, fc, :], eng='dve')
                    S.copy(gb[:, 2:514], gps.v, eng='act')
                    S.copy(carry[:, fc, :], gb[:, 512:514], eng='dve')
                    S.ts(cv.v, gb[:, 0:512], w0, ALU.mult, bc_col[:, fc:fc + 1], ALU.add)
                    S.op('dve', (lambda cv, gb, w1: lambda e: e.scalar_tensor_tensor(
                        out=cv.v.ap, in0=gb[:, 1:513].ap, scalar=w1.ap, in1=cv.v.ap, op0=ALU.mult, op1=ALU.add))(cv, gb, w1),
                        [gb, wc_col, cv], [cv])
                    S.op('dve', (lambda cv, gb, w2: lambda e: e.scalar_tensor_tensor(
                        out=cv.v.ap, in0=gb[:, 2:514].ap, scalar=w2.ap, in1=cv.v.ap, op0=ALU.mult, op1=ALU.add))(cv, gb, w2),
                        [gb, wc_col, cv], [cv])
                    if c0 + NB == TP:
                        S.tr(PS[6][0:2, 0:128], gb[:, 512:514], ident.v)
                        pcs = S.rot('pcs', [2, 128], F32, 2)
                        S.copy(pcs.v, PS[6][0:2, 0:128], eng='act')
                        S.store(o_pconv[:, fc * 128:(fc + 1) * 128], pcs.v)
                else:
                    cs_t = S.rot('cs_t', [32, 128], F32, 2)
                    S.load(cs_t.v, cst[:, fc * 128:(fc + 1) * 128])
                    S.tr(PS[6][:, 0:32], cs_t.v, ident[0:32, 0:32])
                    gb = S.rot('gbs', [128, NSEQ, 6], F32, 2)
                    S.copy(gb[:, :, 0:2], PS[6][:, 0:32].r("p (s j) -> p s j", j=2), eng='act')
                    S.copy(gb[:, :, 2:6], gps[:, 0:TS].r("p (s t) -> p s t", t=4), eng='act')
                    cv3 = cv[:, 0:TS].r("p (s t) -> p s t", t=4)
                    S.ts(cv3, gb[:, :, 0:4], w0, ALU.mult, bc_col[:, fc:fc + 1], ALU.add)
                    tmpc = S.rot('tmpc', [128, NSEQ, 4], F32, 2)
                    S.ts(tmpc.v, gb[:, :, 1:5], w1, ALU.mult)
                    S.tt(cv3, cv3, tmpc.v, ALU.add)
                    S.ts(tmpc.v, gb[:, :, 2:6], w2, ALU.mult)
                    S.tt(cv3, cv3, tmpc.v, ALU.add)
                    gl = S.rot('gl', [128, NSEQ, 2], F32, 2)
                    S.copy(gl.v, gb[:, :, 4:6], eng='dve')
                    S.tr(PS[7][0:32, 0:128], gl.v.r("p s j -> p (s j)"), ident.v)
                    scs = S.rot('scs', [32, 128], F32, 2)
                    S.copy(scs.v, PS[7][0:32, 0:128], eng='act')
                    S.store(o_sconv[:, fc * 128:(fc + 1) * 128], scs.v)
                S.act(cv[:, 0:NB], cv[:, 0:NB], AF.Silu)
                S.tt(aF[:, fc, 0:NB], cv[:, 0:NB], vps[:, 0:NB], ALU.mult)
            for t in range(ntile):
                tc0 = c0 + t * 128
                xr = xres[c0 // 128 + t]
                y_t = S.rot('y_t', [128, D], F32, 2)
                for n in range(2):
                    ps = PS[4 + n]
                    for fc in range(NFC):
                        S.mm(ps[0:P, :], aF[:, fc, t * 128:t * 128 + P], w_down[:, fc, n * 512:(n + 1) * 512],
                             start=(fc == 0), stop=(fc == NFC - 1))
                    S.tt(y_t[0:P, n * 512:(n + 1) * 512], ps[0:P, :], xr[0:P, n * 512:(n + 1) * 512], ALU.add)
                if is_s:
                    S.store(y_s[:, :], y_t[0:P])
                else:
                    S.store(y_p[tc0:tc0 + P, :], y_t[0:P])
        S.pop()
        S.finish()
    return nc


def _consts():
    half = 16
    inv_freq = (10000.0 ** (-np.arange(half, dtype=np.float32) / half)).astype(np.float32)

    def cs(pos):
        ang = pos.astype(np.float32)[..., None] * inv_freq
        return np.cos(ang).astype(np.float32), np.sin(ang).astype(np.float32)
    p = np.arange(128)
    cp, sp = cs(np.arange(NT)[None, :] * 128 + p[:, None])
    cS, sS = cs(PAST + (np.arange(TS) % 4))
    ck, sk = cs((p[:, None] % NPG) * 128 + np.arange(128)[None, :])
    tri = (p[:, None] <= p[None, :]).astype(np.float32)
    t64 = np.arange(64)
    bd = ((t64[:, None] // 4 == t64[None, :] // 4) & (t64[:, None] % 4 <= t64[None, :] % 4)).astype(np.float32)
    return {
        'c_ident': np.eye(128, dtype=np.float32), 'c_tri': tri, 'c_bd': bd,
        'c_cosp': cp.reshape(128, -1), 'c_sinp': sp.reshape(128, -1),
        'c_coss': cS, 'c_sins': sS,
        'c_cosk': ck.reshape(128, -1), 'c_sink': sk.reshape(128, -1),
    }


_CACHE = {}


def kernel(**inp):
    f = lambda a: np.ascontiguousarray(np.asarray(a))
    n_phys = inp['cache_ckv'].shape[1]
    if n_phys not in _CACHE:
        import os as _os
        _CACHE[n_phys] = build(n_phys, _os.environ.get('MK_STOP'))
    nc = _CACHE[n_phys]
    consts = _consts()
    cckv = f(inp['cache_ckv']).reshape(n_phys, 128 * 256)
    ckpe = f(inp['cache_kpe']).reshape(n_phys, 128 * 32)
    wnames = ['g_mix', 'w_in', 'ln_v_g', 'ln_v_b', 'w_s', 'b_s', 'g_q_a', 'w_uq', 'g_kv_a', 'w_uk', 'w_uv', 'g_qk_q',
              'g_qk_k', 'g_out_a', 'g_out_b', 'w_o', 'g_mem_x', 'g_mem_in', 'w_mq', 'w_mk', 'w_mv', 'g_mq', 'g_mk',
              'w_mo', 'g_ffn', 'w_up', 'w_conv', 'b_conv', 'w_down']
    shared = {nm: f(inp[nm])[0].reshape(-1) if inp[nm].ndim == 2 else f(inp[nm])[0] for nm in wnames}
    shared['ln_v_g'] = shared['ln_v_g'].reshape(-1)
    shared['ln_v_b'] = shared['ln_v_b'].reshape(-1)
    shared.update(consts)
    shared['cckv'] = cckv
    shared['ckpe'] = ckpe
    in_maps = []
    for c in range(NCORES):
        sl = slice(c * NSEQ, (c + 1) * NSEQ)
        m = dict(shared)
        m['xp'] = f(inp['x_prompt'][c])
        m['xs'] = f(inp['x_sample'][sl]).reshape(TS, D)
        m['cmk'] = f(inp['cache_mem_k'][0, sl]).reshape(NSEQ, 256, 512)
        m['cmv'] = f(inp['cache_mem_v'][0, sl]).reshape(NSEQ, 256, 512)
        m['cst'] = f(inp['state_ffn_conv'][0, sl]).reshape(NSEQ * 2, DFF)
        m['ptab'] = f(inp['page_table'][sl]).reshape(NSEQ * NPG, 1).astype(np.int32)
        m['memp'] = f(inp['mem_prompt'][c])
        in_maps.append(m)
    res = run_bass_kernel_spmd(nc, in_maps, core_ids=list(range(NCORES))).results
    cat = lambda k: np.concatenate([r[k] for r in res], axis=0)
    y_p = cat('y_p').reshape(8, 2048, D)
    y_s = cat('y_s').reshape(128, 4, D)
    return (y_p, y_s,
            cat('o_pckv').reshape(1, 8, 2048, 256), cat('o_pkpe').reshape(1, 8, 2048, 32),
            cat('o_pmk').reshape(1, 8, 256, 4, 128), cat('o_pmv').reshape(1, 8, 256, 4, 128),
            cat('o_pconv').reshape(1, 8, 2, DFF),
            cat('o_sckv').reshape(1, 128, 4, 256), cat('o_skpe').reshape(1, 128, 4, 32),
            cat('o_scv').reshape(1, 128, 4, 8, 64), cat('o_sconv').reshape(1, 128, 2, DFF))
```

```python
import numpy as np
from contextlib import ExitStack
import concourse.bass as bass
import concourse.mybir as mybir
from concourse.bass_utils import run_bass_kernel_spmd

F32 = mybir.dt.float32
BF16 = mybir.dt.bfloat16
I32 = mybir.dt.int32
AF = mybir.ActivationFunctionType
ALU = mybir.AluOpType
AX = mybir.AxisListType

NCORES = 8
D = 1024
TP = 2048
NT = 16
NSEQ = 16
TS = 64
NPG = 64
NIN = 1568
DFF = 2816
NFC = 22
EPS = 1e-6
PAST = 8192

COMPUTE = ('pe', 'dve', 'act', 'pool')
NPOOL = 24


class V:
    __slots__ = ('tl', 'ap')

    def __init__(self, tl, ap):
        self.tl = tl
        self.ap = ap

    def __getitem__(self, k):
        return V(self.tl, self.ap[k])

    def r(self, s, **kw):
        return V(self.tl, self.ap.rearrange(s, **kw))

    def un(self, ax):
        return V(self.tl, self.ap.unsqueeze(ax))

    def bc(self, shape):
        return V(self.tl, self.ap.to_broadcast(list(shape)))


class Tl:
    __slots__ = ('t', 'name', 'lw', 'rd', 'psum')

    def __init__(self, t, name, psum=False, init_rd=()):
        self.t = t
        self.name = name
        self.lw = None
        self.rd = list(init_rd)
        self.psum = psum

    def __getitem__(self, k):
        return V(self, self.t[k])

    @property
    def v(self):
        return V(self, self.t[:])


class Sched:
    def __init__(self, nc, es):
        self.nc = nc
        self.es = es
        self.scopes = [(es, [])]
        self.prog = {k: [] for k in ('pe', 'dve', 'act', 'pool', 'sp')}
        self.sems = {}
        self.cnt = {}
        self.seen = {k: {} for k in self.prog}
        for k in COMPUTE:
            self.sems[k] = es.enter_context(nc.semaphore('s_' + k))
            self.cnt[k] = 0
        self.dpool = {}
        for q in ('sp', 'pool', 'act'):
            lst = []
            for i in range(NPOOL):
                key = 'd_%s_%d' % (q, i)
                self.sems[key] = es.enter_context(nc.semaphore(key))
                self.cnt[key] = 0
                lst.append(key)
            self.dpool[q] = [lst, 0]
        self.out_waits = []
        self.ntile = 0
        self.pending = []
        self.rots = {}

    def push(self):
        es = ExitStack()
        self.scopes.append((es, []))
        self.scope_ctr = getattr(self, 'scope_ctr', 0) + 1
        self.scope_ids = getattr(self, 'scope_ids', [0]) + [self.scope_ctr]

    def pop(self):
        es, tiles = self.scopes.pop()
        best = {}
        for k, v, e in self.pending:
            if best.get(k, (0, None))[0] < v:
                best[k] = (v, e)
        for t in tiles:
            deps = list(t.rd)
            if t.lw is not None:
                deps.append(t.lw)
            for k, v, e in deps:
                if best.get(k, (0, None))[0] < v:
                    best[k] = (v, e)
        self.pending = [(k, v, 'freed') for k, (v, e) in best.items()]
        self.scope_ids = self.scope_ids[:-1]
        es.close()

    def sb(self, shape, dt, name=None):
        self.ntile += 1
        name = (name or 't') + '_%d' % self.ntile
        es, tiles = self.scopes[-1]
        t = es.enter_context(self.nc.sbuf_tensor(name, list(shape), dt))
        tl = Tl(t, name, init_rd=self.pending)
        tiles.append(tl)
        return tl

    def ps(self, shape, dt=F32, name=None):
        self.ntile += 1
        name = (name or 'p') + '_%d' % self.ntile
        es, tiles = self.scopes[-1]
        t = es.enter_context(self.nc.psum_tensor(name, list(shape), dt))
        tl = Tl(t, name, psum=True, init_rd=self.pending)
        tiles.append(tl)
        return tl

    def rot(self, key, shape, dt, n=2):
        k = (key, getattr(self, 'scope_ids', [0])[-1])
        if k not in self.rots:
            self.rots[k] = [[self.sb(shape, dt, key) for _ in range(n)], 0]
        ent = self.rots[k]
        t = ent[0][ent[1] % n]
        ent[1] += 1
        return t

    def _collect(self, eng, reads, writes):
        need = {}

        def add(dep, same_ok=False):
            key, val, deng = dep
            if deng == eng and same_ok:
                return
            if need.get(key, 0) < val:
                need[key] = val

        for r in reads:
            if r.lw is not None:
                add(r.lw, same_ok=(eng == 'pe' and r.psum))
        for w in writes:
            if w.lw is not None:
                add(w.lw, same_ok=True)
            for d in w.rd:
                add(d, same_ok=True)
        out = []
        seen = self.seen[eng]
        for key, val in need.items():
            if seen.get(key, 0) >= val:
                continue
            seen[key] = val
            out.append((key, val))
        return out

    def _mark(self, reads, writes, dep):
        for w in writes:
            w.lw = dep
            w.rd = []
        for r in reads:
            if r in writes:
                continue
            r.rd.append(dep)
            if len(r.rd) > 48:
                best = {}
                for k, v, e in r.rd:
                    if best.get(k, (0, None))[0] < v:
                        best[k] = (v, e)
                r.rd = [(k, v, e) for k, (v, e) in best.items()]

    def op(self, eng, fn, reads=(), writes=()):
        reads = list({id(t): t for t in reads}.values())
        writes = list({id(t): t for t in writes}.values())
        waits = self._collect(eng, reads, writes)
        self.cnt[eng] += 1
        val = self.cnt[eng]
        self.prog[eng].append((waits, fn, (eng, 1)))
        self._mark(reads, writes, (eng, val, eng))

    def dma(self, q, fn, reads=(), writes=(), is_output=False):
        reads = list({id(t): t for t in reads}.values())
        writes = list({id(t): t for t in writes}.values())
        waits = self._collect(q, reads, writes)
        lst, idx = self.dpool[q]
        key = lst[idx % NPOOL]
        self.dpool[q][1] = idx + 1
        prev = self.cnt[key]
        if prev > 0 and self.seen[q].get(key, 0) < prev:
            self.seen[q][key] = prev
            waits.append((key, prev))
        self.cnt[key] += 16
        val = self.cnt[key]
        self.prog[q].append((waits, fn, (key, 16)))
        self._mark(reads, writes, (key, val, 'dma_' + q))
        if is_output:
            self.out_waits.append((key, val))

    def finish(self):
        need = {}
        for key, val in self.out_waits:
            if need.get(key, 0) < val:
                need[key] = val
        self.prog['sp'].append((list(need.items()), None, None))
        nc, sems, prog = self.nc, self.sems, self.prog

        def run(e, lst):
            for waits, fn, inc in lst:
                for key, val in waits:
                    e.wait_ge(sems[key], val)
                if fn is not None:
                    fn(e).then_inc(sems[inc[0]], inc[1])

        with nc.Block() as block:
            @block.tensor
            def _(e):
                run(e, prog['pe'])

            @block.vector
            def _(e):
                run(e, prog['dve'])

            @block.scalar
            def _(e):
                run(e, prog['act'])

            @block.gpsimd
            def _(e):
                run(e, prog['pool'])

            @block.sync
            def _(e):
                run(e, prog['sp'])

    def act(self, out, in_, func, bias=None, scale=None, accum=None, eng='act'):
        kw = {}
        rd = [in_.tl]
        wr = [out.tl]
        if bias is not None:
            kw['bias'] = bias.ap
            rd.append(bias.tl)
        if scale is not None:
            if isinstance(scale, V):
                kw['scale'] = scale.ap
                rd.append(scale.tl)
            else:
                kw['scale'] = float(scale)
        if accum is not None:
            kw['accum_out'] = accum.ap
            wr.append(accum.tl)
        self.op(eng, lambda e: e.activation(out=out.ap, in_=in_.ap, func=func, **kw), rd, wr)

    def tt(self, out, a, b, op, eng='dve'):
        self.op(eng, lambda e: e.tensor_tensor(out=out.ap, in0=a.ap, in1=b.ap, op=op), [a.tl, b.tl], [out.tl])

    def ts(self, out, a, s1, op0, s2=None, op1=None, eng='dve', accum=None):
        rd = [a.tl]
        wr = [out.tl]
        x1 = s1
        if isinstance(s1, V):
            rd.append(s1.tl)
            x1 = s1.ap
        x2 = s2
        if isinstance(s2, V):
            rd.append(s2.tl)
            x2 = s2.ap
        kw = {}
        if op1 is not None:
            kw['op1'] = op1
        if accum is not None:
            kw['accum_out'] = accum.ap
            wr.append(accum.tl)
        self.op(eng, lambda e: e.tensor_scalar(out=out.ap, in0=a.ap, scalar1=x1, scalar2=x2, op0=op0, **kw), rd, wr)

    def red(self, out, in_, op=ALU.add, eng='dve'):
        self.op(eng, lambda e: e.tensor_reduce(out=out.ap, in_=in_.ap, axis=AX.X, op=op), [in_.tl], [out.tl])

    def copy(self, out, in_, eng='dve'):
        if eng == 'act':
            self.act(out, in_, AF.Copy)
        else:
            self.op(eng, lambda e: e.tensor_copy(out=out.ap, in_=in_.ap), [in_.tl], [out.tl])

    def recip(self, out, in_, eng='dve'):
        self.op(eng, lambda e: e.reciprocal(out=out.ap, in_=in_.ap), [in_.tl], [out.tl])

    def memset(self, out, val, eng='pool'):
        self.op(eng, lambda e: e.memset(out.ap, val), [], [out.tl])

    def mm(self, out, lhsT, rhs, start=True, stop=True):
        self.op('pe', lambda e: e.matmul(out.ap, lhsT=lhsT.ap, rhs=rhs.ap, start=start, stop=stop),
                [lhsT.tl, rhs.tl], [out.tl])

    def tr(self, out, in_, ident):
        self.op('pe', lambda e: e.transpose(out=out.ap, in_=in_.ap, identity=ident.ap),
                [in_.tl, ident.tl], [out.tl])

    def load(self, out, src, q='sp', **kw):
        self.dma(q, lambda e: e.dma_start(out=out.ap, in_=src, **kw), [], [out.tl])

    def store(self, dst, in_, q='sp', **kw):
        self.dma(q, lambda e: e.dma_start(out=dst, in_=in_.ap, **kw), [in_.tl], [], is_output=True)


class _Stop(Exception):
    pass


def build(n_phys, stop=None):
    nc = bass.Bass("TRN2", target_bir_lowering=False)
    try:
        _build(nc, n_phys, stop)
    except _Stop:
        pass
    return nc


def _build(nc, n_phys, stop):

    def din(name, shape, dt=F32):
        return nc.dram_tensor(name, list(shape), dt, kind="ExternalInput").ap()

    def dout(name, shape):
        return nc.dram_tensor(name, list(shape), F32, kind="ExternalOutput").ap()

    xp = din('xp', [TP, D])
    xs = din('xs', [TS, D])
    cckv = din('cckv', [n_phys, 128 * 256])
    ckpe = din('ckpe', [n_phys, 128 * 32])
    cmk = din('cmk', [NSEQ, 256, 512])
    cmv = din('cmv', [NSEQ, 256, 512])
    cst = din('cst', [NSEQ * 2, DFF])
    ptab = din('ptab', [NSEQ * NPG, 1], I32)
    memp = din('memp', [256, D])
    W = {}
    for nm, shp in [('g_mix', [D]), ('w_in', [D, NIN]), ('ln_v_g', [512]), ('ln_v_b', [512]), ('w_s', [8, 128, 128]),
                    ('b_s', [8, 128]), ('g_q_a', [256]), ('w_uq', [256, 768]), ('g_kv_a', [256]), ('w_uk', [256, 512]),
                    ('w_uv', [256, 512]), ('g_qk_q', [96]), ('g_qk_k', [96]), ('g_out_a', [512]), ('g_out_b', [512]),
                    ('w_o', [D, D]), ('g_mem_x', [D]), ('g_mem_in', [D]), ('w_mq', [D, 512]), ('w_mk', [D, 512]),
                    ('w_mv', [D, 512]), ('g_mq', [128]), ('g_mk', [128]), ('w_mo', [512, D]), ('g_ffn', [D]),
                    ('w_up', [D, 2 * DFF]), ('w_conv', [3, DFF]), ('b_conv', [DFF]), ('w_down', [DFF, D])]:
        W[nm] = din(nm, shp)
    c_ident = din('c_ident', [128, 128])
    c_tri = din('c_tri', [128, 128])
    c_bd = din('c_bd', [64, 64])
    c_cosp = din('c_cosp', [128, NT * 16])
    c_sinp = din('c_sinp', [128, NT * 16])
    c_coss = din('c_coss', [64, 16])
    c_sins = din('c_sins', [64, 16])
    c_cosk = din('c_cosk', [128, 128 * 16])
    c_sink = din('c_sink', [128, 128 * 16])

    y_p = dout('y_p', [TP, D])
    y_s = dout('y_s', [TS, D])
    o_pckv = dout('o_pckv', [TP, 256])
    o_pkpe = dout('o_pkpe', [TP, 32])
    o_pmk = dout('o_pmk', [256, 512])
    o_pmv = dout('o_pmv', [256, 512])
    o_pconv = dout('o_pconv', [2, DFF])
    o_sckv = dout('o_sckv', [TS, 256])
    o_skpe = dout('o_skpe', [TS, 32])
    o_scv = dout('o_scv', [TS, 512])
    o_sconv = dout('o_sconv', [NSEQ * 2, DFF])

    SC_MLA = 96.0 ** -0.5
    SC_MEM = 128.0 ** -0.5

    with ExitStack() as es:
        S = Sched(nc, es)

        def ckpt(name):
            if stop == name:
                while len(S.scopes) > 1:
                    S.pop()
                S.finish()
                raise _Stop()
        PS = [S.ps([128, 512], F32, 'bank%d' % i) for i in range(8)]

        ident = S.sb([128, 128], F32, 'ident')
        S.load(ident.v, c_ident)
        identb = S.sb([128, 128], BF16, 'identb')
        S.copy(identb.v, ident.v, eng='pool')
        tri = S.sb([128, 128], F32, 'tri')
        S.load(tri.v, c_tri)
        trib = S.sb([128, 128], BF16, 'trib')
        S.copy(trib.v, tri.v, eng='pool')
        bd = S.sb([64, 64], F32, 'bd')
        S.load(bd.v, c_bd)
        onesb = S.sb([128, 128], BF16, 'onesb')
        S.memset(onesb.v, 1.0)
        epsT = S.sb([128, 1], F32, 'eps')
        S.memset(epsT.v, EPS)
        cosp = S.sb([128, NT, 16], F32, 'cosp')
        sinp = S.sb([128, NT, 16], F32, 'sinp')
        S.load(cosp.v, c_cosp.rearrange("p (t f) -> p t f", f=16))
        S.load(sinp.v, c_sinp.rearrange("p (t f) -> p t f", f=16))
        coss = S.sb([64, 16], F32, 'coss')
        sins = S.sb([64, 16], F32, 'sins')
        S.load(coss.v, c_coss)
        S.load(sins.v, c_sins)

        def bcast_vec(name, n):
            t = S.sb([128, n], F32, 'bc_' + name)
            S.load(t.v, W[name].partition_broadcast(128))
            return t

        def col_vec(name, n):
            t = S.sb([128, n // 128], F32, 'col_' + name)
            S.load(t.v, W[name].rearrange("(c p) -> p c", p=128), allow_slow_non_contiguous=True)
            return t

        g_kv_bc = bcast_vec('g_kv_a', 256)
        g_qq_bc = bcast_vec('g_qk_q', 96)
        g_qk_bc = bcast_vec('g_qk_k', 96)
        lng_bc = bcast_vec('ln_v_g', 512)
        lnb_bc = bcast_vec('ln_v_b', 512)
        g_mq_bc = bcast_vec('g_mq', 128)
        g_mk_bc = bcast_vec('g_mk', 128)

        def bound(ga, gb, n, sc, name):
            ma = S.sb([128, 1], F32, name + 'a')
            mb = S.sb([128, 1], F32, name + 'b')
            S.op('dve', lambda e: e.tensor_reduce(out=ma.v.ap, in_=ga.v.ap, axis=AX.X, op=ALU.max,
                                                  apply_absolute_value=True), [ga], [ma])
            S.op('dve', lambda e: e.tensor_reduce(out=mb.v.ap, in_=gb.v.ap, axis=AX.X, op=ALU.max,
                                                  apply_absolute_value=True), [gb], [mb])
            c = S.sb([128, 1], F32, name)
            S.tt(c.v, ma.v, mb.v, ALU.mult)
            S.ts(c.v, c.v, -float(n) * sc, ALU.mult)
            return c
        negC = bound(g_qq_bc, g_qk_bc, 96, SC_MLA, 'negC')
        negCm = bound(g_mq_bc, g_mk_bc, 128, SC_MEM, 'negCm')

        def load_w(name, K, N, n0=0, tname=None):
            kc = K // 128
            t = S.sb([128, kc, N], BF16, tname or ('w_' + name))
            src = W[name].rearrange("(c p) n -> p c n", p=128)
            for c in range(kc):
                for a in range(0, N, 1024):
                    b = min(N, a + 1024)
                    S.load(t[:, c, a:b], src[:, c, n0 + a:n0 + b], q='pool')
            return t

        def junk_tile(P, n):
            j = S.rot('junk', [128, 1024], BF16, 1)
            return j[0:P, 0:n]

        def rinv_of(src, n, P):
            ss = S.rot('ss', [128, 1], F32, 2)
            S.act(junk_tile(P, n), src, AF.Square, accum=ss[0:P, :])
            rt = S.rot('rt', [128, 1], F32, 2)
            S.act(rt[0:P, :], ss[0:P, :], AF.Ln, bias=epsT[0:P, :], scale=1.0 / n)
            ri = S.rot('ri', [128, 1], F32, 2)
            S.act(ri[0:P, :], rt[0:P, :], AF.Exp, scale=-0.5)
            return ri

        def group_rinv(src3, G, Dg, P):
            sq = S.rot('gsq%d' % (G * Dg), [128, G, Dg], F32, 1)
            S.tt(sq[0:P], src3, src3, ALU.mult, eng='pool')
            ss = S.rot('gss%d' % G, [128, G], F32, 2)
            S.red(ss[0:P], sq[0:P])
            rt = S.rot('grt%d' % G, [128, G], F32, 2)
            S.act(rt[0:P], ss[0:P], AF.Ln, bias=epsT[0:P, :], scale=1.0 / Dg)
            ri = S.rot('gri%d' % G, [128, G], F32, 2)
            S.act(ri[0:P], rt[0:P], AF.Exp, scale=-0.5)
            return ri

        def transposes(dst, srcs, P, bank, scale=None):
            for i0 in range(0, len(srcs), 4):
                grp = srcs[i0:i0 + 4]
                for j, s in enumerate(grp):
                    w = s.ap.shape[-1]
                    S.tr(bank[0:w, j * 128:j * 128 + P], s, ident[0:P, 0:P])
                for j, s in enumerate(grp):
                    w = s.ap.shape[-1]
                    if scale is None:
                        S.copy(dst(i0 + j), bank[0:w, j * 128:j * 128 + P], eng='act')
                    else:
                        S.act(dst(i0 + j), bank[0:w, j * 128:j * 128 + P], AF.Copy, scale=scale(i0 + j))

        def rope(dst1, dst2, x1, x2, cs, sn, P):
            H = x1.ap.shape[1]
            cb = cs.un(1).bc([P, H, 16])
            sb_ = sn.un(1).bc([P, H, 16])
            t1 = S.rot('rp1', [128, H, 16], F32, 1)
            t2 = S.rot('rp2', [128, H, 16], F32, 1)
            S.tt(t1[0:P], x1, cb, ALU.mult)
            S.tt(t2[0:P], x2, sb_, ALU.mult, eng='pool')
            S.tt(dst1, t1[0:P], t2[0:P], ALU.subtract)
            t3 = S.rot('rp3', [128, H, 16], F32, 1)
            t4 = S.rot('rp4', [128, H, 16], F32, 1)
            S.tt(t3[0:P], x2, cb, ALU.mult)
            S.tt(t4[0:P], x1, sb_, ALU.mult, eng='pool')
            S.tt(dst2, t3[0:P], t4[0:P], ALU.add)

        arena = S.sb([128, 17 * 1024], F32, 'arena')

        def alias(lo, hi, parts, dt, pattern=None, **kw):
            ap = arena.t[0:parts, lo:hi]
            if dt == BF16:
                ap = ap.bitcast(BF16)
            if pattern:
                ap = ap.rearrange(pattern, **kw)
            return Tl(ap, 'alias_%d' % lo)
        kT = alias(0, 8192, 96, BF16, "p (h t) -> p h t", h=8)
        Vaug = alias(8192, 12416, 128, BF16, "p (t h c) -> p t h c", t=NT, h=8)
        qT = alias(12416, 14464, 96, BF16, "p (h t) -> p h t", h=8)
        b_out = alias(14464, 16512, 128, F32, "p (t c) -> p t c", t=4)
        arena_alias = [kT, Vaug, qT, b_out]

        mkT = S.sb([128, 4, 256], BF16, 'mkT')
        mvb = S.sb([128, 2, 512], BF16, 'mvb')

        g_oa_col = col_vec('g_out_a', 512)
        g_ob_col = col_vec('g_out_b', 512)

        S.push()
        aT = S.sb([128, 4, TP + TS], BF16, 'aT')
        bT = S.sb([128, 4, TP + TS], BF16, 'bT')

        S.push()
        w_uk = load_w('w_uk', 256, 512)
        w_uv = load_w('w_uv', 256, 512)
        qnT_s = S.sb([128, 4, TS], BF16, 'qnT_s')
        qpeT_s = S.sb([32, 8, TS], BF16, 'qpeT_s')
        kT_s = S.sb([96, 8, TS], BF16, 'kT_s')
        qT_s = S.sb([96, 8, TS], BF16, 'qT_s')
        Cb_new = S.sb([64, 264], BF16, 'Cb_new')
        S.memset(Cb_new.v, 1.0)

        S.push()
        g_mix_col = col_vec('g_mix', D)
        g_qa_col = col_vec('g_q_a', 256)
        w_in = load_w('w_in', D, NIN)
        w_uq = load_w('w_uq', 256, 768)
        WsT = S.sb([128, 8, 128], BF16, 'WsT')
        WsT_s = S.sb([64, 8, 64], BF16, 'WsT_s')
        bsT = S.sb([128, 8], F32, 'bsT')
        S.load(bsT.v, W['b_s'].rearrange("g t -> t g"), allow_slow_non_contiguous=True)
        bsT_s = S.sb([64, 8], F32, 'bsT_s')
        for sq in range(NSEQ):
            S.dma('act', (lambda sq: lambda e: e.dma_start(out=bsT_s.t[sq * 4:(sq + 1) * 4, :],
                                                          in_=W['b_s'][:, 0:4].rearrange("g t -> t g"),
                                                          allow_slow_non_contiguous=True))(sq), [], [bsT_s])
        S.push()
        wsf = S.sb([128, 8, 128], F32, 'wsf')
        S.load(wsf.v, W['w_s'].rearrange("g t s -> t g s"))
        for g0 in range(0, 8, 4):
            for j in range(4):
                S.tr(PS[0][:, j * 128:(j + 1) * 128], wsf[:, g0 + j, :], ident.v)
            S.tt(WsT[:, g0:g0 + 4, :], PS[0].v.r("p (j t) -> p j t", j=4), tri.v.un(1).bc([128, 4, 128]), ALU.mult)
        wsf_s = S.sb([64, 8, 64], F32, 'wsf_s')
        S.memset(wsf_s.v.r("p g s -> p (g s)"), 0.0)
        for sq in range(NSEQ):
            S.dma('act', (lambda sq: lambda e: e.dma_start(
                out=wsf_s.t[sq * 4:(sq + 1) * 4, :, sq * 4:(sq + 1) * 4],
                in_=W['w_s'][:, 0:4, 0:4].rearrange("g t s -> t g s")))(sq), [], [wsf_s])
        for g0 in range(0, 8, 4):
            for j in range(4):
                S.tr(PS[1][0:64, j * 128:j * 128 + 64], wsf_s[:, g0 + j, :], ident[0:64, 0:64])
            S.tt(WsT_s[:, g0:g0 + 4, :], PS[1][0:64, :].r("p (j t) -> p j t", j=4)[:, :, 0:64],
                 bd.v.un(1).bc([64, 4, 64]), ALU.mult)
        S.pop()
        S.memset(Vaug.v.r("p t h c -> p (t h c)"), 1.0)

        ckpt('c1')

        def phase1(ti, P, xsrc, is_sample, qcol):
            tok0 = ti * 128
            x_t = S.rot('x_t', [128, D], F32, 2)
            S.load(x_t[0:P, :], xsrc)
            ri = rinv_of(x_t[0:P, :], D, P)
            S.ts(x_t[0:P, :], x_t[0:P, :], ri[0:P, 0:1], ALU.mult, eng='pool')
            xT = S.rot('xT', [128, 8, 128], BF16, 1)
            transposes(lambda i: xT[:, i, 0:P], [x_t[0:P, i * 128:(i + 1) * 128] for i in range(8)], P, PS[0],
                       scale=lambda i: g_mix_col[:, i:i + 1])
            zb = [PS[1], PS[2], PS[3], PS[4]]
            for n in range(4):
                n0 = n * 512
                n1 = min(NIN, n0 + 512)
                for k in range(8):
                    S.mm(zb[n][0:P, 0:n1 - n0], xT[:, k, 0:P], w_in[:, k, n0:n1], start=(k == 0), stop=(k == 7))
            gu = S.rot('gu', [128, 512], F32, 1)
            S.act(gu[0:P], zb[0][0:P, :], AF.Gelu_apprx_tanh)
            gv = S.rot('gv', [128, 8, 64], F32, 1)
            S.act(gv[0:P].r("p g d -> p (g d)"), zb[1][0:P, :], AF.Gelu_apprx_tanh)
            s1 = S.rot('s1', [128, 8], F32, 2)
            S.red(s1[0:P], gv[0:P])
            S.ts(s1[0:P], s1[0:P], -1.0 / 64, ALU.mult)
            cen = S.rot('cen', [128, 8, 64], F32, 1)
            S.tt(cen[0:P], gv[0:P], s1[0:P].un(2).bc([P, 8, 64]), ALU.add)
            sq = S.rot('lsq', [128, 8, 64], F32, 1)
            S.tt(sq[0:P], cen[0:P], cen[0:P], ALU.mult, eng='pool')
            var = S.rot('var', [128, 8], F32, 2)
            S.red(var[0:P], sq[0:P])
            S.act(var[0:P], var[0:P], AF.Ln, bias=epsT[0:P, :], scale=1.0 / 64)
            S.act(var[0:P], var[0:P], AF.Exp, scale=-0.5)
            S.tt(cen[0:P], cen[0:P], var[0:P].un(2).bc([P, 8, 64]), ALU.mult)
            S.tt(cen[0:P], cen[0:P], lng_bc[0:P].r("p (g d) -> p g d", g=8), ALU.mult, eng='pool')
            vg = S.rot('vg', [128, 8, 64], BF16, 1)
            if is_sample:
                S.tt(sq[0:P], cen[0:P], lnb_bc[0:P].r("p (g d) -> p g d", g=8), ALU.add)
                S.store(o_scv, sq[0:P].r("p g d -> p (g d)"))
                S.copy(vg[0:P], sq[0:P], eng='pool')
            else:
                S.tt(vg[0:P], cen[0:P], lnb_bc[0:P].r("p (g d) -> p g d", g=8), ALU.add)
            sp_ps = PS[5]
            for g in range(8):
                lw = WsT_s[:, g, :] if is_sample else WsT[:, g, :]
                S.mm(sp_ps[0:P, g * 64:(g + 1) * 64], lw, vg[0:P, g, :])
            bt = bsT_s if is_sample else bsT
            S.tt(cen[0:P], sp_ps[0:P, :].r("p (g d) -> p g d", g=8), bt[0:P].un(2).bc([P, 8, 64]), ALU.add)
            a_o = gv
            S.tt(a_o[0:P].r("p g d -> p (g d)"), cen[0:P].r("p g d -> p (g d)"), gu[0:P], ALU.mult, eng='pool')
            a_f = a_o[0:P].r("p g d -> p (g d)")
            ria = rinv_of(a_f, 512, P)
            S.ts(a_f, a_f, ria[0:P, 0:1], ALU.mult, eng='pool')
            acol = TP if is_sample else tok0
            transposes(lambda i: aT[:, i, acol:acol + P], [a_f[:, i * 128:(i + 1) * 128] for i in range(4)], P, PS[6],
                       scale=lambda i: g_oa_col[:, i:i + 1])
            c3 = zb[2]
            cq = S.rot('cq', [128, 256], F32, 1)
            riq = rinv_of(c3[0:P, 0:256], 256, P)
            S.act(cq[0:P], c3[0:P, 0:256], AF.Copy, scale=riq[0:P, 0:1])
            cqT = S.rot('cqT', [128, 2, 128], BF16, 1)
            transposes(lambda i: cqT[:, i, 0:P], [cq[0:P, i * 128:(i + 1) * 128] for i in range(2)], P, PS[6],
                       scale=lambda i: g_qa_col[:, i:i + 1])
            ckn = S.rot('ckn', [128, 256], F32, 2)
            rik = rinv_of(c3[0:P, 256:512], 256, P)
            S.act(ckn[0:P], c3[0:P, 256:512], AF.Copy, scale=rik[0:P, 0:1])
            S.tt(ckn[0:P], ckn[0:P], g_kv_bc[0:P], ALU.mult)
            kpe = S.rot('kpe', [128, 32], F32, 2)
            S.copy(kpe[0:P], zb[3][0:P, 0:32], eng='act')
            if is_sample:
                S.store(o_sckv, ckn[0:P])
                S.store(o_skpe, kpe[0:P])
                S.copy(Cb_new[:, 0:256], ckn[0:P], eng='pool')
            else:
                S.store(o_pckv[tok0:tok0 + P, :], ckn[0:P])
                S.store(o_pkpe[tok0:tok0 + P, :], kpe[0:P])
            ckT = S.rot('ckT', [128, 2, 128], BF16, 1)
            transposes(lambda i: ckT[:, i, 0:P], [ckn[0:P, i * 128:(i + 1) * 128] for i in range(2)], P, PS[6])
            q_ps0, q_ps1 = PS[7], PS[5]
            for k in range(2):
                S.mm(q_ps0[0:P, :], cqT[:, k, 0:P], w_uq[:, k, 0:512], start=(k == 0), stop=(k == 1))
            q_sb = S.rot('q_sb', [128, 8, 96], F32, 1)
            S.copy(q_sb[0:P].r("p h d -> p (h d)")[:, 0:512], q_ps0[0:P, :], eng='act')
            for k in range(2):
                S.mm(q_ps1[0:P, 0:256], cqT[:, k, 0:P], w_uq[:, k, 512:768], start=(k == 0), stop=(k == 1))
            S.copy(q_sb[0:P].r("p h d -> p (h d)")[:, 512:768], q_ps1[0:P, 0:256], eng='act')
            kn_ps, v_ps = PS[1], PS[2]
            for k in range(2):
                S.mm(kn_ps[0:P, :], ckT[:, k, 0:P], w_uk[:, k, :], start=(k == 0), stop=(k == 1))
            for k in range(2):
                S.mm(v_ps[0:P, :], ckT[:, k, 0:P], w_uv[:, k, :], start=(k == 0), stop=(k == 1))
            k_sb = S.rot('k_sb', [128, 8, 96], F32, 1)
            S.copy(k_sb[0:P, :, 0:64], kn_ps[0:P, :].r("p (h d) -> p h d", h=8), eng='act')
            S.copy(k_sb[0:P, :, 64:96], kpe[0:P].un(1).bc([P, 8, 32]), eng='pool')
            if not is_sample:
                S.copy(Vaug[0:P, ti, :, 0:64], v_ps[0:P, :].r("p (h d) -> p h d", h=8), eng='act')
            cs = coss.v if is_sample else cosp[:, ti, :]
            sn = sins.v if is_sample else sinp[:, ti, :]
            for nm, src, gbc in (('q', q_sb, g_qq_bc), ('k', k_sb, g_qk_bc)):
                rg = group_rinv(src[0:P], 8, 96, P)
                S.tt(src[0:P], src[0:P], rg[0:P].un(2).bc([P, 8, 96]), ALU.mult)
                S.tt(src[0:P], src[0:P], gbc[0:P].un(1).bc([P, 8, 96]), ALU.mult, eng='pool')
                fin = S.rot('fin', [128, 8, 96], F32, 1)
                S.copy(fin[0:P, :, 0:64], src[0:P, :, 0:64], eng='pool')
                rope(fin[0:P, :, 64:80], fin[0:P, :, 80:96], src[0:P, :, 64:80], src[0:P, :, 80:96], cs[0:P], sn[0:P], P)
                if is_sample:
                    dstT = qT_s if nm == 'q' else kT_s
                    transposes(lambda h: dstT[:, h, 0:P], [fin[0:P, h, :] for h in range(8)], P, PS[6])
                    if nm == 'q':
                        qg = S.rot('qg', [128, 8, 64], F32, 1)
                        S.tt(qg[0:P], fin[0:P, :, 0:64], g_qk_bc[0:P, 0:64].un(1).bc([P, 8, 64]), ALU.mult)
                        transposes(lambda j: qnT_s[:, j, 0:P],
                                   [qg[0:P, 2 * j:2 * j + 2, :].r("p h d -> p (h d)") for j in range(4)], P, PS[6])
                        qpe = S.rot('qpe', [128, 8, 32], F32, 1)
                        S.copy(qpe[0:P], fin[0:P, :, 64:96], eng='pool')
                        transposes(lambda h: qpeT_s[:, h, 0:P], [qpe[0:P, h, :] for h in range(8)], P, PS[6])
                else:
                    if nm == 'q':
                        transposes(lambda h: qT[:, h, qcol:qcol + P], [fin[0:P, h, :] for h in range(8)], P, PS[6])
                    else:
                        transposes(lambda h: kT[:, h, tok0:tok0 + P], [fin[0:P, h, :] for h in range(8)], P, PS[7])

        att_cnt = [0]

        def attention(qb):
            for h in range(8):
                ot = PS[4 + (att_cnt[0] % 2)]
                att_cnt[0] += 1
                nkb = 4 * qb + 4

                def s_stage(kb):
                    c0 = max(0, kb - 4 * qb) * 128
                    st = PS[kb % 2]
                    S.mm(st[:, c0:512], kT[:, h, kb * 128:(kb + 1) * 128], qT[:, h, c0:512])
                    pt = S.rot('pt', [128, 512], BF16, 3)
                    S.act(pt[:, c0:512], st[:, c0:512], AF.Exp, bias=negC[:, 0:1], scale=SC_MLA)
                    if kb >= 4 * qb:
                        S.tt(pt[:, c0:c0 + 128], pt[:, c0:c0 + 128], trib.v, ALU.mult, eng='pool')
                    return pt, c0
                nxt = s_stage(0)
                for kb in range(nkb):
                    pt, c0 = nxt
                    if kb + 1 < nkb:
                        nxt = s_stage(kb + 1)
                    S.mm(ot[0:65, c0:512], Vaug[:, kb, h, 0:65], pt[:, c0:512], start=(kb == 0), stop=(kb == nkb - 1))
                ot_sb = S.rot('ot_sb', [65, 512], F32, 2)
                S.copy(ot_sb.v, ot[0:65, :], eng='act')
                tp = PS[6 + (att_cnt[0] % 2)]
                for j in range(4):
                    S.tr(tp[:, j * 128:j * 128 + 65], ot_sb[:, j * 128:(j + 1) * 128], ident[0:65, 0:65])
                tpv = tp.v.r("p (j c) -> p j c", j=4)
                rd = S.rot('rd', [128, 4, 1], F32, 2)
                S.recip(rd.v, tpv[:, :, 64:65])
                S.tt(b_out[:, :, h * 64:(h + 1) * 64], tpv[:, :, 0:64], rd.v.bc([128, 4, 64]), ALU.mult)
            for t in range(4):
                ti = 4 * qb + t
                rib = rinv_of(b_out[:, t, :], 512, 128)
                S.ts(b_out[:, t, :], b_out[:, t, :], rib[:, 0:1], ALU.mult, eng='pool')
                transposes(lambda i: bT[:, i, ti * 128:(ti + 1) * 128], [b_out[:, t, i * 128:(i + 1) * 128] for i in range(4)],
                           128, PS[2 + t % 2], scale=lambda i: g_ob_col[:, i:i + 1])

        for qb in range(4):
            for t in range(4):
                ti = 4 * qb + t
                phase1(ti, 128, xp[ti * 128:(ti + 1) * 128, :], False, t * 128)
                ckpt('c2')
            if qb == 3:
                phase1(0, TS, xs[:, :], True, 0)
            attention(qb)
            ckpt('c3')
        S.pop()
        ckpt('c4')

        S.push()
        wukT = S.sb([128, 4, 256], BF16, 'wukT')
        S.push()
        wukf = S.sb([128, 2, 512], F32, 'wukf')
        S.load(wukf.v, W['w_uk'].rearrange("(c p) n -> p c n", p=128))
        for j in range(4):
            for cc in range(2):
                S.tr(PS[0][:, cc * 128:(cc + 1) * 128], wukf[:, cc, j * 128:(j + 1) * 128], ident.v)
            S.copy(wukT[:, j, :], PS[0][:, 0:256], eng='act')
        S.pop()
        ckpt('c40')
        qlatT = S.sb([128, 2, 8, TS], BF16, 'qlatT')
        for h in range(8):
            j, a = h // 2, h % 2
            for cc in range(2):
                col = (j * 2 + cc) * 64
                S.mm(PS[1 + a][:, col:col + TS],
                     wukT[a * 64:(a + 1) * 64, j, cc * 128:(cc + 1) * 128], qnT_s[a * 64:(a + 1) * 64, j, :])
        for a in range(2):
            S.copy(qlatT.v.r("p c (j a) t -> p a j c t", a=2)[:, a],
                   PS[1 + a].v.r("p (j c t) -> p j c t", j=4, c=2), eng='act')
        ckpt('c41')
        ptt = S.sb([128, NSEQ // 2], I32, 'ptt')
        S.load(ptt.v, ptab.rearrange("(g p) o -> p (g o)", p=128), allow_slow_non_contiguous=True)
        ptf = S.sb([128, NSEQ // 2], F32, 'ptf')
        S.copy(ptf.v, ptt.v)
        io16 = S.sb([128, 16], F32, 'io16')
        S.op('pool', lambda e: e.iota(io16.v.ap, pattern=[[1, 16]], base=0, channel_multiplier=0,
                                      allow_small_or_imprecise_dtypes=True), [], [io16])
        idxf = S.sb([128, NSEQ // 2, 16], F32, 'idxf')
        S.ts(idxf.v, ptf.v.un(2).bc([128, NSEQ // 2, 16]), 16.0, ALU.mult)
        idxc = S.sb([128, NSEQ // 2, 16], I32, 'idxc')
        S.tt(idxc.v, idxf.v, io16.v.un(1).bc([128, NSEQ // 2, 16]), ALU.add)
        ckpt('c42')
        cckv16 = cckv.rearrange("n (j x) -> (n j) x", j=16)
        ckpe2 = ckpe.rearrange("n (j x) -> (n j) x", j=2)
        idx2 = S.sb([128, NSEQ // 2, 2], I32, 'idx2')
        S.ts(idxf[:, :, 0:2], ptf.v.un(2).bc([128, NSEQ // 2, 2]), 2.0, ALU.mult)
        S.tt(idx2.v, idxf[:, :, 0:2], io16[:, 0:2].un(1).bc([128, NSEQ // 2, 2]), ALU.add)
        cosk = S.sb([128, 128, 16], F32, 'cosk')
        sink = S.sb([128, 128, 16], F32, 'sink')
        S.load(cosk.v, c_cosk.rearrange("p (r f) -> p r f", f=16))
        S.load(sink.v, c_sink.rearrange("p (r f) -> p r f", f=16))
        gpe_bc = g_qk_bc[:, 64:96]
        b_s_T = S.sb([128, 4, TS], F32, 'b_s_T')
        RCH = 8
        ckpt('c4a')
        STB = [PS[4], PS[5]]
        ptb_bufs = [S.sb([128, 4, 64], BF16, 'ptb') for _ in range(4)]
        for pb_ in ptb_bufs:
            S.memset(pb_.v.r("p g c -> p (g c)"), 0.0)
        ptb_ctr = [0]
        onesf = S.sb([128, 1], F32, 'onesf')
        S.memset(onesf.v, 1.0)
        CT3 = [PS[2], PS[3], PS[7]]
        for pr in range(NSEQ // 2):
            idx = ptt[:, pr:pr + 1]
            sspe = S.rot('sspe', [128, 128], F32, 1)
            kr = S.rot('kr', [128, 128, 32], BF16, 1)
            for hf in range(2):
                rs = slice(hf * 64, (hf + 1) * 64)
                KP = S.rot('KP', [128, 64, 32], F32, 1)
                ix2 = idx2[:, pr, hf:hf + 1]
                S.dma('pool', (lambda KP, ix2: lambda e: e.indirect_dma_start(
                    out=KP.v.ap.rearrange("p r f -> p (r f)"), out_offset=None, in_=ckpe2,
                    in_offset=bass.IndirectOffsetOnAxis(ap=ix2.ap, axis=0)))(KP, ix2), [idx2], [KP])
                ksq = S.rot('ksq', [128, 64, 32], F32, 1)
                S.tt(ksq.v, KP.v, KP.v, ALU.mult, eng='pool')
                S.red(sspe[:, rs], ksq.v)
                S.tt(ksq.v, KP.v, gpe_bc.un(1).bc([128, 64, 32]), ALU.mult, eng='pool')
                t1 = S.rot('kt1', [128, 64, 16], F32, 1)
                t2 = S.rot('kt2', [128, 64, 16], F32, 1)
                S.tt(t1.v, ksq[:, :, 0:16], cosk[:, rs, :], ALU.mult)
                S.tt(t2.v, ksq[:, :, 16:32], sink[:, rs, :], ALU.mult, eng='pool')
                S.tt(kr[:, rs, 0:16], t1.v, t2.v, ALU.subtract)
                S.tt(t1.v, ksq[:, :, 16:32], cosk[:, rs, :], ALU.mult)
                S.tt(t2.v, ksq[:, :, 0:16], sink[:, rs, :], ALU.mult, eng='pool')
                S.tt(kr[:, rs, 16:32], t1.v, t2.v, ALU.add)
            oacc = PS[6]
            ckpt('c4b')
            NCH = 128 // RCH
            Cbs, cTs, ssns, ptbs = {}, {}, {}, {}
            state = {'first': True}

            def load_chunk(ch):
                Cb = S.rot('Cb', [128, RCH, 256], BF16, 3)
                ixc = idxc[:, pr, ch:ch + 1]
                S.dma('pool', (lambda Cb, ixc: lambda e: e.indirect_dma_start(
                    out=Cb.v.ap.rearrange("p r c -> p (r c)"), out_offset=None, in_=cckv16,
                    in_offset=bass.IndirectOffsetOnAxis(ap=ixc.ap, axis=0)))(Cb, ixc), [idxc], [Cb])
                Cbs[ch] = Cb

            def st_T(r):
                ch, rl = divmod(r, RCH)
                Cb = Cbs[ch]
                ctp = CT3[r % 3]
                for cc in range(2):
                    S.mm(ctp[:, cc * 128:(cc + 1) * 128], Cb[:, rl, cc * 128:(cc + 1) * 128], identb.v)
                S.mm(ctp[0:32, 256:384], kr[:, r, :], identb.v)
                cT = S.rot('cT', [128, 384], BF16, 4)
                S.copy(cT[:, 0:256], ctp[:, 0:256], eng='dve')
                S.copy(cT[0:32, 256:384], ctp[0:32, 256:384], eng='act')
                cTs[r] = cT

            def st_K(r):
                cT = cTs.pop(r)
                grp, g = divmod(r, 4)
                if g == 0:
                    ssns[grp] = S.rot('ssn', [128, 4, 8], F32, 3)
                knp = PS[r % 2]
                for cc in range(2):
                    S.mm(knp.v, cT[:, cc * 128:(cc + 1) * 128], w_uk[:, cc, :], start=(cc == 0), stop=(cc == 1))
                sqk = S.rot('sqk', [128, 8, 64], BF16, 3)
                S.act(sqk.v.r("p h d -> p (h d)"), knp.v, AF.Square)
                S.red(ssns[grp][:, g, :], sqk.v)
                stb = STB[grp % 2]
                for cc in range(2):
                    S.mm(stb[:, g * 64:(g + 1) * 64], cT[:, cc * 128:(cc + 1) * 128],
                         qlatT[:, cc, :, pr * 8:pr * 8 + 8].r("p h (a q) -> p a h q", a=2),
                         start=(cc == 0), stop=False)
                S.mm(stb[:, g * 64:(g + 1) * 64], cT[0:32, 256:384],
                     qpeT_s[:, :, pr * 8:pr * 8 + 8].r("p h (a q) -> p a h q", a=2), start=False, stop=True)

            def st_G(grp):
                r0 = grp * 4
                stb = STB[grp % 2]
                tot = S.rot('tot', [128, 4, 8], F32, 2)
                S.tt(tot.v, ssns.pop(grp).v, sspe[:, r0:r0 + 4].un(2).bc([128, 4, 8]), ALU.add)
                S.act(tot.v, tot.v, AF.Ln, bias=epsT[:, 0:1], scale=1.0 / 96)
                S.act(tot.v, tot.v, AF.Exp, scale=-0.5)
                snm = S.rot('snm', [128, 4, 8, 4], F32, 2)
                ptb = ptb_bufs[ptb_ctr[0] % 4]
                ptb_ctr[0] += 1
                for a in range(2):
                    pa = slice(a * 64, (a + 1) * 64)
                    S.tt(snm[pa], stb[pa, 0:256].r("p (g a h q) -> p g a h q", g=4, a=2, h=8)[:, :, a, :, :],
                         tot[pa].un(3).bc([64, 4, 8, 4]), ALU.mult)
                    S.act(ptb[pa].r("p g (a h q) -> p g a h q", a=2, h=8)[:, :, a, :, :], snm[pa], AF.Exp,
                          bias=negC[pa, 0:1], scale=SC_MLA)
                ptbs[grp] = ptb

            def st_PV(grp):
                ptb = ptbs.pop(grp)
                for g in range(4):
                    ch, rl = divmod(grp * 4 + g, RCH)
                    Cb = Cbs[ch]
                    S.mm(oacc[0:64, 0:256], ptb[:, g, :], Cb[:, rl, :], start=state['first'], stop=False)
                    state['first'] = False
                    S.op('pe', (lambda o_, l_, r_: lambda e: e.matmul(o_.ap, lhsT=l_.ap, rhs=r_.ap, start=False,
                                                                      stop=False, skip_group_check=True))(
                        oacc[0:64, 256:257], ptb[:, g, :], onesb[:, 0:1]), [ptb, onesb], [oacc])

            load_chunk(0)
            load_chunk(1)
            for r in range(128 + 2):
                if r < 128:
                    st_T(r)
                if r >= 2:
                    rr = r - 2
                    st_K(rr)
                    if rr % 4 == 3:
                        grp = rr // 4
                        st_G(grp)
                        if grp >= 2:
                            st_PV(grp - 2)
                            if (grp - 2) % (RCH // 4) == (RCH // 4) - 1:
                                nxtc = (grp - 2) // (RCH // 4) + 3
                                if nxtc < NCH:
                                    load_chunk(nxtc)
                        if grp == 0 and 2 < NCH:
                            load_chunk(2)
            st_PV(30)
            st_PV(31)
            tk8 = slice(pr * 8, pr * 8 + 8)
            for a in range(2):
                sq_ = pr * 2 + a
                tk = slice(sq_ * 4, sq_ * 4 + 4)
                snew = PS[2]
                for h in range(8):
                    S.mm(snew[0:64, h * 4:(h + 1) * 4], kT_s[:, h, :], qT_s[:, h, tk])
                pn = S.rot('pn', [64, 8, 4], BF16, 2)
                S.act(pn.v.r("p h q -> p (h q)"), snew[0:64, 0:32], AF.Exp, bias=negC[0:64, 0:1], scale=SC_MLA)
                S.tt(pn.v, pn.v, bd[:, tk].un(1).bc([64, 8, 4]), ALU.mult)
                S.mm(oacc[a * 32:(a + 1) * 32, 0:257], pn.v.r("p h q -> p (h q)"), Cb_new[:, 0:257], start=False, stop=(a == 1))
            ol = S.rot('ol', [64, 264], F32, 2)
            S.copy(ol[:, 0:257], oacc[0:64, 0:257], eng='act')
            rl_ = S.rot('rl_', [64, 1], F32, 2)
            S.recip(rl_.v, ol[:, 256:257])
            S.ts(ol[:, 0:256], ol[:, 0:256], rl_[:, 0:1], ALU.mult)
            olT = S.rot('olT', [128, 2, 64], BF16, 2)
            transposes(lambda i: olT[:, i, :], [ol[:, i * 128:(i + 1) * 128] for i in range(2)], 64, PS[2])
            bps = PS[3]
            for h in range(8):
                j, par = h // 2, h % 2
                for cc in range(2):
                    S.mm(bps[par * 64:(par + 1) * 64, j * 8:(j + 1) * 8], w_uv[:, cc, h * 64:(h + 1) * 64],
                         olT[:, cc, :].r("p (a h q) -> p h a q", a=2, h=8)[:, h, :, :], start=(cc == 0), stop=(cc == 1))
            S.copy(b_s_T[:, :, tk8], bps[:, 0:32].r("p (j q) -> p j q", j=4), eng='act')
            ckpt('c5')
        bsq = S.sb([128, 4, TS], BF16, 'bsq')
        S.tt(bsq.v, b_s_T.v, b_s_T.v, ALU.mult)
        for j in range(4):
            S.mm(PS[0][:, 0:TS], onesb.v, bsq[:, j, :], start=(j == 0), stop=(j == 3))
        rbs = S.sb([128, TS], F32, 'rbs')
        S.act(rbs.v, PS[0][:, 0:TS], AF.Ln, bias=epsT[:, 0:1], scale=1.0 / 512)
        S.act(rbs.v, rbs.v, AF.Exp, scale=-0.5)
        S.tt(b_s_T.v, b_s_T.v, rbs.v.un(1).bc([128, 4, TS]), ALU.mult)
        for j in range(4):
            S.ts(bT[:, j, TP:TP + TS], b_s_T[:, j, :], g_ob_col[:, j:j + 1], ALU.mult)
        if stop == 'c6':
            S.store(y_s.rearrange("t (two d) -> (t two) d", two=2)[:, 0:256], b_s_T.v.r("p j t -> p (j t)"))
        S.pop()
        S.pop()
        ckpt('c6')

        dep_best = {}
        for tl in arena_alias:
            for d in list(tl.rd) + ([tl.lw] if tl.lw is not None else []):
                if dep_best.get(d[0], (0, None))[0] < d[1]:
                    dep_best[d[0]] = (d[1], d[2])
        arena_deps = [(k, v, 'freed') for k, (v, e) in dep_best.items()]
        xres = []
        for t in range(NT + 1):
            tl = Tl(arena.t[:, t * 1024:(t + 1) * 1024], 'xres%d' % t)
            tl.rd = list(arena_deps)
            xres.append(tl)
        S.push()
        w_oa = load_w('w_o', D, D, tname='w_o')
        for t in range(NT + 1):
            is_s = (t == NT)
            P = TS if is_s else 128
            tc0 = t * 128
            x_t = S.rot('x_t3', [128, D], F32, 2)
            S.load(x_t[0:P], xs[:, :] if is_s else xp[tc0:tc0 + P, :])
            for n in range(2):
                ps = PS[(2 * t + n) % 4]
                for k in range(8):
                    src = aT if k < 4 else bT
                    S.mm(ps[0:P, :], src[:, k % 4, tc0:tc0 + P], w_oa[:, k, n * 512:(n + 1) * 512],
                         start=(k == 0), stop=(k == 7))
                S.tt(xres[t][0:P, n * 512:(n + 1) * 512], ps[0:P, :], x_t[0:P, n * 512:(n + 1) * 512], ALU.add)
        S.pop()
        S.pop()
        ckpt('c7')

        S.push()
        g_min_col = col_vec('g_mem_in', D)
        w_mk = load_w('w_mk', D, 512)
        w_mv = load_w('w_mv', D, 512)
        for mt in range(2):
            m_t = S.rot('m_t', [128, D], F32, 2)
            S.load(m_t.v, memp[mt * 128:(mt + 1) * 128, :])
            ri = rinv_of(m_t.v, D, 128)
            S.ts(m_t.v, m_t.v, ri[:, 0:1], ALU.mult, eng='pool')
            mT = S.rot('mT', [128, 8, 128], BF16, 2)
            transposes(lambda i: mT[:, i, :], [m_t[:, i * 128:(i + 1) * 128] for i in range(8)], 128, PS[0],
                       scale=lambda i: g_min_col[:, i:i + 1])
            for k in range(8):
                S.mm(PS[1].v, mT[:, k, :], w_mk[:, k, :], start=(k == 0), stop=(k == 7))
            for k in range(8):
                S.mm(PS[2].v, mT[:, k, :], w_mv[:, k, :], start=(k == 0), stop=(k == 7))
            mk_sb = S.rot('mk_sb', [128, 4, 128], F32, 2)
            S.copy(mk_sb.v.r("p h d -> p (h d)"), PS[1].v, eng='act')
            rg = group_rinv(mk_sb.v, 4, 128, 128)
            S.tt(mk_sb.v, mk_sb.v, rg.v.un(2).bc([128, 4, 128]), ALU.mult)
            S.tt(mk_sb.v, mk_sb.v, g_mk_bc.v.un(1).bc([128, 4, 128]), ALU.mult, eng='pool')
            S.store(o_pmk[mt * 128:(mt + 1) * 128, :], mk_sb.v.r("p h d -> p (h d)"))
            transposes(lambda h: mkT[:, h, mt * 128:(mt + 1) * 128], [mk_sb[:, h, :] for h in range(4)], 128, PS[3])
            mv_sb = S.rot('mv_sb', [128, 512], F32, 2)
            S.copy(mv_sb.v, PS[2].v, eng='act')
            S.store(o_pmv[mt * 128:(mt + 1) * 128, :], mv_sb.v)
            S.copy(mvb[:, mt, :], mv_sb.v, eng='pool')
        S.pop()

        ckpt('c8')
        blocks = [(i * 512, 512, False) for i in range(4)] + [(TP, TS, True)]

        def norm_T(dst, c0, NB, gcol):
            ntile = max(1, NB // 128)
            P = min(NB, 128)
            for t in range(ntile):
                xr = xres[c0 // 128 + t]
                ri = rinv_of(xr[0:P, :], D, P)
                xh = S.rot('xh3', [128, D], F32, 2)
                S.ts(xh[0:P], xr[0:P, :], ri[0:P, 0:1], ALU.mult, eng='pool')
                transposes(lambda i: dst[:, i, t * 128:t * 128 + P], [xh[0:P, i * 128:(i + 1) * 128] for i in range(8)],
                           P, PS[t % 2], scale=lambda i: gcol[:, i:i + 1])

        S.push()
        g_mx_col = col_vec('g_mem_x', D)
        w_mq = load_w('w_mq', D, 512)
        w_mo = load_w('w_mo', 512, D)
        for (c0, NB, is_s) in blocks:
            ntile = max(1, NB // 128)
            P = min(NB, 128)
            hT = S.rot('hT4', [128, 8, 512], BF16, 1)
            norm_T(hT, c0, NB, g_mx_col)
            qmT = S.rot('qmT', [128, 4, 512], BF16, 1)
            for t in range(ntile):
                ps = PS[2 + t % 2]
                for k in range(8):
                    S.mm(ps[0:P, :], hT[:, k, t * 128:t * 128 + P], w_mq[:, k, :], start=(k == 0), stop=(k == 7))
                qm = S.rot('qm', [128, 4, 128], F32, 2)
                S.copy(qm[0:P].r("p h d -> p (h d)"), ps[0:P, :], eng='act')
                rg = group_rinv(qm[0:P], 4, 128, P)
                S.tt(qm[0:P], qm[0:P], rg[0:P].un(2).bc([P, 4, 128]), ALU.mult)
                S.tt(qm[0:P], qm[0:P], g_mq_bc[0:P].un(1).bc([P, 4, 128]), ALU.mult, eng='pool')
                transposes(lambda h: qmT[:, h, t * 128:t * 128 + P], [qm[0:P, h, :] for h in range(4)], P, PS[4 + t % 2])
            omT = S.rot('omT', [128, 4, 512], BF16, 1)
            if not is_s:
                for h in range(4):
                    o_ps, d_ps = PS[4], PS[5]
                    for kb in range(2):
                        st = PS[kb]
                        S.mm(st.v, mkT[:, h, kb * 128:(kb + 1) * 128], qmT[:, h, :])
                        pm = S.rot('pm', [128, 512], BF16, 2)
                        S.act(pm.v, st.v, AF.Exp, bias=negCm[:, 0:1], scale=SC_MEM)
                        S.mm(o_ps.v, mvb[:, kb, h * 128:(h + 1) * 128], pm.v, start=(kb == 0), stop=(kb == 1))
                        S.mm(d_ps.v, onesb.v, pm.v, start=(kb == 0), stop=(kb == 1))
                    rden = S.rot('rden', [128, 512], F32, 1)
                    S.act(rden.v, d_ps.v, AF.Ln)
                    S.act(rden.v, rden.v, AF.Exp, scale=-1.0)
                    S.tt(omT[:, h, :], o_ps.v, rden.v, ALU.mult)
            else:
                for sq_ in range(NSEQ):
                    tk = slice(sq_ * 4, sq_ * 4 + 4)
                    mk_s = S.rot('mk_s', [128, 2, 512], F32, 2)
                    S.load(mk_s.v, cmk[sq_].rearrange("(b p) f -> p b f", p=128))
                    mv_s = S.rot('mv_s', [128, 2, 512], BF16, 2)
                    S.load(mv_s.v, cmv[sq_].rearrange("(b p) f -> p b f", p=128), q='pool')
                    mkT_s = S.rot('mkT_s', [128, 4, 256], BF16, 2)
                    for kb in range(2):
                        for h in range(4):
                            S.tr(PS[kb][:, h * 128:(h + 1) * 128], mk_s[:, kb, h * 128:(h + 1) * 128], ident.v)
                        S.copy(mkT_s[:, :, kb * 128:(kb + 1) * 128], PS[kb].v.r("p (h k) -> p h k", h=4), eng='act')
                    st = PS[2]
                    for kb in range(2):
                        for h in range(4):
                            cl = (kb * 4 + h) * 4
                            S.mm(st[:, cl:cl + 4], mkT_s[:, h, kb * 128:(kb + 1) * 128], qmT[:, h, tk])
                    pm = S.rot('pm_s', [128, 2, 4, 4], BF16, 2)
                    S.act(pm.v.r("p b h q -> p (b h q)"), st[:, 0:32], AF.Exp, bias=negCm[:, 0:1], scale=SC_MEM)
                    o_ps, d_ps = PS[4], PS[5]
                    for h in range(4):
                        for kb in range(2):
                            S.mm(o_ps[:, h * 4:(h + 1) * 4], mv_s[:, kb, h * 128:(h + 1) * 128], pm[:, kb, h, :],
                                 start=(kb == 0), stop=(kb == 1))
                    for kb in range(2):
                        S.mm(d_ps[:, 0:16], onesb.v, pm[:, kb, :, :].r("p h q -> p (h q)"), start=(kb == 0), stop=(kb == 1))
                    rden = S.rot('rden_s', [128, 16], F32, 2)
                    S.recip(rden.v, d_ps[:, 0:16])
                    S.tt(omT[:, :, tk], o_ps[:, 0:16].r("p (h q) -> p h q", h=4), rden.v.r("p (h q) -> p h q", h=4), ALU.mult)
            for t in range(ntile):
                xr = xres[c0 // 128 + t]
                for n in range(2):
                    ps = PS[6 + n]
                    for k in range(4):
                        S.mm(ps[0:P, :], omT[:, k, t * 128:t * 128 + P], w_mo[:, k, n * 512:(n + 1) * 512],
                             start=(k == 0), stop=(k == 3))
                    S.tt(xr[0:P, n * 512:(n + 1) * 512], ps[0:P, :], xr[0:P, n * 512:(n + 1) * 512], ALU.add)
        S.pop()

        ckpt('c9')
        S.push()
        g_ffn_col = col_vec('g_ffn', D)
        wc_col = S.sb([128, 3, NFC], F32, 'wc_col')
        S.load(wc_col.v, W['w_conv'].rearrange("j (c p) -> p j c", p=128), allow_slow_non_contiguous=True)
        bc_col = S.sb([128, NFC], F32, 'bc_col')
        S.load(bc_col.v, W['b_conv'].rearrange("(c p) -> p c", p=128), allow_slow_non_contiguous=True)
        carry = S.sb([128, NFC, 2], F32, 'carry')
        S.memset(carry.v, 0.0)
        w_down = load_w('w_down', DFF, D)
        srcw = W['w_up'].rearrange("(c p) n -> p c n", p=128)
        for (c0, NB, is_s) in blocks:
            ntile = max(1, NB // 128)
            P = min(NB, 128)
            hT = S.rot('hT5', [128, 8, 512], BF16, 1)
            norm_T(hT, c0, NB, g_ffn_col)
            aF = S.rot('aF', [128, NFC, 512], BF16, 1)
            for fc in range(NFC):
                wg = S.rot('wg', [128, 8, 128], BF16, 4)
                wv = S.rot('wv', [128, 8, 128], BF16, 4)
                S.load(wg.v, srcw[:, :, fc * 128:(fc + 1) * 128], q='pool')
                S.load(wv.v, srcw[:, :, DFF + fc * 128:DFF + (fc + 1) * 128], q='pool')
                gps, vps = PS[(fc % 2) * 2], PS[(fc % 2) * 2 + 1]
                for k in range(8):
                    S.mm(gps[:, 0:NB], wg[:, k, :], hT[:, k, 0:NB], start=(k == 0), stop=(k == 7))
                for k in range(8):
                    S.mm(vps[:, 0:NB], wv[:, k, :], hT[:, k, 0:NB], start=(k == 0), stop=(k == 7))
                w0, w1, w2 = (wc_col[:, j, fc:fc + 1] for j in range(3))
                cv = S.rot('cv', [128, 512], F32, 2)
                if not is_s:
                    gb = S.rot('gb', [128, 514], F32, 2)
                    S.copy(gb[:, 0:2], carry[:, fc, :], eng='dve')
                    S.copy(gb[:, 2:514], gps.v, eng='act')
                    S.copy(carry[:, fc, :], gb[:, 512:514], eng='dve')
                    S.ts(cv.v, gb[:, 0:512], w0, ALU.mult, bc_col[:, fc:fc + 1], ALU.add)
                    S.op('dve', (lambda cv, gb, w1: lambda e: e.scalar_tensor_tensor(
                        out=cv.v.ap, in0=gb[:, 1:513].ap, scalar=w1.ap, in1=cv.v.ap, op0=ALU.mult, op1=ALU.add))(cv, gb, w1),
                        [gb, wc_col, cv], [cv])
                    S.op('dve', (lambda cv, gb, w2: lambda e: e.scalar_tensor_tensor(
                        out=cv.v.ap, in0=gb[:, 2:514].ap, scalar=w2.ap, in1=cv.v.ap, op0=ALU.mult, op1=ALU.add))(cv, gb, w2),
                        [gb, wc_col, cv], [cv])
                    if c0 + NB == TP:
                        S.tr(PS[6][0:2, 0:128], gb[:, 512:514], ident.v)
                        pcs = S.rot('pcs', [2, 128], F32, 2)
                        S.copy(pcs.v, PS[6][0:2, 0:128], eng='act')
                        S.store(o_pconv[:, fc * 128:(fc + 1) * 128], pcs.v)
                else:
                    cs_t = S.rot('cs_t', [32, 128], F32, 2)
                    S.load(cs_t.v, cst[:, fc * 128:(fc + 1) * 128])
                    S.tr(PS[6][:, 0:32], cs_t.v, ident[0:32, 0:32])
                    gb = S.rot('gbs', [128, NSEQ, 6], F32, 2)
                    S.copy(gb[:, :, 0:2], PS[6][:, 0:32].r("p (s j) -> p s j", j=2), eng='act')
                    S.copy(gb[:, :, 2:6], gps[:, 0:TS].r("p (s t) -> p s t", t=4), eng='act')
                    cv3 = cv[:, 0:TS].r("p (s t) -> p s t", t=4)
                    S.ts(cv3, gb[:, :, 0:4], w0, ALU.mult, bc_col[:, fc:fc + 1], ALU.add)
                    tmpc = S.rot('tmpc', [128, NSEQ, 4], F32, 2)
                    S.ts(tmpc.v, gb[:, :, 1:5], w1, ALU.mult)
                    S.tt(cv3, cv3, tmpc.v, ALU.add)
                    S.ts(tmpc.v, gb[:, :, 2:6], w2, ALU.mult)
                    S.tt(cv3, cv3, tmpc.v, ALU.add)
                    gl = S.rot('gl', [128, NSEQ, 2], F32, 2)
                    S.copy(gl.v, gb[:, :, 4:6], eng='dve')
                    S.tr(PS[7][0:32, 0:128], gl.v.r("p s j -> p (s j)"), ident.v)
                    scs = S.rot('scs', [32, 128], F32, 2)
                    S.copy(scs.v, PS[7][0:32, 0:128], eng='act')
                    S.store(o_sconv[:, fc * 128:(fc + 1) * 128], scs.v)
                S.act(cv[:, 0:NB], cv[:, 0:NB], AF.Silu)
                S.tt(aF[:, fc, 0:NB], cv[:, 0:NB], vps[:, 0:NB], ALU.mult)
            for t in range(ntile):
                tc0 = c0 + t * 128
                xr = xres[c0 // 128 + t]
                y_t = S.rot('y_t', [128, D], F32, 2)
                for n in range(2):
                    ps = PS[4 + n]
                    for fc in range(NFC):
                        S.mm(ps[0:P, :], aF[:, fc, t * 128:t * 128 + P], w_down[:, fc, n * 512:(n + 1) * 512],
                             start=(fc == 0), stop=(fc == NFC - 1))
                    S.tt(y_t[0:P, n * 512:(n + 1) * 512], ps[0:P, :], xr[0:P, n * 512:(n + 1) * 512], ALU.add)
                if is_s:
                    S.store(y_s[:, :], y_t[0:P])
                else:
                    S.store(y_p[tc0:tc0 + P, :], y_t[0:P])
        S.pop()
        S.finish()
    return nc


def _consts():
    half = 16
    inv_freq = (10000.0 ** (-np.arange(half, dtype=np.float32) / half)).astype(np.float32)

    def cs(pos):
        ang = pos.astype(np.float32)[..., None] * inv_freq
        return np.cos(ang).astype(np.float32), np.sin(ang).astype(np.float32)
    p = np.arange(128)
    cp, sp = cs(np.arange(NT)[None, :] * 128 + p[:, None])
    cS, sS = cs(PAST + (np.arange(TS) % 4))
    ck, sk = cs((p[:, None] % NPG) * 128 + np.arange(128)[None, :])
    tri = (p[:, None] <= p[None, :]).astype(np.float32)
    t64 = np.arange(64)
    bd = ((t64[:, None] // 4 == t64[None, :] // 4) & (t64[:, None] % 4 <= t64[None, :] % 4)).astype(np.float32)
    return {
        'c_ident': np.eye(128, dtype=np.float32), 'c_tri': tri, 'c_bd': bd,
        'c_cosp': cp.reshape(128, -1), 'c_sinp': sp.reshape(128, -1),
        'c_coss': cS, 'c_sins': sS,
        'c_cosk': ck.reshape(128, -1), 'c_sink': sk.reshape(128, -1),
    }


_CACHE = {}


def kernel(**inp):
    f = lambda a: np.ascontiguousarray(np.asarray(a))
    n_phys = inp['cache_ckv'].shape[1]
    if n_phys not in _CACHE:
        import os as _os
        _CACHE[n_phys] = build(n_phys, _os.environ.get('MK_STOP'))
    nc = _CACHE[n_phys]
    consts = _consts()
    cckv = f(inp['cache_ckv']).reshape(n_phys, 128 * 256)
    ckpe = f(inp['cache_kpe']).reshape(n_phys, 128 * 32)
    wnames = ['g_mix', 'w_in', 'ln_v_g', 'ln_v_b', 'w_s', 'b_s', 'g_q_a', 'w_uq', 'g_kv_a', 'w_uk', 'w_uv', 'g_qk_q',
              'g_qk_k', 'g_out_a', 'g_out_b', 'w_o', 'g_mem_x', 'g_mem_in', 'w_mq', 'w_mk', 'w_mv', 'g_mq', 'g_mk',
              'w_mo', 'g_ffn', 'w_up', 'w_conv', 'b_conv', 'w_down']
    shared = {nm: f(inp[nm])[0].reshape(-1) if inp[nm].ndim == 2 else f(inp[nm])[0] for nm in wnames}
    shared['ln_v_g'] = shared['ln_v_g'].reshape(-1)
    shared['ln_v_b'] = shared['ln_v_b'].reshape(-1)
    shared.update(consts)
    shared['cckv'] = cckv
    shared['ckpe'] = ckpe
    in_maps = []
    for c in range(NCORES):
        sl = slice(c * NSEQ, (c + 1) * NSEQ)
        m = dict(shared)
        m['xp'] = f(inp['x_prompt'][c])
        m['xs'] = f(inp['x_sample'][sl]).reshape(TS, D)
        m['cmk'] = f(inp['cache_mem_k'][0, sl]).reshape(NSEQ, 256, 512)
        m['cmv'] = f(inp['cache_mem_v'][0, sl]).reshape(NSEQ, 256, 512)
        m['cst'] = f(inp['state_ffn_conv'][0, sl]).reshape(NSEQ * 2, DFF)
        m['ptab'] = f(inp['page_table'][sl]).reshape(NSEQ * NPG, 1).astype(np.int32)
        m['memp'] = f(inp['mem_prompt'][c])
        in_maps.append(m)
    res = run_bass_kernel_spmd(nc, in_maps, core_ids=list(range(NCORES))).results
    cat = lambda k: np.concatenate([r[k] for r in res], axis=0)
    y_p = cat('y_p').reshape(8, 2048, D)
    y_s = cat('y_s').reshape(128, 4, D)
    return (y_p, y_s,
            cat('o_pckv').reshape(1, 8, 2048, 256), cat('o_pkpe').reshape(1, 8, 2048, 32),
            cat('o_pmk').reshape(1, 8, 256, 4, 128), cat('o_pmv').reshape(1, 8, 256, 4, 128),
            cat('o_pconv').reshape(1, 8, 2, DFF),
            cat('o_sckv').reshape(1, 128, 4, 256), cat('o_skpe').reshape(1, 128, 4, 32),
            cat('o_scv').reshape(1, 128, 4, 8, 64), cat('o_sconv').reshape(1, 128, 2, DFF))
```

```python
import numpy as np
from contextlib import ExitStack
import concourse.bass as bass
import concourse.mybir as mybir
from concourse.bass_utils import run_bass_kernel_spmd

F32 = mybir.dt.float32
BF16 = mybir.dt.bfloat16
I32 = mybir.dt.int32
AF = mybir.ActivationFunctionType
ALU = mybir.AluOpType
AX = mybir.AxisListType

NCORES = 8
D = 1024
TP = 2048
NT = 16
NSEQ = 16
TS = 64
NPG = 64
NIN = 1568
DFF = 2816
NFC = 22
EPS = 1e-6
PAST = 8192

COMPUTE = ('pe', 'dve', 'act', 'pool')
NPOOL = 24


class V:
    __slots__ = ('tl', 'ap')

    def __init__(self, tl, ap):
        self.tl = tl
        self.ap = ap

    def __getitem__(self, k):
        return V(self.tl, self.ap[k])

    def r(self, s, **kw):
        return V(self.tl, self.ap.rearrange(s, **kw))

    def un(self, ax):
        return V(self.tl, self.ap.unsqueeze(ax))

    def bc(self, shape):
        return V(self.tl, self.ap.to_broadcast(list(shape)))


class Tl:
    __slots__ = ('t', 'name', 'lw', 'rd', 'psum')

    def __init__(self, t, name, psum=False, init_rd=()):
        self.t = t
        self.name = name
        self.lw = None
        self.rd = list(init_rd)
        self.psum = psum

    def __getitem__(self, k):
        return V(self, self.t[k])

    @property
    def v(self):
        return V(self, self.t[:])


class Sched:
    def __init__(self, nc, es):
        self.nc = nc
        self.es = es
        self.scopes = [(es, [])]
        self.prog = {k: [] for k in ('pe', 'dve', 'act', 'pool', 'sp')}
        self.sems = {}
        self.cnt = {}
        self.seen = {k: {} for k in self.prog}
        for k in COMPUTE:
            self.sems[k] = es.enter_context(nc.semaphore('s_' + k))
            self.cnt[k] = 0
        self.dpool = {}
        for q in ('sp', 'pool', 'act'):
            lst = []
            for i in range(NPOOL):
                key = 'd_%s_%d' % (q, i)
                self.sems[key] = es.enter_context(nc.semaphore(key))
                self.cnt[key] = 0
                lst.append(key)
            self.dpool[q] = [lst, 0]
        self.out_waits = []
        self.ntile = 0
        self.pending = []
        self.rots = {}

    def push(self):
        es = ExitStack()
        self.scopes.append((es, []))
        self.scope_ctr = getattr(self, 'scope_ctr', 0) + 1
        self.scope_ids = getattr(self, 'scope_ids', [0]) + [self.scope_ctr]

    def pop(self):
        es, tiles = self.scopes.pop()
        best = {}
        for k, v, e in self.pending:
            if best.get(k, (0, None))[0] < v:
                best[k] = (v, e)
        for t in tiles:
            deps = list(t.rd)
            if t.lw is not None:
                deps.append(t.lw)
            for k, v, e in deps:
                if best.get(k, (0, None))[0] < v:
                    best[k] = (v, e)
        self.pending = [(k, v, 'freed') for k, (v, e) in best.items()]
        self.scope_ids = self.scope_ids[:-1]
        es.close()

    def sb(self, shape, dt, name=None):
        self.ntile += 1
        name = (name or 't') + '_%d' % self.ntile
        es, tiles = self.scopes[-1]
        t = es.enter_context(self.nc.sbuf_tensor(name, list(shape), dt))
        tl = Tl(t, name, init_rd=self.pending)
        tiles.append(tl)
        return tl

    def ps(self, shape, dt=F32, name=None):
        self.ntile += 1
        name = (name or 'p') + '_%d' % self.ntile
        es, tiles = self.scopes[-1]
        t = es.enter_context(self.nc.psum_tensor(name, list(shape), dt))
        tl = Tl(t, name, psum=True, init_rd=self.pending)
        tiles.append(tl)
        return tl

    def rot(self, key, shape, dt, n=2):
        k = (key, getattr(self, 'scope_ids', [0])[-1])
        if k not in self.rots:
            self.rots[k] = [[self.sb(shape, dt, key) for _ in range(n)], 0]
        ent = self.rots[k]
        t = ent[0][ent[1] % n]
        ent[1] += 1
        return t

    def _collect(self, eng, reads, writes):
        need = {}

        def add(dep, same_ok=False):
            key, val, deng = dep
            if deng == eng and same_ok:
                return
            if need.get(key, 0) < val:
                need[key] = val

        for r in reads:
            if r.lw is not None:
                add(r.lw, same_ok=(eng == 'pe' and r.psum))
        for w in writes:
            if w.lw is not None:
                add(w.lw, same_ok=True)
            for d in w.rd:
                add(d, same_ok=True)
        out = []
        seen = self.seen[eng]
        for key, val in need.items():
            if seen.get(key, 0) >= val:
                continue
            seen[key] = val
            out.append((key, val))
        return out

    def _mark(self, reads, writes, dep):
        for w in writes:
            w.lw = dep
            w.rd = []
        for r in reads:
            if r in writes:
                continue
            r.rd.append(dep)
            if len(r.rd) > 48:
                best = {}
                for k, v, e in r.rd:
                    if best.get(k, (0, None))[0] < v:
                        best[k] = (v, e)
                r.rd = [(k, v, e) for k, (v, e) in best.items()]

    def op(self, eng, fn, reads=(), writes=()):
        reads = list({id(t): t for t in reads}.values())
        writes = list({id(t): t for t in writes}.values())
        waits = self._collect(eng, reads, writes)
        self.cnt[eng] += 1
        val = self.cnt[eng]
        self.prog[eng].append((waits, fn, (eng, 1)))
        self._mark(reads, writes, (eng, val, eng))

    def dma(self, q, fn, reads=(), writes=(), is_output=False):
        reads = list({id(t): t for t in reads}.values())
        writes = list({id(t): t for t in writes}.values())
        waits = self._collect(q, reads, writes)
        lst, idx = self.dpool[q]
        key = lst[idx % NPOOL]
        self.dpool[q][1] = idx + 1
        prev = self.cnt[key]
        if prev > 0 and self.seen[q].get(key, 0) < prev:
            self.seen[q][key] = prev
            waits.append((key, prev))
        self.cnt[key] += 16
        val = self.cnt[key]
        self.prog[q].append((waits, fn, (key, 16)))
        self._mark(reads, writes, (key, val, 'dma_' + q))
        if is_output:
            self.out_waits.append((key, val))

    def finish(self):
        need = {}
        for key, val in self.out_waits:
            if need.get(key, 0) < val:
                need[key] = val
        self.prog['sp'].append((list(need.items()), None, None))
        nc, sems, prog = self.nc, self.sems, self.prog

        def run(e, lst):
            for waits, fn, inc in lst:
                for key, val in waits:
                    e.wait_ge(sems[key], val)
                if fn is not None:
                    fn(e).then_inc(sems[inc[0]], inc[1])

        with nc.Block() as block:
            @block.tensor
            def _(e):
                run(e, prog['pe'])

            @block.vector
            def _(e):
                run(e, prog['dve'])

            @block.scalar
            def _(e):
                run(e, prog['act'])

            @block.gpsimd
            def _(e):
                run(e, prog['pool'])

            @block.sync
            def _(e):
                run(e, prog['sp'])

    def act(self, out, in_, func, bias=None, scale=None, accum=None, eng='act'):
        kw = {}
        rd = [in_.tl]
        wr = [out.tl]
        if bias is not None:
            kw['bias'] = bias.ap
            rd.append(bias.tl)
        if scale is not None:
            if isinstance(scale, V):
                kw['scale'] = scale.ap
                rd.append(scale.tl)
            else:
                kw['scale'] = float(scale)
        if accum is not None:
            kw['accum_out'] = accum.ap
            wr.append(accum.tl)
        self.op(eng, lambda e: e.activation(out=out.ap, in_=in_.ap, func=func, **kw), rd, wr)

    def tt(self, out, a, b, op, eng='dve'):
        self.op(eng, lambda e: e.tensor_tensor(out=out.ap, in0=a.ap, in1=b.ap, op=op), [a.tl, b.tl], [out.tl])

    def ts(self, out, a, s1, op0, s2=None, op1=None, eng='dve', accum=None):
        rd = [a.tl]
        wr = [out.tl]
        x1 = s1
        if isinstance(s1, V):
            rd.append(s1.tl)
            x1 = s1.ap
        x2 = s2
        if isinstance(s2, V):
            rd.append(s2.tl)
            x2 = s2.ap
        kw = {}
        if op1 is not None:
            kw['op1'] = op1
        if accum is not None:
            kw['accum_out'] = accum.ap
            wr.append(accum.tl)
        self.op(eng, lambda e: e.tensor_scalar(out=out.ap, in0=a.ap, scalar1=x1, scalar2=x2, op0=op0, **kw), rd, wr)

    def red(self, out, in_, op=ALU.add, eng='dve'):
        self.op(eng, lambda e: e.tensor_reduce(out=out.ap, in_=in_.ap, axis=AX.X, op=op), [in_.tl], [out.tl])

    def copy(self, out, in_, eng='dve'):
        if eng == 'act':
            self.act(out, in_, AF.Copy)
        else:
            self.op(eng, lambda e: e.tensor_copy(out=out.ap, in_=in_.ap), [in_.tl], [out.tl])

    def recip(self, out, in_, eng='dve'):
        self.op(eng, lambda e: e.reciprocal(out=out.ap, in_=in_.ap), [in_.tl], [out.tl])

    def memset(self, out, val, eng='pool'):
        self.op(eng, lambda e: e.memset(out.ap, val), [], [out.tl])

    def mm(self, out, lhsT, rhs, start=True, stop=True):
        self.op('pe', lambda e: e.matmul(out.ap, lhsT=lhsT.ap, rhs=rhs.ap, start=start, stop=stop),
                [lhsT.tl, rhs.tl], [out.tl])

    def tr(self, out, in_, ident):
        self.op('pe', lambda e: e.transpose(out=out.ap, in_=in_.ap, identity=ident.ap),
                [in_.tl, ident.tl], [out.tl])

    def load(self, out, src, q='sp', **kw):
        self.dma(q, lambda e: e.dma_start(out=out.ap, in_=src, **kw), [], [out.tl])

    def store(self, dst, in_, q='sp', **kw):
        self.dma(q, lambda e: e.dma_start(out=dst, in_=in_.ap, **kw), [in_.tl], [], is_output=True)


class _Stop(Exception):
    pass


def build(n_phys, stop=None):
    nc = bass.Bass("TRN2", target_bir_lowering=False)
    try:
        _build(nc, n_phys, stop)
    except _Stop:
        pass
    return nc


def _build(nc, n_phys, stop):

    def din(name, shape, dt=F32):
        return nc.dram_tensor(name, list(shape), dt, kind="ExternalInput").ap()

    def dout(name, shape):
        return nc.dram_tensor(name, list(shape), F32, kind="ExternalOutput").ap()

    xp = din('xp', [TP, D])
    xs = din('xs', [TS, D])
    cckv = din('cckv', [n_phys, 128 * 256])
    ckpe = din('ckpe', [n_phys, 128 * 32])
    cmk = din('cmk', [NSEQ, 256, 512])
    cmv = din('cmv', [NSEQ, 256, 512])
    cst = din('cst', [NSEQ * 2, DFF])
    ptab = din('ptab', [NSEQ * NPG, 1], I32)
    memp = din('memp', [256, D])
    W = {}
    for nm, shp in [('g_mix', [D]), ('w_in', [D, NIN]), ('ln_v_g', [512]), ('ln_v_b', [512]), ('w_s', [8, 128, 128]),
                    ('b_s', [8, 128]), ('g_q_a', [256]), ('w_uq', [256, 768]), ('g_kv_a', [256]), ('w_uk', [256, 512]),
                    ('w_uv', [256, 512]), ('g_qk_q', [96]), ('g_qk_k', [96]), ('g_out_a', [512]), ('g_out_b', [512]),
                    ('w_o', [D, D]), ('g_mem_x', [D]), ('g_mem_in', [D]), ('w_mq', [D, 512]), ('w_mk', [D, 512]),
                    ('w_mv', [D, 512]), ('g_mq', [128]), ('g_mk', [128]), ('w_mo', [512, D]), ('g_ffn', [D]),
                    ('w_up', [D, 2 * DFF]), ('w_conv', [3, DFF]), ('b_conv', [DFF]), ('w_down', [DFF, D])]:
        W[nm] = din(nm, shp)
    c_ident = din('c_ident', [128, 128])
    c_tri = din('c_tri', [128, 128])
    c_bd = din('c_bd', [64, 64])
    c_cosp = din('c_cosp', [128, NT * 16])
    c_sinp = din('c_sinp', [128, NT * 16])
    c_coss = din('c_coss', [64, 16])
    c_sins = din('c_sins', [64, 16])
    c_cosk = din('c_cosk', [128, 128 * 16])
    c_sink = din('c_sink', [128, 128 * 16])

    y_p = dout('y_p', [TP, D])
    y_s = dout('y_s', [TS, D])
    o_pckv = dout('o_pckv', [TP, 256])
    o_pkpe = dout('o_pkpe', [TP, 32])
    o_pmk = dout('o_pmk', [256, 512])
    o_pmv = dout('o_pmv', [256, 512])
    o_pconv = dout('o_pconv', [2, DFF])
    o_sckv = dout('o_sckv', [TS, 256])
    o_skpe = dout('o_skpe', [TS, 32])
    o_scv = dout('o_scv', [TS, 512])
    o_sconv = dout('o_sconv', [NSEQ * 2, DFF])

    SC_MLA = 96.0 ** -0.5
    SC_MEM = 128.0 ** -0.5

    with ExitStack() as es:
        S = Sched(nc, es)

        def ckpt(name):
            if stop == name:
                while len(S.scopes) > 1:
                    S.pop()
                S.finish()
                raise _Stop()
        PS = [S.ps([128, 512], F32, 'bank%d' % i) for i in range(8)]

        ident = S.sb([128, 128], F32, 'ident')
        S.load(ident.v, c_ident)
        identb = S.sb([128, 128], BF16, 'identb')
        S.copy(identb.v, ident.v, eng='pool')
        tri = S.sb([128, 128], F32, 'tri')
        S.load(tri.v, c_tri)
        trib = S.sb([128, 128], BF16, 'trib')
        S.copy(trib.v, tri.v, eng='pool')
        bd = S.sb([64, 64], F32, 'bd')
        S.load(bd.v, c_bd)
        onesb = S.sb([128, 128], BF16, 'onesb')
        S.memset(onesb.v, 1.0)
        epsT = S.sb([128, 1], F32, 'eps')
        S.memset(epsT.v, EPS)
        cosp = S.sb([128, NT, 16], F32, 'cosp')
        sinp = S.sb([128, NT, 16], F32, 'sinp')
        S.load(cosp.v, c_cosp.rearrange("p (t f) -> p t f", f=16))
        S.load(sinp.v, c_sinp.rearrange("p (t f) -> p t f", f=16))
        coss = S.sb([64, 16], F32, 'coss')
        sins = S.sb([64, 16], F32, 'sins')
        S.load(coss.v, c_coss)
        S.load(sins.v, c_sins)

        def bcast_vec(name, n):
            t = S.sb([128, n], F32, 'bc_' + name)
            S.load(t.v, W[name].partition_broadcast(128))
            return t

        def col_vec(name, n):
            t = S.sb([128, n // 128], F32, 'col_' + name)
            S.load(t.v, W[name].rearrange("(c p) -> p c", p=128), allow_slow_non_contiguous=True)
            return t

        g_kv_bc = bcast_vec('g_kv_a', 256)
        g_qq_bc = bcast_vec('g_qk_q', 96)
        g_qk_bc = bcast_vec('g_qk_k', 96)
        lng_bc = bcast_vec('ln_v_g', 512)
        lnb_bc = bcast_vec('ln_v_b', 512)
        g_mq_bc = bcast_vec('g_mq', 128)
        g_mk_bc = bcast_vec('g_mk', 128)

        def bound(ga, gb, n, sc, name):
            ma = S.sb([128, 1], F32, name + 'a')
            mb = S.sb([128, 1], F32, name + 'b')
            S.op('dve', lambda e: e.tensor_reduce(out=ma.v.ap, in_=ga.v.ap, axis=AX.X, op=ALU.max,
                                                  apply_absolute_value=True), [ga], [ma])
            S.op('dve', lambda e: e.tensor_reduce(out=mb.v.ap, in_=gb.v.ap, axis=AX.X, op=ALU.max,
                                                  apply_absolute_value=True), [gb], [mb])
            c = S.sb([128, 1], F32, name)
            S.tt(c.v, ma.v, mb.v, ALU.mult)
            S.ts(c.v, c.v, -float(n) * sc, ALU.mult)
            return c
        negC = bound(g_qq_bc, g_qk_bc, 96, SC_MLA, 'negC')
        negCm = bound(g_mq_bc, g_mk_bc, 128, SC_MEM, 'negCm')

        def load_w(name, K, N, n0=0, tname=None):
            kc = K // 128
            t = S.sb([128, kc, N], BF16, tname or ('w_' + name))
            src = W[name].rearrange("(c p) n -> p c n", p=128)
            for c in range(kc):
                for a in range(0, N, 1024):
                    b = min(N, a + 1024)
                    S.load(t[:, c, a:b], src[:, c, n0 + a:n0 + b], q='pool')
            return t

        def junk_tile(P, n):
            j = S.rot('junk', [128, 1024], BF16, 1)
            return j[0:P, 0:n]

        def rinv_of(src, n, P):
            ss = S.rot('ss', [128, 1], F32, 2)
            S.act(junk_tile(P, n), src, AF.Square, accum=ss[0:P, :])
            rt = S.rot('rt', [128, 1], F32, 2)
            S.act(rt[0:P, :], ss[0:P, :], AF.Ln, bias=epsT[0:P, :], scale=1.0 / n)
            ri = S.rot('ri', [128, 1], F32, 2)
            S.act(ri[0:P, :], rt[0:P, :], AF.Exp, scale=-0.5)
            return ri

        def group_rinv(src3, G, Dg, P):
            sq = S.rot('gsq%d' % (G * Dg), [128, G, Dg], F32, 1)
            S.tt(sq[0:P], src3, src3, ALU.mult, eng='pool')
            ss = S.rot('gss%d' % G, [128, G], F32, 2)
            S.red(ss[0:P], sq[0:P])
            rt = S.rot('grt%d' % G, [128, G], F32, 2)
            S.act(rt[0:P], ss[0:P], AF.Ln, bias=epsT[0:P, :], scale=1.0 / Dg)
            ri = S.rot('gri%d' % G, [128, G], F32, 2)
            S.act(ri[0:P], rt[0:P], AF.Exp, scale=-0.5)
            return ri

        def transposes(dst, srcs, P, bank, scale=None):
            for i0 in range(0, len(srcs), 4):
                grp = srcs[i0:i0 + 4]
                for j, s in enumerate(grp):
                    w = s.ap.shape[-1]
                    S.tr(bank[0:w, j * 128:j * 128 + P], s, ident[0:P, 0:P])
                for j, s in enumerate(grp):
                    w = s.ap.shape[-1]
                    if scale is None:
                        S.copy(dst(i0 + j), bank[0:w, j * 128:j * 128 + P], eng='act')
                    else:
                        S.act(dst(i0 + j), bank[0:w, j * 128:j * 128 + P], AF.Copy, scale=scale(i0 + j))

        def rope(dst1, dst2, x1, x2, cs, sn, P):
            H = x1.ap.shape[1]
            cb = cs.un(1).bc([P, H, 16])
            sb_ = sn.un(1).bc([P, H, 16])
            t1 = S.rot('rp1', [128, H, 16], F32, 1)
            t2 = S.rot('rp2', [128, H, 16], F32, 1)
            S.tt(t1[0:P], x1, cb, ALU.mult)
            S.tt(t2[0:P], x2, sb_, ALU.mult, eng='pool')
            S.tt(dst1, t1[0:P], t2[0:P], ALU.subtract)
            t3 = S.rot('rp3', [128, H, 16], F32, 1)
            t4 = S.rot('rp4', [128, H, 16], F32, 1)
            S.tt(t3[0:P], x2, cb, ALU.mult)
            S.tt(t4[0:P], x1, sb_, ALU.mult, eng='pool')
            S.tt(dst2, t3[0:P], t4[0:P], ALU.add)

        arena = S.sb([128, 17 * 1024], F32, 'arena')

        def alias(lo, hi, parts, dt, pattern=None, **kw):
            ap = arena.t[0:parts, lo:hi]
            if dt == BF16:
                ap = ap.bitcast(BF16)
            if pattern:
                ap = ap.rearrange(pattern, **kw)
            return Tl(ap, 'alias_%d' % lo)
        kT = alias(0, 8192, 96, BF16, "p (h t) -> p h t", h=8)
        Vaug = alias(8192, 12416, 128, BF16, "p (t h c) -> p t h c", t=NT, h=8)
        qT = alias(12416, 14464, 96, BF16, "p (h t) -> p h t", h=8)
        b_out = alias(14464, 16512, 128, F32, "p (t c) -> p t c", t=4)
        arena_alias = [kT, Vaug, qT, b_out]

        mkT = S.sb([128, 4, 256], BF16, 'mkT')
        mvb = S.sb([128, 2, 512], BF16, 'mvb')

        g_oa_col = col_vec('g_out_a', 512)
        g_ob_col = col_vec('g_out_b', 512)

        S.push()
        aT = S.sb([128, 4, TP + TS], BF16, 'aT')
        bT = S.sb([128, 4, TP + TS], BF16, 'bT')

        S.push()
        w_uk = load_w('w_uk', 256, 512)
        w_uv = load_w('w_uv', 256, 512)
        qnT_s = S.sb([128, 4, TS], BF16, 'qnT_s')
        qpeT_s = S.sb([32, 8, TS], BF16, 'qpeT_s')
        kT_s = S.sb([96, 8, TS], BF16, 'kT_s')
        qT_s = S.sb([96, 8, TS], BF16, 'qT_s')
        Cb_new = S.sb([64, 264], BF16, 'Cb_new')
        S.memset(Cb_new.v, 1.0)

        S.push()
        g_mix_col = col_vec('g_mix', D)
        g_qa_col = col_vec('g_q_a', 256)
        w_in = load_w('w_in', D, NIN)
        w_uq = load_w('w_uq', 256, 768)
        WsT = S.sb([128, 8, 128], BF16, 'WsT')
        WsT_s = S.sb([64, 8, 64], BF16, 'WsT_s')
        bsT = S.sb([128, 8], F32, 'bsT')
        S.load(bsT.v, W['b_s'].rearrange("g t -> t g"), allow_slow_non_contiguous=True)
        bsT_s = S.sb([64, 8], F32, 'bsT_s')
        for sq in range(NSEQ):
            S.dma('act', (lambda sq: lambda e: e.dma_start(out=bsT_s.t[sq * 4:(sq + 1) * 4, :],
                                                          in_=W['b_s'][:, 0:4].rearrange("g t -> t g"),
                                                          allow_slow_non_contiguous=True))(sq), [], [bsT_s])
        S.push()
        wsf = S.sb([128, 8, 128], F32, 'wsf')
        S.load(wsf.v, W['w_s'].rearrange("g t s -> t g s"))
        for g0 in range(0, 8, 4):
            for j in range(4):
                S.tr(PS[0][:, j * 128:(j + 1) * 128], wsf[:, g0 + j, :], ident.v)
            S.tt(WsT[:, g0:g0 + 4, :], PS[0].v.r("p (j t) -> p j t", j=4), tri.v.un(1).bc([128, 4, 128]), ALU.mult)
        wsf_s = S.sb([64, 8, 64], F32, 'wsf_s')
        S.memset(wsf_s.v.r("p g s -> p (g s)"), 0.0)
        for sq in range(NSEQ):
            S.dma('act', (lambda sq: lambda e: e.dma_start(
                out=wsf_s.t[sq * 4:(sq + 1) * 4, :, sq * 4:(sq + 1) * 4],
                in_=W['w_s'][:, 0:4, 0:4].rearrange("g t s -> t g s")))(sq), [], [wsf_s])
        for g0 in range(0, 8, 4):
            for j in range(4):
                S.tr(PS[1][0:64, j * 128:j * 128 + 64], wsf_s[:, g0 + j, :], ident[0:64, 0:64])
            S.tt(WsT_s[:, g0:g0 + 4, :], PS[1][0:64, :].r("p (j t) -> p j t", j=4)[:, :, 0:64],
                 bd.v.un(1).bc([64, 4, 64]), ALU.mult)
        S.pop()
        S.memset(Vaug.v.r("p t h c -> p (t h c)"), 1.0)

        ckpt('c1')

        def phase1a(ti, P, xsrc, is_sample):
            tok0 = ti * 128
            x_t = S.rot('x_t', [128, D], F32, 2)
            S.load(x_t[0:P, :], xsrc)
            ri = rinv_of(x_t[0:P, :], D, P)
            S.ts(x_t[0:P, :], x_t[0:P, :], ri[0:P, 0:1], ALU.mult, eng='pool')
            xT = S.rot('xT', [128, 8, 128], BF16, 1)
            transposes(lambda i: xT[:, i, 0:P], [x_t[0:P, i * 128:(i + 1) * 128] for i in range(8)], P, PS[0],
                       scale=lambda i: g_mix_col[:, i:i + 1])
            zb = [PS[1], PS[2], PS[3], PS[4]]
            for n in range(4):
                n0 = n * 512
                n1 = min(NIN, n0 + 512)
                for k in range(8):
                    S.mm(zb[n][0:P, 0:n1 - n0], xT[:, k, 0:P], w_in[:, k, n0:n1], start=(k == 0), stop=(k == 7))
            gu = S.rot('gu', [128, 512], F32, 2)
            S.act(gu[0:P], zb[0][0:P, :], AF.Gelu_apprx_tanh)
            gv = S.rot('gv', [128, 8, 64], F32, 2)
            S.act(gv[0:P].r("p g d -> p (g d)"), zb[1][0:P, :], AF.Gelu_apprx_tanh)
            c3 = zb[2]
            cq = S.rot('cq', [128, 256], F32, 2)
            riq = rinv_of(c3[0:P, 0:256], 256, P)
            S.act(cq[0:P], c3[0:P, 0:256], AF.Copy, scale=riq[0:P, 0:1])
            ckn = S.rot('ckn', [128, 256], F32, 2)
            rik = rinv_of(c3[0:P, 256:512], 256, P)
            S.act(ckn[0:P], c3[0:P, 256:512], AF.Copy, scale=rik[0:P, 0:1])
            kpe = S.rot('kpe', [128, 32], F32, 2)
            S.copy(kpe[0:P], zb[3][0:P, 0:32], eng='act')
            return dict(gu=gu, gv=gv, cq=cq, ckn=ckn, kpe=kpe)

        def phase1b(ti, P, is_sample, qcol, st):
            tok0 = ti * 128
            gu, gv, cq, ckn, kpe = st['gu'], st['gv'], st['cq'], st['ckn'], st['kpe']
            s1 = S.rot('s1', [128, 8], F32, 2)
            S.red(s1[0:P], gv[0:P])
            S.ts(s1[0:P], s1[0:P], -1.0 / 64, ALU.mult)
            cen = S.rot('cen', [128, 8, 64], F32, 1)
            S.tt(cen[0:P], gv[0:P], s1[0:P].un(2).bc([P, 8, 64]), ALU.add)
            sq = S.rot('lsq', [128, 8, 64], F32, 1)
            S.tt(sq[0:P], cen[0:P], cen[0:P], ALU.mult, eng='pool')
            var = S.rot('var', [128, 8], F32, 2)
            S.red(var[0:P], sq[0:P])
            S.act(var[0:P], var[0:P], AF.Ln, bias=epsT[0:P, :], scale=1.0 / 64)
            S.act(var[0:P], var[0:P], AF.Exp, scale=-0.5)
            S.tt(cen[0:P], cen[0:P], var[0:P].un(2).bc([P, 8, 64]), ALU.mult)
            S.tt(cen[0:P], cen[0:P], lng_bc[0:P].r("p (g d) -> p g d", g=8), ALU.mult, eng='pool')
            vg = S.rot('vg', [128, 8, 64], BF16, 1)
            if is_sample:
                S.tt(sq[0:P], cen[0:P], lnb_bc[0:P].r("p (g d) -> p g d", g=8), ALU.add)
                S.store(o_scv, sq[0:P].r("p g d -> p (g d)"))
                S.copy(vg[0:P], sq[0:P], eng='pool')
            else:
                S.tt(vg[0:P], cen[0:P], lnb_bc[0:P].r("p (g d) -> p g d", g=8), ALU.add)
            sp_ps = PS[5]
            for g in range(8):
                lw = WsT_s[:, g, :] if is_sample else WsT[:, g, :]
                S.mm(sp_ps[0:P, g * 64:(g + 1) * 64], lw, vg[0:P, g, :])
            bt = bsT_s if is_sample else bsT
            S.tt(cen[0:P], sp_ps[0:P, :].r("p (g d) -> p g d", g=8), bt[0:P].un(2).bc([P, 8, 64]), ALU.add)
            a_o = gv
            S.tt(a_o[0:P].r("p g d -> p (g d)"), cen[0:P].r("p g d -> p (g d)"), gu[0:P], ALU.mult)
            a_f = a_o[0:P].r("p g d -> p (g d)")
            ria = rinv_of(a_f, 512, P)
            S.ts(a_f, a_f, ria[0:P, 0:1], ALU.mult)
            acol = TP if is_sample else tok0
            transposes(lambda i: aT[:, i, acol:acol + P], [a_f[:, i * 128:(i + 1) * 128] for i in range(4)], P, PS[6],
                       scale=lambda i: g_oa_col[:, i:i + 1])
            cqT = S.rot('cqT', [128, 2, 128], BF16, 1)
            transposes(lambda i: cqT[:, i, 0:P], [cq[0:P, i * 128:(i + 1) * 128] for i in range(2)], P, PS[6],
                       scale=lambda i: g_qa_col[:, i:i + 1])
            S.tt(ckn[0:P], ckn[0:P], g_kv_bc[0:P], ALU.mult)
            if is_sample:
                S.store(o_sckv, ckn[0:P])
                S.store(o_skpe, kpe[0:P])
                S.copy(Cb_new[:, 0:256], ckn[0:P], eng='pool')
            else:
                S.store(o_pckv[tok0:tok0 + P, :], ckn[0:P])
                S.store(o_pkpe[tok0:tok0 + P, :], kpe[0:P])
            ckT = S.rot('ckT', [128, 2, 128], BF16, 1)
            transposes(lambda i: ckT[:, i, 0:P], [ckn[0:P, i * 128:(i + 1) * 128] for i in range(2)], P, PS[6])
            q_ps0, q_ps1 = PS[7], PS[5]
            for k in range(2):
                S.mm(q_ps0[0:P, :], cqT[:, k, 0:P], w_uq[:, k, 0:512], start=(k == 0), stop=(k == 1))
            q_sb = S.rot('q_sb', [128, 8, 96], F32, 1)
            S.copy(q_sb[0:P].r("p h d -> p (h d)")[:, 0:512], q_ps0[0:P, :], eng='act')
            for k in range(2):
                S.mm(q_ps1[0:P, 0:256], cqT[:, k, 0:P], w_uq[:, k, 512:768], start=(k == 0), stop=(k == 1))
            S.copy(q_sb[0:P].r("p h d -> p (h d)")[:, 512:768], q_ps1[0:P, 0:256], eng='act')
            kn_ps, v_ps = PS[7], PS[5]
            for k in range(2):
                S.mm(kn_ps[0:P, :], ckT[:, k, 0:P], w_uk[:, k, :], start=(k == 0), stop=(k == 1))
            k_sb = S.rot('k_sb', [128, 8, 96], F32, 1)
            S.copy(k_sb[0:P, :, 0:64], kn_ps[0:P, :].r("p (h d) -> p h d", h=8), eng='act')
            S.copy(k_sb[0:P, :, 64:96], kpe[0:P].un(1).bc([P, 8, 32]), eng='pool')
            for k in range(2):
                S.mm(v_ps[0:P, :], ckT[:, k, 0:P], w_uv[:, k, :], start=(k == 0), stop=(k == 1))
            if not is_sample:
                S.copy(Vaug[0:P, ti, :, 0:64], v_ps[0:P, :].r("p (h d) -> p h d", h=8), eng='act')
            cs = coss.v if is_sample else cosp[:, ti, :]
            sn = sins.v if is_sample else sinp[:, ti, :]
            for nm, src, gbc in (('q', q_sb, g_qq_bc), ('k', k_sb, g_qk_bc)):
                rg = group_rinv(src[0:P], 8, 96, P)
                S.tt(src[0:P], src[0:P], rg[0:P].un(2).bc([P, 8, 96]), ALU.mult)
                S.tt(src[0:P], src[0:P], gbc[0:P].un(1).bc([P, 8, 96]), ALU.mult, eng='pool')
                fin = S.rot('fin', [128, 8, 96], F32, 1)
                S.copy(fin[0:P, :, 0:64], src[0:P, :, 0:64], eng='pool')
                rope(fin[0:P, :, 64:80], fin[0:P, :, 80:96], src[0:P, :, 64:80], src[0:P, :, 80:96], cs[0:P], sn[0:P], P)
                if is_sample:
                    dstT = qT_s if nm == 'q' else kT_s
                    transposes(lambda h: dstT[:, h, 0:P], [fin[0:P, h, :] for h in range(8)], P, PS[6])
                    if nm == 'q':
                        qg = S.rot('lsq', [128, 8, 64], F32, 1)
                        S.tt(qg[0:P], fin[0:P, :, 0:64], g_qk_bc[0:P, 0:64].un(1).bc([P, 8, 64]), ALU.mult)
                        transposes(lambda j: qnT_s[:, j, 0:P],
                                   [qg[0:P, 2 * j:2 * j + 2, :].r("p h d -> p (h d)") for j in range(4)], P, PS[6])
                        qpe = S.rot('cen', [128, 8, 64], F32, 1)
                        S.copy(qpe[0:P, :, 0:32], fin[0:P, :, 64:96], eng='pool')
                        transposes(lambda h: qpeT_s[:, h, 0:P], [qpe[0:P, h, 0:32] for h in range(8)], P, PS[6])
                else:
                    if nm == 'q':
                        transposes(lambda h: qT[:, h, qcol:qcol + P], [fin[0:P, h, :] for h in range(8)], P, PS[6])
                    else:
                        transposes(lambda h: kT[:, h, tok0:tok0 + P], [fin[0:P, h, :] for h in range(8)], P, PS[6])

        att_cnt = [0]

        def attention(qb):
            for h in range(8):
                ot = PS[4 + (att_cnt[0] % 2)]
                att_cnt[0] += 1
                nkb = 4 * qb + 4

                def s_stage(kb):
                    c0 = max(0, kb - 4 * qb) * 128
                    st = PS[kb % 2]
                    S.mm(st[:, c0:512], kT[:, h, kb * 128:(kb + 1) * 128], qT[:, h, c0:512])
                    pt = S.rot('pt', [128, 512], BF16, 3)
                    S.act(pt[:, c0:512], st[:, c0:512], AF.Exp, bias=negC[:, 0:1], scale=SC_MLA)
                    if kb >= 4 * qb:
                        S.tt(pt[:, c0:c0 + 128], pt[:, c0:c0 + 128], trib.v, ALU.mult, eng='pool')
                    return pt, c0
                nxt = s_stage(0)
                for kb in range(nkb):
                    pt, c0 = nxt
                    if kb + 1 < nkb:
                        nxt = s_stage(kb + 1)
                    S.mm(ot[0:65, c0:512], Vaug[:, kb, h, 0:65], pt[:, c0:512], start=(kb == 0), stop=(kb == nkb - 1))
                ot_sb = S.rot('ot_sb', [65, 512], F32, 2)
                S.copy(ot_sb.v, ot[0:65, :], eng='act')
                tp = PS[6 + (att_cnt[0] % 2)]
                for j in range(4):
                    S.tr(tp[:, j * 128:j * 128 + 65], ot_sb[:, j * 128:(j + 1) * 128], ident[0:65, 0:65])
                tpv = tp.v.r("p (j c) -> p j c", j=4)
                rd = S.rot('rd', [128, 4, 1], F32, 2)
                S.recip(rd.v, tpv[:, :, 64:65])
                S.tt(b_out[:, :, h * 64:(h + 1) * 64], tpv[:, :, 0:64], rd.v.bc([128, 4, 64]), ALU.mult)
            for t in range(4):
                ti = 4 * qb + t
                rib = rinv_of(b_out[:, t, :], 512, 128)
                S.ts(b_out[:, t, :], b_out[:, t, :], rib[:, 0:1], ALU.mult, eng='pool')
                transposes(lambda i: bT[:, i, ti * 128:(ti + 1) * 128], [b_out[:, t, i * 128:(i + 1) * 128] for i in range(4)],
                           128, PS[2 + t % 2], scale=lambda i: g_ob_col[:, i:i + 1])

        tiles = [(ti, 128, xp[ti * 128:(ti + 1) * 128, :], False, (ti % 4) * 128) for ti in range(NT)]
        tiles.append((0, TS, xs[:, :], True, 0))
        nxt_st = phase1a(*tiles[0][0:4])
        for i, (ti, P_, src_, smp, qcol) in enumerate(tiles):
            st_cur = nxt_st
            if i + 1 < len(tiles):
                nxt_st = phase1a(*tiles[i + 1][0:4])
            phase1b(ti, P_, smp, qcol, st_cur)
            ckpt('c2')
            if not smp and ti % 4 == 3 and ti < NT - 1:
                attention(ti // 4)
                ckpt('c3')
        attention(3)
        S.pop()
        ckpt('c4')

        S.push()
        wukT = S.sb([128, 4, 256], BF16, 'wukT')
        S.push()
        wukf = S.sb([128, 2, 512], F32, 'wukf')
        S.load(wukf.v, W['w_uk'].rearrange("(c p) n -> p c n", p=128))
        for j in range(4):
            for cc in range(2):
                S.tr(PS[0][:, cc * 128:(cc + 1) * 128], wukf[:, cc, j * 128:(j + 1) * 128], ident.v)
            S.copy(wukT[:, j, :], PS[0][:, 0:256], eng='act')
        S.pop()
        ckpt('c40')
        qlatT = S.sb([128, 2, 8, TS], BF16, 'qlatT')
        for h in range(8):
            j, a = h // 2, h % 2
            for cc in range(2):
                col = (j * 2 + cc) * 64
                S.mm(PS[1 + a][:, col:col + TS],
                     wukT[a * 64:(a + 1) * 64, j, cc * 128:(cc + 1) * 128], qnT_s[a * 64:(a + 1) * 64, j, :])
        for a in range(2):
            S.copy(qlatT.v.r("p c (j a) t -> p a j c t", a=2)[:, a],
                   PS[1 + a].v.r("p (j c t) -> p j c t", j=4, c=2), eng='act')
        ckpt('c41')
        ptt = S.sb([128, NSEQ // 2], I32, 'ptt')
        S.load(ptt.v, ptab.rearrange("(g p) o -> p (g o)", p=128), allow_slow_non_contiguous=True)
        ptf = S.sb([128, NSEQ // 2], F32, 'ptf')
        S.copy(ptf.v, ptt.v)
        io16 = S.sb([128, 16], F32, 'io16')
        S.op('pool', lambda e: e.iota(io16.v.ap, pattern=[[1, 16]], base=0, channel_multiplier=0,
                                      allow_small_or_imprecise_dtypes=True), [], [io16])
        idxf = S.sb([128, NSEQ // 2, 16], F32, 'idxf')
        S.ts(idxf.v, ptf.v.un(2).bc([128, NSEQ // 2, 16]), 16.0, ALU.mult)
        idxc = S.sb([128, NSEQ // 2, 16], I32, 'idxc')
        S.tt(idxc.v, idxf.v, io16.v.un(1).bc([128, NSEQ // 2, 16]), ALU.add)
        ckpt('c42')
        cckv16 = cckv.rearrange("n (j x) -> (n j) x", j=16)
        ckpe2 = ckpe.rearrange("n (j x) -> (n j) x", j=2)
        idx2 = S.sb([128, NSEQ // 2, 2], I32, 'idx2')
        S.ts(idxf[:, :, 0:2], ptf.v.un(2).bc([128, NSEQ // 2, 2]), 2.0, ALU.mult)
        S.tt(idx2.v, idxf[:, :, 0:2], io16[:, 0:2].un(1).bc([128, NSEQ // 2, 2]), ALU.add)
        cosk = S.sb([128, 128, 16], F32, 'cosk')
        sink = S.sb([128, 128, 16], F32, 'sink')
        S.load(cosk.v, c_cosk.rearrange("p (r f) -> p r f", f=16))
        S.load(sink.v, c_sink.rearrange("p (r f) -> p r f", f=16))
        gpe_bc = g_qk_bc[:, 64:96]
        b_s_T = S.sb([128, 4, TS], F32, 'b_s_T')
        RCH = 8
        ckpt('c4a')
        STB = [PS[4], PS[5]]
        ptb_bufs = [S.sb([128, 4, 64], BF16, 'ptb') for _ in range(4)]
        for pb_ in ptb_bufs:
            S.memset(pb_.v.r("p g c -> p (g c)"), 0.0)
        ptb_ctr = [0]
        onesf = S.sb([128, 1], F32, 'onesf')
        S.memset(onesf.v, 1.0)
        CT3 = [PS[2], PS[3], PS[7]]
        for pr in range(NSEQ // 2):
            idx = ptt[:, pr:pr + 1]
            sspe = S.rot('sspe', [128, 128], F32, 1)
            kr = S.rot('kr', [128, 128, 32], BF16, 1)
            for hf in range(2):
                rs = slice(hf * 64, (hf + 1) * 64)
                KP = S.rot('KP', [128, 64, 32], F32, 1)
                ix2 = idx2[:, pr, hf:hf + 1]
                S.dma('pool', (lambda KP, ix2: lambda e: e.indirect_dma_start(
                    out=KP.v.ap.rearrange("p r f -> p (r f)"), out_offset=None, in_=ckpe2,
                    in_offset=bass.IndirectOffsetOnAxis(ap=ix2.ap, axis=0)))(KP, ix2), [idx2], [KP])
                ksq = S.rot('ksq', [128, 64, 32], F32, 1)
                S.tt(ksq.v, KP.v, KP.v, ALU.mult, eng='pool')
                S.red(sspe[:, rs], ksq.v)
                S.tt(ksq.v, KP.v, gpe_bc.un(1).bc([128, 64, 32]), ALU.mult, eng='pool')
                t1 = S.rot('kt1', [128, 64, 16], F32, 1)
                t2 = S.rot('kt2', [128, 64, 16], F32, 1)
                S.tt(t1.v, ksq[:, :, 0:16], cosk[:, rs, :], ALU.mult)
                S.tt(t2.v, ksq[:, :, 16:32], sink[:, rs, :], ALU.mult, eng='pool')
                S.tt(kr[:, rs, 0:16], t1.v, t2.v, ALU.subtract)
                S.tt(t1.v, ksq[:, :, 16:32], cosk[:, rs, :], ALU.mult)
                S.tt(t2.v, ksq[:, :, 0:16], sink[:, rs, :], ALU.mult, eng='pool')
                S.tt(kr[:, rs, 16:32], t1.v, t2.v, ALU.add)
            oacc = PS[6]
            ckpt('c4b')
            NCH = 128 // RCH
            Cbs, cTs, ssns, ptbs = {}, {}, {}, {}
            state = {'first': True}

            def load_chunk(ch):
                Cb = S.rot('Cb', [128, RCH, 256], BF16, 3)
                ixc = idxc[:, pr, ch:ch + 1]
                S.dma('pool', (lambda Cb, ixc: lambda e: e.indirect_dma_start(
                    out=Cb.v.ap.rearrange("p r c -> p (r c)"), out_offset=None, in_=cckv16,
                    in_offset=bass.IndirectOffsetOnAxis(ap=ixc.ap, axis=0)))(Cb, ixc), [idxc], [Cb])
                Cbs[ch] = Cb

            def st_T(r):
                ch, rl = divmod(r, RCH)
                Cb = Cbs[ch]
                ctp = CT3[r % 3]
                for cc in range(2):
                    S.mm(ctp[:, cc * 128:(cc + 1) * 128], Cb[:, rl, cc * 128:(cc + 1) * 128], identb.v)
                S.mm(ctp[0:32, 256:384], kr[:, r, :], identb.v)
                cT = S.rot('cT', [128, 384], BF16, 4)
                S.copy(cT[:, 0:256], ctp[:, 0:256], eng='dve')
                S.copy(cT[0:32, 256:384], ctp[0:32, 256:384], eng='act')
                cTs[r] = cT

            def st_K(r):
                cT = cTs.pop(r)
                grp, g = divmod(r, 4)
                if g == 0:
                    ssns[grp] = S.rot('ssn', [128, 4, 8], F32, 3)
                knp = PS[r % 2]
                for cc in range(2):
                    S.mm(knp.v, cT[:, cc * 128:(cc + 1) * 128], w_uk[:, cc, :], start=(cc == 0), stop=(cc == 1))
                sqk = S.rot('sqk', [128, 8, 64], BF16, 3)
                S.act(sqk.v.r("p h d -> p (h d)"), knp.v, AF.Square)
                S.red(ssns[grp][:, g, :], sqk.v)
                stb = STB[grp % 2]
                for cc in range(2):
                    S.mm(stb[:, g * 64:(g + 1) * 64], cT[:, cc * 128:(cc + 1) * 128],
                         qlatT[:, cc, :, pr * 8:pr * 8 + 8].r("p h (a q) -> p a h q", a=2),
                         start=(cc == 0), stop=False)
                S.mm(stb[:, g * 64:(g + 1) * 64], cT[0:32, 256:384],
                     qpeT_s[:, :, pr * 8:pr * 8 + 8].r("p h (a q) -> p a h q", a=2), start=False, stop=True)

            def st_G(grp):
                r0 = grp * 4
                stb = STB[grp % 2]
                tot = S.rot('tot', [128, 4, 8], F32, 2)
                S.tt(tot.v, ssns.pop(grp).v, sspe[:, r0:r0 + 4].un(2).bc([128, 4, 8]), ALU.add)
                S.act(tot.v, tot.v, AF.Ln, bias=epsT[:, 0:1], scale=1.0 / 96)
                S.act(tot.v, tot.v, AF.Exp, scale=-0.5)
                snm = S.rot('snm', [128, 4, 8, 4], F32, 2)
                ptb = ptb_bufs[ptb_ctr[0] % 4]
                ptb_ctr[0] += 1
                for a in range(2):
                    pa = slice(a * 64, (a + 1) * 64)
                    S.tt(snm[pa], stb[pa, 0:256].r("p (g a h q) -> p g a h q", g=4, a=2, h=8)[:, :, a, :, :],
                         tot[pa].un(3).bc([64, 4, 8, 4]), ALU.mult)
                    S.act(ptb[pa].r("p g (a h q) -> p g a h q", a=2, h=8)[:, :, a, :, :], snm[pa], AF.Exp,
                          bias=negC[pa, 0:1], scale=SC_MLA)
                ptbs[grp] = ptb

            def st_PV(grp):
                ptb = ptbs.pop(grp)
                for g in range(4):
                    ch, rl = divmod(grp * 4 + g, RCH)
                    Cb = Cbs[ch]
                    S.mm(oacc[0:64, 0:256], ptb[:, g, :], Cb[:, rl, :], start=state['first'], stop=False)
                    state['first'] = False
                    S.op('pe', (lambda o_, l_, r_: lambda e: e.matmul(o_.ap, lhsT=l_.ap, rhs=r_.ap, start=False,
                                                                      stop=False, skip_group_check=True))(
                        oacc[0:64, 256:257], ptb[:, g, :], onesb[:, 0:1]), [ptb, onesb], [oacc])

            load_chunk(0)
            load_chunk(1)
            for r in range(128 + 2):
                if r < 128:
                    st_T(r)
                if r >= 2:
                    rr = r - 2
                    st_K(rr)
                    if rr % 4 == 3:
                        grp = rr // 4
                        st_G(grp)
                        if grp >= 2:
                            st_PV(grp - 2)
                            if (grp - 2) % (RCH // 4) == (RCH // 4) - 1:
                                nxtc = (grp - 2) // (RCH // 4) + 3
                                if nxtc < NCH:
                                    load_chunk(nxtc)
                        if grp == 0 and 2 < NCH:
                            load_chunk(2)
            st_PV(30)
            st_PV(31)
            tk8 = slice(pr * 8, pr * 8 + 8)
            for a in range(2):
                sq_ = pr * 2 + a
                tk = slice(sq_ * 4, sq_ * 4 + 4)
                snew = PS[2]
                for h in range(8):
                    S.mm(snew[0:64, h * 4:(h + 1) * 4], kT_s[:, h, :], qT_s[:, h, tk])
                pn = S.rot('pn', [64, 8, 4], BF16, 2)
                S.act(pn.v.r("p h q -> p (h q)"), snew[0:64, 0:32], AF.Exp, bias=negC[0:64, 0:1], scale=SC_MLA)
                S.tt(pn.v, pn.v, bd[:, tk].un(1).bc([64, 8, 4]), ALU.mult)
                S.mm(oacc[a * 32:(a + 1) * 32, 0:257], pn.v.r("p h q -> p (h q)"), Cb_new[:, 0:257], start=False, stop=(a == 1))
            ol = S.rot('ol', [64, 264], F32, 2)
            S.copy(ol[:, 0:257], oacc[0:64, 0:257], eng='act')
            rl_ = S.rot('rl_', [64, 1], F32, 2)
            S.recip(rl_.v, ol[:, 256:257])
            S.ts(ol[:, 0:256], ol[:, 0:256], rl_[:, 0:1], ALU.mult)
            olT = S.rot('olT', [128, 2, 64], BF16, 2)
            transposes(lambda i: olT[:, i, :], [ol[:, i * 128:(i + 1) * 128] for i in range(2)], 64, PS[2])
            bps = PS[3]
            for h in range(8):
                j, par = h // 2, h % 2
                for cc in range(2):
                    S.mm(bps[par * 64:(par + 1) * 64, j * 8:(j + 1) * 8], w_uv[:, cc, h * 64:(h + 1) * 64],
                         olT[:, cc, :].r("p (a h q) -> p h a q", a=2, h=8)[:, h, :, :], start=(cc == 0), stop=(cc == 1))
            S.copy(b_s_T[:, :, tk8], bps[:, 0:32].r("p (j q) -> p j q", j=4), eng='act')
            ckpt('c5')
        bsq = S.sb([128, 4, TS], BF16, 'bsq')
        S.tt(bsq.v, b_s_T.v, b_s_T.v, ALU.mult)
        for j in range(4):
            S.mm(PS[0][:, 0:TS], onesb.v, bsq[:, j, :], start=(j == 0), stop=(j == 3))
        rbs = S.sb([128, TS], F32, 'rbs')
        S.act(rbs.v, PS[0][:, 0:TS], AF.Ln, bias=epsT[:, 0:1], scale=1.0 / 512)
        S.act(rbs.v, rbs.v, AF.Exp, scale=-0.5)
        S.tt(b_s_T.v, b_s_T.v, rbs.v.un(1).bc([128, 4, TS]), ALU.mult)
        for j in range(4):
            S.ts(bT[:, j, TP:TP + TS], b_s_T[:, j, :], g_ob_col[:, j:j + 1], ALU.mult)
        if stop == 'c6':
            S.store(y_s.rearrange("t (two d) -> (t two) d", two=2)[:, 0:256], b_s_T.v.r("p j t -> p (j t)"))
        S.pop()
        S.pop()
        ckpt('c6')

        dep_best = {}
        for tl in arena_alias:
            for d in list(tl.rd) + ([tl.lw] if tl.lw is not None else []):
                if dep_best.get(d[0], (0, None))[0] < d[1]:
                    dep_best[d[0]] = (d[1], d[2])
        arena_deps = [(k, v, 'freed') for k, (v, e) in dep_best.items()]
        xres = []
        for t in range(NT + 1):
            tl = Tl(arena.t[:, t * 1024:(t + 1) * 1024], 'xres%d' % t)
            tl.rd = list(arena_deps)
            xres.append(tl)
        S.push()
        w_oa = load_w('w_o', D, D, tname='w_o')
        for t in range(NT + 1):
            is_s = (t == NT)
            P = TS if is_s else 128
            tc0 = t * 128
            x_t = S.rot('x_t3', [128, D], F32, 2)
            S.load(x_t[0:P], xs[:, :] if is_s else xp[tc0:tc0 + P, :])
            for n in range(2):
                ps = PS[(2 * t + n) % 4]
                for k in range(8):
                    src = aT if k < 4 else bT
                    S.mm(ps[0:P, :], src[:, k % 4, tc0:tc0 + P], w_oa[:, k, n * 512:(n + 1) * 512],
                         start=(k == 0), stop=(k == 7))
                S.tt(xres[t][0:P, n * 512:(n + 1) * 512], ps[0:P, :], x_t[0:P, n * 512:(n + 1) * 512], ALU.add)
        S.pop()
        S.pop()
        ckpt('c7')

        S.push()
        g_min_col = col_vec('g_mem_in', D)
        w_mk = load_w('w_mk', D, 512)
        w_mv = load_w('w_mv', D, 512)
        for mt in range(2):
            m_t = S.rot('m_t', [128, D], F32, 2)
            S.load(m_t.v, memp[mt * 128:(mt + 1) * 128, :])
            ri = rinv_of(m_t.v, D, 128)
            S.ts(m_t.v, m_t.v, ri[:, 0:1], ALU.mult, eng='pool')
            mT = S.rot('mT', [128, 8, 128], BF16, 2)
            transposes(lambda i: mT[:, i, :], [m_t[:, i * 128:(i + 1) * 128] for i in range(8)], 128, PS[0],
                       scale=lambda i: g_min_col[:, i:i + 1])
            for k in range(8):
                S.mm(PS[1].v, mT[:, k, :], w_mk[:, k, :], start=(k == 0), stop=(k == 7))
            for k in range(8):
                S.mm(PS[2].v, mT[:, k, :], w_mv[:, k, :], start=(k == 0), stop=(k == 7))
            mk_sb = S.rot('mk_sb', [128, 4, 128], F32, 2)
            S.copy(mk_sb.v.r("p h d -> p (h d)"), PS[1].v, eng='act')
            rg = group_rinv(mk_sb.v, 4, 128, 128)
            S.tt(mk_sb.v, mk_sb.v, rg.v.un(2).bc([128, 4, 128]), ALU.mult)
            S.tt(mk_sb.v, mk_sb.v, g_mk_bc.v.un(1).bc([128, 4, 128]), ALU.mult, eng='pool')
            S.store(o_pmk[mt * 128:(mt + 1) * 128, :], mk_sb.v.r("p h d -> p (h d)"))
            transposes(lambda h: mkT[:, h, mt * 128:(mt + 1) * 128], [mk_sb[:, h, :] for h in range(4)], 128, PS[3])
            mv_sb = S.rot('mv_sb', [128, 512], F32, 2)
            S.copy(mv_sb.v, PS[2].v, eng='act')
            S.store(o_pmv[mt * 128:(mt + 1) * 128, :], mv_sb.v)
            S.copy(mvb[:, mt, :], mv_sb.v, eng='pool')
        S.pop()

        ckpt('c8')
        blocks = [(i * 512, 512, False) for i in range(4)] + [(TP, TS, True)]

        def norm_T(dst, c0, NB, gcol):
            ntile = max(1, NB // 128)
            P = min(NB, 128)
            for t in range(ntile):
                xr = xres[c0 // 128 + t]
                ri = rinv_of(xr[0:P, :], D, P)
                xh = S.rot('xh3', [128, D], F32, 2)
                S.ts(xh[0:P], xr[0:P, :], ri[0:P, 0:1], ALU.mult, eng='pool')
                transposes(lambda i: dst[:, i, t * 128:t * 128 + P], [xh[0:P, i * 128:(i + 1) * 128] for i in range(8)],
                           P, PS[t % 2], scale=lambda i: gcol[:, i:i + 1])

        S.push()
        g_mx_col = col_vec('g_mem_x', D)
        w_mq = load_w('w_mq', D, 512)
        w_mo = load_w('w_mo', 512, D)
        for (c0, NB, is_s) in blocks:
            ntile = max(1, NB // 128)
            P = min(NB, 128)
            hT = S.rot('hT4', [128, 8, 512], BF16, 1)
            norm_T(hT, c0, NB, g_mx_col)
            qmT = S.rot('qmT', [128, 4, 512], BF16, 1)
            for t in range(ntile):
                ps = PS[2 + t % 2]
                for k in range(8):
                    S.mm(ps[0:P, :], hT[:, k, t * 128:t * 128 + P], w_mq[:, k, :], start=(k == 0), stop=(k == 7))
                qm = S.rot('qm', [128, 4, 128], F32, 2)
                S.copy(qm[0:P].r("p h d -> p (h d)"), ps[0:P, :], eng='act')
                rg = group_rinv(qm[0:P], 4, 128, P)
                S.tt(qm[0:P], qm[0:P], rg[0:P].un(2).bc([P, 4, 128]), ALU.mult)
                S.tt(qm[0:P], qm[0:P], g_mq_bc[0:P].un(1).bc([P, 4, 128]), ALU.mult, eng='pool')
                transposes(lambda h: qmT[:, h, t * 128:t * 128 + P], [qm[0:P, h, :] for h in range(4)], P, PS[4 + t % 2])
            omT = S.rot('omT', [128, 4, 512], BF16, 1)
            if not is_s:
                for h in range(4):
                    o_ps, d_ps = PS[4], PS[5]
                    for kb in range(2):
                        st = PS[kb]
                        S.mm(st.v, mkT[:, h, kb * 128:(kb + 1) * 128], qmT[:, h, :])
                        pm = S.rot('pm', [128, 512], BF16, 2)
                        S.act(pm.v, st.v, AF.Exp, bias=negCm[:, 0:1], scale=SC_MEM)
                        S.mm(o_ps.v, mvb[:, kb, h * 128:(h + 1) * 128], pm.v, start=(kb == 0), stop=(kb == 1))
                        S.mm(d_ps.v, onesb.v, pm.v, start=(kb == 0), stop=(kb == 1))
                    rden = S.rot('rden', [128, 512], F32, 1)
                    S.act(rden.v, d_ps.v, AF.Ln)
                    S.act(rden.v, rden.v, AF.Exp, scale=-1.0)
                    S.tt(omT[:, h, :], o_ps.v, rden.v, ALU.mult)
            else:
                for sq_ in range(NSEQ):
                    tk = slice(sq_ * 4, sq_ * 4 + 4)
                    mk_s = S.rot('mk_s', [128, 2, 512], F32, 2)
                    S.load(mk_s.v, cmk[sq_].rearrange("(b p) f -> p b f", p=128))
                    mv_s = S.rot('mv_s', [128, 2, 512], BF16, 2)
                    S.load(mv_s.v, cmv[sq_].rearrange("(b p) f -> p b f", p=128), q='pool')
                    mkT_s = S.rot('mkT_s', [128, 4, 256], BF16, 2)
                    for kb in range(2):
                        for h in range(4):
                            S.tr(PS[kb][:, h * 128:(h + 1) * 128], mk_s[:, kb, h * 128:(h + 1) * 128], ident.v)
                        S.copy(mkT_s[:, :, kb * 128:(kb + 1) * 128], PS[kb].v.r("p (h k) -> p h k", h=4), eng='act')
                    st = PS[2]
                    for kb in range(2):
                        for h in range(4):
                            cl = (kb * 4 + h) * 4
                            S.mm(st[:, cl:cl + 4], mkT_s[:, h, kb * 128:(kb + 1) * 128], qmT[:, h, tk])
                    pm = S.rot('pm_s', [128, 2, 4, 4], BF16, 2)
                    S.act(pm.v.r("p b h q -> p (b h q)"), st[:, 0:32], AF.Exp, bias=negCm[:, 0:1], scale=SC_MEM)
                    o_ps, d_ps = PS[4], PS[5]
                    for h in range(4):
                        for kb in range(2):
                            S.mm(o_ps[:, h * 4:(h + 1) * 4], mv_s[:, kb, h * 128:(h + 1) * 128], pm[:, kb, h, :],
                                 start=(kb == 0), stop=(kb == 1))
                    for kb in range(2):
                        S.mm(d_ps[:, 0:16], onesb.v, pm[:, kb, :, :].r("p h q -> p (h q)"), start=(kb == 0), stop=(kb == 1))
                    rden = S.rot('rden_s', [128, 16], F32, 2)
                    S.recip(rden.v, d_ps[:, 0:16])
                    S.tt(omT[:, :, tk], o_ps[:, 0:16].r("p (h q) -> p h q", h=4), rden.v.r("p (h q) -> p h q", h=4), ALU.mult)
            for t in range(ntile):
                xr = xres[c0 // 128 + t]
                for n in range(2):
                    ps = PS[6 + n]
                    for k in range(4):
                        S.mm(ps[0:P, :], omT[:, k, t * 128:t * 128 + P], w_mo[:, k, n * 512:(n + 1) * 512],
                             start=(k == 0), stop=(k == 3))
                    S.tt(xr[0:P, n * 512:(n + 1) * 512], ps[0:P, :], xr[0:P, n * 512:(n + 1) * 512], ALU.add)
        S.pop()

        ckpt('c9')
        S.push()
        g_ffn_col = col_vec('g_ffn', D)
        wc_col = S.sb([128, 3, NFC], F32, 'wc_col')
        S.load(wc_col.v, W['w_conv'].rearrange("j (c p) -> p j c", p=128), allow_slow_non_contiguous=True)
        bc_col = S.sb([128, NFC], F32, 'bc_col')
        S.load(bc_col.v, W['b_conv'].rearrange("(c p) -> p c", p=128), allow_slow_non_contiguous=True)
        carry = S.sb([128, NFC, 2], F32, 'carry')
        S.memset(carry.v, 0.0)
        w_down = load_w('w_down', DFF, D)
        srcw = W['w_up'].rearrange("(c p) n -> p c n", p=128)
        for (c0, NB, is_s) in blocks:
            ntile = max(1, NB // 128)
            P = min(NB, 128)
            hT = S.rot('hT5', [128, 8, 512], BF16, 1)
            norm_T(hT, c0, NB, g_ffn_col)
            aF = S.rot('aF', [128, NFC, 512], BF16, 1)
            for fc in range(NFC):
                wg = S.rot('wg', [128, 8, 128], BF16, 4)
                wv = S.rot('wv', [128, 8, 128], BF16, 4)
                S.load(wg.v, srcw[:, :, fc * 128:(fc + 1) * 128], q='pool')
                S.load(wv.v, srcw[:, :, DFF + fc * 128:DFF + (fc + 1) * 128], q='pool')
                gps, vps = PS[(fc % 2) * 2], PS[(fc % 2) * 2 + 1]
                for k in range(8):
                    S.mm(gps[:, 0:NB], wg[:, k, :], hT[:, k, 0:NB], start=(k == 0), stop=(k == 7))
                for k in range(8):
                    S.mm(vps[:, 0:NB], wv[:, k, :], hT[:, k, 0:NB], start=(k == 0), stop=(k == 7))
                w0, w1, w2 = (wc_col[:, j, fc:fc + 1] for j in range(3))
                cv = S.rot('cv', [128, 512], F32, 2)
                if not is_s:
                    gb = S.rot('gb', [128, 514], F32, 2)
                    S.copy(gb[:, 0:2], carry[:, fc, :], eng='dve')
                    S.copy(gb[:, 2:514], gps.v, eng='act')
                    S.copy(carry[:, fc, :], gb[:, 512:514], eng='dve')
                    S.ts(cv.v, gb[:, 0:512], w0, ALU.mult, bc_col[:, fc:fc + 1], ALU.add)
                    S.op('dve', (lambda cv, gb, w1: lambda e: e.scalar_tensor_tensor(
                        out=cv.v.ap, in0=gb[:, 1:513].ap, scalar=w1.ap, in1=cv.v.ap, op0=ALU.mult, op1=ALU.add))(cv, gb, w1),
                        [gb, wc_col, cv], [cv])
                    S.op('dve', (lambda cv, gb, w2: lambda e: e.scalar_tensor_tensor(
                        out=cv.v.ap, in0=gb[:, 2:514].ap, scalar=w2.ap, in1=cv.v.ap, op0=ALU.mult, op1=ALU.add))(cv, gb, w2),
                        [gb, wc_col, cv], [cv])
                    if c0 + NB == TP:
                        S.tr(PS[6][0:2, 0:128], gb[:, 512:514], ident.v)
                        pcs = S.rot('pcs', [2, 128], F32, 2)
                        S.copy(pcs.v, PS[6][0:2, 0:128], eng='act')
                        S.store(o_pconv[:, fc * 128:(fc + 1) * 128], pcs.v)
                else:
                    cs_t = S.rot('cs_t', [32, 128], F32, 2)
                    S.load(cs_t.v, cst[:, fc * 128:(fc + 1) * 128])
                    S.tr(PS[6][:, 0:32], cs_t.v, ident[0:32, 0:32])
                    gb = S.rot('gbs', [128, NSEQ, 6], F32, 2)
                    S.copy(gb[:, :, 0:2], PS[6][:, 0:32].r("p (s j) -> p s j", j=2), eng='act')
                    S.copy(gb[:, :, 2:6], gps[:, 0:TS].r("p (s t) -> p s t", t=4), eng='act')
                    cv3 = cv[:, 0:TS].r("p (s t) -> p s t", t=4)
                    S.ts(cv3, gb[:, :, 0:4], w0, ALU.mult, bc_col[:, fc:fc + 1], ALU.add)
                    tmpc = S.rot('tmpc', [128, NSEQ, 4], F32, 2)
                    S.ts(tmpc.v, gb[:, :, 1:5], w1, ALU.mult)
                    S.tt(cv3, cv3, tmpc.v, ALU.add)
                    S.ts(tmpc.v, gb[:, :, 2:6], w2, ALU.mult)
                    S.tt(cv3, cv3, tmpc.v, ALU.add)
                    gl = S.rot('gl', [128, NSEQ, 2], F32, 2)
                    S.copy(gl.v, gb[:, :, 4:6], eng='dve')
                    S.tr(PS[7][0:32, 0:128], gl.v.r("p s j -> p (s j)"), ident.v)
                    scs = S.rot('scs', [32, 128], F32, 2)
                    S.copy(scs.v, PS[7][0:32, 0:128], eng='act')
                    S.store(o_sconv[:, fc * 128:(fc + 1) * 128], scs.v)
                S.act(cv[:, 0:NB], cv[:, 0:NB], AF.Silu)
                S.tt(aF[:, fc, 0:NB], cv[:, 0:NB], vps[:, 0:NB], ALU.mult)
            for t in range(ntile):
                tc0 = c0 + t * 128
                xr = xres[c0 // 128 + t]
                y_t = S.rot('y_t', [128, D], F32, 2)
                for n in range(2):
                    ps = PS[4 + n]
                    for fc in range(NFC):
                        S.mm(ps[0:P, :], aF[:, fc, t * 128:t * 128 + P], w_down[:, fc, n * 512:(n + 1) * 512],
                             start=(fc == 0), stop=(fc == NFC - 1))
                    S.tt(y_t[0:P, n * 512:(n + 1) * 512], ps[0:P, :], xr[0:P, n * 512:(n + 1) * 512], ALU.add)
                if is_s:
                    S.store(y_s[:, :], y_t[0:P])
                else:
                    S.store(y_p[tc0:tc0 + P, :], y_t[0:P])
        S.pop()
        S.finish()
    return nc


def _consts():
    half = 16
    inv_freq = (10000.0 ** (-np.arange(half, dtype=np.float32) / half)).astype(np.float32)

    def cs(pos):
        ang = pos.astype(np.float32)[..., None] * inv_freq
        return np.cos(ang).astype(np.float32), np.sin(ang).astype(np.float32)
    p = np.arange(128)
    cp, sp = cs(np.arange(NT)[None, :] * 128 + p[:, None])
    cS, sS = cs(PAST + (np.arange(TS) % 4))
    ck, sk = cs((p[:, None] % NPG) * 128 + np.arange(128)[None, :])
    tri = (p[:, None] <= p[None, :]).astype(np.float32)
    t64 = np.arange(64)
    bd = ((t64[:, None] // 4 == t64[None, :] // 4) & (t64[:, None] % 4 <= t64[None, :] % 4)).astype(np.float32)
    return {
        'c_ident': np.eye(128, dtype=np.float32), 'c_tri': tri, 'c_bd': bd,
        'c_cosp': cp.reshape(128, -1), 'c_sinp': sp.reshape(128, -1),
        'c_coss': cS, 'c_sins': sS,
        'c_cosk': ck.reshape(128, -1), 'c_sink': sk.reshape(128, -1),
    }


_CACHE = {}


def kernel(**inp):
    f = lambda a: np.ascontiguousarray(np.asarray(a))
    n_phys = inp['cache_ckv'].shape[1]
    if n_phys not in _CACHE:
        import os as _os
        _CACHE[n_phys] = build(n_phys, _os.environ.get('MK_STOP'))
    nc = _CACHE[n_phys]
    consts = _consts()
    cckv = f(inp['cache_ckv']).reshape(n_phys, 128 * 256)
    ckpe = f(inp['cache_kpe']).reshape(n_phys, 128 * 32)
    wnames = ['g_mix', 'w_in', 'ln_v_g', 'ln_v_b', 'w_s', 'b_s', 'g_q_a', 'w_uq', 'g_kv_a', 'w_uk', 'w_uv', 'g_qk_q',
              'g_qk_k', 'g_out_a', 'g_out_b', 'w_o', 'g_mem_x', 'g_mem_in', 'w_mq', 'w_mk', 'w_mv', 'g_mq', 'g_mk',
              'w_mo', 'g_ffn', 'w_up', 'w_conv', 'b_conv', 'w_down']
    shared = {nm: f(inp[nm])[0].reshape(-1) if inp[nm].ndim == 2 else f(inp[nm])[0] for nm in wnames}
    shared['ln_v_g'] = shared['ln_v_g'].reshape(-1)
    shared['ln_v_b'] = shared['ln_v_b'].reshape(-1)
    shared.update(consts)
    shared['cckv'] = cckv
    shared['ckpe'] = ckpe
    in_maps = []
    for c in range(NCORES):
        sl = slice(c * NSEQ, (c + 1) * NSEQ)
        m = dict(shared)
        m['xp'] = f(inp['x_prompt'][c])
        m['xs'] = f(inp['x_sample'][sl]).reshape(TS, D)
        m['cmk'] = f(inp['cache_mem_k'][0, sl]).reshape(NSEQ, 256, 512)
        m['cmv'] = f(inp['cache_mem_v'][0, sl]).reshape(NSEQ, 256, 512)
        m['cst'] = f(inp['state_ffn_conv'][0, sl]).reshape(NSEQ * 2, DFF)
        m['ptab'] = f(inp['page_table'][sl]).reshape(NSEQ * NPG, 1).astype(np.int32)
        m['memp'] = f(inp['mem_prompt'][c])
        in_maps.append(m)
    res = run_bass_kernel_spmd(nc, in_maps, core_ids=list(range(NCORES))).results
    cat = lambda k: np.concatenate([r[k] for r in res], axis=0)
    y_p = cat('y_p').reshape(8, 2048, D)
    y_s = cat('y_s').reshape(128, 4, D)
    return (y_p, y_s,
            cat('o_pckv').reshape(1, 8, 2048, 256), cat('o_pkpe').reshape(1, 8, 2048, 32),
            cat('o_pmk').reshape(1, 8, 256, 4, 128), cat('o_pmv').reshape(1, 8, 256, 4, 128),
            cat('o_pconv').reshape(1, 8, 2, DFF),
            cat('o_sckv').reshape(1, 128, 4, 256), cat('o_skpe').reshape(1, 128, 4, 32),
            cat('o_scv').reshape(1, 128, 4, 8, 64), cat('o_sconv').reshape(1, 128, 2, DFF))
```

```python
import numpy as np
from contextlib import ExitStack
import concourse.bass as bass
import concourse.mybir as mybir
from concourse.bass_utils import run_bass_kernel_spmd

F32 = mybir.dt.float32
BF16 = mybir.dt.bfloat16
I32 = mybir.dt.int32
AF = mybir.ActivationFunctionType
ALU = mybir.AluOpType
AX = mybir.AxisListType

NCORES = 8
D = 1024
TP = 2048
NT = 16
NSEQ = 16
TS = 64
NPG = 64
NIN = 1568
DFF = 2816
NFC = 22
EPS = 1e-6
PAST = 8192

COMPUTE = ('pe', 'dve', 'act', 'pool')
NPOOL = 24


class V:
    __slots__ = ('tl', 'ap')

    def __init__(self, tl, ap):
        self.tl = tl
        self.ap = ap

    def __getitem__(self, k):
        return V(self.tl, self.ap[k])

    def r(self, s, **kw):
        return V(self.tl, self.ap.rearrange(s, **kw))

    def un(self, ax):
        return V(self.tl, self.ap.unsqueeze(ax))

    def bc(self, shape):
        return V(self.tl, self.ap.to_broadcast(list(shape)))


class Tl:
    __slots__ = ('t', 'name', 'lw', 'rd', 'psum')

    def __init__(self, t, name, psum=False, init_rd=()):
        self.t = t
        self.name = name
        self.lw = None
        self.rd = list(init_rd)
        self.psum = psum

    def __getitem__(self, k):
        return V(self, self.t[k])

    @property
    def v(self):
        return V(self, self.t[:])


class Sched:
    def __init__(self, nc, es):
        self.nc = nc
        self.es = es
        self.scopes = [(es, [])]
        self.prog = {k: [] for k in ('pe', 'dve', 'act', 'pool', 'sp')}
        self.sems = {}
        self.cnt = {}
        self.seen = {k: {} for k in self.prog}
        for k in COMPUTE:
            self.sems[k] = es.enter_context(nc.semaphore('s_' + k))
            self.cnt[k] = 0
        self.dpool = {}
        for q in ('sp', 'pool', 'act'):
            lst = []
            for i in range(NPOOL):
                key = 'd_%s_%d' % (q, i)
                self.sems[key] = es.enter_context(nc.semaphore(key))
                self.cnt[key] = 0
                lst.append(key)
            self.dpool[q] = [lst, 0]
        self.out_waits = []
        self.ntile = 0
        self.pending = []
        self.rots = {}

    def push(self):
        es = ExitStack()
        self.scopes.append((es, []))
        self.scope_ctr = getattr(self, 'scope_ctr', 0) + 1
        self.scope_ids = getattr(self, 'scope_ids', [0]) + [self.scope_ctr]

    def pop(self):
        es, tiles = self.scopes.pop()
        best = {}
        for k, v, e in self.pending:
            if best.get(k, (0, None))[0] < v:
                best[k] = (v, e)
        for t in tiles:
            deps = list(t.rd)
            if t.lw is not None:
                deps.append(t.lw)
            for k, v, e in deps:
                if best.get(k, (0, None))[0] < v:
                    best[k] = (v, e)
        self.pending = [(k, v, 'freed') for k, (v, e) in best.items()]
        self.scope_ids = self.scope_ids[:-1]
        es.close()

    def sb(self, shape, dt, name=None):
        self.ntile += 1
        name = (name or 't') + '_%d' % self.ntile
        es, tiles = self.scopes[-1]
        t = es.enter_context(self.nc.sbuf_tensor(name, list(shape), dt))
        tl = Tl(t, name, init_rd=self.pending)
        tiles.append(tl)
        return tl

    def ps(self, shape, dt=F32, name=None):
        self.ntile += 1
        name = (name or 'p') + '_%d' % self.ntile
        es, tiles = self.scopes[-1]
        t = es.enter_context(self.nc.psum_tensor(name, list(shape), dt))
        tl = Tl(t, name, psum=True, init_rd=self.pending)
        tiles.append(tl)
        return tl

    def rot(self, key, shape, dt, n=2):
        k = (key, getattr(self, 'scope_ids', [0])[-1])
        if k not in self.rots:
            self.rots[k] = [[self.sb(shape, dt, key) for _ in range(n)], 0]
        ent = self.rots[k]
        t = ent[0][ent[1] % n]
        ent[1] += 1
        return t

    def _collect(self, eng, reads, writes):
        need = {}

        def add(dep, same_ok=False):
            key, val, deng = dep
            if deng == eng and same_ok:
                return
            if need.get(key, 0) < val:
                need[key] = val

        for r in reads:
            if r.lw is not None:
                add(r.lw, same_ok=(eng == 'pe' and r.psum))
        for w in writes:
            if w.lw is not None:
                add(w.lw, same_ok=True)
            for d in w.rd:
                add(d, same_ok=True)
        out = []
        seen = self.seen[eng]
        for key, val in need.items():
            if seen.get(key, 0) >= val:
                continue
            seen[key] = val
            out.append((key, val))
        return out

    def _mark(self, reads, writes, dep):
        for w in writes:
            w.lw = dep
            w.rd = []
        for r in reads:
            if r in writes:
                continue
            r.rd.append(dep)
            if len(r.rd) > 48:
                best = {}
                for k, v, e in r.rd:
                    if best.get(k, (0, None))[0] < v:
                        best[k] = (v, e)
                r.rd = [(k, v, e) for k, (v, e) in best.items()]

    def op(self, eng, fn, reads=(), writes=()):
        reads = list({id(t): t for t in reads}.values())
        writes = list({id(t): t for t in writes}.values())
        waits = self._collect(eng, reads, writes)
        self.cnt[eng] += 1
        val = self.cnt[eng]
        self.prog[eng].append((waits, fn, (eng, 1)))
        self._mark(reads, writes, (eng, val, eng))

    def dma(self, q, fn, reads=(), writes=(), is_output=False):
        reads = list({id(t): t for t in reads}.values())
        writes = list({id(t): t for t in writes}.values())
        waits = self._collect(q, reads, writes)
        lst, idx = self.dpool[q]
        key = lst[idx % NPOOL]
        self.dpool[q][1] = idx + 1
        prev = self.cnt[key]
        if prev > 0 and self.seen[q].get(key, 0) < prev:
            self.seen[q][key] = prev
            waits.append((key, prev))
        self.cnt[key] += 16
        val = self.cnt[key]
        self.prog[q].append((waits, fn, (key, 16)))
        self._mark(reads, writes, (key, val, 'dma_' + q))
        if is_output:
            self.out_waits.append((key, val))

    def finish(self):
        need = {}
        for key, val in self.out_waits:
            if need.get(key, 0) < val:
                need[key] = val
        self.prog['sp'].append((list(need.items()), None, None))
        nc, sems, prog = self.nc, self.sems, self.prog

        def run(e, lst):
            for waits, fn, inc in lst:
                for key, val in waits:
                    e.wait_ge(sems[key], val)
                if fn is not None:
                    fn(e).then_inc(sems[inc[0]], inc[1])

        with nc.Block() as block:
            @block.tensor
            def _(e):
                run(e, prog['pe'])

            @block.vector
            def _(e):
                run(e, prog['dve'])

            @block.scalar
            def _(e):
                run(e, prog['act'])

            @block.gpsimd
            def _(e):
                run(e, prog['pool'])

            @block.sync
            def _(e):
                run(e, prog['sp'])

    def act(self, out, in_, func, bias=None, scale=None, accum=None, eng='act'):
        kw = {}
        rd = [in_.tl]
        wr = [out.tl]
        if bias is not None:
            kw['bias'] = bias.ap
            rd.append(bias.tl)
        if scale is not None:
            if isinstance(scale, V):
                kw['scale'] = scale.ap
                rd.append(scale.tl)
            else:
                kw['scale'] = float(scale)
        if accum is not None:
            kw['accum_out'] = accum.ap
            wr.append(accum.tl)
        self.op(eng, lambda e: e.activation(out=out.ap, in_=in_.ap, func=func, **kw), rd, wr)

    def tt(self, out, a, b, op, eng='dve'):
        self.op(eng, lambda e: e.tensor_tensor(out=out.ap, in0=a.ap, in1=b.ap, op=op), [a.tl, b.tl], [out.tl])

    def ts(self, out, a, s1, op0, s2=None, op1=None, eng='dve', accum=None):
        rd = [a.tl]
        wr = [out.tl]
        x1 = s1
        if isinstance(s1, V):
            rd.append(s1.tl)
            x1 = s1.ap
        x2 = s2
        if isinstance(s2, V):
            rd.append(s2.tl)
            x2 = s2.ap
        kw = {}
        if op1 is not None:
            kw['op1'] = op1
        if accum is not None:
            kw['accum_out'] = accum.ap
            wr.append(accum.tl)
        self.op(eng, lambda e: e.tensor_scalar(out=out.ap, in0=a.ap, scalar1=x1, scalar2=x2, op0=op0, **kw), rd, wr)

    def red(self, out, in_, op=ALU.add, eng='dve'):
        self.op(eng, lambda e: e.tensor_reduce(out=out.ap, in_=in_.ap, axis=AX.X, op=op), [in_.tl], [out.tl])

    def copy(self, out, in_, eng='dve'):
        if eng == 'act':
            self.act(out, in_, AF.Copy)
        else:
            self.op(eng, lambda e: e.tensor_copy(out=out.ap, in_=in_.ap), [in_.tl], [out.tl])

    def recip(self, out, in_, eng='dve'):
        self.op(eng, lambda e: e.reciprocal(out=out.ap, in_=in_.ap), [in_.tl], [out.tl])

    def memset(self, out, val, eng='pool'):
        self.op(eng, lambda e: e.memset(out.ap, val), [], [out.tl])

    def mm(self, out, lhsT, rhs, start=True, stop=True):
        self.op('pe', lambda e: e.matmul(out.ap, lhsT=lhsT.ap, rhs=rhs.ap, start=start, stop=stop),
                [lhsT.tl, rhs.tl], [out.tl])

    def tr(self, out, in_, ident):
        self.op('pe', lambda e: e.transpose(out=out.ap, in_=in_.ap, identity=ident.ap),
                [in_.tl, ident.tl], [out.tl])

    def load(self, out, src, q='sp', **kw):
        self.dma(q, lambda e: e.dma_start(out=out.ap, in_=src, **kw), [], [out.tl])

    def store(self, dst, in_, q='sp', **kw):
        self.dma(q, lambda e: e.dma_start(out=dst, in_=in_.ap, **kw), [in_.tl], [], is_output=True)


class _Stop(Exception):
    pass


def build(n_phys, stop=None):
    nc = bass.Bass("TRN2", target_bir_lowering=False)
    try:
        _build(nc, n_phys, stop)
    except _Stop:
        pass
    return nc


def _build(nc, n_phys, stop):

    def din(name, shape, dt=F32):
        return nc.dram_tensor(name, list(shape), dt, kind="ExternalInput").ap()

    def dout(name, shape):
        return nc.dram_tensor(name, list(shape), F32, kind="ExternalOutput").ap()

    xp = din('xp', [TP, D])
    xs = din('xs', [TS, D])
    cckv = din('cckv', [n_phys, 128 * 256])
    ckpe = din('ckpe', [n_phys, 128 * 32])
    cmk = din('cmk', [NSEQ, 256, 512])
    cmv = din('cmv', [NSEQ, 256, 512])
    cst = din('cst', [NSEQ * 2, DFF])
    ptab = din('ptab', [NSEQ * NPG, 1], I32)
    memp = din('memp', [256, D])
    W = {}
    for nm, shp in [('g_mix', [D]), ('w_in', [D, NIN]), ('ln_v_g', [512]), ('ln_v_b', [512]), ('w_s', [8, 128, 128]),
                    ('b_s', [8, 128]), ('g_q_a', [256]), ('w_uq', [256, 768]), ('g_kv_a', [256]), ('w_uk', [256, 512]),
                    ('w_uv', [256, 512]), ('g_qk_q', [96]), ('g_qk_k', [96]), ('g_out_a', [512]), ('g_out_b', [512]),
                    ('w_o', [D, D]), ('g_mem_x', [D]), ('g_mem_in', [D]), ('w_mq', [D, 512]), ('w_mk', [D, 512]),
                    ('w_mv', [D, 512]), ('g_mq', [128]), ('g_mk', [128]), ('w_mo', [512, D]), ('g_ffn', [D]),
                    ('w_up', [D, 2 * DFF]), ('w_conv', [3, DFF]), ('b_conv', [DFF]), ('w_down', [DFF, D])]:
        W[nm] = din(nm, shp)
    c_ident = din('c_ident', [128, 128])
    c_tri = din('c_tri', [128, 128])
    c_bd = din('c_bd', [64, 64])
    c_cosp = din('c_cosp', [128, NT * 16])
    c_sinp = din('c_sinp', [128, NT * 16])
    c_coss = din('c_coss', [64, 16])
    c_sins = din('c_sins', [64, 16])
    c_cosk = din('c_cosk', [128, 128 * 16])
    c_sink = din('c_sink', [128, 128 * 16])

    y_p = dout('y_p', [TP, D])
    y_s = dout('y_s', [TS, D])
    o_pckv = dout('o_pckv', [TP, 256])
    o_pkpe = dout('o_pkpe', [TP, 32])
    o_pmk = dout('o_pmk', [256, 512])
    o_pmv = dout('o_pmv', [256, 512])
    o_pconv = dout('o_pconv', [2, DFF])
    o_sckv = dout('o_sckv', [TS, 256])
    o_skpe = dout('o_skpe', [TS, 32])
    o_scv = dout('o_scv', [TS, 512])
    o_sconv = dout('o_sconv', [NSEQ * 2, DFF])

    SC_MLA = 96.0 ** -0.5
    SC_MEM = 128.0 ** -0.5

    with ExitStack() as es:
        S = Sched(nc, es)

        def ckpt(name):
            if stop == name:
                while len(S.scopes) > 1:
                    S.pop()
                S.finish()
                raise _Stop()
        PS = [S.ps([128, 512], F32, 'bank%d' % i) for i in range(8)]

        ident = S.sb([128, 128], F32, 'ident')
        S.load(ident.v, c_ident)
        identb = S.sb([128, 128], BF16, 'identb')
        S.copy(identb.v, ident.v, eng='pool')
        tri = S.sb([128, 128], F32, 'tri')
        S.load(tri.v, c_tri)
        trib = S.sb([128, 128], BF16, 'trib')
        S.copy(trib.v, tri.v, eng='pool')
        bd = S.sb([64, 64], F32, 'bd')
        S.load(bd.v, c_bd)
        onesb = S.sb([128, 128], BF16, 'onesb')
        S.memset(onesb.v, 1.0)
        epsT = S.sb([128, 1], F32, 'eps')
        S.memset(epsT.v, EPS)
        cosp = S.sb([128, NT, 16], F32, 'cosp')
        sinp = S.sb([128, NT, 16], F32, 'sinp')
        S.load(cosp.v, c_cosp.rearrange("p (t f) -> p t f", f=16))
        S.load(sinp.v, c_sinp.rearrange("p (t f) -> p t f", f=16))
        coss = S.sb([64, 16], F32, 'coss')
        sins = S.sb([64, 16], F32, 'sins')
        S.load(coss.v, c_coss)
        S.load(sins.v, c_sins)

        def bcast_vec(name, n):
            t = S.sb([128, n], F32, 'bc_' + name)
            S.load(t.v, W[name].partition_broadcast(128))
            return t

        def col_vec(name, n):
            t = S.sb([128, n // 128], F32, 'col_' + name)
            S.load(t.v, W[name].rearrange("(c p) -> p c", p=128), allow_slow_non_contiguous=True)
            return t

        g_kv_bc = bcast_vec('g_kv_a', 256)
        g_qq_bc = bcast_vec('g_qk_q', 96)
        g_qk_bc = bcast_vec('g_qk_k', 96)
        lng_bc = bcast_vec('ln_v_g', 512)
        lnb_bc = bcast_vec('ln_v_b', 512)
        g_mq_bc = bcast_vec('g_mq', 128)
        g_mk_bc = bcast_vec('g_mk', 128)

        def bound(ga, gb, n, sc, name):
            ma = S.sb([128, 1], F32, name + 'a')
            mb = S.sb([128, 1], F32, name + 'b')
            S.op('dve', lambda e: e.tensor_reduce(out=ma.v.ap, in_=ga.v.ap, axis=AX.X, op=ALU.max,
                                                  apply_absolute_value=True), [ga], [ma])
            S.op('dve', lambda e: e.tensor_reduce(out=mb.v.ap, in_=gb.v.ap, axis=AX.X, op=ALU.max,
                                                  apply_absolute_value=True), [gb], [mb])
            c = S.sb([128, 1], F32, name)
            S.tt(c.v, ma.v, mb.v, ALU.mult)
            S.ts(c.v, c.v, -float(n) * sc, ALU.mult)
            return c
        negC = bound(g_qq_bc, g_qk_bc, 96, SC_MLA, 'negC')
        negCm = bound(g_mq_bc, g_mk_bc, 128, SC_MEM, 'negCm')

        def load_w(name, K, N, n0=0, tname=None):
            kc = K // 128
            t = S.sb([128, kc, N], BF16, tname or ('w_' + name))
            src = W[name].rearrange("(c p) n -> p c n", p=128)
            for c in range(kc):
                for a in range(0, N, 1024):
                    b = min(N, a + 1024)
                    S.load(t[:, c, a:b], src[:, c, n0 + a:n0 + b], q='pool')
            return t

        def junk_tile(P, n):
            j = S.rot('junk', [128, 1024], BF16, 1)
            return j[0:P, 0:n]

        def rinv_of(src, n, P):
            ss = S.rot('ss', [128, 1], F32, 2)
            S.act(junk_tile(P, n), src, AF.Square, accum=ss[0:P, :])
            rt = S.rot('rt', [128, 1], F32, 2)
            S.act(rt[0:P, :], ss[0:P, :], AF.Ln, bias=epsT[0:P, :], scale=1.0 / n)
            ri = S.rot('ri', [128, 1], F32, 2)
            S.act(ri[0:P, :], rt[0:P, :], AF.Exp, scale=-0.5)
            return ri

        def group_rinv(src3, G, Dg, P):
            sq = S.rot('gsq%d' % (G * Dg), [128, G, Dg], F32, 1)
            S.tt(sq[0:P], src3, src3, ALU.mult)
            ss = S.rot('gss%d' % G, [128, G], F32, 2)
            S.red(ss[0:P], sq[0:P])
            rt = S.rot('grt%d' % G, [128, G], F32, 2)
            S.act(rt[0:P], ss[0:P], AF.Ln, bias=epsT[0:P, :], scale=1.0 / Dg)
            ri = S.rot('gri%d' % G, [128, G], F32, 2)
            S.act(ri[0:P], rt[0:P], AF.Exp, scale=-0.5)
            return ri

        def transposes(dst, srcs, P, bank, scale=None):
            banks = bank if isinstance(bank, (list, tuple)) else [bank]
            for i0 in range(0, len(srcs), 4):
                grp = srcs[i0:i0 + 4]
                bank = banks[(i0 // 4) % len(banks)]
                for j, s in enumerate(grp):
                    w = s.ap.shape[-1]
                    S.tr(bank[0:w, j * 128:j * 128 + P], s, ident[0:P, 0:P])
                for j, s in enumerate(grp):
                    w = s.ap.shape[-1]
                    if scale is None:
                        S.copy(dst(i0 + j), bank[0:w, j * 128:j * 128 + P], eng='act')
                    else:
                        S.act(dst(i0 + j), bank[0:w, j * 128:j * 128 + P], AF.Copy, scale=scale(i0 + j))

        def rope(dst1, dst2, x1, x2, cs, sn, P):
            H = x1.ap.shape[1]
            cb = cs.un(1).bc([P, H, 16])
            sb_ = sn.un(1).bc([P, H, 16])
            t1 = S.rot('rp1', [128, H, 16], F32, 1)
            t2 = S.rot('rp2', [128, H, 16], F32, 1)
            S.tt(t1[0:P], x1, cb, ALU.mult)
            S.tt(t2[0:P], x2, sb_, ALU.mult)
            S.tt(dst1, t1[0:P], t2[0:P], ALU.subtract)
            t3 = S.rot('rp3', [128, H, 16], F32, 1)
            t4 = S.rot('rp4', [128, H, 16], F32, 1)
            S.tt(t3[0:P], x2, cb, ALU.mult)
            S.tt(t4[0:P], x1, sb_, ALU.mult)
            S.tt(dst2, t3[0:P], t4[0:P], ALU.add)

        arena = S.sb([128, 17 * 1024], F32, 'arena')

        def alias(lo, hi, parts, dt, pattern=None, **kw):
            ap = arena.t[0:parts, lo:hi]
            if dt == BF16:
                ap = ap.bitcast(BF16)
            if pattern:
                ap = ap.rearrange(pattern, **kw)
            return Tl(ap, 'alias_%d' % lo)
        kT = alias(0, 8192, 96, BF16, "p (h t) -> p h t", h=8)
        Vaug = alias(8192, 12416, 128, BF16, "p (t h c) -> p t h c", t=NT, h=8)
        qT = alias(12416, 14464, 96, BF16, "p (h t) -> p h t", h=8)
        b_out = alias(14464, 16512, 128, F32, "p (t c) -> p t c", t=4)
        arena_alias = [kT, Vaug, qT, b_out]

        mkT = S.sb([128, 4, 256], BF16, 'mkT')
        mvb = S.sb([128, 2, 512], BF16, 'mvb')

        g_oa_col = col_vec('g_out_a', 512)
        g_ob_col = col_vec('g_out_b', 512)

        S.push()
        aT = S.sb([128, 4, TP + TS], BF16, 'aT')
        bT = S.sb([128, 4, TP + TS], BF16, 'bT')

        S.push()
        w_uk = load_w('w_uk', 256, 512)
        w_uv = load_w('w_uv', 256, 512)
        qnT_s = S.sb([128, 4, TS], BF16, 'qnT_s')
        qpeT_s = S.sb([32, 8, TS], BF16, 'qpeT_s')
        kT_s = S.sb([96, 8, TS], BF16, 'kT_s')
        qT_s = S.sb([96, 8, TS], BF16, 'qT_s')
        Cb_new = S.sb([64, 264], BF16, 'Cb_new')
        S.memset(Cb_new.v, 1.0)

        S.push()
        g_mix_col = col_vec('g_mix', D)
        g_qa_col = col_vec('g_q_a', 256)
        w_in = load_w('w_in', D, NIN)
        w_uq = load_w('w_uq', 256, 768)
        WsT = S.sb([128, 8, 128], BF16, 'WsT')
        WsT_s = S.sb([64, 8, 64], BF16, 'WsT_s')
        bsT = S.sb([128, 8], F32, 'bsT')
        S.load(bsT.v, W['b_s'].rearrange("g t -> t g"), allow_slow_non_contiguous=True)
        bsT_s = S.sb([64, 8], F32, 'bsT_s')
        for sq in range(NSEQ):
            S.dma('act', (lambda sq: lambda e: e.dma_start(out=bsT_s.t[sq * 4:(sq + 1) * 4, :],
                                                          in_=W['b_s'][:, 0:4].rearrange("g t -> t g"),
                                                          allow_slow_non_contiguous=True))(sq), [], [bsT_s])
        S.push()
        wsf = S.sb([128, 8, 128], F32, 'wsf')
        S.load(wsf.v, W['w_s'].rearrange("g t s -> t g s"))
        for g0 in range(0, 8, 4):
            for j in range(4):
                S.tr(PS[0][:, j * 128:(j + 1) * 128], wsf[:, g0 + j, :], ident.v)
            S.tt(WsT[:, g0:g0 + 4, :], PS[0].v.r("p (j t) -> p j t", j=4), tri.v.un(1).bc([128, 4, 128]), ALU.mult)
        wsf_s = S.sb([64, 8, 64], F32, 'wsf_s')
        S.memset(wsf_s.v.r("p g s -> p (g s)"), 0.0)
        for sq in range(NSEQ):
            S.dma('act', (lambda sq: lambda e: e.dma_start(
                out=wsf_s.t[sq * 4:(sq + 1) * 4, :, sq * 4:(sq + 1) * 4],
                in_=W['w_s'][:, 0:4, 0:4].rearrange("g t s -> t g s")))(sq), [], [wsf_s])
        for g0 in range(0, 8, 4):
            for j in range(4):
                S.tr(PS[1][0:64, j * 128:j * 128 + 64], wsf_s[:, g0 + j, :], ident[0:64, 0:64])
            S.tt(WsT_s[:, g0:g0 + 4, :], PS[1][0:64, :].r("p (j t) -> p j t", j=4)[:, :, 0:64],
                 bd.v.un(1).bc([64, 4, 64]), ALU.mult)
        S.pop()
        S.memset(Vaug.v.r("p t h c -> p (t h c)"), 1.0)

        ckpt('c1')

        def phase1a(ti, P, xsrc, is_sample):
            tok0 = ti * 128
            x_t = S.rot('x_t', [128, D], F32, 2)
            S.load(x_t[0:P, :], xsrc)
            ri = rinv_of(x_t[0:P, :], D, P)
            S.ts(x_t[0:P, :], x_t[0:P, :], ri[0:P, 0:1], ALU.mult, eng='dve')
            xT = S.rot('xT', [128, 8, 128], BF16, 1)
            transposes(lambda i: xT[:, i, 0:P], [x_t[0:P, i * 128:(i + 1) * 128] for i in range(8)], P, [PS[0], PS[6]],
                       scale=lambda i: g_mix_col[:, i:i + 1])
            zb = [PS[1], PS[2], PS[3], PS[4]]
            for n in range(4):
                n0 = n * 512
                n1 = min(NIN, n0 + 512)
                for k in range(8):
                    S.mm(zb[n][0:P, 0:n1 - n0], xT[:, k, 0:P], w_in[:, k, n0:n1], start=(k == 0), stop=(k == 7))
            gu = S.rot('gu', [128, 512], F32, 2)
            S.act(gu[0:P], zb[0][0:P, :], AF.Gelu_apprx_tanh)
            gv = S.rot('gv', [128, 8, 64], F32, 2)
            S.act(gv[0:P].r("p g d -> p (g d)"), zb[1][0:P, :], AF.Gelu_apprx_tanh)
            c3 = zb[2]
            cq = S.rot('cq', [128, 256], F32, 2)
            riq = rinv_of(c3[0:P, 0:256], 256, P)
            S.act(cq[0:P], c3[0:P, 0:256], AF.Copy, scale=riq[0:P, 0:1])
            ckn = S.rot('ckn', [128, 256], F32, 2)
            rik = rinv_of(c3[0:P, 256:512], 256, P)
            S.act(ckn[0:P], c3[0:P, 256:512], AF.Copy, scale=rik[0:P, 0:1])
            kpe = S.rot('kpe', [128, 32], F32, 2)
            S.copy(kpe[0:P], zb[3][0:P, 0:32], eng='act')
            return dict(gu=gu, gv=gv, cq=cq, ckn=ckn, kpe=kpe)

        def phase1b(ti, P, is_sample, qcol, st):
            tok0 = ti * 128
            gu, gv, cq, ckn, kpe = st['gu'], st['gv'], st['cq'], st['ckn'], st['kpe']
            s1 = S.rot('s1', [128, 8], F32, 2)
            S.red(s1[0:P], gv[0:P])
            S.ts(s1[0:P], s1[0:P], -1.0 / 64, ALU.mult)
            cen = S.rot('cen', [128, 8, 64], F32, 1)
            S.tt(cen[0:P], gv[0:P], s1[0:P].un(2).bc([P, 8, 64]), ALU.add)
            sq = S.rot('lsq', [128, 8, 64], F32, 1)
            S.tt(sq[0:P], cen[0:P], cen[0:P], ALU.mult, eng='dve')
            var = S.rot('var', [128, 8], F32, 2)
            S.red(var[0:P], sq[0:P])
            S.act(var[0:P], var[0:P], AF.Ln, bias=epsT[0:P, :], scale=1.0 / 64)
            S.act(var[0:P], var[0:P], AF.Exp, scale=-0.5)
            S.tt(cen[0:P], cen[0:P], var[0:P].un(2).bc([P, 8, 64]), ALU.mult)
            S.tt(cen[0:P], cen[0:P], lng_bc[0:P].r("p (g d) -> p g d", g=8), ALU.mult, eng='dve')
            vg = S.rot('vg', [128, 8, 64], BF16, 1)
            if is_sample:
                S.tt(sq[0:P], cen[0:P], lnb_bc[0:P].r("p (g d) -> p g d", g=8), ALU.add)
                S.store(o_scv, sq[0:P].r("p g d -> p (g d)"))
                S.copy(vg[0:P], sq[0:P], eng='dve')
            else:
                S.tt(vg[0:P], cen[0:P], lnb_bc[0:P].r("p (g d) -> p g d", g=8), ALU.add)
            sp_ps = PS[5]
            for g in range(8):
                lw = WsT_s[:, g, :] if is_sample else WsT[:, g, :]
                S.mm(sp_ps[0:P, g * 64:(g + 1) * 64], lw, vg[0:P, g, :])
            bt = bsT_s if is_sample else bsT
            S.tt(cen[0:P], sp_ps[0:P, :].r("p (g d) -> p g d", g=8), bt[0:P].un(2).bc([P, 8, 64]), ALU.add)
            a_o = gv
            S.tt(a_o[0:P].r("p g d -> p (g d)"), cen[0:P].r("p g d -> p (g d)"), gu[0:P], ALU.mult)
            a_f = a_o[0:P].r("p g d -> p (g d)")
            ria = rinv_of(a_f, 512, P)
            S.ts(a_f, a_f, ria[0:P, 0:1], ALU.mult)
            acol = TP if is_sample else tok0
            transposes(lambda i: aT[:, i, acol:acol + P], [a_f[:, i * 128:(i + 1) * 128] for i in range(4)], P, PS[6],
                       scale=lambda i: g_oa_col[:, i:i + 1])
            cqT = S.rot('cqT', [128, 2, 128], BF16, 1)
            transposes(lambda i: cqT[:, i, 0:P], [cq[0:P, i * 128:(i + 1) * 128] for i in range(2)], P, PS[6],
                       scale=lambda i: g_qa_col[:, i:i + 1])
            S.tt(ckn[0:P], ckn[0:P], g_kv_bc[0:P], ALU.mult)
            if is_sample:
                S.store(o_sckv, ckn[0:P])
                S.store(o_skpe, kpe[0:P])
                S.copy(Cb_new[:, 0:256], ckn[0:P], eng='pool')
            else:
                S.store(o_pckv[tok0:tok0 + P, :], ckn[0:P])
                S.store(o_pkpe[tok0:tok0 + P, :], kpe[0:P])
            ckT = S.rot('ckT', [128, 2, 128], BF16, 1)
            transposes(lambda i: ckT[:, i, 0:P], [ckn[0:P, i * 128:(i + 1) * 128] for i in range(2)], P, PS[6])
            q_ps0, q_ps1 = PS[7], PS[5]
            for k in range(2):
                S.mm(q_ps0[0:P, :], cqT[:, k, 0:P], w_uq[:, k, 0:512], start=(k == 0), stop=(k == 1))
            q_sb = S.rot('q_sb', [128, 8, 96], F32, 1)
            S.copy(q_sb[0:P].r("p h d -> p (h d)")[:, 0:512], q_ps0[0:P, :], eng='act')
            for k in range(2):
                S.mm(q_ps1[0:P, 0:256], cqT[:, k, 0:P], w_uq[:, k, 512:768], start=(k == 0), stop=(k == 1))
            S.copy(q_sb[0:P].r("p h d -> p (h d)")[:, 512:768], q_ps1[0:P, 0:256], eng='act')
            kn_ps, v_ps = PS[7], PS[5]
            for k in range(2):
                S.mm(kn_ps[0:P, :], ckT[:, k, 0:P], w_uk[:, k, :], start=(k == 0), stop=(k == 1))
            k_sb = S.rot('k_sb', [128, 8, 96], F32, 1)
            S.copy(k_sb[0:P, :, 0:64], kn_ps[0:P, :].r("p (h d) -> p h d", h=8), eng='act')
            S.copy(k_sb[0:P, :, 64:96], kpe[0:P].un(1).bc([P, 8, 32]), eng='pool')
            for k in range(2):
                S.mm(v_ps[0:P, :], ckT[:, k, 0:P], w_uv[:, k, :], start=(k == 0), stop=(k == 1))
            if not is_sample:
                S.copy(Vaug[0:P, ti, :, 0:64], v_ps[0:P, :].r("p (h d) -> p h d", h=8), eng='act')
            cs = coss.v if is_sample else cosp[:, ti, :]
            sn = sins.v if is_sample else sinp[:, ti, :]
            for nm, src, gbc in (('q', q_sb, g_qq_bc), ('k', k_sb, g_qk_bc)):
                rg = group_rinv(src[0:P], 8, 96, P)
                S.tt(src[0:P], src[0:P], rg[0:P].un(2).bc([P, 8, 96]), ALU.mult)
                S.tt(src[0:P], src[0:P], gbc[0:P].un(1).bc([P, 8, 96]), ALU.mult, eng='dve')
                fin = S.rot('fin', [128, 8, 96], F32, 1)
                S.copy(fin[0:P, :, 0:64], src[0:P, :, 0:64], eng='pool')
                rope(fin[0:P, :, 64:80], fin[0:P, :, 80:96], src[0:P, :, 64:80], src[0:P, :, 80:96], cs[0:P], sn[0:P], P)
                if is_sample:
                    dstT = qT_s if nm == 'q' else kT_s
                    transposes(lambda h: dstT[:, h, 0:P], [fin[0:P, h, :] for h in range(8)], P, PS[6])
                    if nm == 'q':
                        qg = S.rot('lsq', [128, 8, 64], F32, 1)
                        S.tt(qg[0:P], fin[0:P, :, 0:64], g_qk_bc[0:P, 0:64].un(1).bc([P, 8, 64]), ALU.mult)
                        transposes(lambda j: qnT_s[:, j, 0:P],
                                   [qg[0:P, 2 * j:2 * j + 2, :].r("p h d -> p (h d)") for j in range(4)], P, PS[6])
                        qpe = S.rot('cen', [128, 8, 64], F32, 1)
                        S.copy(qpe[0:P, :, 0:32], fin[0:P, :, 64:96], eng='dve')
                        transposes(lambda h: qpeT_s[:, h, 0:P], [qpe[0:P, h, 0:32] for h in range(8)], P, PS[6])
                else:
                    if nm == 'q':
                        transposes(lambda h: qT[:, h, qcol:qcol + P], [fin[0:P, h, :] for h in range(8)], P, [PS[6], PS[7]])
                    else:
                        transposes(lambda h: kT[:, h, tok0:tok0 + P], [fin[0:P, h, :] for h in range(8)], P, [PS[6], PS[7]])

        att_cnt = [0]

        def attention(qb):
            for h in range(8):
                ot = PS[4 + (att_cnt[0] % 2)]
                att_cnt[0] += 1
                nkb = 4 * qb + 4

                def s_stage(kb):
                    c0 = max(0, kb - 4 * qb) * 128
                    st = PS[kb % 2]
                    S.mm(st[:, c0:512], kT[:, h, kb * 128:(kb + 1) * 128], qT[:, h, c0:512])
                    pt = S.rot('pt', [128, 512], BF16, 3)
                    S.act(pt[:, c0:512], st[:, c0:512], AF.Exp, bias=negC[:, 0:1], scale=SC_MLA)
                    if kb >= 4 * qb:
                        S.tt(pt[:, c0:c0 + 128], pt[:, c0:c0 + 128], trib.v, ALU.mult, eng='pool')
                    return pt, c0
                nxt = s_stage(0)
                for kb in range(nkb):
                    pt, c0 = nxt
                    if kb + 1 < nkb:
                        nxt = s_stage(kb + 1)
                    S.mm(ot[0:65, c0:512], Vaug[:, kb, h, 0:65], pt[:, c0:512], start=(kb == 0), stop=(kb == nkb - 1))
                ot_sb = S.rot('ot_sb', [65, 512], F32, 2)
                S.copy(ot_sb.v, ot[0:65, :], eng='act')
                tp = PS[6 + (att_cnt[0] % 2)]
                for j in range(4):
                    S.tr(tp[:, j * 128:j * 128 + 65], ot_sb[:, j * 128:(j + 1) * 128], ident[0:65, 0:65])
                tpv = tp.v.r("p (j c) -> p j c", j=4)
                rd = S.rot('rd', [128, 4, 1], F32, 2)
                S.recip(rd.v, tpv[:, :, 64:65])
                S.tt(b_out[:, :, h * 64:(h + 1) * 64], tpv[:, :, 0:64], rd.v.bc([128, 4, 64]), ALU.mult)
            for t in range(4):
                ti = 4 * qb + t
                rib = rinv_of(b_out[:, t, :], 512, 128)
                S.ts(b_out[:, t, :], b_out[:, t, :], rib[:, 0:1], ALU.mult)
                transposes(lambda i: bT[:, i, ti * 128:(ti + 1) * 128], [b_out[:, t, i * 128:(i + 1) * 128] for i in range(4)],
                           128, PS[2 + t % 2], scale=lambda i: g_ob_col[:, i:i + 1])

        tiles = [(ti, 128, xp[ti * 128:(ti + 1) * 128, :], False, (ti % 4) * 128) for ti in range(NT)]
        tiles.append((0, TS, xs[:, :], True, 0))
        nxt_st = phase1a(*tiles[0][0:4])
        for i, (ti, P_, src_, smp, qcol) in enumerate(tiles):
            st_cur = nxt_st
            if i + 1 < len(tiles):
                nxt_st = phase1a(*tiles[i + 1][0:4])
            phase1b(ti, P_, smp, qcol, st_cur)
            ckpt('c2')
            if not smp and ti % 4 == 3 and ti < NT - 1:
                attention(ti // 4)
                ckpt('c3')
        attention(3)
        S.pop()
        ckpt('c4')

        S.push()
        wukT = S.sb([128, 4, 256], BF16, 'wukT')
        S.push()
        wukf = S.sb([128, 2, 512], F32, 'wukf')
        S.load(wukf.v, W['w_uk'].rearrange("(c p) n -> p c n", p=128))
        for j in range(4):
            for cc in range(2):
                S.tr(PS[0][:, cc * 128:(cc + 1) * 128], wukf[:, cc, j * 128:(j + 1) * 128], ident.v)
            S.copy(wukT[:, j, :], PS[0][:, 0:256], eng='act')
        S.pop()
        ckpt('c40')
        qlatT = S.sb([128, 2, 8, TS], BF16, 'qlatT')
        for h in range(8):
            j, a = h // 2, h % 2
            for cc in range(2):
                col = (j * 2 + cc) * 64
                S.mm(PS[1 + a][:, col:col + TS],
                     wukT[a * 64:(a + 1) * 64, j, cc * 128:(cc + 1) * 128], qnT_s[a * 64:(a + 1) * 64, j, :])
        for a in range(2):
            S.copy(qlatT.v.r("p c (j a) t -> p a j c t", a=2)[:, a],
                   PS[1 + a].v.r("p (j c t) -> p j c t", j=4, c=2), eng='act')
        ckpt('c41')
        ptt = S.sb([128, NSEQ // 2], I32, 'ptt')
        S.load(ptt.v, ptab.rearrange("(g p) o -> p (g o)", p=128), allow_slow_non_contiguous=True)
        ptf = S.sb([128, NSEQ // 2], F32, 'ptf')
        S.copy(ptf.v, ptt.v)
        io16 = S.sb([128, 16], F32, 'io16')
        S.op('pool', lambda e: e.iota(io16.v.ap, pattern=[[1, 16]], base=0, channel_multiplier=0,
                                      allow_small_or_imprecise_dtypes=True), [], [io16])
        idxf = S.sb([128, NSEQ // 2, 16], F32, 'idxf')
        S.ts(idxf.v, ptf.v.un(2).bc([128, NSEQ // 2, 16]), 16.0, ALU.mult)
        idxc = S.sb([128, NSEQ // 2, 16], I32, 'idxc')
        S.tt(idxc.v, idxf.v, io16.v.un(1).bc([128, NSEQ // 2, 16]), ALU.add)
        ckpt('c42')
        cckv16 = cckv.rearrange("n (j x) -> (n j) x", j=16)
        ckpe2 = ckpe.rearrange("n (j x) -> (n j) x", j=2)
        idx2 = S.sb([128, NSEQ // 2, 2], I32, 'idx2')
        S.ts(idxf[:, :, 0:2], ptf.v.un(2).bc([128, NSEQ // 2, 2]), 2.0, ALU.mult)
        S.tt(idx2.v, idxf[:, :, 0:2], io16[:, 0:2].un(1).bc([128, NSEQ // 2, 2]), ALU.add)
        cosk = S.sb([128, 128, 16], F32, 'cosk')
        sink = S.sb([128, 128, 16], F32, 'sink')
        S.load(cosk.v, c_cosk.rearrange("p (r f) -> p r f", f=16))
        S.load(sink.v, c_sink.rearrange("p (r f) -> p r f", f=16))
        gpe_bc = g_qk_bc[:, 64:96]
        b_s_T = S.sb([128, 4, TS], F32, 'b_s_T')
        RCH = 8
        ckpt('c4a')
        STB = [PS[4], PS[5]]
        ptb_bufs = [S.sb([128, 4, 64], BF16, 'ptb') for _ in range(4)]
        for pb_ in ptb_bufs:
            S.memset(pb_.v.r("p g c -> p (g c)"), 0.0)
        ptb_ctr = [0]
        onesf = S.sb([128, 1], F32, 'onesf')
        S.memset(onesf.v, 1.0)
        CT3 = [PS[2], PS[3], PS[7]]
        for pr in range(NSEQ // 2):
            idx = ptt[:, pr:pr + 1]
            sspe = S.rot('sspe', [128, 128], F32, 1)
            kr = S.rot('kr', [128, 128, 32], BF16, 1)
            for hf in range(2):
                rs = slice(hf * 64, (hf + 1) * 64)
                KP = S.rot('KP', [128, 64, 32], F32, 1)
                ix2 = idx2[:, pr, hf:hf + 1]
                S.dma('pool', (lambda KP, ix2: lambda e: e.indirect_dma_start(
                    out=KP.v.ap.rearrange("p r f -> p (r f)"), out_offset=None, in_=ckpe2,
                    in_offset=bass.IndirectOffsetOnAxis(ap=ix2.ap, axis=0)))(KP, ix2), [idx2], [KP])
                ksq = S.rot('ksq', [128, 64, 32], F32, 1)
                S.tt(ksq.v, KP.v, KP.v, ALU.mult, eng='pool')
                S.red(sspe[:, rs], ksq.v)
                S.tt(ksq.v, KP.v, gpe_bc.un(1).bc([128, 64, 32]), ALU.mult, eng='pool')
                t1 = S.rot('kt1', [128, 64, 16], F32, 1)
                t2 = S.rot('kt2', [128, 64, 16], F32, 1)
                S.tt(t1.v, ksq[:, :, 0:16], cosk[:, rs, :], ALU.mult)
                S.tt(t2.v, ksq[:, :, 16:32], sink[:, rs, :], ALU.mult, eng='pool')
                S.tt(kr[:, rs, 0:16], t1.v, t2.v, ALU.subtract)
                S.tt(t1.v, ksq[:, :, 16:32], cosk[:, rs, :], ALU.mult)
                S.tt(t2.v, ksq[:, :, 0:16], sink[:, rs, :], ALU.mult, eng='pool')
                S.tt(kr[:, rs, 16:32], t1.v, t2.v, ALU.add)
            oacc = PS[6]
            ckpt('c4b')
            NCH = 128 // RCH
            Cbs, cTs, ssns, ptbs = {}, {}, {}, {}
            state = {'first': True}

            def load_chunk(ch):
                Cb = S.rot('Cb', [128, RCH, 256], BF16, 3)
                ixc = idxc[:, pr, ch:ch + 1]
                S.dma('pool', (lambda Cb, ixc: lambda e: e.indirect_dma_start(
                    out=Cb.v.ap.rearrange("p r c -> p (r c)"), out_offset=None, in_=cckv16,
                    in_offset=bass.IndirectOffsetOnAxis(ap=ixc.ap, axis=0)))(Cb, ixc), [idxc], [Cb])
                Cbs[ch] = Cb

            def st_T(r):
                ch, rl = divmod(r, RCH)
                Cb = Cbs[ch]
                ctp = CT3[r % 3]
                for cc in range(2):
                    S.mm(ctp[:, cc * 128:(cc + 1) * 128], Cb[:, rl, cc * 128:(cc + 1) * 128], identb.v)
                S.mm(ctp[0:32, 256:384], kr[:, r, :], identb.v)
                cT = S.rot('cT', [128, 384], BF16, 4)
                S.copy(cT[:, 0:256], ctp[:, 0:256], eng='dve')
                S.copy(cT[0:32, 256:384], ctp[0:32, 256:384], eng='act')
                cTs[r] = cT

            def st_K(r):
                cT = cTs.pop(r)
                grp, g = divmod(r, 4)
                if g == 0:
                    ssns[grp] = S.rot('ssn', [128, 4, 8], F32, 3)
                knp = PS[r % 2]
                for cc in range(2):
                    S.mm(knp.v, cT[:, cc * 128:(cc + 1) * 128], w_uk[:, cc, :], start=(cc == 0), stop=(cc == 1))
                sqk = S.rot('sqk', [128, 8, 64], BF16, 3)
                S.act(sqk.v.r("p h d -> p (h d)"), knp.v, AF.Square)
                S.red(ssns[grp][:, g, :], sqk.v)
                stb = STB[grp % 2]
                for cc in range(2):
                    S.mm(stb[:, g * 64:(g + 1) * 64], cT[:, cc * 128:(cc + 1) * 128],
                         qlatT[:, cc, :, pr * 8:pr * 8 + 8].r("p h (a q) -> p a h q", a=2),
                         start=(cc == 0), stop=False)
                S.mm(stb[:, g * 64:(g + 1) * 64], cT[0:32, 256:384],
                     qpeT_s[:, :, pr * 8:pr * 8 + 8].r("p h (a q) -> p a h q", a=2), start=False, stop=True)

            def st_G(grp):
                r0 = grp * 4
                stb = STB[grp % 2]
                tot = S.rot('tot', [128, 4, 8], F32, 2)
                S.tt(tot.v, ssns.pop(grp).v, sspe[:, r0:r0 + 4].un(2).bc([128, 4, 8]), ALU.add)
                S.act(tot.v, tot.v, AF.Ln, bias=epsT[:, 0:1], scale=1.0 / 96)
                S.act(tot.v, tot.v, AF.Exp, scale=-0.5)
                snm = S.rot('snm', [128, 4, 8, 4], F32, 2)
                ptb = ptb_bufs[ptb_ctr[0] % 4]
                ptb_ctr[0] += 1
                for a in range(2):
                    pa = slice(a * 64, (a + 1) * 64)
                    S.tt(snm[pa], stb[pa, 0:256].r("p (g a h q) -> p g a h q", g=4, a=2, h=8)[:, :, a, :, :],
                         tot[pa].un(3).bc([64, 4, 8, 4]), ALU.mult)
                    S.act(ptb[pa].r("p g (a h q) -> p g a h q", a=2, h=8)[:, :, a, :, :], snm[pa], AF.Exp,
                          bias=negC[pa, 0:1], scale=SC_MLA)
                ptbs[grp] = ptb

            def st_PV(grp):
                ptb = ptbs.pop(grp)
                for g in range(4):
                    ch, rl = divmod(grp * 4 + g, RCH)
                    Cb = Cbs[ch]
                    S.mm(oacc[0:64, 0:256], ptb[:, g, :], Cb[:, rl, :], start=state['first'], stop=False)
                    state['first'] = False
                    S.op('pe', (lambda o_, l_, r_: lambda e: e.matmul(o_.ap, lhsT=l_.ap, rhs=r_.ap, start=False,
                                                                      stop=False, skip_group_check=True))(
                        oacc[0:64, 256:257], ptb[:, g, :], onesb[:, 0:1]), [ptb, onesb], [oacc])

            load_chunk(0)
            load_chunk(1)
            for r in range(128 + 2):
                if r < 128:
                    st_T(r)
                if r >= 2:
                    rr = r - 2
                    st_K(rr)
                    if rr % 4 == 3:
                        grp = rr // 4
                        st_G(grp)
                        if grp >= 2:
                            st_PV(grp - 2)
                            if (grp - 2) % (RCH // 4) == (RCH // 4) - 1:
                                nxtc = (grp - 2) // (RCH // 4) + 3
                                if nxtc < NCH:
                                    load_chunk(nxtc)
                        if grp == 0 and 2 < NCH:
                            load_chunk(2)
            st_PV(30)
            st_PV(31)
            tk8 = slice(pr * 8, pr * 8 + 8)
            for a in range(2):
                sq_ = pr * 2 + a
                tk = slice(sq_ * 4, sq_ * 4 + 4)
                snew = PS[2]
                for h in range(8):
                    S.mm(snew[0:64, h * 4:(h + 1) * 4], kT_s[:, h, :], qT_s[:, h, tk])
                pn = S.rot('pn', [64, 8, 4], BF16, 2)
                S.act(pn.v.r("p h q -> p (h q)"), snew[0:64, 0:32], AF.Exp, bias=negC[0:64, 0:1], scale=SC_MLA)
                S.tt(pn.v, pn.v, bd[:, tk].un(1).bc([64, 8, 4]), ALU.mult)
                S.mm(oacc[a * 32:(a + 1) * 32, 0:257], pn.v.r("p h q -> p (h q)"), Cb_new[:, 0:257], start=False, stop=(a == 1))
            ol = S.rot('ol', [64, 264], F32, 2)
            S.copy(ol[:, 0:257], oacc[0:64, 0:257], eng='act')
            rl_ = S.rot('rl_', [64, 1], F32, 2)
            S.recip(rl_.v, ol[:, 256:257])
            S.ts(ol[:, 0:256], ol[:, 0:256], rl_[:, 0:1], ALU.mult)
            olT = S.rot('olT', [128, 2, 64], BF16, 2)
            transposes(lambda i: olT[:, i, :], [ol[:, i * 128:(i + 1) * 128] for i in range(2)], 64, PS[2])
            bps = PS[3]
            for h in range(8):
                j, par = h // 2, h % 2
                for cc in range(2):
                    S.mm(bps[par * 64:(par + 1) * 64, j * 8:(j + 1) * 8], w_uv[:, cc, h * 64:(h + 1) * 64],
                         olT[:, cc, :].r("p (a h q) -> p h a q", a=2, h=8)[:, h, :, :], start=(cc == 0), stop=(cc == 1))
            S.copy(b_s_T[:, :, tk8], bps[:, 0:32].r("p (j q) -> p j q", j=4), eng='act')
            ckpt('c5')
        bsq = S.sb([128, 4, TS], BF16, 'bsq')
        S.tt(bsq.v, b_s_T.v, b_s_T.v, ALU.mult)
        for j in range(4):
            S.mm(PS[0][:, 0:TS], onesb.v, bsq[:, j, :], start=(j == 0), stop=(j == 3))
        rbs = S.sb([128, TS], F32, 'rbs')
        S.act(rbs.v, PS[0][:, 0:TS], AF.Ln, bias=epsT[:, 0:1], scale=1.0 / 512)
        S.act(rbs.v, rbs.v, AF.Exp, scale=-0.5)
        S.tt(b_s_T.v, b_s_T.v, rbs.v.un(1).bc([128, 4, TS]), ALU.mult)
        for j in range(4):
            S.ts(bT[:, j, TP:TP + TS], b_s_T[:, j, :], g_ob_col[:, j:j + 1], ALU.mult)
        if stop == 'c6':
            S.store(y_s.rearrange("t (two d) -> (t two) d", two=2)[:, 0:256], b_s_T.v.r("p j t -> p (j t)"))
        S.pop()
        S.pop()
        ckpt('c6')

        dep_best = {}
        for tl in arena_alias:
            for d in list(tl.rd) + ([tl.lw] if tl.lw is not None else []):
                if dep_best.get(d[0], (0, None))[0] < d[1]:
                    dep_best[d[0]] = (d[1], d[2])
        arena_deps = [(k, v, 'freed') for k, (v, e) in dep_best.items()]
        xres = []
        for t in range(NT + 1):
            tl = Tl(arena.t[:, t * 1024:(t + 1) * 1024], 'xres%d' % t)
            tl.rd = list(arena_deps)
            xres.append(tl)
        S.push()
        w_oa = load_w('w_o', D, D, tname='w_o')
        for t in range(NT + 1):
            is_s = (t == NT)
            P = TS if is_s else 128
            tc0 = t * 128
            x_t = S.rot('x_t3', [128, D], F32, 2)
            S.load(x_t[0:P], xs[:, :] if is_s else xp[tc0:tc0 + P, :])
            for n in range(2):
                ps = PS[(2 * t + n) % 4]
                for k in range(8):
                    src = aT if k < 4 else bT
                    S.mm(ps[0:P, :], src[:, k % 4, tc0:tc0 + P], w_oa[:, k, n * 512:(n + 1) * 512],
                         start=(k == 0), stop=(k == 7))
                S.tt(xres[t][0:P, n * 512:(n + 1) * 512], ps[0:P, :], x_t[0:P, n * 512:(n + 1) * 512], ALU.add)
        S.pop()
        S.pop()
        ckpt('c7')

        S.push()
        g_min_col = col_vec('g_mem_in', D)
        w_mk = load_w('w_mk', D, 512)
        w_mv = load_w('w_mv', D, 512)
        for mt in range(2):
            m_t = S.rot('m_t', [128, D], F32, 2)
            S.load(m_t.v, memp[mt * 128:(mt + 1) * 128, :])
            ri = rinv_of(m_t.v, D, 128)
            S.ts(m_t.v, m_t.v, ri[:, 0:1], ALU.mult)
            mT = S.rot('mT', [128, 8, 128], BF16, 2)
            transposes(lambda i: mT[:, i, :], [m_t[:, i * 128:(i + 1) * 128] for i in range(8)], 128, PS[0],
                       scale=lambda i: g_min_col[:, i:i + 1])
            for k in range(8):
                S.mm(PS[1].v, mT[:, k, :], w_mk[:, k, :], start=(k == 0), stop=(k == 7))
            for k in range(8):
                S.mm(PS[2].v, mT[:, k, :], w_mv[:, k, :], start=(k == 0), stop=(k == 7))
            mk_sb = S.rot('mk_sb', [128, 4, 128], F32, 2)
            S.copy(mk_sb.v.r("p h d -> p (h d)"), PS[1].v, eng='act')
            rg = group_rinv(mk_sb.v, 4, 128, 128)
            S.tt(mk_sb.v, mk_sb.v, rg.v.un(2).bc([128, 4, 128]), ALU.mult)
            S.tt(mk_sb.v, mk_sb.v, g_mk_bc.v.un(1).bc([128, 4, 128]), ALU.mult)
            S.store(o_pmk[mt * 128:(mt + 1) * 128, :], mk_sb.v.r("p h d -> p (h d)"))
            transposes(lambda h: mkT[:, h, mt * 128:(mt + 1) * 128], [mk_sb[:, h, :] for h in range(4)], 128, PS[3])
            mv_sb = S.rot('mv_sb', [128, 512], F32, 2)
            S.copy(mv_sb.v, PS[2].v, eng='act')
            S.store(o_pmv[mt * 128:(mt + 1) * 128, :], mv_sb.v)
            S.copy(mvb[:, mt, :], mv_sb.v, eng='pool')
        S.pop()

        ckpt('c8')
        blocks = [(i * 512, 512, False) for i in range(4)] + [(TP, TS, True)]

        def norm_T(dst, c0, NB, gcol):
            ntile = max(1, NB // 128)
            P = min(NB, 128)
            for t in range(ntile):
                xr = xres[c0 // 128 + t]
                ri = rinv_of(xr[0:P, :], D, P)
                xh = S.rot('xh3', [128, D], F32, 2)
                S.ts(xh[0:P], xr[0:P, :], ri[0:P, 0:1], ALU.mult)
                transposes(lambda i: dst[:, i, t * 128:t * 128 + P], [xh[0:P, i * 128:(i + 1) * 128] for i in range(8)],
                           P, [PS[0], PS[1]], scale=lambda i: gcol[:, i:i + 1])

        S.push()
        g_mx_col = col_vec('g_mem_x', D)
        w_mq = load_w('w_mq', D, 512)
        w_mo = load_w('w_mo', 512, D)
        for (c0, NB, is_s) in blocks:
            ntile = max(1, NB // 128)
            P = min(NB, 128)
            hT = S.rot('hT4', [128, 8, 512], BF16, 1)
            norm_T(hT, c0, NB, g_mx_col)
            qmT = S.rot('qmT', [128, 4, 512], BF16, 1)
            for t in range(ntile):
                ps = PS[2 + t % 2]
                for k in range(8):
                    S.mm(ps[0:P, :], hT[:, k, t * 128:t * 128 + P], w_mq[:, k, :], start=(k == 0), stop=(k == 7))
                qm = S.rot('qm', [128, 4, 128], F32, 2)
                S.copy(qm[0:P].r("p h d -> p (h d)"), ps[0:P, :], eng='act')
                rg = group_rinv(qm[0:P], 4, 128, P)
                S.tt(qm[0:P], qm[0:P], rg[0:P].un(2).bc([P, 4, 128]), ALU.mult)
                S.tt(qm[0:P], qm[0:P], g_mq_bc[0:P].un(1).bc([P, 4, 128]), ALU.mult)
                transposes(lambda h: qmT[:, h, t * 128:t * 128 + P], [qm[0:P, h, :] for h in range(4)], P, PS[4 + t % 2])
            omT = S.rot('omT', [128, 4, 512], BF16, 1)
            if not is_s:
                for h in range(4):
                    o_ps, d_ps = PS[4], PS[5]
                    for kb in range(2):
                        st = PS[kb]
                        S.mm(st.v, mkT[:, h, kb * 128:(kb + 1) * 128], qmT[:, h, :])
                        pm = S.rot('pm', [128, 512], BF16, 2)
                        S.act(pm.v, st.v, AF.Exp, bias=negCm[:, 0:1], scale=SC_MEM)
                        S.mm(o_ps.v, mvb[:, kb, h * 128:(h + 1) * 128], pm.v, start=(kb == 0), stop=(kb == 1))
                        S.mm(d_ps.v, onesb.v, pm.v, start=(kb == 0), stop=(kb == 1))
                    rden = S.rot('rden', [128, 512], F32, 1)
                    S.act(rden.v, d_ps.v, AF.Ln)
                    S.act(rden.v, rden.v, AF.Exp, scale=-1.0)
                    S.tt(omT[:, h, :], o_ps.v, rden.v, ALU.mult)
            else:
                for sq_ in range(NSEQ):
                    tk = slice(sq_ * 4, sq_ * 4 + 4)
                    mk_s = S.rot('mk_s', [128, 2, 512], F32, 2)
                    S.load(mk_s.v, cmk[sq_].rearrange("(b p) f -> p b f", p=128))
                    mv_s = S.rot('mv_s', [128, 2, 512], BF16, 2)
                    S.load(mv_s.v, cmv[sq_].rearrange("(b p) f -> p b f", p=128), q='pool')
                    mkT_s = S.rot('mkT_s', [128, 4, 256], BF16, 2)
                    for kb in range(2):
                        for h in range(4):
                            S.tr(PS[kb][:, h * 128:(h + 1) * 128], mk_s[:, kb, h * 128:(h + 1) * 128], ident.v)
                        S.copy(mkT_s[:, :, kb * 128:(kb + 1) * 128], PS[kb].v.r("p (h k) -> p h k", h=4), eng='act')
                    st = PS[2]
                    for kb in range(2):
                        for h in range(4):
                            cl = (kb * 4 + h) * 4
                            S.mm(st[:, cl:cl + 4], mkT_s[:, h, kb * 128:(kb + 1) * 128], qmT[:, h, tk])
                    pm = S.rot('pm_s', [128, 2, 4, 4], BF16, 2)
                    S.act(pm.v.r("p b h q -> p (b h q)"), st[:, 0:32], AF.Exp, bias=negCm[:, 0:1], scale=SC_MEM)
                    o_ps, d_ps = PS[4], PS[5]
                    for h in range(4):
                        for kb in range(2):
                            S.mm(o_ps[:, h * 4:(h + 1) * 4], mv_s[:, kb, h * 128:(h + 1) * 128], pm[:, kb, h, :],
                                 start=(kb == 0), stop=(kb == 1))
                    for kb in range(2):
                        S.mm(d_ps[:, 0:16], onesb.v, pm[:, kb, :, :].r("p h q -> p (h q)"), start=(kb == 0), stop=(kb == 1))
                    rden = S.rot('rden_s', [128, 16], F32, 2)
                    S.recip(rden.v, d_ps[:, 0:16])
                    S.tt(omT[:, :, tk], o_ps[:, 0:16].r("p (h q) -> p h q", h=4), rden.v.r("p (h q) -> p h q", h=4), ALU.mult)
            for t in range(ntile):
                xr = xres[c0 // 128 + t]
                for n in range(2):
                    ps = PS[6 + n]
                    for k in range(4):
                        S.mm(ps[0:P, :], omT[:, k, t * 128:t * 128 + P], w_mo[:, k, n * 512:(n + 1) * 512],
                             start=(k == 0), stop=(k == 3))
                    S.tt(xr[0:P, n * 512:(n + 1) * 512], ps[0:P, :], xr[0:P, n * 512:(n + 1) * 512], ALU.add)
        S.pop()

        ckpt('c9')
        S.push()
        g_ffn_col = col_vec('g_ffn', D)
        wc_col = S.sb([128, 3, NFC], F32, 'wc_col')
        S.load(wc_col.v, W['w_conv'].rearrange("j (c p) -> p j c", p=128), allow_slow_non_contiguous=True)
        bc_col = S.sb([128, NFC], F32, 'bc_col')
        S.load(bc_col.v, W['b_conv'].rearrange("(c p) -> p c", p=128), allow_slow_non_contiguous=True)
        carry = S.sb([128, NFC, 2], F32, 'carry')
        S.memset(carry.v, 0.0)
        w_down = load_w('w_down', DFF, D)
        srcw = W['w_up'].rearrange("(c p) n -> p c n", p=128)
        for (c0, NB, is_s) in blocks:
            ntile = max(1, NB // 128)
            P = min(NB, 128)
            hT = S.rot('hT5', [128, 8, 512], BF16, 1)
            norm_T(hT, c0, NB, g_ffn_col)
            aF = S.rot('aF', [128, NFC, 512], BF16, 1)
            for fc in range(NFC):
                wg = S.rot('wg', [128, 8, 128], BF16, 4)
                wv = S.rot('wv', [128, 8, 128], BF16, 4)
                S.load(wg.v, srcw[:, :, fc * 128:(fc + 1) * 128], q='pool')
                S.load(wv.v, srcw[:, :, DFF + fc * 128:DFF + (fc + 1) * 128], q='pool')
                gps, vps = PS[(fc % 2) * 2], PS[(fc % 2) * 2 + 1]
                for k in range(8):
                    S.mm(gps[:, 0:NB], wg[:, k, :], hT[:, k, 0:NB], start=(k == 0), stop=(k == 7))
                for k in range(8):
                    S.mm(vps[:, 0:NB], wv[:, k, :], hT[:, k, 0:NB], start=(k == 0), stop=(k == 7))
                w0, w1, w2 = (wc_col[:, j, fc:fc + 1] for j in range(3))
                cv = S.rot('cv', [128, 512], F32, 2)
                if not is_s:
                    gb = S.rot('gb', [128, 514], F32, 2)
                    S.copy(gb[:, 0:2], carry[:, fc, :], eng='dve')
                    S.copy(gb[:, 2:514], gps.v, eng='act')
                    S.copy(carry[:, fc, :], gb[:, 512:514], eng='dve')
                    S.ts(cv.v, gb[:, 0:512], w0, ALU.mult, bc_col[:, fc:fc + 1], ALU.add)
                    S.op('dve', (lambda cv, gb, w1: lambda e: e.scalar_tensor_tensor(
                        out=cv.v.ap, in0=gb[:, 1:513].ap, scalar=w1.ap, in1=cv.v.ap, op0=ALU.mult, op1=ALU.add))(cv, gb, w1),
                        [gb, wc_col, cv], [cv])
                    S.op('dve', (lambda cv, gb, w2: lambda e: e.scalar_tensor_tensor(
                        out=cv.v.ap, in0=gb[:, 2:514].ap, scalar=w2.ap, in1=cv.v.ap, op0=ALU.mult, op1=ALU.add))(cv, gb, w2),
                        [gb, wc_col, cv], [cv])
                    if c0 + NB == TP:
                        S.tr(PS[6][0:2, 0:128], gb[:, 512:514], ident.v)
                        pcs = S.rot('pcs', [2, 128], F32, 2)
                        S.copy(pcs.v, PS[6][0:2, 0:128], eng='act')
                        S.store(o_pconv[:, fc * 128:(fc + 1) * 128], pcs.v)
                else:
                    cs_t = S.rot('cs_t', [32, 128], F32, 2)
                    S.load(cs_t.v, cst[:, fc * 128:(fc + 1) * 128])
                    S.tr(PS[6][:, 0:32], cs_t.v, ident[0:32, 0:32])
                    gb = S.rot('gbs', [128, NSEQ, 6], F32, 2)
                    S.copy(gb[:, :, 0:2], PS[6][:, 0:32].r("p (s j) -> p s j", j=2), eng='act')
                    S.copy(gb[:, :, 2:6], gps[:, 0:TS].r("p (s t) -> p s t", t=4), eng='act')
                    cv3 = cv[:, 0:TS].r("p (s t) -> p s t", t=4)
                    S.ts(cv3, gb[:, :, 0:4], w0, ALU.mult, bc_col[:, fc:fc + 1], ALU.add)
                    tmpc = S.rot('tmpc', [128, NSEQ, 4], F32, 2)
                    S.ts(tmpc.v, gb[:, :, 1:5], w1, ALU.mult)
                    S.tt(cv3, cv3, tmpc.v, ALU.add)
                    S.ts(tmpc.v, gb[:, :, 2:6], w2, ALU.mult)
                    S.tt(cv3, cv3, tmpc.v, ALU.add)
                    gl = S.rot('gl', [128, NSEQ, 2], F32, 2)
                    S.copy(gl.v, gb[:, :, 4:6], eng='dve')
                    S.tr(PS[7][0:32, 0:128], gl.v.r("p s j -> p (s j)"), ident.v)
                    scs = S.rot('scs', [32, 128], F32, 2)
                    S.copy(scs.v, PS[7][0:32, 0:128], eng='act')
                    S.store(o_sconv[:, fc * 128:(fc + 1) * 128], scs.v)
                S.act(cv[:, 0:NB], cv[:, 0:NB], AF.Silu)
                S.tt(aF[:, fc, 0:NB], cv[:, 0:NB], vps[:, 0:NB], ALU.mult)
            for t in range(ntile):
                tc0 = c0 + t * 128
                xr = xres[c0 // 128 + t]
                y_t = S.rot('y_t', [128, D], F32, 2)
                for n in range(2):
                    ps = PS[4 + n]
                    for fc in range(NFC):
                        S.mm(ps[0:P, :], aF[:, fc, t * 128:t * 128 + P], w_down[:, fc, n * 512:(n + 1) * 512],
                             start=(fc == 0), stop=(fc == NFC - 1))
                    S.tt(y_t[0:P, n * 512:(n + 1) * 512], ps[0:P, :], xr[0:P, n * 512:(n + 1) * 512], ALU.add)
                if is_s:
                    S.store(y_s[:, :], y_t[0:P])
                else:
                    S.store(y_p[tc0:tc0 + P, :], y_t[0:P])
        S.pop()
        S.finish()
    return nc


def _consts():
    half = 16
    inv_freq = (10000.0 ** (-np.arange(half, dtype=np.float32) / half)).astype(np.float32)

    def cs(pos):
        ang = pos.astype(np.float32)[..., None] * inv_freq
        return np.cos(ang).astype(np.float32), np.sin(ang).astype(np.float32)
    p = np.arange(128)
    cp, sp = cs(np.arange(NT)[None, :] * 128 + p[:, None])
    cS, sS = cs(PAST + (np.arange(TS) % 4))
    ck, sk = cs((p[:, None] % NPG) * 128 + np.arange(128)[None, :])
    tri = (p[:, None] <= p[None, :]).astype(np.float32)
    t64 = np.arange(64)
    bd = ((t64[:, None] // 4 == t64[None, :] // 4) & (t64[:, None] % 4 <= t64[None, :] % 4)).astype(np.float32)
    return {
        'c_ident': np.eye(128, dtype=np.float32), 'c_tri': tri, 'c_bd': bd,
        'c_cosp': cp.reshape(128, -1), 'c_sinp': sp.reshape(128, -1),
        'c_coss': cS, 'c_sins': sS,
        'c_cosk': ck.reshape(128, -1), 'c_sink': sk.reshape(128, -1),
    }


_CACHE = {}


def kernel(**inp):
    f = lambda a: np.ascontiguousarray(np.asarray(a))
    n_phys = inp['cache_ckv'].shape[1]
    if n_phys not in _CACHE:
        import os as _os
        _CACHE[n_phys] = build(n_phys, _os.environ.get('MK_STOP'))
    nc = _CACHE[n_phys]
    consts = _consts()
    cckv = f(inp['cache_ckv']).reshape(n_phys, 128 * 256)
    ckpe = f(inp['cache_kpe']).reshape(n_phys, 128 * 32)
    wnames = ['g_mix', 'w_in', 'ln_v_g', 'ln_v_b', 'w_s', 'b_s', 'g_q_a', 'w_uq', 'g_kv_a', 'w_uk', 'w_uv', 'g_qk_q',
              'g_qk_k', 'g_out_a', 'g_out_b', 'w_o', 'g_mem_x', 'g_mem_in', 'w_mq', 'w_mk', 'w_mv', 'g_mq', 'g_mk',
              'w_mo', 'g_ffn', 'w_up', 'w_conv', 'b_conv', 'w_down']
    shared = {nm: f(inp[nm])[0].reshape(-1) if inp[nm].ndim == 2 else f(inp[nm])[0] for nm in wnames}
    shared['ln_v_g'] = shared['ln_v_g'].reshape(-1)
    shared['ln_v_b'] = shared['ln_v_b'].reshape(-1)
    shared.update(consts)
    shared['cckv'] = cckv
    shared['ckpe'] = ckpe
    in_maps = []
    for c in range(NCORES):
        sl = slice(c * NSEQ, (c + 1) * NSEQ)
        m = dict(shared)
        m['xp'] = f(inp['x_prompt'][c])
        m['xs'] = f(inp['x_sample'][sl]).reshape(TS, D)
        m['cmk'] = f(inp['cache_mem_k'][0, sl]).reshape(NSEQ, 256, 512)
        m['cmv'] = f(inp['cache_mem_v'][0, sl]).reshape(NSEQ, 256, 512)
        m['cst'] = f(inp['state_ffn_conv'][0, sl]).reshape(NSEQ * 2, DFF)
        m['ptab'] = f(inp['page_table'][sl]).reshape(NSEQ * NPG, 1).astype(np.int32)
        m['memp'] = f(inp['mem_prompt'][c])
        in_maps.append(m)
    res = run_bass_kernel_spmd(nc, in_maps, core_ids=list(range(NCORES))).results
    cat = lambda k: np.concatenate([r[k] for r in res], axis=0)
    y_p = cat('y_p').reshape(8, 2048, D)
    y_s = cat('y_s').reshape(128, 4, D)
    return (y_p, y_s,
            cat('o_pckv').reshape(1, 8, 2048, 256), cat('o_pkpe').reshape(1, 8, 2048, 32),
            cat('o_pmk').reshape(1, 8, 256, 4, 128), cat('o_pmv').reshape(1, 8, 256, 4, 128),
            cat('o_pconv').reshape(1, 8, 2, DFF),
            cat('o_sckv').reshape(1, 128, 4, 256), cat('o_skpe').reshape(1, 128, 4, 32),
            cat('o_scv').reshape(1, 128, 4, 8, 64), cat('o_sconv').reshape(1, 128, 2, DFF))
```

```python
import numpy as np
from contextlib import ExitStack
import concourse.bass as bass
import concourse.mybir as mybir
from concourse.bass_utils import run_bass_kernel_spmd

F32 = mybir.dt.float32
BF16 = mybir.dt.bfloat16
I32 = mybir.dt.int32
AF = mybir.ActivationFunctionType
ALU = mybir.AluOpType
AX = mybir.AxisListType

NCORES = 8
D = 1024
TP = 2048
NT = 16
NSEQ = 16
TS = 64
NPG = 64
NIN = 1568
DFF = 2816
NFC = 22
EPS = 1e-6
PAST = 8192

COMPUTE = ('pe', 'dve', 'act', 'pool')
NPOOL = 24


class V:
    __slots__ = ('tl', 'ap')

    def __init__(self, tl, ap):
        self.tl = tl
        self.ap = ap

    def __getitem__(self, k):
        return V(self.tl, self.ap[k])

    def r(self, s, **kw):
        return V(self.tl, self.ap.rearrange(s, **kw))

    def un(self, ax):
        return V(self.tl, self.ap.unsqueeze(ax))

    def bc(self, shape):
        return V(self.tl, self.ap.to_broadcast(list(shape)))


class Tl:
    __slots__ = ('t', 'name', 'lw', 'rd', 'psum')

    def __init__(self, t, name, psum=False, init_rd=()):
        self.t = t
        self.name = name
        self.lw = None
        self.rd = list(init_rd)
        self.psum = psum

    def __getitem__(self, k):
        return V(self, self.t[k])

    @property
    def v(self):
        return V(self, self.t[:])


class Sched:
    def __init__(self, nc, es):
        self.nc = nc
        self.es = es
        self.scopes = [(es, [])]
        self.prog = {k: [] for k in ('pe', 'dve', 'act', 'pool', 'sp')}
        self.sems = {}
        self.cnt = {}
        self.seen = {k: {} for k in self.prog}
        for k in COMPUTE:
            self.sems[k] = es.enter_context(nc.semaphore('s_' + k))
            self.cnt[k] = 0
        self.dpool = {}
        for q in ('sp', 'pool', 'act'):
            lst = []
            for i in range(NPOOL):
                key = 'd_%s_%d' % (q, i)
                self.sems[key] = es.enter_context(nc.semaphore(key))
                self.cnt[key] = 0
                lst.append(key)
            self.dpool[q] = [lst, 0]
        self.out_waits = []
        self.ntile = 0
        self.pending = []
        self.rots = {}

    def push(self):
        es = ExitStack()
        self.scopes.append((es, []))
        self.scope_ctr = getattr(self, 'scope_ctr', 0) + 1
        self.scope_ids = getattr(self, 'scope_ids', [0]) + [self.scope_ctr]

    def pop(self):
        es, tiles = self.scopes.pop()
        best = {}
        for k, v, e in self.pending:
            if best.get(k, (0, None))[0] < v:
                best[k] = (v, e)
        for t in tiles:
            deps = list(t.rd)
            if t.lw is not None:
                deps.append(t.lw)
            for k, v, e in deps:
                if best.get(k, (0, None))[0] < v:
                    best[k] = (v, e)
        self.pending = [(k, v, 'freed') for k, (v, e) in best.items()]
        self.scope_ids = self.scope_ids[:-1]
        es.close()

    def sb(self, shape, dt, name=None):
        self.ntile += 1
        name = (name or 't') + '_%d' % self.ntile
        es, tiles = self.scopes[-1]
        t = es.enter_context(self.nc.sbuf_tensor(name, list(shape), dt))
        tl = Tl(t, name, init_rd=self.pending)
        tiles.append(tl)
        return tl

    def ps(self, shape, dt=F32, name=None):
        self.ntile += 1
        name = (name or 'p') + '_%d' % self.ntile
        es, tiles = self.scopes[-1]
        t = es.enter_context(self.nc.psum_tensor(name, list(shape), dt))
        tl = Tl(t, name, psum=True, init_rd=self.pending)
        tiles.append(tl)
        return tl

    def rot(self, key, shape, dt, n=2):
        k = (key, getattr(self, 'scope_ids', [0])[-1])
        if k not in self.rots:
            self.rots[k] = [[self.sb(shape, dt, key) for _ in range(n)], 0]
        ent = self.rots[k]
        t = ent[0][ent[1] % n]
        ent[1] += 1
        return t

    def _collect(self, eng, reads, writes):
        need = {}

        def add(dep, same_ok=False):
            key, val, deng = dep
            if deng == eng and same_ok:
                return
            if need.get(key, 0) < val:
                need[key] = val

        for r in reads:
            if r.lw is not None:
                add(r.lw, same_ok=(eng == 'pe' and r.psum))
        for w in writes:
            if w.lw is not None:
                add(w.lw, same_ok=True)
            for d in w.rd:
                add(d, same_ok=True)
        out = []
        seen = self.seen[eng]
        for key, val in need.items():
            if seen.get(key, 0) >= val:
                continue
            seen[key] = val
            out.append((key, val))
        return out

    def _mark(self, reads, writes, dep):
        for w in writes:
            w.lw = dep
            w.rd = []
        for r in reads:
            if r in writes:
                continue
            r.rd.append(dep)
            if len(r.rd) > 48:
                best = {}
                for k, v, e in r.rd:
                    if best.get(k, (0, None))[0] < v:
                        best[k] = (v, e)
                r.rd = [(k, v, e) for k, (v, e) in best.items()]

    def op(self, eng, fn, reads=(), writes=()):
        reads = list({id(t): t for t in reads}.values())
        writes = list({id(t): t for t in writes}.values())
        waits = self._collect(eng, reads, writes)
        self.cnt[eng] += 1
        val = self.cnt[eng]
        self.prog[eng].append((waits, fn, (eng, 1)))
        self._mark(reads, writes, (eng, val, eng))

    def dma(self, q, fn, reads=(), writes=(), is_output=False):
        reads = list({id(t): t for t in reads}.values())
        writes = list({id(t): t for t in writes}.values())
        waits = self._collect(q, reads, writes)
        lst, idx = self.dpool[q]
        key = lst[idx % NPOOL]
        self.dpool[q][1] = idx + 1
        prev = self.cnt[key]
        if prev > 0 and self.seen[q].get(key, 0) < prev:
            self.seen[q][key] = prev
            waits.append((key, prev))
        self.cnt[key] += 16
        val = self.cnt[key]
        self.prog[q].append((waits, fn, (key, 16)))
        self._mark(reads, writes, (key, val, 'dma_' + q))
        if is_output:
            self.out_waits.append((key, val))

    def finish(self):
        need = {}
        for key, val in self.out_waits:
            if need.get(key, 0) < val:
                need[key] = val
        self.prog['sp'].append((list(need.items()), None, None))
        nc, sems, prog = self.nc, self.sems, self.prog

        def run(e, lst):
            for waits, fn, inc in lst:
                for key, val in waits:
                    e.wait_ge(sems[key], val)
                if fn is not None:
                    fn(e).then_inc(sems[inc[0]], inc[1])

        with nc.Block() as block:
            @block.tensor
            def _(e):
                run(e, prog['pe'])

            @block.vector
            def _(e):
                run(e, prog['dve'])

            @block.scalar
            def _(e):
                run(e, prog['act'])

            @block.gpsimd
            def _(e):
                run(e, prog['pool'])

            @block.sync
            def _(e):
                run(e, prog['sp'])

    def act(self, out, in_, func, bias=None, scale=None, accum=None, eng='act'):
        kw = {}
        rd = [in_.tl]
        wr = [out.tl]
        if bias is not None:
            kw['bias'] = bias.ap
            rd.append(bias.tl)
        if scale is not None:
            if isinstance(scale, V):
                kw['scale'] = scale.ap
                rd.append(scale.tl)
            else:
                kw['scale'] = float(scale)
        if accum is not None:
            kw['accum_out'] = accum.ap
            wr.append(accum.tl)
        self.op(eng, lambda e: e.activation(out=out.ap, in_=in_.ap, func=func, **kw), rd, wr)

    def tt(self, out, a, b, op, eng='dve'):
        self.op(eng, lambda e: e.tensor_tensor(out=out.ap, in0=a.ap, in1=b.ap, op=op), [a.tl, b.tl], [out.tl])

    def ts(self, out, a, s1, op0, s2=None, op1=None, eng='dve', accum=None):
        rd = [a.tl]
        wr = [out.tl]
        x1 = s1
        if isinstance(s1, V):
            rd.append(s1.tl)
            x1 = s1.ap
        x2 = s2
        if isinstance(s2, V):
            rd.append(s2.tl)
            x2 = s2.ap
        kw = {}
        if op1 is not None:
            kw['op1'] = op1
        if accum is not None:
            kw['accum_out'] = accum.ap
            wr.append(accum.tl)
        self.op(eng, lambda e: e.tensor_scalar(out=out.ap, in0=a.ap, scalar1=x1, scalar2=x2, op0=op0, **kw), rd, wr)

    def red(self, out, in_, op=ALU.add, eng='dve'):
        self.op(eng, lambda e: e.tensor_reduce(out=out.ap, in_=in_.ap, axis=AX.X, op=op), [in_.tl], [out.tl])

    def copy(self, out, in_, eng='dve'):
        if eng == 'act':
            self.act(out, in_, AF.Copy)
        else:
            self.op(eng, lambda e: e.tensor_copy(out=out.ap, in_=in_.ap), [in_.tl], [out.tl])

    def recip(self, out, in_, eng='dve'):
        self.op(eng, lambda e: e.reciprocal(out=out.ap, in_=in_.ap), [in_.tl], [out.tl])

    def memset(self, out, val, eng='pool'):
        self.op(eng, lambda e: e.memset(out.ap, val), [], [out.tl])

    def mm(self, out, lhsT, rhs, start=True, stop=True):
        self.op('pe', lambda e: e.matmul(out.ap, lhsT=lhsT.ap, rhs=rhs.ap, start=start, stop=stop),
                [lhsT.tl, rhs.tl], [out.tl])

    def tr(self, out, in_, ident):
        self.op('pe', lambda e: e.transpose(out=out.ap, in_=in_.ap, identity=ident.ap),
                [in_.tl, ident.tl], [out.tl])

    def load(self, out, src, q='sp', **kw):
        self.dma(q, lambda e: e.dma_start(out=out.ap, in_=src, **kw), [], [out.tl])

    def store(self, dst, in_, q='sp', **kw):
        self.dma(q, lambda e: e.dma_start(out=dst, in_=in_.ap, **kw), [in_.tl], [], is_output=True)


class _Stop(Exception):
    pass


def build(n_phys, stop=None):
    nc = bass.Bass("TRN2", target_bir_lowering=False)
    try:
        _build(nc, n_phys, stop)
    except _Stop:
        pass
    return nc


def _build(nc, n_phys, stop):

    def din(name, shape, dt=F32):
        return nc.dram_tensor(name, list(shape), dt, kind="ExternalInput").ap()

    def dout(name, shape):
        return nc.dram_tensor(name, list(shape), F32, kind="ExternalOutput").ap()

    xp = din('xp', [TP, D])
    xs = din('xs', [TS, D])
    cckv = din('cckv', [n_phys, 128 * 256])
    ckpe = din('ckpe', [n_phys, 128 * 32])
    cmk = din('cmk', [NSEQ, 256, 512])
    cmv = din('cmv', [NSEQ, 256, 512])
    cst = din('cst', [NSEQ * 2, DFF])
    ptab = din('ptab', [NSEQ * NPG, 1], I32)
    memp = din('memp', [256, D])
    W = {}
    for nm, shp in [('g_mix', [D]), ('w_in', [D, NIN]), ('ln_v_g', [512]), ('ln_v_b', [512]), ('w_s', [8, 128, 128]),
                    ('b_s', [8, 128]), ('g_q_a', [256]), ('w_uq', [256, 768]), ('g_kv_a', [256]), ('w_uk', [256, 512]),
                    ('w_uv', [256, 512]), ('g_qk_q', [96]), ('g_qk_k', [96]), ('g_out_a', [512]), ('g_out_b', [512]),
                    ('w_o', [D, D]), ('g_mem_x', [D]), ('g_mem_in', [D]), ('w_mq', [D, 512]), ('w_mk', [D, 512]),
                    ('w_mv', [D, 512]), ('g_mq', [128]), ('g_mk', [128]), ('w_mo', [512, D]), ('g_ffn', [D]),
                    ('w_up', [D, 2 * DFF]), ('w_conv', [3, DFF]), ('b_conv', [DFF]), ('w_down', [DFF, D])]:
        W[nm] = din(nm, shp)
    c_ident = din('c_ident', [128, 128])
    c_tri = din('c_tri', [128, 128])
    c_bd = din('c_bd', [64, 64])
    c_cosp = din('c_cosp', [128, NT * 16])
    c_sinp = din('c_sinp', [128, NT * 16])
    c_coss = din('c_coss', [64, 16])
    c_sins = din('c_sins', [64, 16])
    c_cosk = din('c_cosk', [128, 128 * 16])
    c_sink = din('c_sink', [128, 128 * 16])

    y_p = dout('y_p', [TP, D])
    y_s = dout('y_s', [TS, D])
    o_pckv = dout('o_pckv', [TP, 256])
    o_pkpe = dout('o_pkpe', [TP, 32])
    o_pmk = dout('o_pmk', [256, 512])
    o_pmv = dout('o_pmv', [256, 512])
    o_pconv = dout('o_pconv', [2, DFF])
    o_sckv = dout('o_sckv', [TS, 256])
    o_skpe = dout('o_skpe', [TS, 32])
    o_scv = dout('o_scv', [TS, 512])
    o_sconv = dout('o_sconv', [NSEQ * 2, DFF])

    SC_MLA = 96.0 ** -0.5
    SC_MEM = 128.0 ** -0.5

    with ExitStack() as es:
        S = Sched(nc, es)

        def ckpt(name):
            if stop == name:
                while len(S.scopes) > 1:
                    S.pop()
                S.finish()
                raise _Stop()
        PS = [S.ps([128, 512], F32, 'bank%d' % i) for i in range(8)]

        ident = S.sb([128, 128], F32, 'ident')
        S.load(ident.v, c_ident)
        identb = S.sb([128, 128], BF16, 'identb')
        S.copy(identb.v, ident.v, eng='pool')
        tri = S.sb([128, 128], F32, 'tri')
        S.load(tri.v, c_tri)
        trib = S.sb([128, 128], BF16, 'trib')
        S.copy(trib.v, tri.v, eng='pool')
        bd = S.sb([64, 64], F32, 'bd')
        S.load(bd.v, c_bd)
        onesb = S.sb([128, 128], BF16, 'onesb')
        S.memset(onesb.v, 1.0)
        epsT = S.sb([128, 1], F32, 'eps')
        S.memset(epsT.v, EPS)
        cosp = S.sb([128, NT, 16], F32, 'cosp')
        sinp = S.sb([128, NT, 16], F32, 'sinp')
        S.load(cosp.v, c_cosp.rearrange("p (t f) -> p t f", f=16))
        S.load(sinp.v, c_sinp.rearrange("p (t f) -> p t f", f=16))
        coss = S.sb([64, 16], F32, 'coss')
        sins = S.sb([64, 16], F32, 'sins')
        S.load(coss.v, c_coss)
        S.load(sins.v, c_sins)

        def bcast_vec(name, n):
            t = S.sb([128, n], F32, 'bc_' + name)
            S.load(t.v, W[name].partition_broadcast(128))
            return t

        def col_vec(name, n):
            t = S.sb([128, n // 128], F32, 'col_' + name)
            S.load(t.v, W[name].rearrange("(c p) -> p c", p=128), allow_slow_non_contiguous=True)
            return t

        g_kv_bc = bcast_vec('g_kv_a', 256)
        g_qq_bc = bcast_vec('g_qk_q', 96)
        g_qk_bc = bcast_vec('g_qk_k', 96)
        lng_bc = bcast_vec('ln_v_g', 512)
        lnb_bc = bcast_vec('ln_v_b', 512)
        g_mq_bc = bcast_vec('g_mq', 128)
        g_mk_bc = bcast_vec('g_mk', 128)

        def bound(ga, gb, n, sc, name):
            ma = S.sb([128, 1], F32, name + 'a')
            mb = S.sb([128, 1], F32, name + 'b')
            S.op('dve', lambda e: e.tensor_reduce(out=ma.v.ap, in_=ga.v.ap, axis=AX.X, op=ALU.max,
                                                  apply_absolute_value=True), [ga], [ma])
            S.op('dve', lambda e: e.tensor_reduce(out=mb.v.ap, in_=gb.v.ap, axis=AX.X, op=ALU.max,
                                                  apply_absolute_value=True), [gb], [mb])
            c = S.sb([128, 1], F32, name)
            S.tt(c.v, ma.v, mb.v, ALU.mult)
            S.ts(c.v, c.v, -float(n) * sc, ALU.mult)
            return c
        negC = bound(g_qq_bc, g_qk_bc, 96, SC_MLA, 'negC')
        negCm = bound(g_mq_bc, g_mk_bc, 128, SC_MEM, 'negCm')

        def load_w(name, K, N, n0=0, tname=None):
            kc = K // 128
            t = S.sb([128, kc, N], BF16, tname or ('w_' + name))
            src = W[name].rearrange("(c p) n -> p c n", p=128)
            for c in range(kc):
                for a in range(0, N, 1024):
                    b = min(N, a + 1024)
                    S.load(t[:, c, a:b], src[:, c, n0 + a:n0 + b], q='pool')
            return t

        def junk_tile(P, n):
            j = S.rot('junk', [128, 1024], BF16, 1)
            return j[0:P, 0:n]

        def rinv_of(src, n, P):
            ss = S.rot('ss', [128, 1], F32, 2)
            S.act(junk_tile(P, n), src, AF.Square, accum=ss[0:P, :])
            rt = S.rot('rt', [128, 1], F32, 2)
            S.act(rt[0:P, :], ss[0:P, :], AF.Ln, bias=epsT[0:P, :], scale=1.0 / n)
            ri = S.rot('ri', [128, 1], F32, 2)
            S.act(ri[0:P, :], rt[0:P, :], AF.Exp, scale=-0.5)
            return ri

        def group_rinv(src3, G, Dg, P):
            sq = S.rot('gsq%d' % (G * Dg), [128, G, Dg], F32, 1)
            S.tt(sq[0:P], src3, src3, ALU.mult)
            ss = S.rot('gss%d' % G, [128, G], F32, 2)
            S.red(ss[0:P], sq[0:P])
            rt = S.rot('grt%d' % G, [128, G], F32, 2)
            S.act(rt[0:P], ss[0:P], AF.Ln, bias=epsT[0:P, :], scale=1.0 / Dg)
            ri = S.rot('gri%d' % G, [128, G], F32, 2)
            S.act(ri[0:P], rt[0:P], AF.Exp, scale=-0.5)
            return ri

        def transposes(dst, srcs, P, bank, scale=None):
            banks = bank if isinstance(bank, (list, tuple)) else [bank]
            for i0 in range(0, len(srcs), 4):
                grp = srcs[i0:i0 + 4]
                bank = banks[(i0 // 4) % len(banks)]
                for j, s in enumerate(grp):
                    w = s.ap.shape[-1]
                    S.tr(bank[0:w, j * 128:j * 128 + P], s, ident[0:P, 0:P])
                for j, s in enumerate(grp):
                    w = s.ap.shape[-1]
                    if scale is None:
                        S.copy(dst(i0 + j), bank[0:w, j * 128:j * 128 + P], eng='act')
                    else:
                        S.act(dst(i0 + j), bank[0:w, j * 128:j * 128 + P], AF.Copy, scale=scale(i0 + j))

        def rope(dst1, dst2, x1, x2, cs, sn, P):
            H = x1.ap.shape[1]
            cb = cs.un(1).bc([P, H, 16])
            sb_ = sn.un(1).bc([P, H, 16])
            t1 = S.rot('rp1', [128, H, 16], F32, 1)
            t2 = S.rot('rp2', [128, H, 16], F32, 1)
            S.tt(t1[0:P], x1, cb, ALU.mult)
            S.tt(t2[0:P], x2, sb_, ALU.mult)
            S.tt(dst1, t1[0:P], t2[0:P], ALU.subtract)
            t3 = S.rot('rp3', [128, H, 16], F32, 1)
            t4 = S.rot('rp4', [128, H, 16], F32, 1)
            S.tt(t3[0:P], x2, cb, ALU.mult)
            S.tt(t4[0:P], x1, sb_, ALU.mult)
            S.tt(dst2, t3[0:P], t4[0:P], ALU.add)

        arena = S.sb([128, 17 * 1024], F32, 'arena')

        def alias(lo, hi, parts, dt, pattern=None, **kw):
            ap = arena.t[0:parts, lo:hi]
            if dt == BF16:
                ap = ap.bitcast(BF16)
            if pattern:
                ap = ap.rearrange(pattern, **kw)
            return Tl(ap, 'alias_%d' % lo)
        kT = alias(0, 8192, 96, BF16, "p (h t) -> p h t", h=8)
        Vaug = alias(8192, 12416, 128, BF16, "p (t h c) -> p t h c", t=NT, h=8)
        qT = alias(12416, 14464, 96, BF16, "p (h t) -> p h t", h=8)
        b_out = alias(14464, 16512, 128, F32, "p (t c) -> p t c", t=4)
        arena_alias = [kT, Vaug, qT, b_out]

        mkT = S.sb([128, 4, 256], BF16, 'mkT')
        mvb = S.sb([128, 2, 512], BF16, 'mvb')

        g_oa_col = col_vec('g_out_a', 512)
        g_ob_col = col_vec('g_out_b', 512)

        S.push()
        aT = S.sb([128, 4, TP + TS], BF16, 'aT')
        bT = S.sb([128, 4, TP + TS], BF16, 'bT')

        S.push()
        w_uk = load_w('w_uk', 256, 512)
        w_uv = load_w('w_uv', 256, 512)
        qnT_s = S.sb([128, 4, TS], BF16, 'qnT_s')
        qpeT_s = S.sb([32, 8, TS], BF16, 'qpeT_s')
        kT_s = S.sb([96, 8, TS], BF16, 'kT_s')
        qT_s = S.sb([96, 8, TS], BF16, 'qT_s')
        Cb_new = S.sb([64, 264], BF16, 'Cb_new')
        S.memset(Cb_new.v, 1.0)

        S.push()
        g_mix_col = col_vec('g_mix', D)
        g_qa_col = col_vec('g_q_a', 256)
        w_in = load_w('w_in', D, NIN)
        w_uq = load_w('w_uq', 256, 768)
        WsT = S.sb([128, 8, 128], BF16, 'WsT')
        WsT_s = S.sb([64, 8, 64], BF16, 'WsT_s')
        bsT = S.sb([128, 8], F32, 'bsT')
        S.load(bsT.v, W['b_s'].rearrange("g t -> t g"), allow_slow_non_contiguous=True)
        bsT_s = S.sb([64, 8], F32, 'bsT_s')
        for sq in range(NSEQ):
            S.dma('act', (lambda sq: lambda e: e.dma_start(out=bsT_s.t[sq * 4:(sq + 1) * 4, :],
                                                          in_=W['b_s'][:, 0:4].rearrange("g t -> t g"),
                                                          allow_slow_non_contiguous=True))(sq), [], [bsT_s])
        S.push()
        wsf = S.sb([128, 8, 128], F32, 'wsf')
        S.load(wsf.v, W['w_s'].rearrange("g t s -> t g s"))
        for g0 in range(0, 8, 4):
            for j in range(4):
                S.tr(PS[0][:, j * 128:(j + 1) * 128], wsf[:, g0 + j, :], ident.v)
            S.tt(WsT[:, g0:g0 + 4, :], PS[0].v.r("p (j t) -> p j t", j=4), tri.v.un(1).bc([128, 4, 128]), ALU.mult)
        wsf_s = S.sb([64, 8, 64], F32, 'wsf_s')
        S.memset(wsf_s.v.r("p g s -> p (g s)"), 0.0)
        for sq in range(NSEQ):
            S.dma('act', (lambda sq: lambda e: e.dma_start(
                out=wsf_s.t[sq * 4:(sq + 1) * 4, :, sq * 4:(sq + 1) * 4],
                in_=W['w_s'][:, 0:4, 0:4].rearrange("g t s -> t g s")))(sq), [], [wsf_s])
        for g0 in range(0, 8, 4):
            for j in range(4):
                S.tr(PS[1][0:64, j * 128:j * 128 + 64], wsf_s[:, g0 + j, :], ident[0:64, 0:64])
            S.tt(WsT_s[:, g0:g0 + 4, :], PS[1][0:64, :].r("p (j t) -> p j t", j=4)[:, :, 0:64],
                 bd.v.un(1).bc([64, 4, 64]), ALU.mult)
        S.pop()
        S.memset(Vaug.v.r("p t h c -> p (t h c)"), 1.0)

        ckpt('c1')

        def phase1a(ti, P, xsrc, is_sample):
            tok0 = ti * 128
            x_t = S.rot('x_t', [128, D], F32, 2)
            S.load(x_t[0:P, :], xsrc)
            ri = rinv_of(x_t[0:P, :], D, P)
            S.ts(x_t[0:P, :], x_t[0:P, :], ri[0:P, 0:1], ALU.mult, eng='dve')
            xT = S.rot('xT', [128, 8, 128], BF16, 1)
            transposes(lambda i: xT[:, i, 0:P], [x_t[0:P, i * 128:(i + 1) * 128] for i in range(8)], P, [PS[0], PS[6]],
                       scale=lambda i: g_mix_col[:, i:i + 1])
            zb = [PS[1], PS[2], PS[3], PS[4]]
            for n in range(4):
                n0 = n * 512
                n1 = min(NIN, n0 + 512)
                for k in range(8):
                    S.mm(zb[n][0:P, 0:n1 - n0], xT[:, k, 0:P], w_in[:, k, n0:n1], start=(k == 0), stop=(k == 7))
            gu = S.rot('gu', [128, 512], F32, 2)
            S.act(gu[0:P], zb[0][0:P, :], AF.Gelu_apprx_tanh)
            gv = S.rot('gv', [128, 8, 64], F32, 2)
            S.act(gv[0:P].r("p g d -> p (g d)"), zb[1][0:P, :], AF.Gelu_apprx_tanh)
            c3 = zb[2]
            cq = S.rot('cq', [128, 256], F32, 2)
            riq = rinv_of(c3[0:P, 0:256], 256, P)
            S.act(cq[0:P], c3[0:P, 0:256], AF.Copy, scale=riq[0:P, 0:1])
            ckn = S.rot('ckn', [128, 256], F32, 2)
            rik = rinv_of(c3[0:P, 256:512], 256, P)
            S.act(ckn[0:P], c3[0:P, 256:512], AF.Copy, scale=rik[0:P, 0:1])
            kpe = S.rot('kpe', [128, 32], F32, 2)
            S.copy(kpe[0:P], zb[3][0:P, 0:32], eng='act')
            return dict(gu=gu, gv=gv, cq=cq, ckn=ckn, kpe=kpe)

        def phase1b(ti, P, is_sample, qcol, st):
            tok0 = ti * 128
            gu, gv, cq, ckn, kpe = st['gu'], st['gv'], st['cq'], st['ckn'], st['kpe']
            s1 = S.rot('s1', [128, 8], F32, 2)
            S.red(s1[0:P], gv[0:P])
            S.ts(s1[0:P], s1[0:P], -1.0 / 64, ALU.mult)
            cen = S.rot('cen', [128, 8, 64], F32, 1)
            S.tt(cen[0:P], gv[0:P], s1[0:P].un(2).bc([P, 8, 64]), ALU.add)
            sq = S.rot('lsq', [128, 8, 64], F32, 1)
            S.tt(sq[0:P], cen[0:P], cen[0:P], ALU.mult, eng='dve')
            var = S.rot('var', [128, 8], F32, 2)
            S.red(var[0:P], sq[0:P])
            S.act(var[0:P], var[0:P], AF.Ln, bias=epsT[0:P, :], scale=1.0 / 64)
            S.act(var[0:P], var[0:P], AF.Exp, scale=-0.5)
            S.tt(cen[0:P], cen[0:P], var[0:P].un(2).bc([P, 8, 64]), ALU.mult)
            S.tt(cen[0:P], cen[0:P], lng_bc[0:P].r("p (g d) -> p g d", g=8), ALU.mult, eng='dve')
            vg = S.rot('vg', [128, 8, 64], BF16, 1)
            if is_sample:
                S.tt(sq[0:P], cen[0:P], lnb_bc[0:P].r("p (g d) -> p g d", g=8), ALU.add)
                S.store(o_scv, sq[0:P].r("p g d -> p (g d)"))
                S.copy(vg[0:P], sq[0:P], eng='dve')
            else:
                S.tt(vg[0:P], cen[0:P], lnb_bc[0:P].r("p (g d) -> p g d", g=8), ALU.add)
            sp_ps = PS[5]
            for g in range(8):
                lw = WsT_s[:, g, :] if is_sample else WsT[:, g, :]
                S.mm(sp_ps[0:P, g * 64:(g + 1) * 64], lw, vg[0:P, g, :])
            bt = bsT_s if is_sample else bsT
            S.tt(cen[0:P], sp_ps[0:P, :].r("p (g d) -> p g d", g=8), bt[0:P].un(2).bc([P, 8, 64]), ALU.add)
            a_o = gv
            S.tt(a_o[0:P].r("p g d -> p (g d)"), cen[0:P].r("p g d -> p (g d)"), gu[0:P], ALU.mult)
            a_f = a_o[0:P].r("p g d -> p (g d)")
            ria = rinv_of(a_f, 512, P)
            S.ts(a_f, a_f, ria[0:P, 0:1], ALU.mult)
            acol = TP if is_sample else tok0
            transposes(lambda i: aT[:, i, acol:acol + P], [a_f[:, i * 128:(i + 1) * 128] for i in range(4)], P, PS[6],
                       scale=lambda i: g_oa_col[:, i:i + 1])
            cqT = S.rot('cqT', [128, 2, 128], BF16, 1)
            transposes(lambda i: cqT[:, i, 0:P], [cq[0:P, i * 128:(i + 1) * 128] for i in range(2)], P, PS[6],
                       scale=lambda i: g_qa_col[:, i:i + 1])
            S.tt(ckn[0:P], ckn[0:P], g_kv_bc[0:P], ALU.mult)
            if is_sample:
                S.store(o_sckv, ckn[0:P])
                S.store(o_skpe, kpe[0:P])
                S.copy(Cb_new[:, 0:256], ckn[0:P], eng='pool')
            else:
                S.store(o_pckv[tok0:tok0 + P, :], ckn[0:P])
                S.store(o_pkpe[tok0:tok0 + P, :], kpe[0:P])
            ckT = S.rot('ckT', [128, 2, 128], BF16, 1)
            transposes(lambda i: ckT[:, i, 0:P], [ckn[0:P, i * 128:(i + 1) * 128] for i in range(2)], P, PS[6])
            q_ps0, q_ps1 = PS[7], PS[5]
            for k in range(2):
                S.mm(q_ps0[0:P, :], cqT[:, k, 0:P], w_uq[:, k, 0:512], start=(k == 0), stop=(k == 1))
            q_sb = S.rot('q_sb', [128, 8, 96], F32, 1)
            S.copy(q_sb[0:P].r("p h d -> p (h d)")[:, 0:512], q_ps0[0:P, :], eng='act')
            for k in range(2):
                S.mm(q_ps1[0:P, 0:256], cqT[:, k, 0:P], w_uq[:, k, 512:768], start=(k == 0), stop=(k == 1))
            S.copy(q_sb[0:P].r("p h d -> p (h d)")[:, 512:768], q_ps1[0:P, 0:256], eng='act')
            kn_ps, v_ps = PS[7], PS[5]
            for k in range(2):
                S.mm(kn_ps[0:P, :], ckT[:, k, 0:P], w_uk[:, k, :], start=(k == 0), stop=(k == 1))
            k_sb = S.rot('k_sb', [128, 8, 96], F32, 1)
            S.copy(k_sb[0:P, :, 0:64], kn_ps[0:P, :].r("p (h d) -> p h d", h=8), eng='act')
            S.copy(k_sb[0:P, :, 64:96], kpe[0:P].un(1).bc([P, 8, 32]), eng='pool')
            for k in range(2):
                S.mm(v_ps[0:P, :], ckT[:, k, 0:P], w_uv[:, k, :], start=(k == 0), stop=(k == 1))
            if not is_sample:
                S.copy(Vaug[0:P, ti, :, 0:64], v_ps[0:P, :].r("p (h d) -> p h d", h=8), eng='act')
            cs = coss.v if is_sample else cosp[:, ti, :]
            sn = sins.v if is_sample else sinp[:, ti, :]
            for nm, src, gbc in (('q', q_sb, g_qq_bc), ('k', k_sb, g_qk_bc)):
                rg = group_rinv(src[0:P], 8, 96, P)
                S.tt(src[0:P], src[0:P], rg[0:P].un(2).bc([P, 8, 96]), ALU.mult)
                S.tt(src[0:P], src[0:P], gbc[0:P].un(1).bc([P, 8, 96]), ALU.mult, eng='dve')
                fin = S.rot('fin', [128, 8, 96], F32, 1)
                S.copy(fin[0:P, :, 0:64], src[0:P, :, 0:64], eng='pool')
                rope(fin[0:P, :, 64:80], fin[0:P, :, 80:96], src[0:P, :, 64:80], src[0:P, :, 80:96], cs[0:P], sn[0:P], P)
                if is_sample:
                    dstT = qT_s if nm == 'q' else kT_s
                    transposes(lambda h: dstT[:, h, 0:P], [fin[0:P, h, :] for h in range(8)], P, PS[6])
                    if nm == 'q':
                        qg = S.rot('lsq', [128, 8, 64], F32, 1)
                        S.tt(qg[0:P], fin[0:P, :, 0:64], g_qk_bc[0:P, 0:64].un(1).bc([P, 8, 64]), ALU.mult)
                        transposes(lambda j: qnT_s[:, j, 0:P],
                                   [qg[0:P, 2 * j:2 * j + 2, :].r("p h d -> p (h d)") for j in range(4)], P, PS[6])
                        qpe = S.rot('cen', [128, 8, 64], F32, 1)
                        S.copy(qpe[0:P, :, 0:32], fin[0:P, :, 64:96], eng='dve')
                        transposes(lambda h: qpeT_s[:, h, 0:P], [qpe[0:P, h, 0:32] for h in range(8)], P, PS[6])
                else:
                    if nm == 'q':
                        transposes(lambda h: qT[:, h, qcol:qcol + P], [fin[0:P, h, :] for h in range(8)], P, [PS[6], PS[7]])
                    else:
                        transposes(lambda h: kT[:, h, tok0:tok0 + P], [fin[0:P, h, :] for h in range(8)], P, [PS[6], PS[7]])

        att_cnt = [0]

        def attention(qb):
            for h in range(8):
                ot = PS[4 + (att_cnt[0] % 2)]
                att_cnt[0] += 1
                nkb = 4 * qb + 4

                def s_stage(kb):
                    c0 = max(0, kb - 4 * qb) * 128
                    st = PS[kb % 2]
                    S.mm(st[:, c0:512], kT[:, h, kb * 128:(kb + 1) * 128], qT[:, h, c0:512])
                    pt = S.rot('pt', [128, 512], BF16, 3)
                    S.act(pt[:, c0:512], st[:, c0:512], AF.Exp, bias=negC[:, 0:1], scale=SC_MLA)
                    if kb >= 4 * qb:
                        S.tt(pt[:, c0:c0 + 128], pt[:, c0:c0 + 128], trib.v, ALU.mult, eng='pool')
                    return pt, c0
                nxt = s_stage(0)
                for kb in range(nkb):
                    pt, c0 = nxt
                    if kb + 1 < nkb:
                        nxt = s_stage(kb + 1)
                    S.mm(ot[0:65, c0:512], Vaug[:, kb, h, 0:65], pt[:, c0:512], start=(kb == 0), stop=(kb == nkb - 1))
                ot_sb = S.rot('ot_sb', [65, 512], F32, 2)
                S.copy(ot_sb.v, ot[0:65, :], eng='act')
                tp = PS[6 + (att_cnt[0] % 2)]
                for j in range(4):
                    S.tr(tp[:, j * 128:j * 128 + 65], ot_sb[:, j * 128:(j + 1) * 128], ident[0:65, 0:65])
                tpv = tp.v.r("p (j c) -> p j c", j=4)
                rd = S.rot('rd', [128, 4, 1], F32, 2)
                S.recip(rd.v, tpv[:, :, 64:65])
                S.tt(b_out[:, :, h * 64:(h + 1) * 64], tpv[:, :, 0:64], rd.v.bc([128, 4, 64]), ALU.mult)
            for t in range(4):
                ti = 4 * qb + t
                rib = rinv_of(b_out[:, t, :], 512, 128)
                S.ts(b_out[:, t, :], b_out[:, t, :], rib[:, 0:1], ALU.mult)
                transposes(lambda i: bT[:, i, ti * 128:(ti + 1) * 128], [b_out[:, t, i * 128:(i + 1) * 128] for i in range(4)],
                           128, PS[2 + t % 2], scale=lambda i: g_ob_col[:, i:i + 1])

        tiles = [(ti, 128, xp[ti * 128:(ti + 1) * 128, :], False, (ti % 4) * 128) for ti in range(NT)]
        tiles.append((0, TS, xs[:, :], True, 0))
        nxt_st = phase1a(*tiles[0][0:4])
        for i, (ti, P_, src_, smp, qcol) in enumerate(tiles):
            st_cur = nxt_st
            if i + 1 < len(tiles):
                nxt_st = phase1a(*tiles[i + 1][0:4])
            phase1b(ti, P_, smp, qcol, st_cur)
            ckpt('c2')
            if not smp and ti % 4 == 3 and ti < NT - 1:
                attention(ti // 4)
                ckpt('c3')
        attention(3)
        S.pop()
        ckpt('c4')

        S.push()
        wukT = S.sb([128, 4, 256], BF16, 'wukT')
        S.push()
        wukf = S.sb([128, 2, 512], F32, 'wukf')
        S.load(wukf.v, W['w_uk'].rearrange("(c p) n -> p c n", p=128))
        for j in range(4):
            for cc in range(2):
                S.tr(PS[0][:, cc * 128:(cc + 1) * 128], wukf[:, cc, j * 128:(j + 1) * 128], ident.v)
            S.copy(wukT[:, j, :], PS[0][:, 0:256], eng='act')
        S.pop()
        ckpt('c40')
        qlatT = S.sb([128, 2, 8, TS], BF16, 'qlatT')
        for h in range(8):
            j, a = h // 2, h % 2
            for cc in range(2):
                col = (j * 2 + cc) * 64
                S.mm(PS[1 + a][:, col:col + TS],
                     wukT[a * 64:(a + 1) * 64, j, cc * 128:(cc + 1) * 128], qnT_s[a * 64:(a + 1) * 64, j, :])
        for a in range(2):
            S.copy(qlatT.v.r("p c (j a) t -> p a j c t", a=2)[:, a],
                   PS[1 + a].v.r("p (j c t) -> p j c t", j=4, c=2), eng='act')
        ckpt('c41')
        ptt = S.sb([128, NSEQ // 2], I32, 'ptt')
        S.load(ptt.v, ptab.rearrange("(g p) o -> p (g o)", p=128), allow_slow_non_contiguous=True)
        ptf = S.sb([128, NSEQ // 2], F32, 'ptf')
        S.copy(ptf.v, ptt.v)
        io16 = S.sb([128, 16], F32, 'io16')
        S.op('pool', lambda e: e.iota(io16.v.ap, pattern=[[1, 16]], base=0, channel_multiplier=0,
                                      allow_small_or_imprecise_dtypes=True), [], [io16])
        idxf = S.sb([128, NSEQ // 2, 16], F32, 'idxf')
        S.ts(idxf.v, ptf.v.un(2).bc([128, NSEQ // 2, 16]), 16.0, ALU.mult)
        idxc = S.sb([128, NSEQ // 2, 16], I32, 'idxc')
        S.tt(idxc.v, idxf.v, io16.v.un(1).bc([128, NSEQ // 2, 16]), ALU.add)
        ckpt('c42')
        cckv16 = cckv.rearrange("n (j x) -> (n j) x", j=16)
        ckpe4 = ckpe.rearrange("n (j x) -> (n j) x", j=4)
        idx4 = S.sb([128, NSEQ // 2, 4], I32, 'idx4')
        S.ts(idxf[:, :, 0:4], ptf.v.un(2).bc([128, NSEQ // 2, 4]), 4.0, ALU.mult)
        S.tt(idx4.v, idxf[:, :, 0:4], io16[:, 0:4].un(1).bc([128, NSEQ // 2, 4]), ALU.add)
        cosk = S.sb([128, 128, 16], F32, 'cosk')
        sink = S.sb([128, 128, 16], F32, 'sink')
        S.load(cosk.v, c_cosk.rearrange("p (r f) -> p r f", f=16))
        S.load(sink.v, c_sink.rearrange("p (r f) -> p r f", f=16))
        gpe_bc = g_qk_bc[:, 64:96]
        b_s_T = S.sb([128, 4, TS], F32, 'b_s_T')
        RCH = 8
        ckpt('c4a')
        STB = [PS[4], PS[5]]
        ptb_bufs = [S.sb([128, 4, 64], BF16, 'ptb') for _ in range(4)]
        for pb_ in ptb_bufs:
            S.memset(pb_.v.r("p g c -> p (g c)"), 0.0)
        ptb_ctr = [0]
        onesf = S.sb([128, 1], F32, 'onesf')
        S.memset(onesf.v, 1.0)
        CT3 = [PS[2], PS[3], PS[7]]
        def make_prologue(pr):
            sspe = S.rot('sspe', [128, 128], F32, 2)
            kr = S.rot('kr', [128, 128, 32], BF16, 2)
            th = []
            for qtr in range(4):
                rs = slice(qtr * 32, (qtr + 1) * 32)
                box = {}

                def t_gather(qtr=qtr, box=box):
                    KP = S.rot('KP', [128, 32, 32], F32, 1)
                    ix4 = idx4[:, pr, qtr:qtr + 1]
                    S.dma('pool', (lambda KP, ix4: lambda e: e.indirect_dma_start(
                        out=KP.v.ap.rearrange("p r f -> p (r f)"), out_offset=None, in_=ckpe4,
                        in_offset=bass.IndirectOffsetOnAxis(ap=ix4.ap, axis=0)))(KP, ix4), [idx4], [KP])
                    box['KP'] = KP
                    box['ksq'] = S.rot('ksq', [128, 32, 32], F32, 1)
                    box['t1'] = S.rot('kt1', [128, 32, 16], F32, 1)
                    box['t2'] = S.rot('kt2', [128, 32, 16], F32, 1)
                th.append(t_gather)
                th.append(lambda box=box: S.tt(box['ksq'].v, box['KP'].v, box['KP'].v, ALU.mult, eng='pool'))
                th.append(lambda box=box, rs=rs: S.red(sspe[:, rs], box['ksq'].v))
                th.append(lambda box=box: S.tt(box['ksq'].v, box['KP'].v, gpe_bc.un(1).bc([128, 32, 32]), ALU.mult, eng='pool'))
                th.append(lambda box=box, rs=rs: S.tt(box['t1'].v, box['ksq'][:, :, 0:16], cosk[:, rs, :], ALU.mult, eng='pool'))
                th.append(lambda box=box, rs=rs: S.tt(box['t2'].v, box['ksq'][:, :, 16:32], sink[:, rs, :], ALU.mult, eng='pool'))
                th.append(lambda box=box, rs=rs: S.tt(kr[:, rs, 0:16], box['t1'].v, box['t2'].v, ALU.subtract, eng='pool'))
                th.append(lambda box=box, rs=rs: S.tt(box['t1'].v, box['ksq'][:, :, 16:32], cosk[:, rs, :], ALU.mult, eng='pool'))
                th.append(lambda box=box, rs=rs: S.tt(box['t2'].v, box['ksq'][:, :, 0:16], sink[:, rs, :], ALU.mult, eng='pool'))
                th.append(lambda box=box, rs=rs: S.tt(kr[:, rs, 16:32], box['t1'].v, box['t2'].v, ALU.add, eng='pool'))
            return kr, sspe, th

        nxt_pro = make_prologue(0)
        for t_ in nxt_pro[2]:
            t_()
        for pr in range(NSEQ // 2):
            kr, sspe, _ = nxt_pro
            pend = []
            if pr + 1 < NSEQ // 2:
                nxt_pro = make_prologue(pr + 1)
                pend = list(nxt_pro[2])
            oacc = PS[6]
            ckpt('c4b')
            NCH = 128 // RCH
            Cbs, cTs, ssns, ptbs = {}, {}, {}, {}
            state = {'first': True}

            def load_chunk(ch):
                Cb = S.rot('Cb', [128, RCH, 256], BF16, 3)
                ixc = idxc[:, pr, ch:ch + 1]
                S.dma('pool', (lambda Cb, ixc: lambda e: e.indirect_dma_start(
                    out=Cb.v.ap.rearrange("p r c -> p (r c)"), out_offset=None, in_=cckv16,
                    in_offset=bass.IndirectOffsetOnAxis(ap=ixc.ap, axis=0)))(Cb, ixc), [idxc], [Cb])
                Cbs[ch] = Cb

            def st_T(r):
                ch, rl = divmod(r, RCH)
                Cb = Cbs[ch]
                ctp = CT3[r % 3]
                for cc in range(2):
                    S.mm(ctp[:, cc * 128:(cc + 1) * 128], Cb[:, rl, cc * 128:(cc + 1) * 128], identb.v)
                S.mm(ctp[0:32, 256:384], kr[:, r, :], identb.v)
                cT = S.rot('cT', [128, 384], BF16, 4)
                S.copy(cT[:, 0:256], ctp[:, 0:256], eng='dve')
                S.copy(cT[0:32, 256:384], ctp[0:32, 256:384], eng='act')
                cTs[r] = cT

            def st_K(r):
                cT = cTs.pop(r)
                grp, g = divmod(r, 4)
                if g == 0:
                    ssns[grp] = S.rot('ssn', [128, 4, 8], F32, 3)
                knp = PS[r % 2]
                for cc in range(2):
                    S.mm(knp.v, cT[:, cc * 128:(cc + 1) * 128], w_uk[:, cc, :], start=(cc == 0), stop=(cc == 1))
                sqk = S.rot('sqk', [128, 8, 64], BF16, 3)
                S.act(sqk.v.r("p h d -> p (h d)"), knp.v, AF.Square)
                S.red(ssns[grp][:, g, :], sqk.v)
                stb = STB[grp % 2]
                for cc in range(2):
                    S.mm(stb[:, g * 64:(g + 1) * 64], cT[:, cc * 128:(cc + 1) * 128],
                         qlatT[:, cc, :, pr * 8:pr * 8 + 8].r("p h (a q) -> p a h q", a=2),
                         start=(cc == 0), stop=False)
                S.mm(stb[:, g * 64:(g + 1) * 64], cT[0:32, 256:384],
                     qpeT_s[:, :, pr * 8:pr * 8 + 8].r("p h (a q) -> p a h q", a=2), start=False, stop=True)

            def st_G(grp):
                r0 = grp * 4
                stb = STB[grp % 2]
                tot = S.rot('tot', [128, 4, 8], F32, 2)
                S.tt(tot.v, ssns.pop(grp).v, sspe[:, r0:r0 + 4].un(2).bc([128, 4, 8]), ALU.add)
                S.act(tot.v, tot.v, AF.Ln, bias=epsT[:, 0:1], scale=1.0 / 96)
                S.act(tot.v, tot.v, AF.Exp, scale=-0.5)
                snm = S.rot('snm', [128, 4, 8, 4], F32, 2)
                ptb = ptb_bufs[ptb_ctr[0] % 4]
                ptb_ctr[0] += 1
                for a in range(2):
                    pa = slice(a * 64, (a + 1) * 64)
                    S.tt(snm[pa], stb[pa, 0:256].r("p (g a h q) -> p g a h q", g=4, a=2, h=8)[:, :, a, :, :],
                         tot[pa].un(3).bc([64, 4, 8, 4]), ALU.mult)
                    S.act(ptb[pa].r("p g (a h q) -> p g a h q", a=2, h=8)[:, :, a, :, :], snm[pa], AF.Exp,
                          bias=negC[pa, 0:1], scale=SC_MLA)
                ptbs[grp] = ptb

            def st_PV(grp):
                ptb = ptbs.pop(grp)
                for g in range(4):
                    ch, rl = divmod(grp * 4 + g, RCH)
                    Cb = Cbs[ch]
                    S.mm(oacc[0:64, 0:256], ptb[:, g, :], Cb[:, rl, :], start=state['first'], stop=False)
                    state['first'] = False
                    S.op('pe', (lambda o_, l_, r_: lambda e: e.matmul(o_.ap, lhsT=l_.ap, rhs=r_.ap, start=False,
                                                                      stop=False, skip_group_check=True))(
                        oacc[0:64, 256:257], ptb[:, g, :], onesb[:, 0:1]), [ptb, onesb], [oacc])

            load_chunk(0)
            load_chunk(1)
            for r in range(128 + 2):
                if pend and r % 3 == 0:
                    pend.pop(0)()
                if r < 128:
                    st_T(r)
                if r >= 2:
                    rr = r - 2
                    st_K(rr)
                    if rr % 4 == 3:
                        grp = rr // 4
                        st_G(grp)
                        if grp >= 2:
                            st_PV(grp - 2)
                            if (grp - 2) % (RCH // 4) == (RCH // 4) - 1:
                                nxtc = (grp - 2) // (RCH // 4) + 3
                                if nxtc < NCH:
                                    load_chunk(nxtc)
                        if grp == 0 and 2 < NCH:
                            load_chunk(2)
            st_PV(30)
            while pend:
                pend.pop(0)()
            st_PV(31)
            tk8 = slice(pr * 8, pr * 8 + 8)
            for a in range(2):
                sq_ = pr * 2 + a
                tk = slice(sq_ * 4, sq_ * 4 + 4)
                snew = PS[2]
                for h in range(8):
                    S.mm(snew[0:64, h * 4:(h + 1) * 4], kT_s[:, h, :], qT_s[:, h, tk])
                pn = S.rot('pn', [64, 8, 4], BF16, 2)
                S.act(pn.v.r("p h q -> p (h q)"), snew[0:64, 0:32], AF.Exp, bias=negC[0:64, 0:1], scale=SC_MLA)
                S.tt(pn.v, pn.v, bd[:, tk].un(1).bc([64, 8, 4]), ALU.mult)
                S.mm(oacc[a * 32:(a + 1) * 32, 0:257], pn.v.r("p h q -> p (h q)"), Cb_new[:, 0:257], start=False, stop=(a == 1))
            ol = S.rot('ol', [64, 264], F32, 2)
            S.copy(ol[:, 0:257], oacc[0:64, 0:257], eng='act')
            rl_ = S.rot('rl_', [64, 1], F32, 2)
            S.recip(rl_.v, ol[:, 256:257])
            S.ts(ol[:, 0:256], ol[:, 0:256], rl_[:, 0:1], ALU.mult)
            olT = S.rot('olT', [128, 2, 64], BF16, 2)
            transposes(lambda i: olT[:, i, :], [ol[:, i * 128:(i + 1) * 128] for i in range(2)], 64, PS[2])
            bps = PS[3]
            for h in range(8):
                j, par = h // 2, h % 2
                for cc in range(2):
                    S.mm(bps[par * 64:(par + 1) * 64, j * 8:(j + 1) * 8], w_uv[:, cc, h * 64:(h + 1) * 64],
                         olT[:, cc, :].r("p (a h q) -> p h a q", a=2, h=8)[:, h, :, :], start=(cc == 0), stop=(cc == 1))
            S.copy(b_s_T[:, :, tk8], bps[:, 0:32].r("p (j q) -> p j q", j=4), eng='act')
            ckpt('c5')
        bsq = S.sb([128, 4, TS], BF16, 'bsq')
        S.tt(bsq.v, b_s_T.v, b_s_T.v, ALU.mult)
        for j in range(4):
            S.mm(PS[0][:, 0:TS], onesb.v, bsq[:, j, :], start=(j == 0), stop=(j == 3))
        rbs = S.sb([128, TS], F32, 'rbs')
        S.act(rbs.v, PS[0][:, 0:TS], AF.Ln, bias=epsT[:, 0:1], scale=1.0 / 512)
        S.act(rbs.v, rbs.v, AF.Exp, scale=-0.5)
        S.tt(b_s_T.v, b_s_T.v, rbs.v.un(1).bc([128, 4, TS]), ALU.mult)
        for j in range(4):
            S.ts(bT[:, j, TP:TP + TS], b_s_T[:, j, :], g_ob_col[:, j:j + 1], ALU.mult)
        if stop == 'c6':
            S.store(y_s.rearrange("t (two d) -> (t two) d", two=2)[:, 0:256], b_s_T.v.r("p j t -> p (j t)"))
        S.pop()
        S.pop()
        ckpt('c6')

        dep_best = {}
        for tl in arena_alias:
            for d in list(tl.rd) + ([tl.lw] if tl.lw is not None else []):
                if dep_best.get(d[0], (0, None))[0] < d[1]:
                    dep_best[d[0]] = (d[1], d[2])
        arena_deps = [(k, v, 'freed') for k, (v, e) in dep_best.items()]
        xres = []
        for t in range(NT + 1):
            tl = Tl(arena.t[:, t * 1024:(t + 1) * 1024], 'xres%d' % t)
            tl.rd = list(arena_deps)
            xres.append(tl)
        S.push()
        w_oa = load_w('w_o', D, D, tname='w_o')
        for t in range(NT + 1):
            is_s = (t == NT)
            P = TS if is_s else 128
            tc0 = t * 128
            x_t = S.rot('x_t3', [128, D], F32, 2)
            S.load(x_t[0:P], xs[:, :] if is_s else xp[tc0:tc0 + P, :])
            for n in range(2):
                ps = PS[(2 * t + n) % 4]
                for k in range(8):
                    src = aT if k < 4 else bT
                    S.mm(ps[0:P, :], src[:, k % 4, tc0:tc0 + P], w_oa[:, k, n * 512:(n + 1) * 512],
                         start=(k == 0), stop=(k == 7))
                S.tt(xres[t][0:P, n * 512:(n + 1) * 512], ps[0:P, :], x_t[0:P, n * 512:(n + 1) * 512], ALU.add)
        S.pop()
        S.pop()
        ckpt('c7')

        S.push()
        g_min_col = col_vec('g_mem_in', D)
        w_mk = load_w('w_mk', D, 512)
        w_mv = load_w('w_mv', D, 512)
        for mt in range(2):
            m_t = S.rot('m_t', [128, D], F32, 2)
            S.load(m_t.v, memp[mt * 128:(mt + 1) * 128, :])
            ri = rinv_of(m_t.v, D, 128)
            S.ts(m_t.v, m_t.v, ri[:, 0:1], ALU.mult)
            mT = S.rot('mT', [128, 8, 128], BF16, 2)
            transposes(lambda i: mT[:, i, :], [m_t[:, i * 128:(i + 1) * 128] for i in range(8)], 128, PS[0],
                       scale=lambda i: g_min_col[:, i:i + 1])
            for k in range(8):
                S.mm(PS[1].v, mT[:, k, :], w_mk[:, k, :], start=(k == 0), stop=(k == 7))
            for k in range(8):
                S.mm(PS[2].v, mT[:, k, :], w_mv[:, k, :], start=(k == 0), stop=(k == 7))
            mk_sb = S.rot('mk_sb', [128, 4, 128], F32, 2)
            S.copy(mk_sb.v.r("p h d -> p (h d)"), PS[1].v, eng='act')
            rg = group_rinv(mk_sb.v, 4, 128, 128)
            S.tt(mk_sb.v, mk_sb.v, rg.v.un(2).bc([128, 4, 128]), ALU.mult)
            S.tt(mk_sb.v, mk_sb.v, g_mk_bc.v.un(1).bc([128, 4, 128]), ALU.mult)
            S.store(o_pmk[mt * 128:(mt + 1) * 128, :], mk_sb.v.r("p h d -> p (h d)"))
            transposes(lambda h: mkT[:, h, mt * 128:(mt + 1) * 128], [mk_sb[:, h, :] for h in range(4)], 128, PS[3])
            mv_sb = S.rot('mv_sb', [128, 512], F32, 2)
            S.copy(mv_sb.v, PS[2].v, eng='act')
            S.store(o_pmv[mt * 128:(mt + 1) * 128, :], mv_sb.v)
            S.copy(mvb[:, mt, :], mv_sb.v, eng='pool')
        S.pop()

        ckpt('c8')
        blocks = [(i * 512, 512, False) for i in range(4)] + [(TP, TS, True)]

        def norm_T(dst, c0, NB, gcol):
            ntile = max(1, NB // 128)
            P = min(NB, 128)
            for t in range(ntile):
                xr = xres[c0 // 128 + t]
                ri = rinv_of(xr[0:P, :], D, P)
                xh = S.rot('xh3', [128, D], F32, 2)
                S.ts(xh[0:P], xr[0:P, :], ri[0:P, 0:1], ALU.mult)
                transposes(lambda i: dst[:, i, t * 128:t * 128 + P], [xh[0:P, i * 128:(i + 1) * 128] for i in range(8)],
                           P, [PS[0], PS[1]], scale=lambda i: gcol[:, i:i + 1])

        S.push()
        g_mx_col = col_vec('g_mem_x', D)
        w_mq = load_w('w_mq', D, 512)
        w_mo = load_w('w_mo', 512, D)
        for (c0, NB, is_s) in blocks:
            ntile = max(1, NB // 128)
            P = min(NB, 128)
            hT = S.rot('hT4', [128, 8, 512], BF16, 1)
            norm_T(hT, c0, NB, g_mx_col)
            qmT = S.rot('qmT', [128, 4, 512], BF16, 1)
            for t in range(ntile):
                ps = PS[2 + t % 2]
                for k in range(8):
                    S.mm(ps[0:P, :], hT[:, k, t * 128:t * 128 + P], w_mq[:, k, :], start=(k == 0), stop=(k == 7))
                qm = S.rot('qm', [128, 4, 128], F32, 2)
                S.copy(qm[0:P].r("p h d -> p (h d)"), ps[0:P, :], eng='act')
                rg = group_rinv(qm[0:P], 4, 128, P)
                S.tt(qm[0:P], qm[0:P], rg[0:P].un(2).bc([P, 4, 128]), ALU.mult)
                S.tt(qm[0:P], qm[0:P], g_mq_bc[0:P].un(1).bc([P, 4, 128]), ALU.mult)
                transposes(lambda h: qmT[:, h, t * 128:t * 128 + P], [qm[0:P, h, :] for h in range(4)], P, PS[4 + t % 2])
            omT = S.rot('omT', [128, 4, 512], BF16, 1)
            if not is_s:
                for h in range(4):
                    o_ps, d_ps = PS[4], PS[5]
                    for kb in range(2):
                        st = PS[kb]
                        S.mm(st.v, mkT[:, h, kb * 128:(kb + 1) * 128], qmT[:, h, :])
                        pm = S.rot('pm', [128, 512], BF16, 2)
                        S.act(pm.v, st.v, AF.Exp, bias=negCm[:, 0:1], scale=SC_MEM)
                        S.mm(o_ps.v, mvb[:, kb, h * 128:(h + 1) * 128], pm.v, start=(kb == 0), stop=(kb == 1))
                        S.mm(d_ps.v, onesb.v, pm.v, start=(kb == 0), stop=(kb == 1))
                    rden = S.rot('rden', [128, 512], F32, 1)
                    S.act(rden.v, d_ps.v, AF.Ln)
                    S.act(rden.v, rden.v, AF.Exp, scale=-1.0)
                    S.tt(omT[:, h, :], o_ps.v, rden.v, ALU.mult)
            else:
                for sq_ in range(NSEQ):
                    tk = slice(sq_ * 4, sq_ * 4 + 4)
                    mk_s = S.rot('mk_s', [128, 2, 512], F32, 2)
                    S.load(mk_s.v, cmk[sq_].rearrange("(b p) f -> p b f", p=128))
                    mv_s = S.rot('mv_s', [128, 2, 512], BF16, 2)
                    S.load(mv_s.v, cmv[sq_].rearrange("(b p) f -> p b f", p=128), q='pool')
                    mkT_s = S.rot('mkT_s', [128, 4, 256], BF16, 2)
                    for kb in range(2):
                        for h in range(4):
                            S.tr(PS[kb][:, h * 128:(h + 1) * 128], mk_s[:, kb, h * 128:(h + 1) * 128], ident.v)
                        S.copy(mkT_s[:, :, kb * 128:(kb + 1) * 128], PS[kb].v.r("p (h k) -> p h k", h=4), eng='act')
                    st = PS[2]
                    for kb in range(2):
                        for h in range(4):
                            cl = (kb * 4 + h) * 4
                            S.mm(st[:, cl:cl + 4], mkT_s[:, h, kb * 128:(kb + 1) * 128], qmT[:, h, tk])
                    pm = S.rot('pm_s', [128, 2, 4, 4], BF16, 2)
                    S.act(pm.v.r("p b h q -> p (b h q)"), st[:, 0:32], AF.Exp, bias=negCm[:, 0:1], scale=SC_MEM)
                    o_ps, d_ps = PS[4], PS[5]
                    for h in range(4):
                        for kb in range(2):
                            S.mm(o_ps[:, h * 4:(h + 1) * 4], mv_s[:, kb, h * 128:(h + 1) * 128], pm[:, kb, h, :],
                                 start=(kb == 0), stop=(kb == 1))
                    for kb in range(2):
                        S.mm(d_ps[:, 0:16], onesb.v, pm[:, kb, :, :].r("p h q -> p (h q)"), start=(kb == 0), stop=(kb == 1))
                    rden = S.rot('rden_s', [128, 16], F32, 2)
                    S.recip(rden.v, d_ps[:, 0:16])
                    S.tt(omT[:, :, tk], o_ps[:, 0:16].r("p (h q) -> p h q", h=4), rden.v.r("p (h q) -> p h q", h=4), ALU.mult)
            for t in range(ntile):
                xr = xres[c0 // 128 + t]
                for n in range(2):
                    ps = PS[6 + n]
                    for k in range(4):
                        S.mm(ps[0:P, :], omT[:, k, t * 128:t * 128 + P], w_mo[:, k, n * 512:(n + 1) * 512],
                             start=(k == 0), stop=(k == 3))
                    S.tt(xr[0:P, n * 512:(n + 1) * 512], ps[0:P, :], xr[0:P, n * 512:(n + 1) * 512], ALU.add)
        S.pop()

        ckpt('c9')
        S.push()
        g_ffn_col = col_vec('g_ffn', D)
        wc_col = S.sb([128, 3, NFC], F32, 'wc_col')
        S.load(wc_col.v, W['w_conv'].rearrange("j (c p) -> p j c", p=128), allow_slow_non_contiguous=True)
        bc_col = S.sb([128, NFC], F32, 'bc_col')
        S.load(bc_col.v, W['b_conv'].rearrange("(c p) -> p c", p=128), allow_slow_non_contiguous=True)
        carry = S.sb([128, NFC, 2], F32, 'carry')
        S.memset(carry.v, 0.0)
        w_down = load_w('w_down', DFF, D)
        srcw = W['w_up'].rearrange("(c p) n -> p c n", p=128)
        for (c0, NB, is_s) in blocks:
            ntile = max(1, NB // 128)
            P = min(NB, 128)
            hT = S.rot('hT5', [128, 8, 512], BF16, 1)
            norm_T(hT, c0, NB, g_ffn_col)
            aF = S.rot('aF', [128, NFC, 512], BF16, 1)
            for fc in range(NFC):
                wg = S.rot('wg', [128, 8, 128], BF16, 4)
                wv = S.rot('wv', [128, 8, 128], BF16, 4)
                S.load(wg.v, srcw[:, :, fc * 128:(fc + 1) * 128], q='pool')
                S.load(wv.v, srcw[:, :, DFF + fc * 128:DFF + (fc + 1) * 128], q='pool')
                gps, vps = PS[(fc % 2) * 2], PS[(fc % 2) * 2 + 1]
                for k in range(8):
                    S.mm(gps[:, 0:NB], wg[:, k, :], hT[:, k, 0:NB], start=(k == 0), stop=(k == 7))
                for k in range(8):
                    S.mm(vps[:, 0:NB], wv[:, k, :], hT[:, k, 0:NB], start=(k == 0), stop=(k == 7))
                w0, w1, w2 = (wc_col[:, j, fc:fc + 1] for j in range(3))
                cv = S.rot('cv', [128, 512], F32, 2)
                if not is_s:
                    gb = S.rot('gb', [128, 514], F32, 2)
                    S.copy(gb[:, 0:2], carry[:, fc, :], eng='dve')
                    S.copy(gb[:, 2:514], gps.v, eng='act')
                    S.copy(carry[:, fc, :], gb[:, 512:514], eng='dve')
                    S.ts(cv.v, gb[:, 0:512], w0, ALU.mult, bc_col[:, fc:fc + 1], ALU.add)
                    S.op('dve', (lambda cv, gb, w1: lambda e: e.scalar_tensor_tensor(
                        out=cv.v.ap, in0=gb[:, 1:513].ap, scalar=w1.ap, in1=cv.v.ap, op0=ALU.mult, op1=ALU.add))(cv, gb, w1),
                        [gb, wc_col, cv], [cv])
                    S.op('dve', (lambda cv, gb, w2: lambda e: e.scalar_tensor_tensor(
                        out=cv.v.ap, in0=gb[:, 2:514].ap, scalar=w2.ap, in1=cv.v.ap, op0=ALU.mult, op1=ALU.add))(cv, gb, w2),
                        [gb, wc_col, cv], [cv])
                    if c0 + NB == TP:
                        S.tr(PS[6][0:2, 0:128], gb[:, 512:514], ident.v)
                        pcs = S.rot('pcs', [2, 128], F32, 2)
                        S.copy(pcs.v, PS[6][0:2, 0:128], eng='act')
                        S.store(o_pconv[:, fc * 128:(fc + 1) * 128], pcs.v)
                else:
                    cs_t = S.rot('cs_t', [32, 128], F32, 2)
                    S.load(cs_t.v, cst[:, fc * 128:(fc + 1) * 128])
                    S.tr(PS[6][:, 0:32], cs_t.v, ident[0:32, 0:32])
                    gb = S.rot('gbs', [128, NSEQ, 6], F32, 2)
                    S.copy(gb[:, :, 0:2], PS[6][:, 0:32].r("p (s j) -> p s j", j=2), eng='act')
                    S.copy(gb[:, :, 2:6], gps[:, 0:TS].r("p (s t) -> p s t", t=4), eng='act')
                    cv3 = cv[:, 0:TS].r("p (s t) -> p s t", t=4)
                    S.ts(cv3, gb[:, :, 0:4], w0, ALU.mult, bc_col[:, fc:fc + 1], ALU.add)
                    tmpc = S.rot('tmpc', [128, NSEQ, 4], F32, 2)
                    S.ts(tmpc.v, gb[:, :, 1:5], w1, ALU.mult)
                    S.tt(cv3, cv3, tmpc.v, ALU.add)
                    S.ts(tmpc.v, gb[:, :, 2:6], w2, ALU.mult)
                    S.tt(cv3, cv3, tmpc.v, ALU.add)
                    gl = S.rot('gl', [128, NSEQ, 2], F32, 2)
                    S.copy(gl.v, gb[:, :, 4:6], eng='dve')
                    S.tr(PS[7][0:32, 0:128], gl.v.r("p s j -> p (s j)"), ident.v)
                    scs = S.rot('scs', [32, 128], F32, 2)
                    S.copy(scs.v, PS[7][0:32, 0:128], eng='act')
                    S.store(o_sconv[:, fc * 128:(fc + 1) * 128], scs.v)
                S.act(cv[:, 0:NB], cv[:, 0:NB], AF.Silu)
                S.tt(aF[:, fc, 0:NB], cv[:, 0:NB], vps[:, 0:NB], ALU.mult)
            for t in range(ntile):
                tc0 = c0 + t * 128
                xr = xres[c0 // 128 + t]
                y_t = S.rot('y_t', [128, D], F32, 2)
                for n in range(2):
                    ps = PS[4 + n]
                    for fc in range(NFC):
                        S.mm(ps[0:P, :], aF[:, fc, t * 128:t * 128 + P], w_down[:, fc, n * 512:(n + 1) * 512],
                             start=(fc == 0), stop=(fc == NFC - 1))
                    S.tt(y_t[0:P, n * 512:(n + 1) * 512], ps[0:P, :], xr[0:P, n * 512:(n + 1) * 512], ALU.add)
                if is_s:
                    S.store(y_s[:, :], y_t[0:P])
                else:
                    S.store(y_p[tc0:tc0 + P, :], y_t[0:P])
        S.pop()
        S.finish()
    return nc


def _consts():
    half = 16
    inv_freq = (10000.0 ** (-np.arange(half, dtype=np.float32) / half)).astype(np.float32)

    def cs(pos):
        ang = pos.astype(np.float32)[..., None] * inv_freq
        return np.cos(ang).astype(np.float32), np.sin(ang).astype(np.float32)
    p = np.arange(128)
    cp, sp = cs(np.arange(NT)[None, :] * 128 + p[:, None])
    cS, sS = cs(PAST + (np.arange(TS) % 4))
    ck, sk = cs((p[:, None] % NPG) * 128 + np.arange(128)[None, :])
    tri = (p[:, None] <= p[None, :]).astype(np.float32)
    t64 = np.arange(64)
    bd = ((t64[:, None] // 4 == t64[None, :] // 4) & (t64[:, None] % 4 <= t64[None, :] % 4)).astype(np.float32)
    return {
        'c_ident': np.eye(128, dtype=np.float32), 'c_tri': tri, 'c_bd': bd,
        'c_cosp': cp.reshape(128, -1), 'c_sinp': sp.reshape(128, -1),
        'c_coss': cS, 'c_sins': sS,
        'c_cosk': ck.reshape(128, -1), 'c_sink': sk.reshape(128, -1),
    }


_CACHE = {}


def kernel(**inp):
    f = lambda a: np.ascontiguousarray(np.asarray(a))
    n_phys = inp['cache_ckv'].shape[1]
    if n_phys not in _CACHE:
        import os as _os
        _CACHE[n_phys] = build(n_phys, _os.environ.get('MK_STOP'))
    nc = _CACHE[n_phys]
    consts = _consts()
    cckv = f(inp['cache_ckv']).reshape(n_phys, 128 * 256)
    ckpe = f(inp['cache_kpe']).reshape(n_phys, 128 * 32)
    wnames = ['g_mix', 'w_in', 'ln_v_g', 'ln_v_b', 'w_s', 'b_s', 'g_q_a', 'w_uq', 'g_kv_a', 'w_uk', 'w_uv', 'g_qk_q',
              'g_qk_k', 'g_out_a', 'g_out_b', 'w_o', 'g_mem_x', 'g_mem_in', 'w_mq', 'w_mk', 'w_mv', 'g_mq', 'g_mk',
              'w_mo', 'g_ffn', 'w_up', 'w_conv', 'b_conv', 'w_down']
    shared = {nm: f(inp[nm])[0].reshape(-1) if inp[nm].ndim == 2 else f(inp[nm])[0] for nm in wnames}
    shared['ln_v_g'] = shared['ln_v_g'].reshape(-1)
    shared['ln_v_b'] = shared['ln_v_b'].reshape(-1)
    shared.update(consts)
    shared['cckv'] = cckv
    shared['ckpe'] = ckpe
    in_maps = []
    for c in range(NCORES):
        sl = slice(c * NSEQ, (c + 1) * NSEQ)
        m = dict(shared)
        m['xp'] = f(inp['x_prompt'][c])
        m['xs'] = f(inp['x_sample'][sl]).reshape(TS, D)
        m['cmk'] = f(inp['cache_mem_k'][0, sl]).reshape(NSEQ, 256, 512)
        m['cmv'] = f(inp['cache_mem_v'][0, sl]).reshape(NSEQ, 256, 512)
        m['cst'] = f(inp['state_ffn_conv'][0, sl]).reshape(NSEQ * 2, DFF)
        m['ptab'] = f(inp['page_table'][sl]).reshape(NSEQ * NPG, 1).astype(np.int32)
        m['memp'] = f(inp['mem_prompt'][c])
        in_maps.append(m)
    res = run_bass_kernel_spmd(nc, in_maps, core_ids=list(range(NCORES))).results
    cat = lambda k: np.concatenate([r[k] for r in res], axis=0)
    y_p = cat('y_p').reshape(8, 2048, D)
    y_s = cat('y_s').reshape(128, 4, D)
    return (y_p, y_s,
            cat('o_pckv').reshape(1, 8, 2048, 256), cat('o_pkpe').reshape(1, 8, 2048, 32),
            cat('o_pmk').reshape(1, 8, 256, 4, 128), cat('o_pmv').reshape(1, 8, 256, 4, 128),
            cat('o_pconv').reshape(1, 8, 2, DFF),
            cat('o_sckv').reshape(1, 128, 4, 256), cat('o_skpe').reshape(1, 128, 4, 32),
            cat('o_scv').reshape(1, 128, 4, 8, 64), cat('o_sconv').reshape(1, 128, 2, DFF))
```
